# Optimizing a Trainium2 kernel written in Bass

```python
import math
import jax, jax.numpy as jnp
from jax import lax
import numpy as np

D_MODEL = 1024
BATCH = 16
SEQ = 4096
DEPTH = 2
DEC_BATCH = 4
DEC_SEQ = 4096
PAST_LEN = 128

D_MIX = D_MODEL
N_DIR = 2
NORM_EPS = 1e-6
A_HEADS = 4
A_NOPE = 64
A_ROPE = 32
A_V = 64
A_Q_LORA = 384
A_KV_LORA = 128
A_W = A_HEADS * A_V
ATTN_BLOCK = 128
ROPE_BASE = 10000.0
RW_HEADS = 4
RW_N = 64
RW_W = RW_HEADS * RW_N
RW_LORA = 32
RW_LN_EPS = 64e-5
M_HEADS = 4
M_P = 64
M_W = M_HEADS * M_P
M_GROUPS = 2
M_N = 128
M_CONV = 4
M_CONV_CH = M_W + 2 * M_GROUPS * M_N
M_CHUNK = 128
GD_HEADS = 4
GD_DK = 64
GD_DV = 64
GD_W = GD_HEADS * GD_DV
GD_QKV = 2 * GD_HEADS * GD_DK + GD_W
GD_CONV = 4
GD_CHUNK = 64
IN_SIZES = (A_Q_LORA, A_KV_LORA, A_ROPE, 3 * RW_W, 2 * RW_LORA, 2 * RW_LORA, M_CONV_CH, N_DIR * M_HEADS, GD_QKV, N_DIR * GD_HEADS, N_DIR * GD_HEADS, A_W + RW_W + M_W + GD_W)
P_IN = sum(IN_SIZES)

kernel_name = 'hybrid_bidir_mla_rwkv7_ssd_gdn_encoder'


def rms_norm(x, g, eps=NORM_EPS):
    xf = x.astype(jnp.float32)
    y = xf * lax.rsqrt(jnp.mean(xf * xf, axis=-1, keepdims=True) + eps)
    return (y * g.astype(jnp.float32)).astype(x.dtype)


def l2_normalize(x, eps=1e-6):
    xf = x.astype(jnp.float32)
    return xf * lax.rsqrt(jnp.sum(xf * xf, axis=-1, keepdims=True) + eps)


def split_cols(t, sizes):
    return jnp.split(t, np.cumsum(sizes)[:-1].tolist(), axis=-1)


def flip_time(t, d):
    return t[:, ::-1] if d == 1 else t


def centred_dwconv(x, w):
    k = w.shape[0]
    left = (k - 1) // 2
    xp = jnp.pad(x, ((0, 0), (left, k - 1 - left), (0, 0)))
    return lax.conv_general_dilated(xp, w[:, None, :].astype(x.dtype), (1,), 'VALID', dimension_numbers=('NWC', 'WIO', 'NWC'), feature_group_count=x.shape[-1])


def rope_cos_sin(T, dim):
    inv_freq = 1.0 / (ROPE_BASE ** (jnp.arange(0, dim, 2, dtype=jnp.float32) / dim))
    ang = jnp.arange(T, dtype=jnp.float32)[:, None] * inv_freq[None, :]
    return jnp.cos(ang), jnp.sin(ang)


def apply_rope(x, cos, sin):
    half = x.shape[-1] // 2
    x1, x2 = x[..., :half], x[..., half:]
    return jnp.concatenate([x1 * cos - x2 * sin, x2 * cos + x1 * sin], axis=-1).astype(x.dtype)


def mla_mixer(cq, ckv, k_rope, p):
    B, T, _ = cq.shape
    q = (rms_norm(cq, p['a_g_q']) @ p['a_w_uq']).reshape(B, T, A_HEADS, A_NOPE + A_ROPE)
    kv = (rms_norm(ckv, p['a_g_kv']) @ p['a_w_ukv']).reshape(B, T, A_HEADS, A_NOPE + A_V)
    q_nope, q_rope = q[..., :A_NOPE], q[..., A_NOPE:]
    k_nope, v = kv[..., :A_NOPE], kv[..., A_NOPE:]
    cos, sin = rope_cos_sin(T, A_ROPE)
    q_rope = apply_rope(q_rope, cos[:, None, :], sin[:, None, :])
    k_rope = apply_rope(k_rope, cos, sin)
    scale = (A_NOPE + A_ROPE) ** -0.5
    nb = T // ATTN_BLOCK
    qn = q_nope.reshape(B, nb, ATTN_BLOCK, A_HEADS, A_NOPE).swapaxes(0, 1)
    qr = q_rope.reshape(B, nb, ATTN_BLOCK, A_HEADS, A_ROPE).swapaxes(0, 1)

    def attend_block(blk):
        qn_b, qr_b = blk
        s = jnp.einsum('bqhd,bkhd->bhqk', qn_b, k_nope) + jnp.einsum('bqhd,bkd->bhqk', qr_b, k_rope)
        prob = jax.nn.softmax(s.astype(jnp.float32) * scale, axis=-1)
        return jnp.einsum('bhqk,bkhd->bqhd', prob.astype(v.dtype), v)

    o = lax.map(attend_block, (qn, qr))
    return o.swapaxes(0, 1).reshape(B, T, A_W)


def rwkv7_scan(r, w, k, v, a, b):
    f32 = jnp.float32
    xs = tuple(jnp.moveaxis(t.astype(f32), 2, 0) for t in (r, w, k, v, a, b))

    def step(S, inp):
        r_t, w_t, k_t, v_t, a_t, b_t = inp
        sa = jnp.einsum('...ij,...j->...i', S, a_t)
        S = S * w_t[..., None, :] + sa[..., :, None] * b_t[..., None, :] + v_t[..., :, None] * k_t[..., None, :]
        return S, jnp.einsum('...ij,...j->...i', S, r_t)

    S0 = jnp.zeros(r.shape[:2] + r.shape[3:] + (r.shape[-1],), f32)
    _, y = lax.scan(step, S0, xs)
    return jnp.moveaxis(y, 0, 2)


def rwkv7_mixer(rkv, wa_f, wa_b, z, p):
    B, T, _ = rkv.shape
    f32 = jnp.float32
    cur = jnp.stack([jnp.concatenate([rkv, wa_f], axis=-1), jnp.concatenate([rkv, wa_b], axis=-1)[:, ::-1]], axis=1)
    prev = jnp.pad(cur[:, :, :-1], ((0, 0), (0, 0), (1, 0), (0, 0)))
    mixed = cur + (prev - cur) * p['rw_mu'][None, :, None, :]
    r, k, v, xw, xa = split_cols(mixed, (RW_W, RW_W, RW_W, RW_LORA, RW_LORA))
    w_log = -jax.nn.softplus(-(p['rw_w0'][None, :, None, :] + jnp.einsum('bdtr,drc->bdtc', jnp.tanh(xw), p['rw_w2']))) - 0.5
    decay = jnp.exp(-jnp.exp(w_log.astype(f32)))
    a_lr = jax.nn.sigmoid((p['rw_a0'][None, :, None, :] + jnp.einsum('bdtr,drc->bdtc', xa, p['rw_a2'])).astype(f32))
    heads = lambda t: t.reshape(B, N_DIR, T, RW_HEADS, RW_N)
    r, k, v, decay, a_lr = heads(r), heads(k), heads(v), heads(decay), heads(a_lr)
    kk = l2_normalize(k * p['rw_k_k'].reshape(RW_HEADS, RW_N))
    k = k * (1.0 + (a_lr - 1.0) * p['rw_k_a'].reshape(RW_HEADS, RW_N))
    y = rwkv7_scan(r, decay, k, v, -kk, kk * a_lr)
    bonus = jnp.sum(r * k * p['rw_r_k'], axis=-1, keepdims=True) * v
    y = y[:, 0] + y[:, 1, ::-1]
    bonus = (bonus[:, 0] + bonus[:, 1, ::-1]).astype(f32)
    mu = jnp.mean(y, axis=-1, keepdims=True)
    var = jnp.mean(jnp.square(y - mu), axis=-1, keepdims=True)
    y = ((y - mu) * lax.rsqrt(var + RW_LN_EPS)).reshape(B, T, RW_W) * p['rw_ln_w'] + p['rw_ln_b']
    y = y + bonus.reshape(B, T, RW_W)
    return y * jax.nn.silu(z.astype(f32))


def ssd_chunked(x, a, b_in, c_in):
    f32 = jnp.float32
    Bsz, T, H, P = x.shape
    G, N = b_in.shape[2], b_in.shape[3]
    K = H // G
    L = M_CHUNK
    nc = T // L
    x = x.astype(f32).reshape(Bsz, nc, L, G, K, P)
    b_in = b_in.astype(f32).reshape(Bsz, nc, L, G, N)
    c_in = c_in.astype(f32).reshape(Bsz, nc, L, G, N)
    a_cs = jnp.cumsum(a.astype(f32).reshape(Bsz, nc, L, G, K), axis=2).transpose(0, 1, 3, 4, 2)
    causal = jnp.tril(jnp.ones((L, L), bool))
    decay = jnp.exp(jnp.where(causal, a_cs[..., :, None] - a_cs[..., None, :], -jnp.inf))
    cb = jnp.einsum('bclgn,bcsgn->bcgls', c_in, b_in)
    y_diag = jnp.einsum('bcgkls,bcsgkp->bclgkp', cb[:, :, :, None] * decay, x)
    to_lgk = lambda t: t.transpose(0, 1, 4, 2, 3)
    states = jnp.einsum('bclgn,bclgkp->bcgkpn', b_in, x * to_lgk(jnp.exp(a_cs[..., -1:] - a_cs))[..., None])
    chunk_decay = jnp.exp(a_cs[..., -1])

    def step(s, inp):
        st, dec = inp
        return s * dec[..., None, None] + st, s

    _, prev = lax.scan(step, jnp.zeros((Bsz, G, K, P, N), f32), (jnp.moveaxis(states, 1, 0), jnp.moveaxis(chunk_decay, 1, 0)))
    prev = jnp.moveaxis(prev, 0, 1)
    y_off = jnp.einsum('bclgn,bcgkpn->bclgkp', c_in, prev) * to_lgk(jnp.exp(a_cs))[..., None]
    return (y_diag + y_off).reshape(Bsz, T, H, P)


def mamba2_mixer(xbc, dt_raw, z, p):
    B, T, _ = xbc.shape
    f32 = jnp.float32
    xbc = jax.nn.silu(centred_dwconv(xbc, p['m_conv_w']) + p['m_conv_b'])
    xs, b_in, c_in = split_cols(xbc, (M_W, M_GROUPS * M_N, M_GROUPS * M_N))
    xs = xs.reshape(B, T, M_HEADS, M_P).astype(f32)
    b_in = b_in.reshape(B, T, M_GROUPS, M_N)
    c_in = c_in.reshape(B, T, M_GROUPS, M_N)
    dt = jax.nn.softplus((dt_raw.reshape(B, T, N_DIR, M_HEADS) + p['m_dt_bias']).astype(f32))
    a = -jnp.exp(p['m_a_log'].astype(f32))
    y = xs * p['m_d'].astype(f32)[:, None]
    for d in range(N_DIR):
        y_d = ssd_chunked(flip_time(xs * dt[:, :, d, :, None], d), flip_time(dt[:, :, d] * a[d], d), flip_time(b_in, d), flip_time(c_in, d))
        y = y + flip_time(y_d, d)
    return rms_norm(y.reshape(B, T, M_W) * jax.nn.silu(z.astype(f32)), p['m_norm_w'])


def gated_delta_chunked(q, k, v, g, beta):
    f32 = jnp.float32
    Bsz, T, H, K = q.shape
    V = v.shape[-1]
    L = GD_CHUNK
    nc = T // L

    def chunks(t):
        t = t.astype(f32).reshape((Bsz, nc, L, H) + t.shape[3:])
        return jnp.moveaxis(t, 3, 1)

    q = chunks(q) * (K ** -0.5)
    k, v, g, beta = chunks(k), chunks(v), chunks(g), chunks(beta)
    gc = jnp.cumsum(g, axis=-1)
    incl = jnp.tril(jnp.ones((L, L), bool))
    strict = jnp.tril(jnp.ones((L, L), bool), -1)
    decay = jnp.exp(jnp.where(incl, gc[..., :, None] - gc[..., None, :], -jnp.inf))
    kb = k * beta[..., None]
    m = jnp.where(strict, jnp.einsum('bhcld,bhcsd->bhcls', kb, k) * decay, 0.0)
    tmat = m + jnp.eye(L, dtype=f32)
    rhs = jnp.concatenate([v * beta[..., None], kb * jnp.exp(gc)[..., None]], axis=-1)
    sol = lax.linalg.triangular_solve(tmat, rhs, left_side=True, lower=True)
    u, w = sol[..., :V], sol[..., V:]
    qk = jnp.einsum('bhcld,bhcsd->bhcls', q, k) * decay
    qg = q * jnp.exp(gc)[..., None]
    g_last = gc[..., -1]
    kd = k * jnp.exp(g_last[..., None] - gc)[..., None]

    def step(S, inp):
        u_c, w_c, qg_c, qk_c, kd_c, gl_c = inp
        v_new = u_c - jnp.einsum('bhlk,bhkv->bhlv', w_c, S)
        o = jnp.einsum('bhlk,bhkv->bhlv', qg_c, S) + jnp.einsum('bhls,bhsv->bhlv', qk_c, v_new)
        S = S * jnp.exp(gl_c)[..., None, None] + jnp.einsum('bhlk,bhlv->bhkv', kd_c, v_new)
        return S, o

    xs = tuple(jnp.moveaxis(t, 2, 0) for t in (u, w, qg, qk, kd, g_last))
    _, o = lax.scan(step, jnp.zeros((Bsz, H, K, V), f32), xs)
    o = jnp.moveaxis(o, 0, 2)
    return jnp.moveaxis(o, 1, 3).reshape(Bsz, T, H, V)


def gated_deltanet_mixer(qkv, b_raw, a_raw, z, p):
    B, T, _ = qkv.shape
    f32 = jnp.float32
    qkv = jax.nn.silu(centred_dwconv(qkv, p['gd_conv_w']))
    q, k, v = split_cols(qkv, (GD_HEADS * GD_DK, GD_HEADS * GD_DK, GD_W))
    q = l2_normalize(q.reshape(B, T, GD_HEADS, GD_DK))
    k = l2_normalize(k.reshape(B, T, GD_HEADS, GD_DK))
    v = v.reshape(B, T, GD_HEADS, GD_DV).astype(f32)
    beta = jax.nn.sigmoid(b_raw.reshape(B, T, N_DIR, GD_HEADS).astype(f32))
    g = -jnp.exp(p['gd_a_log'].astype(f32)) * jax.nn.softplus((a_raw.reshape(B, T, N_DIR, GD_HEADS) + p['gd_dt_bias']).astype(f32))
    o = gated_delta_chunked(q, k, v, g[:, :, 0], beta[:, :, 0])
    o = o + flip_time(gated_delta_chunked(flip_time(q, 1), flip_time(k, 1), flip_time(v, 1), flip_time(g[:, :, 1], 1), flip_time(beta[:, :, 1], 1)), 1)
    o = rms_norm(o, p['gd_norm_w']) * jax.nn.silu(z.reshape(B, T, GD_HEADS, GD_DV).astype(f32))
    return o.reshape(B, T, GD_W)


def encoder_layer(x, c, p):
    mod = jax.nn.silu(c) @ p['w_ada'] + p['b_ada']
    shift, scale, gate = jnp.split(mod[:, None, :], 3, axis=-1)
    h = rms_norm(x, p['g_pre']) * (1.0 + scale) + shift
    proj = h @ p['w_in']
    cq, ckv, k_rope, rkv, wa_f, wa_b, xbc, m_dt, qkv, gd_b, gd_a, z = split_cols(proj, IN_SIZES)
    z_a, z_b, z_c, z_d = split_cols(z, (A_W, RW_W, M_W, GD_W))
    o_a = mla_mixer(cq, ckv, k_rope, p) * jax.nn.silu(z_a)
    o_b = rwkv7_mixer(rkv, wa_f, wa_b, z_b, p)
    o_c = mamba2_mixer(xbc, m_dt, z_c, p)
    o_d = gated_deltanet_mixer(qkv, gd_b, gd_a, z_d, p)
    mixed = jnp.concatenate([o_a.astype(x.dtype), o_b.astype(x.dtype), o_c.astype(x.dtype), o_d.astype(x.dtype)], axis=-1)
    return x + gate * rms_norm(mixed @ p['w_out'], p['g_post'])


def setup_inputs(seed: int = 0) -> dict:
    key = jax.random.key(seed)
    ks = iter(jax.random.split(key, 48))
    f32 = jnp.float32

    def nrm(shape, scale):
        return jax.random.normal(next(ks), shape, f32) * scale

    def gain(shape):
        return 1.0 + nrm(shape, 0.05)

    def unif(shape, lo, hi):
        return jax.random.uniform(next(ks), shape, f32, lo, hi)

    def dt_bias(shape):
        dt = jnp.exp(unif(shape, math.log(1e-3), math.log(1e-1)))
        return dt + jnp.log(-jnp.expm1(-dt))

    return {
        'x_prompt': nrm((BATCH, SEQ, D_MODEL), 1.0),
        'x_sample': nrm((DEC_BATCH, DEC_SEQ, D_MODEL), 1.0),
        'c_prompt': nrm((BATCH, D_MODEL), 1.0),
        'c_sample': nrm((DEC_BATCH, D_MODEL), 1.0),
        'w_ada': nrm((DEPTH, D_MODEL, 3 * D_MODEL), 0.5 * D_MODEL ** -0.5),
        'b_ada': nrm((DEPTH, 3 * D_MODEL), 0.02),
        'g_pre': gain((DEPTH, D_MODEL)),
        'g_post': gain((DEPTH, D_MODEL)),
        'w_in': nrm((DEPTH, D_MODEL, P_IN), D_MODEL ** -0.5),
        'w_out': nrm((DEPTH, D_MIX, D_MODEL), D_MIX ** -0.5),
        'a_g_q': gain((DEPTH, A_Q_LORA)),
        'a_w_uq': nrm((DEPTH, A_Q_LORA, A_HEADS * (A_NOPE + A_ROPE)), A_Q_LORA ** -0.5),
        'a_g_kv': gain((DEPTH, A_KV_LORA)),
        'a_w_ukv': nrm((DEPTH, A_KV_LORA, A_HEADS * (A_NOPE + A_V)), A_KV_LORA ** -0.5),
        'rw_mu': unif((DEPTH, N_DIR, 3 * RW_W + 2 * RW_LORA), 0.0, 1.0),
        'rw_w0': unif((DEPTH, N_DIR, RW_W), -6.0, -1.0),
        'rw_w2': nrm((DEPTH, N_DIR, RW_LORA, RW_W), 0.1 * RW_LORA ** -0.5),
        'rw_a0': nrm((DEPTH, N_DIR, RW_W), 0.1),
        'rw_a2': nrm((DEPTH, N_DIR, RW_LORA, RW_W), 0.5 * RW_LORA ** -0.5),
        'rw_k_k': 0.85 + nrm((DEPTH, RW_W), 0.05),
        'rw_k_a': gain((DEPTH, RW_W)),
        'rw_r_k': nrm((DEPTH, RW_HEADS, RW_N), 0.1),
        'rw_ln_w': gain((DEPTH, RW_W)),
        'rw_ln_b': nrm((DEPTH, RW_W), 0.02),
        'm_conv_w': nrm((DEPTH, M_CONV, M_CONV_CH), M_CONV ** -0.5),
        'm_conv_b': nrm((DEPTH, M_CONV_CH), 0.02),
        'm_dt_bias': dt_bias((DEPTH, N_DIR, M_HEADS)),
        'm_a_log': jnp.log(unif((DEPTH, N_DIR, M_HEADS), 1.0, 16.0)),
        'm_d': gain((DEPTH, M_HEADS)),
        'm_norm_w': gain((DEPTH, M_W)),
        'gd_conv_w': nrm((DEPTH, GD_CONV, GD_QKV), GD_CONV ** -0.5),
        'gd_dt_bias': dt_bias((DEPTH, N_DIR, GD_HEADS)),
        'gd_a_log': jnp.log(unif((DEPTH, N_DIR, GD_HEADS), 1.0, 16.0)),
        'gd_norm_w': gain((DEPTH, GD_DV)),
    }


def reference(x_prompt, x_sample, c_prompt, c_sample, w_ada, b_ada, g_pre, g_post, w_in, w_out, a_g_q, a_w_uq, a_g_kv, a_w_ukv, rw_mu, rw_w0, rw_w2, rw_a0, rw_a2, rw_k_k, rw_k_a, rw_r_k, rw_ln_w, rw_ln_b, m_conv_w, m_conv_b, m_dt_bias, m_a_log, m_d, m_norm_w, gd_conv_w, gd_dt_bias, gd_a_log, gd_norm_w):
    weights = dict(w_ada=w_ada, b_ada=b_ada, g_pre=g_pre, g_post=g_post, w_in=w_in, w_out=w_out,
                   a_g_q=a_g_q, a_w_uq=a_w_uq, a_g_kv=a_g_kv, a_w_ukv=a_w_ukv,
                   rw_mu=rw_mu, rw_w0=rw_w0, rw_w2=rw_w2, rw_a0=rw_a0, rw_a2=rw_a2, rw_k_k=rw_k_k,
                   rw_k_a=rw_k_a, rw_r_k=rw_r_k, rw_ln_w=rw_ln_w, rw_ln_b=rw_ln_b,
                   m_conv_w=m_conv_w, m_conv_b=m_conv_b, m_dt_bias=m_dt_bias, m_a_log=m_a_log,
                   m_d=m_d, m_norm_w=m_norm_w,
                   gd_conv_w=gd_conv_w, gd_dt_bias=gd_dt_bias, gd_a_log=gd_a_log, gd_norm_w=gd_norm_w)

    def trunk(x, c):
        for l in range(DEPTH):
            p = {name: arr[l] for name, arr in weights.items()}
            x = encoder_layer(x, c, p)
        return x

    y_prompt = trunk(x_prompt, c_prompt)
    y_sample = trunk(x_sample, c_sample)
    return (y_prompt, y_sample)
```

```python
import math
import numpy as np
import concourse.bass as bass
import concourse.mybir as mybir
from concourse.bass_utils import run_bass_kernel_spmd
from contextlib import ExitStack

F32 = mybir.dt.float32
BF16 = mybir.dt.bfloat16
AF = mybir.ActivationFunctionType
ALU = mybir.AluOpType
AX = mybir.AxisListType

D_MODEL = 1024
P_IN = 4024
N_CORES = 8
EPS = 1e-6
C_CQ, C_CKV, C_KR, C_RKV, C_WAF, C_WAB, C_XBC, C_MDT, C_QKV, C_GDB, C_GDA, C_Z = (
    0, 384, 512, 544, 1312, 1376, 1440, 2208, 2216, 2984, 2992, 3000)
K_ID, K_MUS, K_MUI, K_BO, K_ONE, K_SEL, K_MU128, K_ML128, K_MLS, K_END = 0, 128, 192, 256, 384, 512, 576, 704, 832, 896


class Dep:
    __slots__ = ("w", "r", "name", "excl")

    def __init__(self, name=""):
        self.w = {}
        self.r = {}
        self.name = name
        self.excl = False


class T:
    def __init__(self, h, name, dep=None):
        self.h = h
        self.name = name
        self.dep = dep or Dep(name)
        self.dsem = None
        self.dcount = 0

    def __getitem__(self, k):
        return self.h[k]


class Ctx:
    def __init__(self, nc):
        self.nc = nc
        self.E = {"pe": nc.tensor, "act": nc.scalar, "dve": nc.vector, "pool": nc.gpsimd, "sp": nc.sync}
        self.sem, self.cnt, self.known = {}, {}, {}
        self.stack = ExitStack()
        for e in self.E:
            self.sem[e] = self.stack.enter_context(nc.semaphore("s_" + e))
            self.cnt[e] = 0
            self.known[e] = {}
        self.uid = 0
        self.ninst = 0
        self.dtiles = []
        self.free_dsems = []
        self.all_dsems = []

    def sb(self, shape, dt=F32, name=None, stack=None):
        self.uid += 1
        name = (name or "t") + "_%d" % self.uid
        h = (stack or self.stack).enter_context(self.nc.sbuf_tensor(name, list(shape), dt))
        return T(h, name)

    def ps(self, shape, dt=F32, name=None, stack=None):
        self.uid += 1
        name = (name or "p") + "_%d" % self.uid
        esz = 4 if dt == F32 else 2
        n = 1
        for v in shape[1:]:
            n *= v
        per_bank = 2048 // esz
        nb = -(-n // per_bank)
        h = (stack or self.stack).enter_context(self.nc.psum_tensor(name, [128, nb * per_bank], dt))
        v = h[0:shape[0], 0:n]
        if len(shape) == 3:
            v = v.rearrange("p (a b) -> p a b", a=shape[1])
        t = T(v, name)
        t.dep.excl = True
        return t

    def sub(self, ap, name="s"):
        self.uid += 1
        return T(ap, name + "_%d" % self.uid)

    def dram(self, name, shape, dt=F32, kind="Internal"):
        return T(self.nc.dram_tensor(name, list(shape), dt, kind=kind), name)

    def _wait(self, eng, sem, val):
        k = self.known[eng]
        sid = id(sem)
        if k.get(sid, 0) >= val:
            return
        k[sid] = val
        self.E[eng].wait_ge(sem, val)
        self.ninst += 1

    def _acquire(self, eng, reads, writes):
        own = id(self.sem[eng])
        for d in reads:
            for sid, (sem, val) in d.dep.w.items():
                self._wait(eng, sem, val)
            if d.dep.excl:
                for sid, (sem, val) in d.dep.r.items():
                    if sid != own:
                        self._wait(eng, sem, val)
        for d in writes:
            for sid, (sem, val) in d.dep.w.items():
                if sid != own or eng != "pe":
                    self._wait(eng, sem, val)
            for sid, (sem, val) in d.dep.r.items():
                self._wait(eng, sem, val)

    def _record(self, sem, val, reads, writes):
        sid = id(sem)
        for d in reads:
            d.dep.r[sid] = (sem, val)
        for d in writes:
            d.dep.w[sid] = (sem, val)

    def op(self, eng, fn, reads=(), writes=()):
        self._acquire(eng, reads, writes)
        ins = fn()
        self.cnt[eng] += 1
        ins.then_inc(self.sem[eng], 1)
        self.ninst += 1
        self._record(self.sem[eng], self.cnt[eng], reads, writes)
        return ins

    def dma(self, q, out_ap, in_ap, sbt, reads=(), writes=(), **kw):
        ds = sbt.dsem
        if ds is None:
            if self.free_dsems:
                ds = self.free_dsems.pop()
            else:
                ds = [self.stack.enter_context(self.nc.semaphore("d%d" % len(self.all_dsems))), 0]
                self.all_dsems.append(ds)
            sbt.dsem = ds
            self.dtiles.append(sbt)
        if ds[1]:
            self._wait(q, ds[0], 16 * ds[1])
        self._acquire(q, reads, writes)
        ins = self.E[q].dma_start(out=out_ap, in_=in_ap, **kw)
        ds[1] += 1
        ins.then_inc(ds[0], 16)
        self.ninst += 1
        self._record(ds[0], 16 * ds[1], reads, writes)
        return ins

    def finish(self, out_tiles):
        for t in out_tiles:
            for sid, (sem, val) in t.dep.w.items():
                self._wait("sp", sem, val)

    def barrier(self):
        for e in self.E:
            for f in self.E:
                if self.cnt[f]:
                    self._wait(e, self.sem[f], self.cnt[f])
            for ds in self.all_dsems:
                if ds[1]:
                    self._wait(e, ds[0], 16 * ds[1])
        for t in self.dtiles:
            self.free_dsems.append(t.dsem)
            t.dsem = None
        self.dtiles = []


def V3(ap, a):
    return ap.rearrange("p (a b) -> p a b", a=a)


def bc_mid(ap2, n):
    return ap2.unsqueeze(1).to_broadcast([ap2.shape[0], n, ap2.shape[1]])


def bc_last(ap2, n):
    return ap2.unsqueeze(2).to_broadcast([ap2.shape[0], ap2.shape[1], n])


def rev(ap):
    pat = [list(x) for x in ap.ap]
    st, n = pat[-1]
    pat[-1] = [-st, n]
    return bass.AP(ap.tensor, ap.offset + st * (n - 1), pat)


class Builder:
    def __init__(self, Tn, NS, DEPTH, mixers="ABCD", dbg=False):
        self.Tn, self.NS, self.DEPTH, self.mixers, self.dbg = Tn, NS, DEPTH, mixers, dbg
        self.nc = bass.Bass("TRN2", target_bir_lowering=False)
        self.c = Ctx(self.nc)
        self.rr = 0

    def eng2(self):
        self.rr ^= 1
        return "dve" if self.rr else "act"

    def cp(self, eng, out, in_, R, W):
        nc = self.nc
        if eng == "act":
            return self.c.op("act", lambda: nc.scalar.copy(out=out, in_=in_), R, W)
        e = nc.vector if eng == "dve" else nc.gpsimd
        return self.c.op(eng, lambda: e.tensor_copy(out=out, in_=in_), R, W)

    def tt(self, eng, out, a, b, op, R, W):
        e = self.nc.vector if eng == "dve" else self.nc.gpsimd
        return self.c.op(eng, lambda: e.tensor_tensor(out=out, in0=a, in1=b, op=op), R, W)

    def ts(self, eng, out, a, s1, s2, op0, op1, R, W):
        e = self.nc.vector if eng == "dve" else self.nc.gpsimd
        if s2 is None:
            return self.c.op(eng, lambda: e.tensor_scalar(out=out, in0=a, scalar1=s1, scalar2=None, op0=op0), R, W)
        return self.c.op(eng, lambda: e.tensor_scalar(out=out, in0=a, scalar1=s1, scalar2=s2, op0=op0, op1=op1), R, W)

    def stt(self, out, a, s, b, op0, op1, R, W):
        nc = self.nc
        return self.c.op("dve", lambda: nc.vector.scalar_tensor_tensor(out=out, in0=a, scalar=s, in1=b, op0=op0, op1=op1), R, W)

    def act(self, out, in_, func, R, W, **kw):
        nc = self.nc
        return self.c.op("act", lambda: nc.scalar.activation(out=out, in_=in_, func=func, **kw), R, W)

    def mm(self, out, lhsT, rhs, R, W, start=True, stop=True, skip=False):
        nc = self.nc
        if skip:
            return self.c.op("pe", lambda: nc.tensor.matmul(out, lhsT=lhsT, rhs=rhs, start=False, stop=False, skip_group_check=True), R, W)
        return self.c.op("pe", lambda: nc.tensor.matmul(out, lhsT=lhsT, rhs=rhs, start=start, stop=stop), R, W)

    def tr(self, out, in_, ident, R, W):
        nc = self.nc
        return self.c.op("pe", lambda: nc.tensor.transpose(out=out, in_=in_, identity=ident), R, W)

    def memset(self, eng, ap, val, W):
        e = self.nc.vector if eng == "dve" else self.nc.gpsimd
        return self.c.op(eng, lambda: e.memset(ap, val), (), W)

    def ld(self, dst_tile, dst_ap, src_ap, R=(), **kw):
        return self.c.dma("sp", dst_ap, src_ap, dst_tile, reads=R, writes=[dst_tile], **kw)

    def rsqrt_(self, out, in_, scale, eps_ap, R, W):
        self.act(out, in_, AF.Ln, R, W, scale=scale, bias=eps_ap)
        self.act(out, out, AF.Exp, W, W, scale=-0.5)

    def load_w(self, dst, dcol, src3, n, st, neg=False, eng=None):
        KC = src3.shape[1]
        self.ld(st, st[:, 0:KC, 0:n], src3)
        eng = eng or self.eng2()
        if neg:
            nc = self.nc
            self.c.op("act", lambda: nc.scalar.mul(out=dst[:, 0:KC, dcol:dcol + n], in_=st[:, 0:KC, 0:n], mul=-1.0), [st], [dst])
        else:
            self.cp(eng, dst[:, 0:KC, dcol:dcol + n], st[:, 0:KC, 0:n], [st], [dst])

    def win(self, l, c0, n):
        return self.w["w_in"].h[l, :, c0:c0 + n].rearrange("(kc p) n -> p kc n", p=128)

    def hT_ap(self, kc, tau0, n, d=0, shift=0):
        Tn = self.Tn
        if d == 0:
            col = 1 + tau0 - shift
            return self.hT[:, kc, col:col + n]
        lo = Tn - tau0 - n + 1 + shift
        return rev(self.hT[:, kc, lo:lo + n])

    def proj(self, out_ps, Wt, wc, M, tau0, n, R_extra=(), d=0, W2=None, w2c=0):
        steps = [(Wt, wc, 0)] + ([(W2, w2c, 1)] if W2 is not None else [])
        tot = 8 * len(steps)
        i = 0
        for (Wx, cc, sh) in steps:
            for kc in range(8):
                self.mm(out_ps[0:M, 0:n], Wx[:, kc, cc:cc + M], self.hT_ap(kc, tau0, n, d, sh),
                        [Wx, self.hT_t], [out_ps], start=(i == 0), stop=(i == tot - 1))
                i += 1

    def declare(self):
        c, Tn, NS, DEPTH = self.c, self.Tn, self.NS, self.DEPTH
        shapes = dict(
            w_ada=[DEPTH, 1024, 3072], b_ada=[DEPTH, 3072], g_pre=[DEPTH, 1024], g_post=[DEPTH, 1024],
            w_in=[DEPTH, 1024, P_IN], w_out=[DEPTH, 1024, 1024], a_g_q=[DEPTH, 384], a_w_uq=[DEPTH, 384, 384],
            a_g_kv=[DEPTH, 128], a_w_ukv=[DEPTH, 128, 512], rw_mu=[DEPTH, 2, 832], rw_w0=[DEPTH, 2, 256],
            rw_w2=[DEPTH, 2, 32, 256], rw_a0=[DEPTH, 2, 256], rw_a2=[DEPTH, 2, 32, 256], rw_k_k=[DEPTH, 256],
            rw_k_a=[DEPTH, 256], rw_r_k=[DEPTH, 4, 64], rw_ln_w=[DEPTH, 256], rw_ln_b=[DEPTH, 256],
            m_conv_w=[DEPTH, 4, 768], m_conv_b=[DEPTH, 768], m_dt_bias=[DEPTH, 2, 4], m_a_log=[DEPTH, 2, 4],
            m_d=[DEPTH, 4], m_norm_w=[DEPTH, 256], gd_conv_w=[DEPTH, 4, 768], gd_dt_bias=[DEPTH, 2, 4],
            gd_a_log=[DEPTH, 2, 4], gd_norm_w=[DEPTH, 64])
        self.w = {k: c.dram(k, v, F32, kind="ExternalInput") for k, v in shapes.items()}
        self.x = c.dram("x", [NS, Tn, 1024], F32, kind="ExternalInput")
        self.cin = c.dram("c", [NS, 1024], F32, kind="ExternalInput")
        self.kconst = c.dram("kconst", [128, K_END], F32, kind="ExternalInput")
        self.rope = c.dram("rope", [96, 2, Tn], F32, kind="ExternalInput")
        self.selg = c.dram("selg", [8, 4, 128], F32, kind="ExternalInput")
        self.selm = c.dram("selm", [8, 8, 128], F32, kind="ExternalInput")
        self.y = c.dram("y", [NS, Tn, 1024], F32, kind="ExternalOutput")
        self.mixT = c.dram("mixT", [1024, Tn], BF16, kind="ExternalOutput" if self.dbg else "Internal")

    def consts(self):
        c = self.c
        self.kc = c.sb([128, K_END], F32, "kc")
        self.ld(self.kc, self.kc[:], self.kconst.h[:])
        self.kb = c.sb([128, K_END], BF16, "kb")
        self.cp("dve", self.kb[:], self.kc[:], [self.kc], [self.kb])
        self.epsc = c.sb([128, 2], F32, "epsc")
        self.memset("pool", self.epsc[:, 0:1], EPS, [self.epsc])
        self.memset("pool", self.epsc[:, 1:2], 64e-5, [self.epsc])
        self.hT_t = c.sb([128, 8, self.Tn + 2], BF16, "hT")
        self.hT = self.hT_t
        self.memset("pool", self.hT[:, :, 0:1], 0.0, [self.hT_t])
        self.memset("pool", self.hT[:, :, self.Tn + 1:self.Tn + 2], 0.0, [self.hT_t])
        self.gg = [c.sb([128, 1024], F32, "gg%d" % s) for s in range(self.NS)]
        self.a0 = c.sb([128, 8, self.NS], F32, "a0")
        self.a1 = c.sb([128, 8, self.NS], F32, "a1")

    def p0(self, l):
        c, nc, NS = self.c, self.nc, self.NS
        with ExitStack() as es:
            cT = c.sb([128, 8, NS], F32, "cT", es)
            for s in range(NS):
                self.ld(cT, cT[:, :, s], self.cin.h[s, :].rearrange("(kc p) -> p kc", p=128), allow_slow_non_contiguous=True)
            scb = c.sb([128, 8, NS], BF16, "scb", es)
            self.act(scb[:], cT[:], AF.Silu, [cT], [scb])
            wada = c.sb([128, 8, 3072], BF16, "wada", es)
            st = c.sb([128, 8, 512], F32, "st0", es)
            for j in range(6):
                self.load_w(wada, j * 512, self.w["w_ada"].h[l, :, j * 512:(j + 1) * 512].rearrange("(kc p) n -> p kc n", p=128), 512, st)
            bT = c.sb([128, 24], F32, "bT", es)
            self.ld(bT, bT[:], self.w["b_ada"].h[l, :].rearrange("(j p) -> p j", p=128), allow_slow_non_contiguous=True)
            gp = c.sb([128, 8], F32, "gp", es)
            self.ld(gp, gp[:], self.w["g_pre"].h[l, :].rearrange("(j p) -> p j", p=128), allow_slow_non_contiguous=True)
            pm = c.ps([128, 16 * NS], F32, "pm", es)
            for idx in range(16):
                for kc in range(8):
                    self.mm(pm[:, idx * NS:(idx + 1) * NS], wada[:, kc, idx * 128:(idx + 1) * 128], scb[:, kc, :],
                            [wada, scb], [pm], start=(kc == 0), stop=(kc == 7))
            pm3 = V3(pm[:], 16)
            self.tt("dve", self.a0[:], pm3[:, 0:8, :], bc_last(bT[:, 0:8], NS), ALU.add, [pm, bT], [self.a0])
            tmp = c.sb([128, 8, NS], F32, "tmp0", es)
            self.tt("dve", tmp[:], pm3[:, 8:16, :], bc_last(bT[:, 8:16], NS), ALU.add, [pm, bT], [tmp])
            self.ts("pool", tmp[:], tmp[:], 1.0, None, ALU.add, None, [tmp], [tmp])
            self.tt("pool", self.a1[:], tmp[:], bc_last(gp[:], NS), ALU.mult, [tmp, gp], [self.a1])
            bg = c.sb([128, 1024], F32, "bg", es)
            self.ld(bg, bg[:], self.w["b_ada"].h[l, 2048:3072].partition_broadcast(128))
            gpo = c.sb([128, 1024], F32, "gpo", es)
            self.ld(gpo, gpo[:], self.w["g_post"].h[l, :].partition_broadcast(128))
            for s in range(NS):
                pg = c.ps([128, 1024], F32, "pg", es) if s == 0 else pg
                for n in range(2):
                    for kc in range(8):
                        self.mm(pg[:, n * 512:(n + 1) * 512], scb[:, kc, s:s + 1].to_broadcast([128, 128]),
                                wada[:, kc, 2048 + n * 512:2048 + (n + 1) * 512], [scb, wada], [pg], start=(kc == 0), stop=(kc == 7))
                self.tt("dve", self.gg[s][:], pg[:], bg[:], ALU.add, [pg, bg], [self.gg[s]])
                self.tt("pool", self.gg[s][:], self.gg[s][:], gpo[:], ALU.mult, [self.gg[s], gpo], [self.gg[s]])
            c.barrier()

    def xsrc(self, l, s):
        return (self.x if l == 0 else self.y), s

    def p1(self, l, s):
        c, nc, Tn = self.c, self.nc, self.Tn
        src, _ = self.xsrc(l, s)
        with ExitStack() as es:
            xt = [c.sb([128, 1024], F32, "xt", es) for _ in range(2)]
            xn = [c.sb([128, 1024], BF16, "xn", es) for _ in range(2)]
            junk = c.sb([128, 1024], BF16, "junk", es)
            ss = [c.sb([128, 2], F32, "ss", es) for _ in range(2)]
            pt = [c.ps([128, 1024], BF16, "pt", es) for _ in range(2)]
            tm = [c.sb([128, 8, 128], F32, "tm", es) for _ in range(2)]
            for tt_ in range(Tn // 128):
                b = tt_ % 2
                self.ld(xt[b], xt[b][:], src.h[s, tt_ * 128:(tt_ + 1) * 128, :], R=[src])
                self.act(junk[:], xt[b][:], AF.Square, [xt[b]], [junk, ss[b]], accum_out=ss[b][:, 0:1])
                self.rsqrt_(ss[b][:, 1:2], ss[b][:, 0:1], 1.0 / 1024, self.epsc[:, 0:1], [ss[b], self.epsc], [ss[b]])
                self.ts("dve", xn[b][:], xt[b][:], ss[b][:, 1:2], None, ALU.mult, None, [xt[b], ss[b]], [xn[b]])
                for fc in range(8):
                    self.tr(pt[b][:, fc * 128:(fc + 1) * 128], xn[b][:, fc * 128:(fc + 1) * 128], self.kb[:, K_ID:K_ID + 128],
                            [xn[b], self.kb], [pt[b]])
                self.tt("dve", tm[b][:], V3(pt[b][:], 8), bc_last(self.a1[:, :, s], 128), ALU.mult, [pt[b], self.a1], [tm[b]])
                self.tt("pool", self.hT[:, :, 1 + tt_ * 128:1 + (tt_ + 1) * 128], tm[b][:], bc_last(self.a0[:, :, s], 128), ALU.add,
                        [tm[b], self.a0], [self.hT_t])
            c.barrier()

    def p6(self, l, s):
        c, nc, Tn = self.c, self.nc, self.Tn
        src, _ = self.xsrc(l, s)
        TW = min(512, Tn)
        with ExitStack() as es:
            wo = c.sb([128, 8, 1024], BF16, "wo", es)
            st = c.sb([128, 8, 512], F32, "st6", es)
            for j in range(2):
                self.load_w(wo, j * 512, self.w["w_out"].h[l, :, j * 512:(j + 1) * 512].rearrange("(kc p) n -> p kc n", p=128), 512, st)
            mt = [c.sb([128, 8, TW], BF16, "mt", es) for _ in range(2)]
            xt = [c.sb([128, 1024], F32, "xt6", es) for _ in range(2)]
            yt = [c.sb([128, 1024], F32, "yt6", es) for _ in range(2)]
            junk = c.sb([128, 1024], BF16, "junk6", es)
            ss = [c.sb([128, 2], F32, "ss6", es) for _ in range(2)]
            po = [c.ps([128, 1024], F32, "po", es) for _ in range(2)]
            k = 0
            for j in range(Tn // TW):
                mb = mt[j % 2]
                self.ld(mb, mb[:], self.mixT.h[:, j * TW:(j + 1) * TW].rearrange("(a p) t -> p a t", p=128), R=[self.mixT])
                for u in range(TW // 128):
                    b = k % 2
                    k += 1
                    t0 = j * TW + u * 128
                    self.ld(xt[b], xt[b][:], src.h[s, t0:t0 + 128, :], R=[src])
                    for n in range(2):
                        for kc in range(8):
                            self.mm(po[b][:, n * 512:(n + 1) * 512], mb[:, kc, u * 128:(u + 1) * 128], wo[:, kc, n * 512:(n + 1) * 512],
                                    [mb, wo], [po[b]], start=(kc == 0), stop=(kc == 7))
                    self.act(junk[:], po[b][:], AF.Square, [po[b]], [junk, ss[b]], accum_out=ss[b][:, 0:1])
                    self.rsqrt_(ss[b][:, 1:2], ss[b][:, 0:1], 1.0 / 1024, self.epsc[:, 0:1], [ss[b], self.epsc], [ss[b]])
                    self.tt("dve", yt[b][:], po[b][:], self.gg[s][:], ALU.mult, [po[b], self.gg[s]], [yt[b]])
                    self.stt(yt[b][:], yt[b][:], ss[b][:, 1:2], xt[b][:], ALU.mult, ALU.add, [yt[b], ss[b], xt[b]], [yt[b]])
                    self.c.dma("sp", self.y.h[s, t0:t0 + 128, :], yt[b][:], yt[b], reads=[yt[b]], writes=[self.y])
            c.barrier()

    def mla(self, l, s):
        c, nc, Tn = self.c, self.nc, self.Tn
        TW = min(512, Tn)
        NQG = Tn // TW
        NKB = Tn // 128
        U = TW // 128
        scale = 96 ** -0.5
        ID_b = self.kb[:, K_ID:K_ID + 128]
        ONE_b = self.kb[:, K_ONE:K_ONE + 128]
        with ExitStack() as es:
            Wm = c.sb([128, 8, 960], BF16, "Wm", es)
            Wq = c.sb([128, 3, 384], BF16, "Wq", es)
            Wqb = c.sb([128, 3, 384], BF16, "Wqb", es)
            Wkv = c.sb([128, 512], BF16, "Wkv", es)
            with ExitStack() as e1:
                st = c.sb([128, 8, 512], F32, "stA", e1)
                self.memset("pool", Wm[:, :, 512:704], 0.0, [Wm])
                self.load_w(Wm, 0, self.win(l, C_CQ, 384), 384, st)
                self.load_w(Wm, 384, self.win(l, C_CKV, 128), 128, st)
                self.load_w(Wm, 512 + 64, self.win(l, C_KR, 32), 32, st)
                self.load_w(Wm, 608 + 64, self.win(l, C_KR + 16, 16), 16, st, neg=True)
                self.load_w(Wm, 608 + 80, self.win(l, C_KR, 16), 16, st)
                self.load_w(Wm, 704, self.win(l, C_Z, 256), 256, st)
                stq = c.sb([128, 3, 384], F32, "stq", e1)
                self.ld(stq, stq[:], self.w["a_w_uq"].h[l].rearrange("(kc p) n -> p kc n", p=128))
                gq = c.sb([128, 3], F32, "gq", e1)
                self.ld(gq, gq[:], self.w["a_g_q"].h[l, :].rearrange("(j p) -> p j", p=128), allow_slow_non_contiguous=True)
                self.memset("pool", Wqb[:], 0.0, [Wqb])
                for fc in range(3):
                    self.ts("dve", Wq[:, fc, :], stq[:, fc, :], gq[:, fc:fc + 1], None, ALU.mult, None, [stq, gq], [Wq])
                    src4 = V3(Wq[:, fc, :], 4)
                    dst4 = V3(Wqb[:, fc, :], 4)
                    self.c.op("act", lambda s4=src4, d4=dst4: nc.scalar.mul(out=d4[:, :, 64:80], in_=s4[:, :, 80:96], mul=-1.0), [Wq], [Wqb])
                    self.cp("pool", dst4[:, :, 80:96], src4[:, :, 64:80], [Wq], [Wqb])
                stk = c.sb([128, 512], F32, "stk", e1)
                self.ld(stk, stk[:], self.w["a_w_ukv"].h[l])
                gkv = c.sb([128, 1], F32, "gkv", e1)
                self.ld(gkv, gkv[:], self.w["a_g_kv"].h[l, :].rearrange("(p o) -> p o", o=1))
                self.ts("dve", Wkv[:], stk[:], gkv[:, 0:1], None, ALU.mult, None, [stk, gkv], [Wkv])
                c.barrier()
            KT = [c.sb([96, Tn], BF16, "KT%d" % h, es) for h in range(4)]
            Va = c.sb([128, NKB, 4, 65], BF16, "Va", es)
            self.memset("pool", Va[:, :, :, 64:65], 1.0, [Va])
            cosx = c.sb([96, TW], F32, "cosx", es)
            sinx = c.sb([96, TW], F32, "sinx", es)
            t1 = c.sb([96, TW], F32, "t1", es)
            t2 = c.sb([96, TW], F32, "t2", es)
            with ExitStack() as e2:
                ckT = c.sb([128, TW], BF16, "ckT", e2)
                sqk = c.sb([128, TW], BF16, "sqk", e2)
                rk = c.sb([128, TW], F32, "rk", e2)
                rcol = c.sb([128, 2], F32, "rcol", e2)
                pA = [c.ps([128, 512], F32, "pA", e2) for _ in range(4)]
                pB = c.ps([128, 512], F32, "pB", e2)
                pV = c.ps([128, 256], F32, "pV", e2)
                pc = c.ps([128, 2], F32, "pc", e2)
                for j in range(NQG):
                    t0 = j * TW
                    self.ld(cosx, cosx[:], self.rope.h[:, 0, t0:t0 + TW])
                    self.ld(sinx, sinx[:], self.rope.h[:, 1, t0:t0 + TW])
                    self.proj(pB, Wm, 384, 128, t0, TW)
                    self.cp("act", ckT[:], pB[:, 0:TW], [pB], [ckT])
                    self.tt("pool", sqk[:], ckT[:], ckT[:], ALU.mult, [ckT], [sqk])
                    self.mm(pA[0][:, 0:TW], ONE_b, sqk[:], [self.kb, sqk], [pA[0]])
                    self.rsqrt_(rk[:], pA[0][:, 0:TW], 1.0 / 128, self.epsc[:, 0:1], [pA[0], self.epsc], [rk])
                    for h in range(4):
                        pa = pA[1 + h % 2]
                        self.mm(pa[0:64, 0:TW], Wkv[:, h * 128:h * 128 + 64], ckT[:], [Wkv, ckT], [pa])
                        self.tt("dve", KT[h][0:64, t0:t0 + TW], pa[0:64, 0:TW], rk[0:64, :], ALU.mult, [pa, rk], [KT[h]])
                    self.proj(pA[3], Wm, 512, 96, t0, TW)
                    self.proj(pB, Wm, 608, 96, t0, TW)
                    self.tt("dve", t1[64:96, :], pA[3][64:96, 0:TW], cosx[64:96, :], ALU.mult, [pA[3], cosx], [t1])
                    self.tt("dve", t2[64:96, :], pB[64:96, 0:TW], sinx[64:96, :], ALU.mult, [pB, sinx], [t2])
                    for h in range(4):
                        self.tt("pool", KT[h][64:96, t0:t0 + TW], t1[64:96, :], t2[64:96, :], ALU.add, [t1, t2], [KT[h]])
                    for u in range(U):
                        kb_ = (t0 + u * 128) // 128
                        self.mm(pc[:, 0:1], sqk[:, u * 128:(u + 1) * 128], self.kb[:, K_ONE:K_ONE + 1], [sqk, self.kb], [pc])
                        self.rsqrt_(rcol[:, 0:1], pc[:, 0:1], 1.0 / 128, self.epsc[:, 0:1], [pc, self.epsc], [rcol])
                        for h in range(4):
                            self.mm(pV[:, h * 64:(h + 1) * 64], ckT[:, u * 128:(u + 1) * 128], Wkv[:, h * 128 + 64:h * 128 + 128], [ckT, Wkv], [pV])
                        self.ts("dve", Va[:, kb_, :, 0:64], V3(pV[:], 4), rcol[:, 0:1], None, ALU.mult, None, [pV, rcol], [Va])
                c.barrier()
            with ExitStack() as e3:
                NB = 3
                pS = [c.ps([128, 512], F32, "pS", e3) for _ in range(NB)]
                pz = c.ps([128, 512], F32, "pz", e3)
                pO = [c.ps([128, 512], F32, "pO", e3) for _ in range(2)]
                pBC = c.ps([128, 512], F32, "pBC", e3)
                Pt = [c.sb([128, TW], BF16, "Pt", e3) for _ in range(NB)]
                cqT = c.sb([128, 3, TW], BF16, "cqT", e3)
                sq = c.sb([128, 3, TW], BF16, "sq", e3)
                rq = c.sb([128, TW], F32, "rq", e3)
                CR = c.sb([96, TW], F32, "CR", e3)
                SR = c.sb([96, TW], F32, "SR", e3)
                qg_t = [c.sb([96, TW], BF16, "qg%d" % h, e3) for h in range(4)]
                rs = c.sb([128, TW], F32, "rs", e3)
                bcs = c.sb([64, TW], F32, "bcs", e3)
                sz = c.sb([64, TW], F32, "sz", e3)
                on = c.sb([64, TW], F32, "on", e3)
                og = [c.sb([64, TW], BF16, "og", e3) for _ in range(2)]
                it = 0
                for qg in range(NQG):
                    q0 = qg * TW
                    self.ld(cosx, cosx[:], self.rope.h[:, 0, q0:q0 + TW])
                    self.ld(sinx, sinx[:], self.rope.h[:, 1, q0:q0 + TW])
                    for fc in range(3):
                        self.proj(pS[fc], Wm, fc * 128, 128, q0, TW)
                        self.cp("act", cqT[:, fc, :], pS[fc][:, 0:TW], [pS[fc]], [cqT])
                        self.tt("pool", sq[:, fc, :], cqT[:, fc, :], cqT[:, fc, :], ALU.mult, [cqT], [sq])
                    for fc in range(3):
                        self.mm(pz[:, 0:TW], ONE_b, sq[:, fc, :], [self.kb, sq], [pz], start=(fc == 0), stop=(fc == 2))
                    self.rsqrt_(rq[:], pz[:, 0:TW], 1.0 / 384, self.epsc[:, 0:1], [pz, self.epsc], [rq])
                    self.tt("pool", CR[:], cosx[:], rq[0:96, :], ALU.mult, [cosx, rq], [CR])
                    self.tt("pool", SR[:], sinx[:], rq[0:96, :], ALU.mult, [sinx, rq], [SR])
                    for h in range(4):
                        pa, pb = pS[0], pS[1]
                        for fc in range(3):
                            self.mm(pa[0:96, 0:TW], Wq[:, fc, h * 96:(h + 1) * 96], cqT[:, fc, :], [Wq, cqT], [pa], start=(fc == 0), stop=(fc == 2))
                        for fc in range(3):
                            self.mm(pb[0:96, 0:TW], Wqb[:, fc, h * 96:(h + 1) * 96], cqT[:, fc, :], [Wqb, cqT], [pb], start=(fc == 0), stop=(fc == 2))
                        self.tt("dve", t1[:], pa[0:96, 0:TW], CR[:], ALU.mult, [pa, CR], [t1])
                        self.tt("dve", t2[:], pb[0:96, 0:TW], SR[:], ALU.mult, [pb, SR], [t2])
                        self.tt("pool", qg_t[h][:], t1[:], t2[:], ALU.add, [t1, t2], [qg_t[h]])
                    for h in range(4):
                        po_ = pO[h % 2]
                        def score(kb_, b):
                            self.mm(pS[b][:, 0:TW], KT[h][:, kb_ * 128:(kb_ + 1) * 128], qg_t[h][:], [KT[h], qg_t[h]], [pS[b]])
                        score(0, it % NB)
                        if NKB > 1:
                            score(1, (it + 1) % NB)
                        for kb_ in range(NKB):
                            b = it % NB
                            it += 1
                            if kb_ + 2 < NKB:
                                score(kb_ + 2, (it + 1) % NB)
                            self.act(Pt[b][:], pS[b][:, 0:TW], AF.Exp, [pS[b]], [Pt[b]], scale=scale)
                            self.mm(po_[0:65, 0:TW], Va[:, kb_, h, :], Pt[b][:], [Va, Pt[b]], [po_], start=(kb_ == 0), stop=(kb_ == NKB - 1))
                        self.proj(pz, Wm, 704 + h * 64, 64, q0, TW)
                        self.act(sz[:], pz[0:64, 0:TW], AF.Silu, [pz], [sz])
                        self.c.op("dve", lambda p=po_: nc.vector.reciprocal(out=rs[64:65, :], in_=p[64:65, 0:TW]), [po_], [rs])
                        self.mm(pBC[0:64, 0:TW], self.kc[64:65, K_ONE:K_ONE + 64], rs[64:65, :], [self.kc, rs], [pBC])
                        self.cp("act", bcs[:], pBC[0:64, 0:TW], [pBC], [bcs])
                        self.tt("dve", on[:], po_[0:64, 0:TW], bcs[:], ALU.mult, [po_, bcs], [on])
                        ob = og[h % 2]
                        self.tt("pool", ob[:], on[:], sz[:], ALU.mult, [on, sz], [ob])
                        self.c.dma("sp", self.mixT.h[h * 64:(h + 1) * 64, q0:q0 + TW], ob[:], ob, reads=[ob], writes=[self.mixT])
            c.barrier()

    def dplr_env(self, es, nsets=3):
        c = self.c
        sets = []

        def sl(bank, lo, n=128):
            t = c.sub(bank[:, lo:lo + n])
            t.dep = bank.dep
            return t
        cb = c.ps([128, 512], F32, "dpc", es)
        chain = {"pW": sl(cb, 0), "pU": sl(cb, 128), "pZ": sl(cb, 256), "pY": sl(cb, 384, 64)}
        for i in range(nsets):
            e = dict(chain)
            bA = c.ps([128, 512], F32, "dpa", es)
            bB = c.ps([128, 512], F32, "dpb", es)
            e["pQB"], e["pQK"], e["pNL"] = sl(bA, 0, 192), sl(bA, 192, 192), sl(bA, 384)
            e["pXP"] = sl(bA, 0, 256)
            e["pPt"] = sl(bB, 0)
            e["pT"] = c.sub(V3(bB[:, 256:448].bitcast(BF16), 3))
            e["pT"].dep = bB.dep
            for nm in ["Nt", "Pt0", "Pt1"]:
                e[nm] = c.sb([128, 128], F32, "d" + nm, es)
            for nm in ["XP0", "XP1"]:
                e[nm] = c.sb([128, 256], F32, "d" + nm, es)
            for nm in ["AK", "AR", "TT", "W", "U"]:
                e[nm] = c.sb([128, 128], BF16, "d" + nm, es)
            e["Tr"] = c.sb([128, 3, 128], BF16, "dTr", es)
            sets.append(e)
        st = {"Zf": c.sb([128, 128], F32, "Zf", es), "Zb": c.sb([128, 128], BF16, "Zb", es), "Zg": c.sb([128, 128], F32, "Zg", es)}
        return sets, st

    def dplr_pre(self, e, o, ci):
        kc, kb = self.kc, self.kb
        IDf = kc[:, K_ID:K_ID + 128]
        IDb = kb[:, K_ID:K_ID + 128]
        PAR, QB, QK = o["PAR"](ci), o["QB"](ci), o["QK"](ci)
        ms, mi, ml = o["ms"](ci), o["mi"](ci), o["ml"](ci)
        PA = PAR[0][:, 0:128]
        self.mm(e["pQB"][:], QB[0], PAR[0], [QB[1], PAR[1]], [e["pQB"]])
        self.mm(e["pQK"][:], QK[0], PAR[0], [QK[1], PAR[1]], [e["pQK"]])
        self.mm(e["pNL"][:], PA, QB[0], [QB[1], PAR[1]], [e["pNL"]])
        for j, nm in enumerate(["SB", "SK", "V"]):
            src = o[nm](ci)
            self.tr(e["pT"][:, j, :], src[0], IDb, [src[1], kb], [e["pT"]])
        XP = e["XP0"]
        self.tt("dve", V3(XP[:, 128:256], 2), V3(e["pQB"][:, 0:128], 2), bc_mid(ms[0], 2), ALU.mult, [e["pQB"], ms[1]], [XP])
        self.tt("dve", V3(e["Nt"][:], 2), V3(e["pNL"][:], 2), bc_mid(ml[0], 2), ALU.mult, [e["pNL"], ml[1]], [e["Nt"]])
        self.tt("dve", V3(e["AK"][:], 2), V3(e["pQK"][:, 0:128], 2), bc_mid(ms[0], 2), ALU.mult, [e["pQK"], ms[1]], [e["AK"]])
        self.tt("dve", e["AR"][:, 0:64], e["pQB"][:, 128:192], mi[0], ALU.mult, [e["pQB"], mi[1]], [e["AR"]])
        self.tt("dve", e["AR"][:, 64:128], e["pQK"][:, 128:192], mi[0], ALU.mult, [e["pQK"], mi[1]], [e["AR"]])
        self.cp("act", e["Tr"][:], e["pT"][:], [e["pT"]], [e["Tr"]])
        self.tt("pool", XP[:, 0:128], XP[:, 128:256], IDf, ALU.add, [XP, kc], [XP])
        yield
        Pt = e["Nt"]
        Ptn = e["Pt1"]
        self.mm(e["pPt"][:], XP[:, 128:256], Pt[:], [XP, Pt], [e["pPt"]])
        self.mm(e["pXP"][:, 128:256], Pt[:], XP[:, 128:256], [XP, Pt], [e["pXP"]])
        XPn = e["XP1"]
        self.cp("act", Ptn[:], e["pPt"][:], [e["pPt"]], [Ptn])
        self.cp("act", XPn[:, 128:256], e["pXP"][:, 128:256], [e["pXP"]], [XPn])
        self.cp("pool", XPn[:, 0:128], XP[:, 0:128], [XP], [XPn])
        XP, Pt = XPn, Ptn
        yield
        for k in range(1, 6):
            XPn = e["XP%d" % (k % 2)]
            Ptn = e["Pt%d" % (k % 2)]
            if k < 4:
                self.mm(e["pXP"][:], Pt[:], XP[:], [Pt, XP], [e["pXP"]])
            else:
                self.mm(e["pXP"][:, 0:128], Pt[:], XP[:, 0:128], [Pt, XP], [e["pXP"]])
            if k < 5:
                self.mm(e["pPt"][:], XP[:, 128:256], Pt[:], [XP, Pt], [e["pPt"]])
            self.tt("dve", XPn[:, 0:128], XP[:, 0:128], e["pXP"][:, 0:128], ALU.add, [XP, e["pXP"]], [XPn])
            if k < 4:
                self.cp("act", XPn[:, 128:256], e["pXP"][:, 128:256], [e["pXP"]], [XPn])
            if k < 5:
                self.cp("act", Ptn[:], e["pPt"][:], [e["pPt"]], [Ptn])
            XP, Pt = XPn, Ptn
            yield
        self.cp("pool", e["TT"][:], XP[:, 0:128], [XP], [e["TT"]])

    def dplr_chain(self, e, st, o, ci):
        nc = self.nc
        Zf, Zb, Zg = st["Zf"], st["Zb"], st["Zg"]
        ZA, ZR = o["ZA"](ci), o["ZR"](ci)
        SBt, SKt, Vt = e["Tr"][:, 0, :], e["Tr"][:, 1, :], e["Tr"][:, 2, :]
        gz, gs = o["gZ"](ci), o["gS"](ci)
        self.c.op("act", lambda: nc.scalar.mul(out=Zg[:], in_=Zf[:], mul=gz[0]), [Zf, gz[1]], [Zg])
        self.mm(e["pW"][:], ZA[0], Zb[:], [ZA[1], Zb], [e["pW"]], start=True, stop=False)
        self.mm(e["pW"][:], e["AK"][:], Vt, [e["AK"], e["Tr"]], [e["pW"]], start=False, stop=True)
        self.cp("act", e["W"][:], e["pW"][:], [e["pW"]], [e["W"]])
        yield
        self.mm(e["pU"][:], e["TT"][:], e["W"][:], [e["TT"], e["W"]], [e["pU"]])
        self.cp("dve", e["U"][:], e["pU"][:], [e["pU"]], [e["U"]])
        yield
        self.mm(e["pZ"][:], SBt, e["U"][:], [e["Tr"], e["U"]], [e["pZ"]], start=True, stop=False)
        self.mm(e["pZ"][:], SKt, Vt, [e["Tr"]], [e["pZ"]], start=False, stop=True)
        self.mm(e["pY"][:], Zb[:], ZR[0], [Zb, ZR[1]], [e["pY"]], start=True, stop=False)
        self.mm(e["pY"][:], e["U"][:], e["AR"][:, 0:64], [e["U"], e["AR"]], [e["pY"]], start=False, stop=False)
        self.mm(e["pY"][:], Vt, e["AR"][:, 64:128], [e["Tr"], e["AR"]], [e["pY"]], start=False, stop=True)
        if gs is None:
            self.tt("dve", Zb[:], e["pZ"][:], Zg[:], ALU.add, [e["pZ"], Zg], [Zb])
            self.tt("dve", Zf[:], e["pZ"][:], Zg[:], ALU.add, [e["pZ"], Zg], [Zf])
        else:
            self.stt(Zb[:], e["pZ"][:], gs[0], Zg[:], ALU.mult, ALU.add, [e["pZ"], gs[1], Zg], [Zb])
            self.stt(Zf[:], e["pZ"][:], gs[0], Zg[:], ALU.mult, ALU.add, [e["pZ"], gs[1], Zg], [Zf])
        o["yout"](ci, e["pY"])
        yield

    def dplr_run(self, sets, st, o, nch, side=None):
        NS_ = len(sets)
        active, done = [], set()
        chain, next_chain, next_pre = None, 0, 0
        while next_chain < nch:
            while next_pre < nch and next_pre < next_chain + NS_ and len(active) < NS_:
                active.append([next_pre, self.dplr_pre(sets[next_pre % NS_], o, next_pre)])
                next_pre += 1
            if chain is None and next_chain in done:
                chain = self.dplr_chain(sets[next_chain % NS_], st, o, next_chain)
            for item in list(active):
                try:
                    next(item[1])
                except StopIteration:
                    active.remove(item)
                    done.add(item[0])
            if chain is not None:
                try:
                    next(chain)
                except StopIteration:
                    chain = None
                    next_chain += 1
            if side is not None:
                try:
                    next(side)
                except StopIteration:
                    side = None
        if side is not None:
            for _ in side:
                pass

    def to_bd(self, eng_fn, dst, nch):
        for hh in range(2):
            lo, hi = hh * 64, hh * 64 + 64
            eng_fn(dst[lo:hi, :, lo:hi], lo, hi)

    def col(self, es, src_ap, name="col", n=128):
        t = self.c.sb([n, 1], F32, name, es)
        self.ld(t, t[:], src_ap.rearrange("(p o) -> p o", o=1))
        return t

    def rwkv(self, l, s):
        c, nc, Tn = self.c, self.nc, self.Tn
        SG = min(256, Tn)
        nch = SG // 64
        NSG = Tn // SG
        kc, kb = self.kc, self.kb
        BOb = kb[:, K_BO:K_BO + 128]
        BOf = kc[:, K_BO:K_BO + 128]
        w = self.w
        with ExitStack() as es:
            sets, st = self.dplr_env(es)
            Yacc = c.sb([128, Tn], F32, "Yacc", es)
            Bacc = c.sb([128, Tn], F32, "Bacc", es)
            ones = c.sb([128, SG], F32, "ones", es)
            self.memset("pool", ones[:], 1.0, [ones])
            W1 = c.sb([128, 8, 448], BF16, "W1", es)
            Pc = c.sb([128, SG + 1], F32, "Pc", es)
            carry = c.sb([128, 5], F32, "carry", es)
            mu5 = c.sb([128, 5], F32, "mu5", es)
            OS = []
            for i in range(2):
                S_ = {"PAR": c.sb([128, nch, 192], BF16, "PAR", es), "g": c.sb([128, nch], F32, "gcol", es)}
                for nm in ["B", "K", "V"]:
                    S_[nm] = c.sb([128, nch, 128], BF16, "bd" + nm, es)
                for nm in ["PAR", "B", "K", "V"]:
                    self.memset("pool", S_[nm][:], 0.0, [S_[nm]])
                OS.append(S_)
            w2b = c.sb([32, 128], BF16, "w2b", es)
            a2b = c.sb([32, 128], BF16, "a2b", es)
            F = {nm: c.sb([128, SG], F32, "f" + nm, es) for nm in ["r", "k", "v", "lw", "alr", "G", "lg", "E1", "E2", "E3", "kk", "kp", "t0", "t1"]}
            Gs = c.sb([128, nch], F32, "Gs", es)
            self.memset("pool", Gs[:, 0:1], 0.0, [Gs])
            Hb = {nm: c.sb([128, SG], BF16, "h" + nm, es) for nm in ["sq", "rkr"]}
            th = c.sb([32, SG], BF16, "th", es)
            xab = c.sb([32, SG], BF16, "xab", es)
            pp = [c.ps([128, 512], F32, "rpp", es)] * 2
            ppi = [0]

            def nextp():
                ppi[0] ^= 1
                return pp[ppi[0]]

            for pr in range(2):
                cs = slice(pr * 128, pr * 128 + 128)
                kk_c = self.col(es, w["rw_k_k"].h[l, cs], "kkc")
                ka_c = self.col(es, w["rw_k_a"].h[l, cs], "kac")
                rk_c = self.col(es, w["rw_r_k"].h[l].rearrange("h n -> (h n)")[cs], "rkc")
                lnw_c = self.col(es, w["rw_ln_w"].h[l, cs], "lnw")
                lnb_c = self.col(es, w["rw_ln_b"].h[l, cs], "lnb")
                for d in range(2):
                    with ExitStack() as e1:
                        stg = c.sb([128, 8, 128], F32, "stg", e1)
                        srcs = [(C_RKV + pr * 128, 128, pr * 128), (C_RKV + 256 + pr * 128, 128, 256 + pr * 128),
                                (C_RKV + 512 + pr * 128, 128, 512 + pr * 128), ((C_WAF if d == 0 else C_WAB), 32, 768),
                                ((C_WAF if d == 0 else C_WAB) + 32, 32, 800)]
                        o_ = 0
                        for j5, (c0, n, m0) in enumerate(srcs):
                            self.load_w(W1, o_, self.win(l, c0, n), n, stg)
                            self.ld(mu5, mu5[0:n, j5:j5 + 1], w["rw_mu"].h[l, d, m0:m0 + n].rearrange("(p o) -> p o", o=1))
                            o_ += n
                        s32 = c.sb([32, 256], F32, "s32", e1)
                        self.ld(s32, s32[:, 0:128], w["rw_w2"].h[l, d, :, cs])
                        self.ld(s32, s32[:, 128:256], w["rw_a2"].h[l, d, :, cs])
                        self.cp("dve", w2b[:], s32[:, 0:128], [s32], [w2b])
                        self.cp("dve", a2b[:], s32[:, 128:256], [s32], [a2b])
                        c.barrier()
                    self.memset("pool", carry[:], 0.0, [carry])
                    w0_c = self.col(es, w["rw_w0"].h[l, d, cs], "w0c")
                    a0_c = self.col(es, w["rw_a0"].h[l, d, cs], "a0c")
                    self.memset("pool", st["Zf"][:], 0.0, [st["Zf"]])
                    self.memset("pool", st["Zb"][:], 0.0, [st["Zb"]])
                    def prep(sg, S, d=d, w0_c=w0_c, a0_c=a0_c):
                        tau0 = sg * SG
                        PAR, BDB, BDK, BDV = S["PAR"], S["B"], S["K"], S["V"]

                        def shifted(j5, wc, M, out_ap, out_t):
                            p_ = nextp()
                            self.proj(p_, W1, wc, M, tau0, SG, d=d)
                            self.cp("pool", Pc[0:M, 0:1], carry[0:M, j5:j5 + 1], [carry], [Pc])
                            self.cp("act", Pc[0:M, 1:SG + 1], p_[0:M, 0:SG], [p_], [Pc])
                            self.cp("pool", carry[0:M, j5:j5 + 1], Pc[0:M, SG:SG + 1], [Pc], [carry])
                            self.tt("pool", F["t0"][0:M, :], Pc[0:M, 0:SG], Pc[0:M, 1:SG + 1], ALU.subtract, [Pc], [F["t0"]])
                            self.stt(out_ap, F["t0"][0:M, :], mu5[0:M, j5:j5 + 1], Pc[0:M, 1:SG + 1], ALU.mult, ALU.add, [F["t0"], mu5, Pc], [out_t])
                        shifted(0, 0, 128, F["r"][:], F["r"])
                        yield
                        shifted(1, 128, 128, F["k"][:], F["k"])
                        yield
                        shifted(2, 256, 128, F["v"][:], F["v"])
                        self.to_bd(lambda o_ap, lo, hi: self.cp("pool", o_ap, V3(F["v"][lo:hi, :], nch), [F["v"]], [BDV]), BDV, nch)
                        yield
                        shifted(3, 384, 32, F["t1"][0:32, :], F["t1"])
                        self.act(th[:], F["t1"][0:32, :], AF.Tanh, [F["t1"]], [th])
                        yield
                        shifted(4, 416, 32, xab[:], xab)
                        yield
                        p_ = nextp()
                        self.mm(p_[:, 0:SG], w2b[:], th[:], [w2b, th], [p_])
                        self.act(F["lw"][:], p_[:, 0:SG], AF.Sigmoid, [p_, w0_c], [F["lw"]], bias=w0_c[:, 0:1])
                        self.ts("dve", F["lw"][:], F["lw"][:], -math.exp(-0.5), None, ALU.mult, None, [F["lw"]], [F["lw"]])
                        p_ = nextp()
                        self.mm(p_[:, 0:SG], a2b[:], xab[:], [a2b, xab], [p_])
                        self.act(F["alr"][:], p_[:, 0:SG], AF.Sigmoid, [p_, a0_c], [F["alr"]], bias=a0_c[:, 0:1])
                        yield
                        self.c.op("dve", lambda: nc.vector.tensor_tensor_scan(out=F["G"][:], data0=ones[:], data1=F["lw"][:], initial=0.0,
                                                                                op0=ALU.mult, op1=ALU.add), [ones, F["lw"]], [F["G"]])
                        if nch > 1:
                            self.cp("pool", Gs[:, 1:nch], F["G"][:, 63:SG - 1:64], [F["G"]], [Gs])
                        self.tt("pool", V3(F["lg"][:], nch), V3(F["G"][:], nch), bc_last(Gs[:], 64), ALU.subtract, [F["G"], Gs], [F["lg"]])
                        yield
                        self.act(F["E1"][:], F["lg"][:], AF.Exp, [F["lg"]], [F["E1"]])
                        self.act(F["E3"][:], F["lg"][:], AF.Exp, [F["lg"]], [F["E3"]], scale=-1.0)
                        self.tt("pool", F["t0"][:], F["lg"][:], F["lw"][:], ALU.subtract, [F["lg"], F["lw"]], [F["t0"]])
                        self.act(F["E2"][:], F["t0"][:], AF.Exp, [F["t0"]], [F["E2"]])
                        self.cp("pool", S["g"][:], F["E1"][:, 63:SG:64], [F["E1"]], [S["g"]])
                        yield
                        self.ts("dve", F["t0"][:], F["k"][:], kk_c[:, 0:1], None, ALU.mult, None, [F["k"], kk_c], [F["t0"]])
                        self.tt("pool", Hb["sq"][:], F["t0"][:], F["t0"][:], ALU.mult, [F["t0"]], [Hb["sq"]])
                        p_ = nextp()
                        self.mm(p_[:, 0:SG], BOb, Hb["sq"][:], [kb, Hb["sq"]], [p_])
                        self.rsqrt_(F["t1"][:], p_[:, 0:SG], 1.0, self.epsc[:, 0:1], [p_, self.epsc], [F["t1"]])
                        self.tt("pool", F["kk"][:], F["t0"][:], F["t1"][:], ALU.mult, [F["t0"], F["t1"]], [F["kk"]])
                        yield
                        self.ts("dve", F["t0"][:], F["alr"][:], -1.0, ka_c[:, 0:1], ALU.add, ALU.mult, [F["alr"], ka_c], [F["t0"]])
                        self.stt(F["kp"][:], F["t0"][:], 1.0, F["k"][:], ALU.add, ALU.mult, [F["t0"], F["k"]], [F["kp"]])
                        self.tt("pool", F["t0"][:], F["r"][:], F["kp"][:], ALU.mult, [F["r"], F["kp"]], [F["t0"]])
                        self.ts("dve", Hb["rkr"][:], F["t0"][:], rk_c[:, 0:1], None, ALU.mult, None, [F["t0"], rk_c], [Hb["rkr"]])
                        yield
                        p_ = nextp()
                        self.mm(p_[:, 0:SG], BOb, Hb["rkr"][:], [kb, Hb["rkr"]], [p_])
                        if d == 0:
                            self.tt("dve", Bacc[:, tau0:tau0 + SG], p_[:, 0:SG], F["v"][:], ALU.mult, [p_, F["v"]], [Bacc])
                        else:
                            self.tt("dve", F["t1"][:], p_[:, 0:SG], F["v"][:], ALU.mult, [p_, F["v"]], [F["t1"]])
                            rb = rev(Bacc[:, Tn - tau0 - SG:Tn - tau0])
                            self.tt("pool", rb, rb, F["t1"][:], ALU.add, [Bacc, F["t1"]], [Bacc])
                        yield
                        self.tt("pool", PAR[:, :, 128:192], V3(F["r"][:], nch), V3(F["E1"][:], nch), ALU.mult, [F["r"], F["E1"]], [PAR])
                        self.to_bd(lambda o_ap, lo, hi: self.stt(o_ap, V3(F["kk"][lo:hi, :], nch), -1.0, V3(F["E2"][lo:hi, :], nch), ALU.mult, ALU.mult,
                                                                 [F["kk"], F["E2"]], [PAR]), PAR, nch)
                        yield
                        self.to_bd(lambda o_ap, lo, hi: self.tt("pool", o_ap, V3(F["kp"][lo:hi, :], nch), V3(F["E3"][lo:hi, :], nch), ALU.mult,
                                                                [F["kp"], F["E3"]], [BDK]), BDK, nch)
                        yield
                        self.tt("pool", F["t0"][:], F["kk"][:], F["alr"][:], ALU.mult, [F["kk"], F["alr"]], [F["t0"]])
                        self.to_bd(lambda o_ap, lo, hi: self.tt("pool", o_ap, V3(F["t0"][lo:hi, :], nch), V3(F["E3"][lo:hi, :], nch), ALU.mult,
                                                                [F["t0"], F["E3"]], [BDB]), BDB, nch)
                        yield

                    def mk_o(sg, S, d=d):
                        tau0 = sg * SG
                        PAR, BDB, BDK, BDV, G_ = S["PAR"], S["B"], S["K"], S["V"], S["g"]

                        def yout(ci, pY):
                            a0_ = tau0 + ci * 64
                            if d == 0:
                                self.cp("dve", Yacc[:, a0_:a0_ + 64], pY[:], [pY], [Yacc])
                            else:
                                ry = rev(Yacc[:, Tn - a0_ - 64:Tn - a0_])
                                self.tt("dve", ry, ry, pY[:], ALU.add, [Yacc, pY], [Yacc])
                        return dict(PAR=lambda ci: (PAR[:, ci, :], PAR), QB=lambda ci: (BDB[:, ci, :], BDB), QK=lambda ci: (BDK[:, ci, :], BDK),
                                    ZA=lambda ci: (PAR[:, ci, 0:128], PAR), ZR=lambda ci: (PAR[:, ci, 128:192], PAR),
                                    SB=lambda ci: (BDB[:, ci, :], BDB), SK=lambda ci: (BDK[:, ci, :], BDK), V=lambda ci: (BDV[:, ci, :], BDV),
                                    ms=lambda ci: (kc[:, K_MUS:K_MUS + 64], kc), mi=lambda ci: (kc[:, K_MUI:K_MUI + 64], kc),
                                    ml=lambda ci: (kc[:, K_MLS:K_MLS + 64], kc),
                                    gZ=lambda ci: (G_[:, ci:ci + 1], G_), gS=lambda ci: (G_[:, ci:ci + 1], G_), yout=yout)

                    for _ in prep(0, OS[0]):
                        pass
                    for sg in range(NSG):
                        side = prep(sg + 1, OS[(sg + 1) % 2]) if sg + 1 < NSG else None
                        self.dplr_run(sets, st, mk_o(sg, OS[sg % 2]), nch, side=side)
                with ExitStack() as e2:
                    Wz = c.sb([128, 8, 128], BF16, "Wz", e2)
                    stz = c.sb([128, 8, 128], F32, "stz", e2)
                    self.load_w(Wz, 0, self.win(l, C_Z + 256 + pr * 128, 128), 128, stz)
                    ob = [c.sb([128, SG], BF16, "rob", e2) for _ in range(2)]
                    for sg in range(NSG):
                        t0 = sg * SG
                        Y = Yacc[:, t0:t0 + SG]
                        p_ = nextp()
                        self.mm(p_[:, 0:SG], BOf, Y, [kc, Yacc], [p_])
                        self.stt(F["t0"][:], p_[:, 0:SG], -1.0 / 64, Y, ALU.mult, ALU.add, [p_, Yacc], [F["t0"]])
                        self.tt("pool", F["t1"][:], F["t0"][:], F["t0"][:], ALU.mult, [F["t0"]], [F["t1"]])
                        p_ = nextp()
                        self.mm(p_[:, 0:SG], BOf, F["t1"][:], [kc, F["t1"]], [p_])
                        self.rsqrt_(F["t1"][:], p_[:, 0:SG], 1.0 / 64, self.epsc[:, 1:2], [p_, self.epsc], [F["t1"]])
                        self.tt("pool", F["t0"][:], F["t0"][:], F["t1"][:], ALU.mult, [F["t0"], F["t1"]], [F["t0"]])
                        self.ts("dve", F["t0"][:], F["t0"][:], lnw_c[:, 0:1], lnb_c[:, 0:1], ALU.mult, ALU.add, [F["t0"], lnw_c, lnb_c], [F["t0"]])
                        self.tt("pool", F["t0"][:], F["t0"][:], Bacc[:, t0:t0 + SG], ALU.add, [F["t0"], Bacc], [F["t0"]])
                        p_ = nextp()
                        self.proj(p_, Wz, 0, 128, t0, SG)
                        self.act(F["t1"][:], p_[:, 0:SG], AF.Silu, [p_], [F["t1"]])
                        o_b = ob[sg % 2]
                        self.tt("pool", o_b[:], F["t0"][:], F["t1"][:], ALU.mult, [F["t0"], F["t1"]], [o_b])
                        r0 = 256 + pr * 128
                        self.c.dma("sp", self.mixT.h[r0:r0 + 128, t0:t0 + SG], o_b[:], o_b, reads=[o_b], writes=[self.mixT])
                    c.barrier()
            c.barrier()

    def gdn(self, l, s):
        c, nc, Tn = self.c, self.nc, self.Tn
        SG = min(256, Tn)
        nch = SG // 64
        NSG = Tn // SG
        TW = min(512, Tn)
        kc, kb = self.kc, self.kb
        BOb = kb[:, K_BO:K_BO + 128]
        BOf = kc[:, K_BO:K_BO + 128]
        w = self.w
        with ExitStack() as es:
            sets, st = self.dplr_env(es)
            Yacc = c.sb([128, Tn], F32, "Yacc", es)
            ones = c.sb([128, SG], F32, "ones", es)
            self.memset("pool", ones[:], 1.0, [ones])
            QKV = {nm: c.sb([128, Tn], BF16, "g" + nm, es) for nm in "qkv"}
            Wg = c.sb([128, 8, 16], BF16, "Wg", es)
            with ExitStack() as e1:
                stg = c.sb([128, 8, 16], F32, "stgg", e1)
                self.load_w(Wg, 0, self.win(l, C_GDB, 16), 16, stg)
                c.barrier()
            sg_t = c.sb([8, 4, 128], F32, "selg", es)
            self.ld(sg_t, sg_t[:], self.selg.h[:])
            dtb = self.col(es, w["gd_dt_bias"].h[l].rearrange("d h -> (d h)"), "dtb", 8)
            nA = self.col(es, w["gd_a_log"].h[l].rearrange("d h -> (d h)"), "nA", 8)
            self.act(nA[:], nA[:], AF.Exp, [nA], [nA])
            self.ts("pool", nA[:], nA[:], -1.0, None, ALU.mult, None, [nA], [nA])
            gnw = c.sb([128, 1], F32, "gnw", es)
            for hh in range(2):
                self.ld(gnw, gnw[hh * 64:(hh + 1) * 64, :], w["gd_norm_w"].h[l, :].rearrange("(p o) -> p o", o=1))
            F = {nm: c.sb([128, SG], F32, "f" + nm, es) for nm in ["g", "b", "G", "gc", "E1", "Es", "nbk", "t0", "Ex"]}
            g8 = c.sb([8, SG], F32, "g8", es)
            b8 = c.sb([8, SG], F32, "b8", es)
            Gs = c.sb([128, nch], F32, "Gs", es)
            gcT = c.sb([128, nch], F32, "gcT", es)
            self.memset("pool", Gs[:, 0:1], 0.0, [Gs])
            Ms = c.sb([128, nch, 64], F32, "Ms", es)
            Mi = c.sb([128, nch, 64], F32, "Mi", es)
            Ml = c.sb([128, nch, 64], F32, "Ml", es)
            Hb = {nm: c.sb([128, SG], BF16, "h" + nm, es) for nm in ["ZR"]}
            BD = {nm: c.sb([128, nch, 128], BF16, "bd" + nm, es) for nm in ["ZA", "K", "SK", "V"]}
            for nm in BD:
                self.memset("pool", BD[nm][:], 0.0, [BD[nm]])
            PAR = c.sb([128, nch, 192], BF16, "gPAR", es)
            self.memset("pool", PAR[:], 0.0, [PAR])
            pp = [c.ps([128, 512], F32, "gpp", es)] * 2
            ppi = [0]

            def nextp():
                ppi[0] ^= 1
                return pp[ppi[0]]

            for pr in range(2):
                with ExitStack() as e1:
                    Wq = c.sb([128, 8, 384], BF16, "Wqkv", e1)
                    stg = c.sb([128, 8, 128], F32, "stgq", e1)
                    cw = c.sb([128, 3, 4], F32, "cw", e1)
                    for j in range(3):
                        ch0 = j * 256 + pr * 128
                        self.load_w(Wq, j * 128, self.win(l, C_QKV + ch0, 128), 128, stg)
                        self.ld(cw, cw[:, j, :], w["gd_conv_w"].h[l, :, ch0:ch0 + 128].rearrange("i p -> p i"), allow_slow_non_contiguous=True)
                    raw = c.sb([128, Tn + 3], BF16, "raw", e1)
                    self.memset("pool", raw[:, 0:1], 0.0, [raw])
                    self.memset("pool", raw[:, Tn + 1:Tn + 3], 0.0, [raw])
                    acc = c.sb([128, TW], F32, "acc", e1)
                    sl_ = c.sb([128, TW], F32, "sl", e1)
                    sqb = c.sb([128, TW], BF16, "sqb", e1)
                    rn = c.sb([128, TW], F32, "rn", e1)
                    for j, nm in enumerate("qkv"):
                        for tt_ in range(Tn // TW):
                            p_ = nextp()
                            self.proj(p_, Wq, j * 128, 128, tt_ * TW, TW)
                            self.cp(self.eng2(), raw[:, 1 + tt_ * TW:1 + (tt_ + 1) * TW], p_[:, 0:TW], [p_], [raw])
                        for tt_ in range(Tn // TW):
                            t0 = tt_ * TW
                            self.ts("dve", acc[:], raw[:, t0:t0 + TW], cw[:, j, 0:1], None, ALU.mult, None, [raw, cw], [acc])
                            for i in range(1, 4):
                                self.stt(acc[:], raw[:, t0 + i:t0 + i + TW], cw[:, j, i:i + 1], acc[:], ALU.mult, ALU.add, [raw, cw, acc], [acc])
                            if nm == "v":
                                self.act(QKV[nm][:, t0:t0 + TW], acc[:], AF.Silu, [acc], [QKV[nm]])
                            else:
                                self.act(sl_[:], acc[:], AF.Silu, [acc], [sl_])
                                self.tt("pool", sqb[:], sl_[:], sl_[:], ALU.mult, [sl_], [sqb])
                                p_ = nextp()
                                self.mm(p_[:, 0:TW], BOb, sqb[:], [kb, sqb], [p_])
                                self.rsqrt_(rn[:], p_[:, 0:TW], 1.0, self.epsc[:, 0:1], [p_, self.epsc], [rn])
                                if nm == "q":
                                    self.stt(QKV[nm][:, t0:t0 + TW], sl_[:], 0.125, rn[:], ALU.mult, ALU.mult, [sl_, rn], [QKV[nm]])
                                else:
                                    self.tt("pool", QKV[nm][:, t0:t0 + TW], sl_[:], rn[:], ALU.mult, [sl_, rn], [QKV[nm]])
                    c.barrier()
                for d in range(2):
                    self.memset("pool", st["Zf"][:], 0.0, [st["Zf"]])
                    self.memset("pool", st["Zb"][:], 0.0, [st["Zb"]])
                    for sg in range(NSG):
                        tau0 = sg * SG

                        def sa(t, d=d, tau0=tau0):
                            return t[:, tau0:tau0 + SG] if d == 0 else rev(t[:, Tn - tau0 - SG:Tn - tau0])
                        p_ = nextp()
                        self.proj(p_, Wg, 0, 8, tau0, SG, d=d)
                        self.act(b8[:], p_[0:8, 0:SG], AF.Sigmoid, [p_], [b8])
                        p_ = nextp()
                        self.proj(p_, Wg, 8, 8, tau0, SG, d=d)
                        self.act(g8[:], p_[0:8, 0:SG], AF.Exp, [p_, dtb], [g8], bias=dtb[:, 0:1])
                        self.act(g8[:], g8[:], AF.Ln, [g8], [g8], bias=1.0)
                        self.ts("dve", g8[:], g8[:], nA[:, 0:1], None, ALU.mult, None, [g8, nA], [g8])
                        p_ = nextp()
                        self.mm(p_[:, 0:SG], sg_t[:, d * 2 + pr, :], g8[:], [sg_t, g8], [p_])
                        self.cp("act", F["g"][:], p_[:, 0:SG], [p_], [F["g"]])
                        p_ = nextp()
                        self.mm(p_[:, 0:SG], sg_t[:, d * 2 + pr, :], b8[:], [sg_t, b8], [p_])
                        self.cp("dve", F["b"][:], p_[:, 0:SG], [p_], [F["b"]])
                        self.c.op("dve", lambda: nc.vector.tensor_tensor_scan(out=F["G"][:], data0=ones[:], data1=F["g"][:], initial=0.0,
                                                                                op0=ALU.mult, op1=ALU.add), [ones, F["g"]], [F["G"]])
                        if nch > 1:
                            self.cp("pool", Gs[:, 1:nch], F["G"][:, 63:SG - 1:64], [F["G"]], [Gs])
                        self.tt("pool", V3(F["gc"][:], nch), V3(F["G"][:], nch), bc_last(Gs[:], 64), ALU.subtract, [F["G"], Gs], [F["gc"]])
                        self.act(F["E1"][:], F["gc"][:], AF.Exp, [F["gc"]], [F["E1"]])
                        self.tt("pool", V3(F["t0"][:], nch), bc_last(F["gc"][:, 63:SG:64], 64), V3(F["gc"][:], nch), ALU.subtract, [F["gc"]], [F["t0"]])
                        self.act(F["Es"][:], F["t0"][:], AF.Exp, [F["t0"]], [F["Es"]])
                        self.tt("pool", V3(F["t0"][:], nch), V3(F["gc"][:], nch), bc_mid(kc[:, K_SEL:K_SEL + 64], nch), ALU.mult, [F["gc"], kc], [F["t0"]])
                        self.c.op("dve", lambda: nc.vector.reduce_sum(out=gcT[:], in_=V3(F["t0"][:], nch), axis=AX.X), [F["t0"]], [gcT])
                        self.tt("pool", V3(F["t0"][:], nch), V3(F["gc"][:], nch), bc_last(gcT[:], 64), ALU.subtract, [F["gc"], gcT], [F["t0"]])
                        self.ts("dve", F["nbk"][:], F["t0"][:], 0.0, None, ALU.max, None, [F["t0"]], [F["nbk"]])
                        self.ts("dve", F["t0"][:], F["t0"][:], 0.0, None, ALU.min, None, [F["t0"]], [F["t0"]])
                        self.act(F["Ex"][:], F["t0"][:], AF.Exp, [F["t0"]], [F["Ex"]])
                        self.tt("pool", Ms[:], V3(F["Ex"][:], nch), bc_mid(kc[:, K_MUS:K_MUS + 64], nch), ALU.mult, [F["Ex"], kc], [Ms])
                        self.tt("pool", Mi[:], V3(F["Ex"][:], nch), bc_mid(kc[:, K_MUI:K_MUI + 64], nch), ALU.mult, [F["Ex"], kc], [Mi])
                        self.act(F["Ex"][:], F["nbk"][:], AF.Exp, [F["nbk"]], [F["Ex"]], scale=-1.0)
                        self.tt("pool", Ml[:], V3(F["Ex"][:], nch), bc_mid(kc[:, K_MLS:K_MLS + 64], nch), ALU.mult, [F["Ex"], kc], [Ml])
                        kS, qS, vS = sa(QKV["k"]), sa(QKV["q"]), sa(QKV["v"])
                        self.stt(F["nbk"][:], kS, -1.0, F["b"][:], ALU.mult, ALU.mult, [QKV["k"], F["b"]], [F["nbk"]])
                        self.to_bd(lambda o_ap, lo, hi: self.cp("pool", o_ap, V3(F["nbk"][lo:hi, :], nch), [F["nbk"]], [PAR]), PAR, nch)
                        self.to_bd(lambda o_ap, lo, hi: self.tt("pool", o_ap, V3(F["nbk"][lo:hi, :], nch), V3(F["E1"][lo:hi, :], nch), ALU.mult,
                                                                [F["nbk"], F["E1"]], [BD["ZA"]]), BD["ZA"], nch)
                        self.cp("pool", PAR[:, :, 128:192], V3(qS, nch) if d == 0 else qS.rearrange("p (a b) -> p a b", a=nch), [QKV["q"]], [PAR])
                        self.tt("pool", V3(Hb["ZR"][:], nch), PAR[:, :, 128:192], V3(F["E1"][:], nch), ALU.mult, [PAR, F["E1"]], [Hb["ZR"]])
                        self.cp("pool", F["t0"][:], kS, [QKV["k"]], [F["t0"]])
                        self.to_bd(lambda o_ap, lo, hi: self.cp("pool", o_ap, V3(F["t0"][lo:hi, :], nch), [F["t0"]], [BD["K"]]), BD["K"], nch)
                        self.to_bd(lambda o_ap, lo, hi: self.tt("pool", o_ap, V3(F["t0"][lo:hi, :], nch), V3(F["Es"][lo:hi, :], nch), ALU.mult,
                                                                [F["t0"], F["Es"]], [BD["SK"]]), BD["SK"], nch)
                        self.tt("dve", F["nbk"][:], vS, F["b"][:], ALU.mult, [QKV["v"], F["b"]], [F["nbk"]])
                        self.to_bd(lambda o_ap, lo, hi: self.cp("pool", o_ap, V3(F["nbk"][lo:hi, :], nch), [F["nbk"]], [BD["V"]]), BD["V"], nch)

                        def yout(ci, pY, d=d, tau0=tau0):
                            a0_ = tau0 + ci * 64
                            if d == 0:
                                self.cp("dve", Yacc[:, a0_:a0_ + 64], pY[:], [pY], [Yacc])
                            else:
                                ry = rev(Yacc[:, Tn - a0_ - 64:Tn - a0_])
                                self.tt("dve", ry, ry, pY[:], ALU.add, [Yacc, pY], [Yacc])

                        o = dict(PAR=lambda ci: (PAR[:, ci, :], PAR), QB=lambda ci: (BD["K"][:, ci, :], BD["K"]),
                                 QK=lambda ci: (BD["K"][:, ci, :], BD["K"]),
                                 ZA=lambda ci: (BD["ZA"][:, ci, :], BD["ZA"]), ZR=lambda ci: (Hb["ZR"][:, ci * 64:ci * 64 + 64], Hb["ZR"]),
                                 SB=lambda ci: (BD["SK"][:, ci, :], BD["SK"]), SK=lambda ci: (BD["SK"][:, ci, :], BD["SK"]),
                                 V=lambda ci: (BD["V"][:, ci, :], BD["V"]),
                                 ms=lambda ci: (Ms[:, ci, :], Ms), mi=lambda ci: (Mi[:, ci, :], Mi), ml=lambda ci: (Ml[:, ci, :], Ml),
                                 gZ=lambda ci: (F["E1"][:, ci * 64 + 63:ci * 64 + 64], F["E1"]),
                                 gS=lambda ci: None, yout=yout)
                        self.dplr_run(sets, st, o, nch)
                with ExitStack() as e2:
                    Wz = c.sb([128, 8, 128], BF16, "Wz", e2)
                    stz = c.sb([128, 8, 128], F32, "stz", e2)
                    self.load_w(Wz, 0, self.win(l, C_Z + 768 + pr * 128, 128), 128, stz)
                    ob = [c.sb([128, SG], BF16, "gob", e2) for _ in range(2)]
                    for sg in range(NSG):
                        t0 = sg * SG
                        Y = Yacc[:, t0:t0 + SG]
                        self.tt("pool", F["t0"][:], Y, Y, ALU.mult, [Yacc], [F["t0"]])
                        p_ = nextp()
                        self.mm(p_[:, 0:SG], BOf, F["t0"][:], [kc, F["t0"]], [p_])
                        self.rsqrt_(F["Ex"][:], p_[:, 0:SG], 1.0 / 64, self.epsc[:, 0:1], [p_, self.epsc], [F["Ex"]])
                        self.stt(F["t0"][:], Y, gnw[:, 0:1], F["Ex"][:], ALU.mult, ALU.mult, [Yacc, gnw, F["Ex"]], [F["t0"]])
                        p_ = nextp()
                        self.proj(p_, Wz, 0, 128, t0, SG)
                        self.act(F["Ex"][:], p_[:, 0:SG], AF.Silu, [p_], [F["Ex"]])
                        o_b = ob[sg % 2]
                        self.tt("pool", o_b[:], F["t0"][:], F["Ex"][:], ALU.mult, [F["t0"], F["Ex"]], [o_b])
                        r0 = 768 + pr * 128
                        self.c.dma("sp", self.mixT.h[r0:r0 + 128, t0:t0 + SG], o_b[:], o_b, reads=[o_b], writes=[self.mixT])
                    c.barrier()
            c.barrier()

    def ssd(self, l, s):
        c, nc, Tn = self.c, self.nc, self.Tn
        L = 128
        NC_ = Tn // L
        TW = min(512, Tn)
        kc, kb = self.kc, self.kb
        IDb = kb[:, K_ID:K_ID + 128]
        w = self.w
        with ExitStack() as es:
            XC = [c.sb([128, Tn], BF16, "xc%d" % j, es) for j in range(6)]
            Yacc = [c.sb([128, Tn], F32, "Yc%d" % p, es) for p in range(2)]
            banks = [c.ps([128, 512], F32, "sbk", es) for _ in range(6)]

            def sl(b, lo, n):
                t = c.sub(banks[b][:, lo:lo + n])
                t.dep = banks[b].dep
                return t
            pY = [sl(0, 0, 128), sl(1, 0, 128)]
            pCB = [sl(2, 0, 128), sl(2, 128, 128)]
            pRep4 = c.sub(V3(banks[3][:, 0:512], 4))
            pRep4.dep = banks[3].dep
            pTb = c.sub(V3(banks[4][:, 0:256].bitcast(BF16), 4))
            pTb.dep = banks[4].dep
            pTf = sl(4, 256, 16)
            pZU4 = c.sub(V3(banks[5][:, 0:512], 4))
            pZU4.dep = banks[5].dep
            pPj = sl(4, 384, 128)
            pbig = [banks[3], banks[5]]
            with ExitStack() as e1:
                Wx = c.sb([128, 8, 768], BF16, "Wx", e1)
                stg = c.sb([128, 8, 128], F32, "stgx", e1)
                cw = c.sb([128, 6, 4], F32, "cwx", e1)
                cb = c.sb([128, 6], F32, "cbx", e1)
                self.ld(cb, cb[:], w["m_conv_b"].h[l, :].rearrange("(j p) -> p j", p=128), allow_slow_non_contiguous=True)
                for j in range(6):
                    self.load_w(Wx, j * 128, self.win(l, C_XBC + j * 128, 128), 128, stg)
                    self.ld(cw, cw[:, j, :], w["m_conv_w"].h[l, :, j * 128:(j + 1) * 128].rearrange("i p -> p i"), allow_slow_non_contiguous=True)
                raw = c.sb([128, Tn + 3], BF16, "rawx", e1)
                self.memset("pool", raw[:, 0:1], 0.0, [raw])
                self.memset("pool", raw[:, Tn + 1:Tn + 3], 0.0, [raw])
                acc = c.sb([128, TW], F32, "accx", e1)
                for j in range(6):
                    for tt_ in range(Tn // TW):
                        p_ = pbig[tt_ % 2]
                        self.proj(p_, Wx, j * 128, 128, tt_ * TW, TW)
                        self.cp(self.eng2(), raw[:, 1 + tt_ * TW:1 + (tt_ + 1) * TW], p_[:, 0:TW], [p_], [raw])
                    for tt_ in range(Tn // TW):
                        t0 = tt_ * TW
                        self.ts("dve", acc[:], raw[:, t0:t0 + TW], cw[:, j, 0:1], cb[:, j:j + 1], ALU.mult, ALU.add, [raw, cw, cb], [acc])
                        for i in range(1, 4):
                            self.stt(acc[:], raw[:, t0 + i:t0 + i + TW], cw[:, j, i:i + 1], acc[:], ALU.mult, ALU.add, [raw, cw, acc], [acc])
                        self.act(XC[j][:, t0:t0 + TW], acc[:], AF.Silu, [acc], [XC[j]])
                c.barrier()
            Wd = c.sb([128, 8, 8], BF16, "Wd", es)
            with ExitStack() as e1:
                stg = c.sb([128, 8, 8], F32, "stgd", e1)
                self.load_w(Wd, 0, self.win(l, C_MDT, 8), 8, stg)
                c.barrier()
            sm_t = c.sb([8, 8, 128], F32, "selm", es)
            self.ld(sm_t, sm_t[:], self.selm.h[:])
            dtb = self.col(es, w["m_dt_bias"].h[l].rearrange("d h -> (d h)"), "mdtb", 8)
            nA = self.col(es, w["m_a_log"].h[l].rearrange("d h -> (d h)"), "mnA", 8)
            self.act(nA[:], nA[:], AF.Exp, [nA], [nA])
            self.ts("pool", nA[:], nA[:], -1.0, None, ALU.mult, None, [nA], [nA])
            ones8 = c.sb([8, L], F32, "ones8", es)
            self.memset("pool", ones8[:], 1.0, [ones8])
            dt8 = c.sb([8, L], F32, "dt8", es)
            a8 = c.sb([8, L], F32, "a8", es)
            acs8 = c.sb([8, L], F32, "acs8", es)
            TT_ = c.sb([128, 16], F32, "TT", es)
            xsT = c.sb([128, 256], BF16, "xsT", es)
            BT = c.sb([128, 2, 128], BF16, "BT", es)
            CBs = [c.sb([128, 128], F32, "CBs", es) for _ in range(2)]
            dd = c.sb([128, 128], F32, "dd", es)
            Em = c.sb([128, 128], F32, "Em", es)
            SD = [c.sb([128, 128], BF16, "SD", es) for _ in range(4)]
            Er = c.sb([128, 128], F32, "Er", es)
            Ce = [c.sb([128, 128], BF16, "Ce", es) for _ in range(4)]
            cols = c.sb([128, 4, 4], F32, "cols", es)
            Xp = [c.sb([128, 128], BF16, "Xp%d" % i, es) for i in range(4)]
            Xe = [c.sb([128, 128], BF16, "Xe%d" % i, es) for i in range(4)]
            Zf = [c.sb([128, 128], F32, "sZf%d" % h, es) for h in range(4)]
            Zb = [c.sb([128, 128], BF16, "sZb%d" % h, es) for h in range(4)]
            for i in range(4):
                self.memset("pool", Xp[i][:], 0.0, [Xp[i]])
                self.memset("pool", Xe[i][:], 0.0, [Xe[i]])
            for d in range(2):
                for h in range(4):
                    self.memset("pool", Zf[h][:], 0.0, [Zf[h]])
                    self.memset("pool", Zb[h][:], 0.0, [Zb[h]])
                mask = kc[:, K_MU128:K_MU128 + 128] if d == 0 else kc[:, K_ML128:K_ML128 + 128]
                last = L - 1 if d == 0 else 0
                order = range(NC_) if d == 0 else range(NC_ - 1, -1, -1)
                for ci in order:
                    t0 = ci * L
                    tk = slice(t0, t0 + L)
                    self.proj(pPj, Wd, 0, 8, t0, L)
                    self.act(dt8[:], pPj[0:8, 0:L], AF.Exp, [pPj, dtb], [dt8], bias=dtb[:, 0:1])
                    self.act(dt8[:], dt8[:], AF.Ln, [dt8], [dt8], bias=1.0)
                    self.ts("dve", a8[:], dt8[:], nA[:, 0:1], None, ALU.mult, None, [dt8, nA], [a8])
                    if d == 0:
                        self.c.op("dve", lambda: nc.vector.tensor_tensor_scan(out=acs8[:], data0=ones8[:], data1=a8[:], initial=0.0,
                                                                                op0=ALU.mult, op1=ALU.add), [ones8, a8], [acs8])
                    else:
                        self.c.op("dve", lambda: nc.vector.tensor_tensor_scan(out=rev(acs8[:]), data0=ones8[:], data1=rev(a8[:]), initial=0.0,
                                                                                op0=ALU.mult, op1=ALU.add), [ones8, a8], [acs8])
                    self.mm(pTf[:, 0:8], dt8[:], kc[0:8, K_ID:K_ID + 8], [dt8, kc], [pTf])
                    self.mm(pTf[:, 8:16], acs8[:], kc[0:8, K_ID:K_ID + 8], [acs8, kc], [pTf])
                    self.cp("act", TT_[:], pTf[:], [pTf], [TT_])
                    for g in range(2):
                        self.mm(pCB[g][:], XC[2 + g][:, tk], XC[4 + g][:, tk], [XC[2 + g], XC[4 + g]], [pCB[g]])
                        self.cp("act", CBs[g][:], pCB[g][:], [pCB[g]], [CBs[g]])
                    for j in range(2):
                        self.tr(pTb[:, j, :], XC[j][:, tk], IDb, [XC[j], kb], [pTb])
                        self.tr(pTb[:, 2 + j, :], XC[2 + j][:, tk], IDb, [XC[2 + j], kb], [pTb])
                    self.cp("dve", V3(xsT[:], 2), pTb[:, 0:2, :], [pTb], [xsT])
                    self.cp("act", BT[:], pTb[:, 2:4, :], [pTb], [BT])
                    for h in range(4):
                        self.mm(pRep4[:, h, :], sm_t[:, d * 4 + h, :], acs8[:], [sm_t, acs8], [pRep4])
                    for h in range(4):
                        g, pr, hh = h // 2, h // 2, h % 2
                        dh = d * 4 + h
                        self.ts("dve", dd[:], pRep4[:, h, :], TT_[:, 8 + dh:9 + dh], 0.0, ALU.subtract, ALU.min, [pRep4, TT_], [dd])
                        self.act(Em[:], dd[:], AF.Exp, [dd], [Em])
                        self.tt("pool", Em[:], Em[:], mask, ALU.mult, [Em, kc], [Em])
                        self.tt("pool", SD[h][:], CBs[g][:], Em[:], ALU.mult, [CBs[g], Em], [SD[h]])
                        self.act(Er[:], pRep4[:, h, :], AF.Exp, [pRep4], [Er])
                        self.tt("pool", Ce[h][:], XC[4 + g][:, tk], Er[:], ALU.mult, [XC[4 + g], Er], [Ce[h]])
                        self.cp("dve", cols[:, h, 0:1], pRep4[:, h, last:last + 1], [pRep4], [cols])
                        self.act(cols[:, h, 1:2], TT_[:, 8 + dh:9 + dh], AF.Exp, [TT_, cols], [cols], scale=-1.0, bias=cols[:, h, 0:1])
                        self.act(cols[:, h, 2:3], cols[:, h, 0:1], AF.Exp, [cols], [cols])
                        xs_h = xsT[:, h * 64:(h + 1) * 64]
                        self.ts("dve", Xp[h][:, hh * 64:(hh + 1) * 64], xs_h, TT_[:, dh:dh + 1], None, ALU.mult, None, [xsT, TT_], [Xp[h]])
                        self.ts("dve", Xe[h][:, hh * 64:(hh + 1) * 64], Xp[h][:, hh * 64:(hh + 1) * 64], cols[:, h, 1:2], None, ALU.mult, None,
                                [Xp[h], cols], [Xe[h]])
                    for h in range(4):
                        g, pr, hh = h // 2, h // 2, h % 2
                        self.mm(pY[pr][:], Xp[h][:], SD[h][:], [Xp[h], SD[h]], [pY[pr]], start=(hh == 0), stop=False)
                        self.mm(pY[pr][:], Zb[h][:], Ce[h][:], [Zb[h], Ce[h]], [pY[pr]], start=False, stop=(hh == 1))
                    for h in range(4):
                        self.mm(pZU4[:, h, :], BT[:, h // 2, :], Xe[h][:], [BT, Xe[h]], [pZU4])
                    for pr in range(2):
                        if d == 0:
                            self.cp("act", Yacc[pr][:, tk], pY[pr][:], [pY[pr]], [Yacc[pr]])
                        else:
                            self.tt("dve", Yacc[pr][:, tk], Yacc[pr][:, tk], pY[pr][:], ALU.add, [Yacc[pr], pY[pr]], [Yacc[pr]])
                    for h in range(4):
                        self.stt(Zf[h][:], Zf[h][:], cols[:, h, 2:3], pZU4[:, h, :], ALU.mult, ALU.add, [Zf[h], cols, pZU4], [Zf[h]])
                        self.cp("act", Zb[h][:], Zf[h][:], [Zf[h]], [Zb[h]])
            with ExitStack() as e2:
                Wz = c.sb([128, 8, 256], BF16, "Wzc", e2)
                stz = c.sb([128, 8, 256], F32, "stzc", e2)
                self.load_w(Wz, 0, self.win(l, C_Z + 512, 256), 256, stz)
                Dc = c.sb([128, 2], F32, "Dc", e2)
                nw = c.sb([128, 2], F32, "nw", e2)
                for pr in range(2):
                    for hh in range(2):
                        self.ld(Dc, Dc[hh * 64:(hh + 1) * 64, pr:pr + 1], w["m_d"].h[l, pr * 2 + hh:pr * 2 + hh + 1].partition_broadcast(64))
                self.ld(nw, nw[:], w["m_norm_w"].h[l, :].rearrange("(j p) -> p j", p=128), allow_slow_non_contiguous=True)
                gt = [c.sb([128, TW], F32, "gt%d" % p, e2) for p in range(2)]
                sq = c.sb([128, TW], F32, "sqc", e2)
                sz = c.sb([128, TW], F32, "szc", e2)
                rs = c.sb([128, TW], F32, "rsc", e2)
                ob = [c.sb([128, 2, TW], BF16, "cob", e2) for _ in range(2)]
                for tt_ in range(Tn // TW):
                    t0 = tt_ * TW
                    for pr in range(2):
                        self.stt(gt[pr][:], XC[pr][:, t0:t0 + TW], Dc[:, pr:pr + 1], Yacc[pr][:, t0:t0 + TW], ALU.mult, ALU.add,
                                 [XC[pr], Dc, Yacc[pr]], [gt[pr]])
                        p_ = banks[2 + pr]
                        self.proj(p_, Wz, pr * 128, 128, t0, TW)
                        self.act(sz[:], p_[:, 0:TW], AF.Silu, [p_], [sz])
                        self.tt("pool", gt[pr][:], gt[pr][:], sz[:], ALU.mult, [gt[pr], sz], [gt[pr]])
                        self.tt("pool", sq[:], gt[pr][:], gt[pr][:], ALU.mult, [gt[pr]], [sq])
                        self.mm(banks[0][:, 0:TW], kc[:, K_ONE:K_ONE + 128], sq[:], [kc, sq], [banks[0]], start=(pr == 0), stop=(pr == 1))
                    self.rsqrt_(rs[:], banks[0][:, 0:TW], 1.0 / 256, self.epsc[:, 0:1], [banks[0], self.epsc], [rs])
                    o_b = ob[tt_ % 2]
                    for pr in range(2):
                        self.stt(o_b[:, pr, :], gt[pr][:], nw[:, pr:pr + 1], rs[:], ALU.mult, ALU.mult, [gt[pr], nw, rs], [o_b])
                    self.c.dma("sp", self.mixT.h[512:768, t0:t0 + TW].rearrange("(a p) t -> p a t", p=128), o_b[:], o_b, reads=[o_b], writes=[self.mixT])
                c.barrier()
            c.barrier()

    def zero_mix(self, i):
        c, Tn = self.c, self.Tn
        TW = min(512, Tn)
        with ExitStack() as es:
            z = c.sb([128, 2, TW], BF16, "zmix", es)
            self.memset("pool", z[:], 0.0, [z])
            for j in range(Tn // TW):
                self.c.dma("sp", self.mixT.h[i * 256:(i + 1) * 256, j * TW:(j + 1) * TW].rearrange("(a p) t -> p a t", p=128), z[:], z,
                           reads=[z], writes=[self.mixT])
            c.barrier()

    def build(self):
        self.declare()
        self.consts()
        for l in range(self.DEPTH):
            self.p0(l)
            for s in range(self.NS):
                self.p1(l, s)
                if "A" in self.mixers:
                    self.mla(l, s)
                if "B" in self.mixers:
                    self.rwkv(l, s)
                if "C" in self.mixers:
                    self.ssd(l, s)
                if "D" in self.mixers:
                    self.gdn(l, s)
                for i, m in enumerate("ABCD"):
                    if m not in self.mixers:
                        self.zero_mix(i)
                self.p6(l, s)
        self.c.finish([self.y] + ([self.mixT] if self.dbg else []))
        return self.nc


def make_consts(Tn):
    p = np.arange(128)
    kc = np.zeros((128, K_END), np.float32)
    kc[:, K_ID:K_ID + 128] = np.eye(128)
    col = np.arange(64)
    kc[:, K_MUS:K_MUS + 64] = ((p % 64)[:, None] < col[None, :])
    kc[:, K_MUI:K_MUI + 64] = ((p % 64)[:, None] <= col[None, :])
    kc[:, K_BO:K_BO + 128] = ((p // 64)[:, None] == (p // 64)[None, :])
    kc[:, K_ONE:K_ONE + 128] = 1.0
    kc[:, K_SEL:K_SEL + 64] = ((p % 64)[:, None] == col[None, :])
    kc[:, K_MU128:K_MU128 + 128] = (p[:, None] <= p[None, :])
    kc[:, K_ML128:K_ML128 + 128] = (p[:, None] >= p[None, :])
    kc[:, K_MLS:K_MLS + 64] = ((p % 64)[:, None] > col[None, :])
    inv = 1.0 / (10000.0 ** (np.arange(0, 32, 2, dtype=np.float32) / np.float32(32)))
    ang = np.arange(Tn, dtype=np.float32)[None, :] * inv.astype(np.float32)[:, None]
    rope = np.zeros((96, 2, Tn), np.float32)
    rope[0:64, 0, :] = 1.0
    rope[64:80, 0, :] = np.cos(ang)
    rope[80:96, 0, :] = np.cos(ang)
    rope[64:80, 1, :] = np.sin(ang)
    rope[80:96, 1, :] = np.sin(ang)
    selg = np.zeros((8, 4, 128), np.float32)
    for d in range(2):
        for pr in range(2):
            for hh in range(2):
                selg[d * 4 + pr * 2 + hh, d * 2 + pr, hh * 64:(hh + 1) * 64] = 1.0
    selm = np.zeros((8, 8, 128), np.float32)
    for r in range(8):
        selm[r, r, :] = 1.0
    return dict(kconst=kc, rope=rope, selg=selg, selm=selm)


WNAMES = ["w_ada", "b_ada", "g_pre", "g_post", "w_in", "w_out", "a_g_q", "a_w_uq", "a_g_kv", "a_w_ukv", "rw_mu", "rw_w0",
          "rw_w2", "rw_a0", "rw_a2", "rw_k_k", "rw_k_a", "rw_r_k", "rw_ln_w", "rw_ln_b", "m_conv_w", "m_conv_b",
          "m_dt_bias", "m_a_log", "m_d", "m_norm_w", "gd_conv_w", "gd_dt_bias", "gd_a_log", "gd_norm_w"]


def kernel(x_prompt, x_sample, c_prompt, c_sample, **weights):
    x_prompt = np.asarray(x_prompt, np.float32)
    x_sample = np.asarray(x_sample, np.float32)
    Tn = x_prompt.shape[1]
    DEPTH = np.asarray(weights["w_ada"]).shape[0]
    xs = np.concatenate([x_prompt, x_sample], axis=0)
    cs = np.concatenate([np.asarray(c_prompt, np.float32), np.asarray(c_sample, np.float32)], axis=0)
    nseq = xs.shape[0]
    NS = -(-nseq // N_CORES)
    b = Builder(Tn, NS, DEPTH)
    nc = b.build()
    consts = make_consts(Tn)
    wmap = {k: np.ascontiguousarray(np.asarray(weights[k], np.float32)) for k in WNAMES}
    in_maps = []
    for core in range(N_CORES):
        idx = [(core * NS + j) if (core * NS + j) < nseq else 0 for j in range(NS)]
        m = dict(wmap)
        m.update(consts)
        m["x"] = np.ascontiguousarray(xs[idx])
        m["c"] = np.ascontiguousarray(cs[idx])
        in_maps.append(m)
    res = run_bass_kernel_spmd(nc, in_maps, core_ids=list(range(N_CORES)))
    out = np.zeros_like(xs)
    for core in range(N_CORES):
        yc = res.results[core]["y"]
        for j in range(NS):
            sidx = core * NS + j
            if sidx < nseq:
                out[sidx] = yc[j]
    nb = x_prompt.shape[0]
    return (out[:nb], out[nb:])
```

```python
import math
import numpy as np
import concourse.bass as bass
import concourse.mybir as mybir
from concourse.bass_utils import run_bass_kernel_spmd
from contextlib import ExitStack

F32 = mybir.dt.float32
BF16 = mybir.dt.bfloat16
AF = mybir.ActivationFunctionType
ALU = mybir.AluOpType
AX = mybir.AxisListType

D_MODEL = 1024
P_IN = 4024
N_CORES = 8
EPS = 1e-6
EMBED_WAITS = 1
C_CQ, C_CKV, C_KR, C_RKV, C_WAF, C_WAB, C_XBC, C_MDT, C_QKV, C_GDB, C_GDA, C_Z = (
    0, 384, 512, 544, 1312, 1376, 1440, 2208, 2216, 2984, 2992, 3000)
K_ID, K_MUS, K_MUI, K_BO, K_ONE, K_SEL, K_MU128, K_ML128, K_MLS, K_END = 0, 128, 192, 256, 384, 512, 576, 704, 832, 896


class Dep:
    __slots__ = ("w", "r", "name", "excl")

    def __init__(self, name=""):
        self.w = {}
        self.r = {}
        self.name = name
        self.excl = False


class T:
    def __init__(self, h, name, dep=None):
        self.h = h
        self.name = name
        self.dep = dep or Dep(name)
        self.dsem = None
        self.dcount = 0

    def __getitem__(self, k):
        return self.h[k]


class Ctx:
    def __init__(self, nc):
        self.nc = nc
        self.E = {"pe": nc.tensor, "act": nc.scalar, "dve": nc.vector, "pool": nc.gpsimd, "sp": nc.sync}
        self.sem, self.cnt, self.known = {}, {}, {}
        self.stack = ExitStack()
        for e in self.E:
            self.sem[e] = self.stack.enter_context(nc.semaphore("s_" + e))
            self.cnt[e] = 0
            self.known[e] = {}
        self.uid = 0
        self.ninst = 0
        self.dtiles = []
        self.free_dsems = []
        self.all_dsems = []
        self.pending = None

    def sb(self, shape, dt=F32, name=None, stack=None):
        self.uid += 1
        name = (name or "t") + "_%d" % self.uid
        h = (stack or self.stack).enter_context(self.nc.sbuf_tensor(name, list(shape), dt))
        return T(h, name)

    def ps(self, shape, dt=F32, name=None, stack=None):
        self.uid += 1
        name = (name or "p") + "_%d" % self.uid
        esz = 4 if dt == F32 else 2
        n = 1
        for v in shape[1:]:
            n *= v
        per_bank = 2048 // esz
        nb = -(-n // per_bank)
        h = (stack or self.stack).enter_context(self.nc.psum_tensor(name, [128, nb * per_bank], dt))
        v = h[0:shape[0], 0:n]
        if len(shape) == 3:
            v = v.rearrange("p (a b) -> p a b", a=shape[1])
        t = T(v, name)
        t.dep.excl = True
        return t

    def sub(self, ap, name="s"):
        self.uid += 1
        return T(ap, name + "_%d" % self.uid)

    def dram(self, name, shape, dt=F32, kind="Internal"):
        return T(self.nc.dram_tensor(name, list(shape), dt, kind=kind), name)

    def _wait(self, eng, sem, val):
        k = self.known[eng]
        sid = id(sem)
        if k.get(sid, 0) >= val:
            return
        k[sid] = val
        if self.pending is not None:
            self.pending.append((sem, val))
            return
        self.E[eng].wait_ge(sem, val)
        self.ninst += 1

    def _acquire(self, eng, reads, writes):
        own = id(self.sem[eng])
        for d in reads:
            for sid, (sem, val) in d.dep.w.items():
                self._wait(eng, sem, val)
            if d.dep.excl:
                for sid, (sem, val) in d.dep.r.items():
                    if sid != own:
                        self._wait(eng, sem, val)
        for d in writes:
            for sid, (sem, val) in d.dep.w.items():
                if sid != own or eng != "pe":
                    self._wait(eng, sem, val)
            for sid, (sem, val) in d.dep.r.items():
                self._wait(eng, sem, val)

    def _flush(self, eng, ins=None):
        p, self.pending = self.pending, None
        last = []
        if ins is not None and EMBED_WAITS:
            while p and len(last) < EMBED_WAITS:
                last.append(p.pop())
        for (sem, val) in p:
            self.E[eng].wait_ge(sem, val)
            self.ninst += 1
        return last

    def _record(self, sem, val, reads, writes):
        sid = id(sem)
        for d in reads:
            d.dep.r[sid] = (sem, val)
        for d in writes:
            d.dep.w[sid] = (sem, val)

    def op(self, eng, fn, reads=(), writes=()):
        self.pending = []
        self._acquire(eng, reads, writes)
        last = self._flush(eng, True)
        ins = fn()
        for (sem_, val_) in last:
            ins._wait_ge(sem_, val_)
        self.cnt[eng] += 1
        ins.then_inc(self.sem[eng], 1)
        self.ninst += 1
        self._record(self.sem[eng], self.cnt[eng], reads, writes)
        return ins

    def dma(self, q, out_ap, in_ap, sbt, reads=(), writes=(), **kw):
        ds = sbt.dsem
        if ds is None:
            if self.free_dsems:
                ds = self.free_dsems.pop()
            else:
                ds = [self.stack.enter_context(self.nc.semaphore("d%d" % len(self.all_dsems))), 0]
                self.all_dsems.append(ds)
            sbt.dsem = ds
            self.dtiles.append(sbt)
        if ds[1]:
            self._wait(q, ds[0], 16 * ds[1])
        self._acquire(q, reads, writes)
        ins = self.E[q].dma_start(out=out_ap, in_=in_ap, **kw)
        ds[1] += 1
        ins.then_inc(ds[0], 16)
        self.ninst += 1
        self._record(ds[0], 16 * ds[1], reads, writes)
        return ins

    def finish(self, out_tiles):
        for t in out_tiles:
            for sid, (sem, val) in t.dep.w.items():
                self._wait("sp", sem, val)

    def barrier(self):
        for e in self.E:
            for f in self.E:
                if self.cnt[f]:
                    self._wait(e, self.sem[f], self.cnt[f])
            for ds in self.all_dsems:
                if ds[1]:
                    self._wait(e, ds[0], 16 * ds[1])
        for t in self.dtiles:
            self.free_dsems.append(t.dsem)
            t.dsem = None
        self.dtiles = []


def V3(ap, a):
    return ap.rearrange("p (a b) -> p a b", a=a)


def bc_mid(ap2, n):
    return ap2.unsqueeze(1).to_broadcast([ap2.shape[0], n, ap2.shape[1]])


def bc_last(ap2, n):
    return ap2.unsqueeze(2).to_broadcast([ap2.shape[0], ap2.shape[1], n])


def rev(ap):
    pat = [list(x) for x in ap.ap]
    st, n = pat[-1]
    pat[-1] = [-st, n]
    return bass.AP(ap.tensor, ap.offset + st * (n - 1), pat)


class Builder:
    def __init__(self, Tn, NS, DEPTH, mixers="ABCD", dbg=False):
        self.Tn, self.NS, self.DEPTH, self.mixers, self.dbg = Tn, NS, DEPTH, mixers, dbg
        self.nc = bass.Bass("TRN2", target_bir_lowering=False)
        self.c = Ctx(self.nc)
        self.rr = 0

    def eng2(self):
        self.rr ^= 1
        return "dve" if self.rr else "act"

    def cp(self, eng, out, in_, R, W):
        nc = self.nc
        if eng == "act":
            return self.c.op("act", lambda: nc.scalar.copy(out=out, in_=in_), R, W)
        e = nc.vector if eng == "dve" else nc.gpsimd
        return self.c.op(eng, lambda: e.tensor_copy(out=out, in_=in_), R, W)

    def tt(self, eng, out, a, b, op, R, W):
        e = self.nc.vector if eng == "dve" else self.nc.gpsimd
        return self.c.op(eng, lambda: e.tensor_tensor(out=out, in0=a, in1=b, op=op), R, W)

    def ts(self, eng, out, a, s1, s2, op0, op1, R, W):
        e = self.nc.vector if eng == "dve" else self.nc.gpsimd
        if s2 is None:
            return self.c.op(eng, lambda: e.tensor_scalar(out=out, in0=a, scalar1=s1, scalar2=None, op0=op0), R, W)
        return self.c.op(eng, lambda: e.tensor_scalar(out=out, in0=a, scalar1=s1, scalar2=s2, op0=op0, op1=op1), R, W)

    def stt(self, out, a, s, b, op0, op1, R, W):
        nc = self.nc
        return self.c.op("dve", lambda: nc.vector.scalar_tensor_tensor(out=out, in0=a, scalar=s, in1=b, op0=op0, op1=op1), R, W)

    def act(self, out, in_, func, R, W, **kw):
        nc = self.nc
        return self.c.op("act", lambda: nc.scalar.activation(out=out, in_=in_, func=func, **kw), R, W)

    def mm(self, out, lhsT, rhs, R, W, start=True, stop=True, skip=False):
        nc = self.nc
        if skip:
            return self.c.op("pe", lambda: nc.tensor.matmul(out, lhsT=lhsT, rhs=rhs, start=False, stop=False, skip_group_check=True), R, W)
        return self.c.op("pe", lambda: nc.tensor.matmul(out, lhsT=lhsT, rhs=rhs, start=start, stop=stop), R, W)

    def tr(self, out, in_, ident, R, W):
        nc = self.nc
        return self.c.op("pe", lambda: nc.tensor.transpose(out=out, in_=in_, identity=ident), R, W)

    def memset(self, eng, ap, val, W):
        e = self.nc.vector if eng == "dve" else self.nc.gpsimd
        return self.c.op(eng, lambda: e.memset(ap, val), (), W)

    def ld(self, dst_tile, dst_ap, src_ap, R=(), **kw):
        return self.c.dma("sp", dst_ap, src_ap, dst_tile, reads=R, writes=[dst_tile], **kw)

    def rsqrt_(self, out, in_, scale, eps_ap, R, W):
        self.act(out, in_, AF.Ln, R, W, scale=scale, bias=eps_ap)
        self.act(out, out, AF.Exp, W, W, scale=-0.5)

    def load_w(self, dst, dcol, src3, n, st, neg=False, eng=None):
        KC = src3.shape[1]
        self.ld(st, st[:, 0:KC, 0:n], src3)
        eng = eng or self.eng2()
        if neg:
            nc = self.nc
            self.c.op("act", lambda: nc.scalar.mul(out=dst[:, 0:KC, dcol:dcol + n], in_=st[:, 0:KC, 0:n], mul=-1.0), [st], [dst])
        else:
            self.cp(eng, dst[:, 0:KC, dcol:dcol + n], st[:, 0:KC, 0:n], [st], [dst])

    def win(self, l, c0, n):
        return self.w["w_in"].h[l, :, c0:c0 + n].rearrange("(kc p) n -> p kc n", p=128)

    def hT_ap(self, kc, tau0, n, d=0, shift=0):
        Tn = self.Tn
        if d == 0:
            col = 1 + tau0 - shift
            return self.hT[:, kc, col:col + n]
        lo = Tn - tau0 - n + 1 + shift
        return rev(self.hT[:, kc, lo:lo + n])

    def proj(self, out_ps, Wt, wc, M, tau0, n, R_extra=(), d=0, W2=None, w2c=0):
        steps = [(Wt, wc, 0)] + ([(W2, w2c, 1)] if W2 is not None else [])
        tot = 8 * len(steps)
        i = 0
        for (Wx, cc, sh) in steps:
            for kc in range(8):
                self.mm(out_ps[0:M, 0:n], Wx[:, kc, cc:cc + M], self.hT_ap(kc, tau0, n, d, sh),
                        [Wx, self.hT_t], [out_ps], start=(i == 0), stop=(i == tot - 1))
                i += 1

    def declare(self):
        c, Tn, NS, DEPTH = self.c, self.Tn, self.NS, self.DEPTH
        shapes = dict(
            w_ada=[DEPTH, 1024, 3072], b_ada=[DEPTH, 3072], g_pre=[DEPTH, 1024], g_post=[DEPTH, 1024],
            w_in=[DEPTH, 1024, P_IN], w_out=[DEPTH, 1024, 1024], a_g_q=[DEPTH, 384], a_w_uq=[DEPTH, 384, 384],
            a_g_kv=[DEPTH, 128], a_w_ukv=[DEPTH, 128, 512], rw_mu=[DEPTH, 2, 832], rw_w0=[DEPTH, 2, 256],
            rw_w2=[DEPTH, 2, 32, 256], rw_a0=[DEPTH, 2, 256], rw_a2=[DEPTH, 2, 32, 256], rw_k_k=[DEPTH, 256],
            rw_k_a=[DEPTH, 256], rw_r_k=[DEPTH, 4, 64], rw_ln_w=[DEPTH, 256], rw_ln_b=[DEPTH, 256],
            m_conv_w=[DEPTH, 4, 768], m_conv_b=[DEPTH, 768], m_dt_bias=[DEPTH, 2, 4], m_a_log=[DEPTH, 2, 4],
            m_d=[DEPTH, 4], m_norm_w=[DEPTH, 256], gd_conv_w=[DEPTH, 4, 768], gd_dt_bias=[DEPTH, 2, 4],
            gd_a_log=[DEPTH, 2, 4], gd_norm_w=[DEPTH, 64])
        self.w = {k: c.dram(k, v, F32, kind="ExternalInput") for k, v in shapes.items()}
        self.x = c.dram("x", [NS, Tn, 1024], F32, kind="ExternalInput")
        self.cin = c.dram("c", [NS, 1024], F32, kind="ExternalInput")
        self.kconst = c.dram("kconst", [128, K_END], F32, kind="ExternalInput")
        self.rope = c.dram("rope", [96, 2, Tn], F32, kind="ExternalInput")
        self.selg = c.dram("selg", [8, 4, 128], F32, kind="ExternalInput")
        self.selm = c.dram("selm", [8, 8, 128], F32, kind="ExternalInput")
        self.y = c.dram("y", [NS, Tn, 1024], F32, kind="ExternalOutput")
        self.mixT = c.dram("mixT", [1024, Tn], BF16, kind="ExternalOutput" if self.dbg else "Internal")

    def consts(self):
        c = self.c
        self.kc = c.sb([128, K_END], F32, "kc")
        self.ld(self.kc, self.kc[:], self.kconst.h[:])
        self.kb = c.sb([128, K_END], BF16, "kb")
        self.cp("dve", self.kb[:], self.kc[:], [self.kc], [self.kb])
        self.epsc = c.sb([128, 2], F32, "epsc")
        self.memset("pool", self.epsc[:, 0:1], EPS, [self.epsc])
        self.memset("pool", self.epsc[:, 1:2], 64e-5, [self.epsc])
        self.hT_t = c.sb([128, 8, self.Tn + 2], BF16, "hT")
        self.hT = self.hT_t
        self.memset("pool", self.hT[:, :, 0:1], 0.0, [self.hT_t])
        self.memset("pool", self.hT[:, :, self.Tn + 1:self.Tn + 2], 0.0, [self.hT_t])
        self.gg = [c.sb([128, 1024], F32, "gg%d" % s) for s in range(self.NS)]
        self.a0 = c.sb([128, 8, self.NS], F32, "a0")
        self.a1 = c.sb([128, 8, self.NS], F32, "a1")

    def p0(self, l):
        c, nc, NS = self.c, self.nc, self.NS
        with ExitStack() as es:
            cT = c.sb([128, 8, NS], F32, "cT", es)
            for s in range(NS):
                self.ld(cT, cT[:, :, s], self.cin.h[s, :].rearrange("(kc p) -> p kc", p=128), allow_slow_non_contiguous=True)
            scb = c.sb([128, 8, NS], BF16, "scb", es)
            self.act(scb[:], cT[:], AF.Silu, [cT], [scb])
            wada = c.sb([128, 8, 3072], BF16, "wada", es)
            st = c.sb([128, 8, 512], F32, "st0", es)
            for j in range(6):
                self.load_w(wada, j * 512, self.w["w_ada"].h[l, :, j * 512:(j + 1) * 512].rearrange("(kc p) n -> p kc n", p=128), 512, st)
            bT = c.sb([128, 24], F32, "bT", es)
            self.ld(bT, bT[:], self.w["b_ada"].h[l, :].rearrange("(j p) -> p j", p=128), allow_slow_non_contiguous=True)
            gp = c.sb([128, 8], F32, "gp", es)
            self.ld(gp, gp[:], self.w["g_pre"].h[l, :].rearrange("(j p) -> p j", p=128), allow_slow_non_contiguous=True)
            pm = c.ps([128, 16 * NS], F32, "pm", es)
            for idx in range(16):
                for kc in range(8):
                    self.mm(pm[:, idx * NS:(idx + 1) * NS], wada[:, kc, idx * 128:(idx + 1) * 128], scb[:, kc, :],
                            [wada, scb], [pm], start=(kc == 0), stop=(kc == 7))
            pm3 = V3(pm[:], 16)
            self.tt("dve", self.a0[:], pm3[:, 0:8, :], bc_last(bT[:, 0:8], NS), ALU.add, [pm, bT], [self.a0])
            tmp = c.sb([128, 8, NS], F32, "tmp0", es)
            self.tt("dve", tmp[:], pm3[:, 8:16, :], bc_last(bT[:, 8:16], NS), ALU.add, [pm, bT], [tmp])
            self.ts("pool", tmp[:], tmp[:], 1.0, None, ALU.add, None, [tmp], [tmp])
            self.tt("pool", self.a1[:], tmp[:], bc_last(gp[:], NS), ALU.mult, [tmp, gp], [self.a1])
            bg = c.sb([128, 1024], F32, "bg", es)
            self.ld(bg, bg[:], self.w["b_ada"].h[l, 2048:3072].partition_broadcast(128))
            gpo = c.sb([128, 1024], F32, "gpo", es)
            self.ld(gpo, gpo[:], self.w["g_post"].h[l, :].partition_broadcast(128))
            for s in range(NS):
                pg = c.ps([128, 1024], F32, "pg", es) if s == 0 else pg
                for n in range(2):
                    for kc in range(8):
                        self.mm(pg[:, n * 512:(n + 1) * 512], scb[:, kc, s:s + 1].to_broadcast([128, 128]),
                                wada[:, kc, 2048 + n * 512:2048 + (n + 1) * 512], [scb, wada], [pg], start=(kc == 0), stop=(kc == 7))
                self.tt("dve", self.gg[s][:], pg[:], bg[:], ALU.add, [pg, bg], [self.gg[s]])
                self.tt("pool", self.gg[s][:], self.gg[s][:], gpo[:], ALU.mult, [self.gg[s], gpo], [self.gg[s]])
            c.barrier()

    def xsrc(self, l, s):
        return (self.x if l == 0 else self.y), s

    def p1(self, l, s):
        c, nc, Tn = self.c, self.nc, self.Tn
        src, _ = self.xsrc(l, s)
        with ExitStack() as es:
            xt = [c.sb([128, 1024], F32, "xt", es) for _ in range(2)]
            xn = [c.sb([128, 1024], BF16, "xn", es) for _ in range(2)]
            junk = c.sb([128, 1024], BF16, "junk", es)
            ss = [c.sb([128, 2], F32, "ss", es) for _ in range(2)]
            pt = [c.ps([128, 1024], BF16, "pt", es) for _ in range(2)]
            tm = [c.sb([128, 8, 128], F32, "tm", es) for _ in range(2)]
            for tt_ in range(Tn // 128):
                b = tt_ % 2
                self.ld(xt[b], xt[b][:], src.h[s, tt_ * 128:(tt_ + 1) * 128, :], R=[src])
                self.act(junk[:], xt[b][:], AF.Square, [xt[b]], [junk, ss[b]], accum_out=ss[b][:, 0:1])
                self.rsqrt_(ss[b][:, 1:2], ss[b][:, 0:1], 1.0 / 1024, self.epsc[:, 0:1], [ss[b], self.epsc], [ss[b]])
                self.ts("dve", xn[b][:], xt[b][:], ss[b][:, 1:2], None, ALU.mult, None, [xt[b], ss[b]], [xn[b]])
                for fc in range(8):
                    self.tr(pt[b][:, fc * 128:(fc + 1) * 128], xn[b][:, fc * 128:(fc + 1) * 128], self.kb[:, K_ID:K_ID + 128],
                            [xn[b], self.kb], [pt[b]])
                self.tt("dve", tm[b][:], V3(pt[b][:], 8), bc_last(self.a1[:, :, s], 128), ALU.mult, [pt[b], self.a1], [tm[b]])
                self.tt("pool", self.hT[:, :, 1 + tt_ * 128:1 + (tt_ + 1) * 128], tm[b][:], bc_last(self.a0[:, :, s], 128), ALU.add,
                        [tm[b], self.a0], [self.hT_t])
            c.barrier()

    def p6(self, l, s):
        c, nc, Tn = self.c, self.nc, self.Tn
        src, _ = self.xsrc(l, s)
        TW = min(512, Tn)
        with ExitStack() as es:
            wo = c.sb([128, 8, 1024], BF16, "wo", es)
            st = c.sb([128, 8, 512], F32, "st6", es)
            for j in range(2):
                self.load_w(wo, j * 512, self.w["w_out"].h[l, :, j * 512:(j + 1) * 512].rearrange("(kc p) n -> p kc n", p=128), 512, st)
            mt = [c.sb([128, 8, TW], BF16, "mt", es) for _ in range(2)]
            xt = [c.sb([128, 1024], F32, "xt6", es) for _ in range(2)]
            yt = [c.sb([128, 1024], F32, "yt6", es) for _ in range(2)]
            junk = c.sb([128, 1024], BF16, "junk6", es)
            ss = [c.sb([128, 2], F32, "ss6", es) for _ in range(2)]
            po = [c.ps([128, 1024], F32, "po", es) for _ in range(2)]
            k = 0
            for j in range(Tn // TW):
                mb = mt[j % 2]
                self.ld(mb, mb[:], self.mixT.h[:, j * TW:(j + 1) * TW].rearrange("(a p) t -> p a t", p=128), R=[self.mixT])
                for u in range(TW // 128):
                    b = k % 2
                    k += 1
                    t0 = j * TW + u * 128
                    self.ld(xt[b], xt[b][:], src.h[s, t0:t0 + 128, :], R=[src])
                    for n in range(2):
                        for kc in range(8):
                            self.mm(po[b][:, n * 512:(n + 1) * 512], mb[:, kc, u * 128:(u + 1) * 128], wo[:, kc, n * 512:(n + 1) * 512],
                                    [mb, wo], [po[b]], start=(kc == 0), stop=(kc == 7))
                    self.act(junk[:], po[b][:], AF.Square, [po[b]], [junk, ss[b]], accum_out=ss[b][:, 0:1])
                    self.rsqrt_(ss[b][:, 1:2], ss[b][:, 0:1], 1.0 / 1024, self.epsc[:, 0:1], [ss[b], self.epsc], [ss[b]])
                    self.tt("dve", yt[b][:], po[b][:], self.gg[s][:], ALU.mult, [po[b], self.gg[s]], [yt[b]])
                    self.stt(yt[b][:], yt[b][:], ss[b][:, 1:2], xt[b][:], ALU.mult, ALU.add, [yt[b], ss[b], xt[b]], [yt[b]])
                    self.c.dma("sp", self.y.h[s, t0:t0 + 128, :], yt[b][:], yt[b], reads=[yt[b]], writes=[self.y])
            c.barrier()

    def mla(self, l, s):
        c, nc, Tn = self.c, self.nc, self.Tn
        TW = min(512, Tn)
        NQG = Tn // TW
        NKB = Tn // 128
        U = TW // 128
        scale = 96 ** -0.5
        ID_b = self.kb[:, K_ID:K_ID + 128]
        ONE_b = self.kb[:, K_ONE:K_ONE + 128]
        with ExitStack() as es:
            Wm = c.sb([128, 8, 960], BF16, "Wm", es)
            Wq = c.sb([128, 3, 384], BF16, "Wq", es)
            Wqb = c.sb([128, 3, 384], BF16, "Wqb", es)
            Wkv = c.sb([128, 512], BF16, "Wkv", es)
            with ExitStack() as e1:
                st = c.sb([128, 8, 512], F32, "stA", e1)
                self.memset("pool", Wm[:, :, 512:704], 0.0, [Wm])
                self.load_w(Wm, 0, self.win(l, C_CQ, 384), 384, st)
                self.load_w(Wm, 384, self.win(l, C_CKV, 128), 128, st)
                self.load_w(Wm, 512 + 64, self.win(l, C_KR, 32), 32, st)
                self.load_w(Wm, 608 + 64, self.win(l, C_KR + 16, 16), 16, st, neg=True)
                self.load_w(Wm, 608 + 80, self.win(l, C_KR, 16), 16, st)
                self.load_w(Wm, 704, self.win(l, C_Z, 256), 256, st)
                stq = c.sb([128, 3, 384], F32, "stq", e1)
                self.ld(stq, stq[:], self.w["a_w_uq"].h[l].rearrange("(kc p) n -> p kc n", p=128))
                gq = c.sb([128, 3], F32, "gq", e1)
                self.ld(gq, gq[:], self.w["a_g_q"].h[l, :].rearrange("(j p) -> p j", p=128), allow_slow_non_contiguous=True)
                self.memset("pool", Wqb[:], 0.0, [Wqb])
                for fc in range(3):
                    self.ts("dve", Wq[:, fc, :], stq[:, fc, :], gq[:, fc:fc + 1], None, ALU.mult, None, [stq, gq], [Wq])
                    src4 = V3(Wq[:, fc, :], 4)
                    dst4 = V3(Wqb[:, fc, :], 4)
                    self.c.op("act", lambda s4=src4, d4=dst4: nc.scalar.mul(out=d4[:, :, 64:80], in_=s4[:, :, 80:96], mul=-1.0), [Wq], [Wqb])
                    self.cp("pool", dst4[:, :, 80:96], src4[:, :, 64:80], [Wq], [Wqb])
                stk = c.sb([128, 512], F32, "stk", e1)
                self.ld(stk, stk[:], self.w["a_w_ukv"].h[l])
                gkv = c.sb([128, 1], F32, "gkv", e1)
                self.ld(gkv, gkv[:], self.w["a_g_kv"].h[l, :].rearrange("(p o) -> p o", o=1))
                self.ts("dve", Wkv[:], stk[:], gkv[:, 0:1], None, ALU.mult, None, [stk, gkv], [Wkv])
                c.barrier()
            KT = [c.sb([96, Tn], BF16, "KT%d" % h, es) for h in range(4)]
            Va = c.sb([128, NKB, 4, 65], BF16, "Va", es)
            self.memset("pool", Va[:, :, :, 64:65], 1.0, [Va])
            cosx = c.sb([96, TW], F32, "cosx", es)
            sinx = c.sb([96, TW], F32, "sinx", es)
            t1 = c.sb([96, TW], F32, "t1", es)
            t2 = c.sb([96, TW], F32, "t2", es)
            with ExitStack() as e2:
                ckT = c.sb([128, TW], BF16, "ckT", e2)
                sqk = c.sb([128, TW], BF16, "sqk", e2)
                rk = c.sb([128, TW], F32, "rk", e2)
                rcol = c.sb([128, 2], F32, "rcol", e2)
                pA = [c.ps([128, 512], F32, "pA", e2) for _ in range(4)]
                pB = c.ps([128, 512], F32, "pB", e2)
                pV = c.ps([128, 256], F32, "pV", e2)
                pc = c.ps([128, 2], F32, "pc", e2)
                for j in range(NQG):
                    t0 = j * TW
                    self.ld(cosx, cosx[:], self.rope.h[:, 0, t0:t0 + TW])
                    self.ld(sinx, sinx[:], self.rope.h[:, 1, t0:t0 + TW])
                    self.proj(pB, Wm, 384, 128, t0, TW)
                    self.cp("act", ckT[:], pB[:, 0:TW], [pB], [ckT])
                    self.tt("pool", sqk[:], ckT[:], ckT[:], ALU.mult, [ckT], [sqk])
                    self.mm(pA[0][:, 0:TW], ONE_b, sqk[:], [self.kb, sqk], [pA[0]])
                    self.rsqrt_(rk[:], pA[0][:, 0:TW], 1.0 / 128, self.epsc[:, 0:1], [pA[0], self.epsc], [rk])
                    for h in range(4):
                        pa = pA[1 + h % 2]
                        self.mm(pa[0:64, 0:TW], Wkv[:, h * 128:h * 128 + 64], ckT[:], [Wkv, ckT], [pa])
                        self.tt("dve", KT[h][0:64, t0:t0 + TW], pa[0:64, 0:TW], rk[0:64, :], ALU.mult, [pa, rk], [KT[h]])
                    self.proj(pA[3], Wm, 512, 96, t0, TW)
                    self.proj(pB, Wm, 608, 96, t0, TW)
                    self.tt("dve", t1[64:96, :], pA[3][64:96, 0:TW], cosx[64:96, :], ALU.mult, [pA[3], cosx], [t1])
                    self.tt("dve", t2[64:96, :], pB[64:96, 0:TW], sinx[64:96, :], ALU.mult, [pB, sinx], [t2])
                    for h in range(4):
                        self.tt("pool", KT[h][64:96, t0:t0 + TW], t1[64:96, :], t2[64:96, :], ALU.add, [t1, t2], [KT[h]])
                    for u in range(U):
                        kb_ = (t0 + u * 128) // 128
                        self.mm(pc[:, 0:1], sqk[:, u * 128:(u + 1) * 128], self.kb[:, K_ONE:K_ONE + 1], [sqk, self.kb], [pc])
                        self.rsqrt_(rcol[:, 0:1], pc[:, 0:1], 1.0 / 128, self.epsc[:, 0:1], [pc, self.epsc], [rcol])
                        for h in range(4):
                            self.mm(pV[:, h * 64:(h + 1) * 64], ckT[:, u * 128:(u + 1) * 128], Wkv[:, h * 128 + 64:h * 128 + 128], [ckT, Wkv], [pV])
                        self.ts("dve", Va[:, kb_, :, 0:64], V3(pV[:], 4), rcol[:, 0:1], None, ALU.mult, None, [pV, rcol], [Va])
                c.barrier()
            with ExitStack() as e3:
                NB = 3
                pS = [c.ps([128, 512], F32, "pS", e3) for _ in range(NB)]
                pz = c.ps([128, 512], F32, "pz", e3)
                pO = [c.ps([128, 512], F32, "pO", e3) for _ in range(2)]
                pBC = c.ps([128, 512], F32, "pBC", e3)
                Pt = [c.sb([128, TW], BF16, "Pt", e3) for _ in range(NB)]
                cqT = c.sb([128, 3, TW], BF16, "cqT", e3)
                sq = c.sb([128, 3, TW], BF16, "sq", e3)
                rq = c.sb([128, TW], F32, "rq", e3)
                CR = c.sb([96, TW], F32, "CR", e3)
                SR = c.sb([96, TW], F32, "SR", e3)
                qg_t = [c.sb([96, TW], BF16, "qg%d" % h, e3) for h in range(4)]
                rs = c.sb([128, TW], F32, "rs", e3)
                bcs = c.sb([64, TW], F32, "bcs", e3)
                sz = c.sb([64, TW], F32, "sz", e3)
                on = c.sb([64, TW], F32, "on", e3)
                og = [c.sb([64, TW], BF16, "og", e3) for _ in range(2)]
                it = 0
                for qg in range(NQG):
                    q0 = qg * TW
                    self.ld(cosx, cosx[:], self.rope.h[:, 0, q0:q0 + TW])
                    self.ld(sinx, sinx[:], self.rope.h[:, 1, q0:q0 + TW])
                    for fc in range(3):
                        self.proj(pS[fc], Wm, fc * 128, 128, q0, TW)
                        self.cp("act", cqT[:, fc, :], pS[fc][:, 0:TW], [pS[fc]], [cqT])
                        self.tt("pool", sq[:, fc, :], cqT[:, fc, :], cqT[:, fc, :], ALU.mult, [cqT], [sq])
                    for fc in range(3):
                        self.mm(pz[:, 0:TW], ONE_b, sq[:, fc, :], [self.kb, sq], [pz], start=(fc == 0), stop=(fc == 2))
                    self.rsqrt_(rq[:], pz[:, 0:TW], 1.0 / 384, self.epsc[:, 0:1], [pz, self.epsc], [rq])
                    self.tt("pool", CR[:], cosx[:], rq[0:96, :], ALU.mult, [cosx, rq], [CR])
                    self.tt("pool", SR[:], sinx[:], rq[0:96, :], ALU.mult, [sinx, rq], [SR])
                    for h in range(4):
                        pa, pb = pS[0], pS[1]
                        for fc in range(3):
                            self.mm(pa[0:96, 0:TW], Wq[:, fc, h * 96:(h + 1) * 96], cqT[:, fc, :], [Wq, cqT], [pa], start=(fc == 0), stop=(fc == 2))
                        for fc in range(3):
                            self.mm(pb[0:96, 0:TW], Wqb[:, fc, h * 96:(h + 1) * 96], cqT[:, fc, :], [Wqb, cqT], [pb], start=(fc == 0), stop=(fc == 2))
                        self.tt("dve", t1[:], pa[0:96, 0:TW], CR[:], ALU.mult, [pa, CR], [t1])
                        self.tt("dve", t2[:], pb[0:96, 0:TW], SR[:], ALU.mult, [pb, SR], [t2])
                        self.tt("pool", qg_t[h][:], t1[:], t2[:], ALU.add, [t1, t2], [qg_t[h]])
                    for h in range(4):
                        po_ = pO[h % 2]
                        def score(kb_, b):
                            self.mm(pS[b][:, 0:TW], KT[h][:, kb_ * 128:(kb_ + 1) * 128], qg_t[h][:], [KT[h], qg_t[h]], [pS[b]])
                        score(0, it % NB)
                        if NKB > 1:
                            score(1, (it + 1) % NB)
                        for kb_ in range(NKB):
                            b = it % NB
                            it += 1
                            if kb_ + 2 < NKB:
                                score(kb_ + 2, (it + 1) % NB)
                            self.act(Pt[b][:], pS[b][:, 0:TW], AF.Exp, [pS[b]], [Pt[b]], scale=scale)
                            self.mm(po_[0:65, 0:TW], Va[:, kb_, h, :], Pt[b][:], [Va, Pt[b]], [po_], start=(kb_ == 0), stop=(kb_ == NKB - 1))
                        self.proj(pz, Wm, 704 + h * 64, 64, q0, TW)
                        self.act(sz[:], pz[0:64, 0:TW], AF.Silu, [pz], [sz])
                        self.c.op("dve", lambda p=po_: nc.vector.reciprocal(out=rs[64:65, :], in_=p[64:65, 0:TW]), [po_], [rs])
                        self.mm(pBC[0:64, 0:TW], self.kc[64:65, K_ONE:K_ONE + 64], rs[64:65, :], [self.kc, rs], [pBC])
                        self.cp("act", bcs[:], pBC[0:64, 0:TW], [pBC], [bcs])
                        self.tt("dve", on[:], po_[0:64, 0:TW], bcs[:], ALU.mult, [po_, bcs], [on])
                        ob = og[h % 2]
                        self.tt("pool", ob[:], on[:], sz[:], ALU.mult, [on, sz], [ob])
                        self.c.dma("sp", self.mixT.h[h * 64:(h + 1) * 64, q0:q0 + TW], ob[:], ob, reads=[ob], writes=[self.mixT])
            c.barrier()

    def dplr_env(self, es, nsets=3):
        c = self.c
        sets = []

        def sl(bank, lo, n=128):
            t = c.sub(bank[:, lo:lo + n])
            t.dep = bank.dep
            return t
        cb = c.ps([128, 512], F32, "dpc", es)
        chain = {"pW": sl(cb, 0), "pU": sl(cb, 128), "pZ": sl(cb, 256), "pY": sl(cb, 384, 64)}
        for i in range(nsets):
            e = dict(chain)
            bA = c.ps([128, 512], F32, "dpa", es)
            bB = c.ps([128, 512], F32, "dpb", es)
            e["pQB"], e["pQK"], e["pNL"] = sl(bA, 0, 192), sl(bA, 192, 192), sl(bA, 384)
            e["pXP"] = sl(bA, 0, 256)
            e["pPt"] = sl(bB, 0)
            e["pT"] = c.sub(V3(bB[:, 256:448].bitcast(BF16), 3))
            e["pT"].dep = bB.dep
            for nm in ["Nt", "Pt0", "Pt1"]:
                e[nm] = c.sb([128, 128], F32, "d" + nm, es)
            for nm in ["XP0", "XP1"]:
                e[nm] = c.sb([128, 256], F32, "d" + nm, es)
            for nm in ["AK", "AR", "TT", "W", "U"]:
                e[nm] = c.sb([128, 128], BF16, "d" + nm, es)
            e["Tr"] = c.sb([128, 3, 128], BF16, "dTr", es)
            sets.append(e)
        st = {"Zf": c.sb([128, 128], F32, "Zf", es), "Zb": c.sb([128, 128], BF16, "Zb", es), "Zg": c.sb([128, 128], F32, "Zg", es)}
        return sets, st

    def dplr_pre(self, e, o, ci):
        kc, kb = self.kc, self.kb
        IDf = kc[:, K_ID:K_ID + 128]
        IDb = kb[:, K_ID:K_ID + 128]
        PAR, QB, QK = o["PAR"](ci), o["QB"](ci), o["QK"](ci)
        ms, mi, ml = o["ms"](ci), o["mi"](ci), o["ml"](ci)
        PA = PAR[0][:, 0:128]
        same = o.get("same", False)
        self.mm(e["pQB"][:], QB[0], PAR[0], [QB[1], PAR[1]], [e["pQB"]])
        if not same:
            self.mm(e["pQK"][:], QK[0], PAR[0], [QK[1], PAR[1]], [e["pQK"]])
        self.mm(e["pNL"][:], PA, QB[0], [QB[1], PAR[1]], [e["pNL"]])
        for j, nm in enumerate(["SB", "SK", "V"]):
            if same and nm == "SB":
                continue
            src = o[nm](ci)
            self.tr(e["pT"][:, j, :], src[0], IDb, [src[1], kb], [e["pT"]])
        XP = e["XP0"]
        self.tt("dve", V3(XP[:, 128:256], 2), V3(e["pQB"][:, 0:128], 2), bc_mid(ms[0], 2), ALU.mult, [e["pQB"], ms[1]], [XP])
        self.tt("dve", V3(e["Nt"][:], 2), V3(e["pNL"][:], 2), bc_mid(ml[0], 2), ALU.mult, [e["pNL"], ml[1]], [e["Nt"]])
        self.tt("dve", e["AR"][:, 0:64], e["pQB"][:, 128:192], mi[0], ALU.mult, [e["pQB"], mi[1]], [e["AR"]])
        if same:
            self.cp("pool", e["AK"][:], XP[:, 128:256], [XP], [e["AK"]])
            self.cp("act", e["Tr"][:, 1:3, :], e["pT"][:, 1:3, :], [e["pT"]], [e["Tr"]])
        else:
            self.tt("dve", V3(e["AK"][:], 2), V3(e["pQK"][:, 0:128], 2), bc_mid(ms[0], 2), ALU.mult, [e["pQK"], ms[1]], [e["AK"]])
            self.tt("dve", e["AR"][:, 64:128], e["pQK"][:, 128:192], mi[0], ALU.mult, [e["pQK"], mi[1]], [e["AR"]])
            self.cp("act", e["Tr"][:], e["pT"][:], [e["pT"]], [e["Tr"]])
        self.tt("pool", XP[:, 0:128], XP[:, 128:256], IDf, ALU.add, [XP, kc], [XP])
        yield
        Pt = e["Nt"]
        Ptn = e["Pt1"]
        self.mm(e["pPt"][:], XP[:, 128:256], Pt[:], [XP, Pt], [e["pPt"]])
        self.mm(e["pXP"][:, 128:256], Pt[:], XP[:, 128:256], [XP, Pt], [e["pXP"]])
        XPn = e["XP1"]
        self.cp("act", Ptn[:], e["pPt"][:], [e["pPt"]], [Ptn])
        self.cp("act", XPn[:, 128:256], e["pXP"][:, 128:256], [e["pXP"]], [XPn])
        self.cp("pool", XPn[:, 0:128], XP[:, 0:128], [XP], [XPn])
        XP, Pt = XPn, Ptn
        yield
        for k in range(1, 6):
            XPn = e["XP%d" % (k % 2)]
            Ptn = e["Pt%d" % (k % 2)]
            if k < 4:
                self.mm(e["pXP"][:], Pt[:], XP[:], [Pt, XP], [e["pXP"]])
            else:
                self.mm(e["pXP"][:, 0:128], Pt[:], XP[:, 0:128], [Pt, XP], [e["pXP"]])
            if k < 5:
                self.mm(e["pPt"][:], XP[:, 128:256], Pt[:], [XP, Pt], [e["pPt"]])
            self.tt("dve", XPn[:, 0:128], XP[:, 0:128], e["pXP"][:, 0:128], ALU.add, [XP, e["pXP"]], [XPn])
            if k < 4:
                self.cp("act", XPn[:, 128:256], e["pXP"][:, 128:256], [e["pXP"]], [XPn])
            if k < 5:
                self.cp("act", Ptn[:], e["pPt"][:], [e["pPt"]], [Ptn])
            XP, Pt = XPn, Ptn
            yield
        self.cp("pool", e["TT"][:], XP[:, 0:128], [XP], [e["TT"]])

    def dplr_chain(self, e, st, o, ci):
        nc = self.nc
        Zf, Zb, Zg = st["Zf"], st["Zb"], st["Zg"]
        ZA, ZR = o["ZA"](ci), o["ZR"](ci)
        SBt, SKt, Vt = e["Tr"][:, 0, :], e["Tr"][:, 1, :], e["Tr"][:, 2, :]
        gz, gs = o["gZ"](ci), o["gS"](ci)
        self.c.op("act", lambda: nc.scalar.mul(out=Zg[:], in_=Zf[:], mul=gz[0]), [Zf, gz[1]], [Zg])
        self.mm(e["pW"][:], ZA[0], Zb[:], [ZA[1], Zb], [e["pW"]], start=True, stop=False)
        self.mm(e["pW"][:], e["AK"][:], Vt, [e["AK"], e["Tr"]], [e["pW"]], start=False, stop=True)
        self.cp("act", e["W"][:], e["pW"][:], [e["pW"]], [e["W"]])
        yield
        self.mm(e["pU"][:], e["TT"][:], e["W"][:], [e["TT"], e["W"]], [e["pU"]])
        if o.get("same", False):
            self.tt("dve", e["U"][:], e["pU"][:], Vt, ALU.add, [e["pU"], e["Tr"]], [e["U"]])
            yield
            self.mm(e["pZ"][:], SKt, e["U"][:], [e["Tr"], e["U"]], [e["pZ"]])
            self.mm(e["pY"][:], Zb[:], ZR[0], [Zb, ZR[1]], [e["pY"]], start=True, stop=False)
            self.mm(e["pY"][:], e["U"][:], e["AR"][:, 0:64], [e["U"], e["AR"]], [e["pY"]], start=False, stop=True)
        else:
            self.cp("dve", e["U"][:], e["pU"][:], [e["pU"]], [e["U"]])
            yield
            self.mm(e["pZ"][:], SBt, e["U"][:], [e["Tr"], e["U"]], [e["pZ"]], start=True, stop=False)
            self.mm(e["pZ"][:], SKt, Vt, [e["Tr"]], [e["pZ"]], start=False, stop=True)
            self.mm(e["pY"][:], Zb[:], ZR[0], [Zb, ZR[1]], [e["pY"]], start=True, stop=False)
            self.mm(e["pY"][:], e["U"][:], e["AR"][:, 0:64], [e["U"], e["AR"]], [e["pY"]], start=False, stop=False)
            self.mm(e["pY"][:], Vt, e["AR"][:, 64:128], [e["Tr"], e["AR"]], [e["pY"]], start=False, stop=True)
        if gs is None:
            self.tt("dve", Zb[:], e["pZ"][:], Zg[:], ALU.add, [e["pZ"], Zg], [Zb])
            self.tt("dve", Zf[:], e["pZ"][:], Zg[:], ALU.add, [e["pZ"], Zg], [Zf])
        else:
            self.stt(Zb[:], e["pZ"][:], gs[0], Zg[:], ALU.mult, ALU.add, [e["pZ"], gs[1], Zg], [Zb])
            self.stt(Zf[:], e["pZ"][:], gs[0], Zg[:], ALU.mult, ALU.add, [e["pZ"], gs[1], Zg], [Zf])
        o["yout"](ci, e["pY"])
        yield

    def dplr_run(self, sets, st, o, nch, side=None):
        NS_ = len(sets)
        active, done = [], set()
        chain, next_chain, next_pre = None, 0, 0
        while next_chain < nch:
            while next_pre < nch and next_pre < next_chain + NS_ and len(active) < NS_:
                active.append([next_pre, self.dplr_pre(sets[next_pre % NS_], o, next_pre)])
                next_pre += 1
            if chain is None and next_chain in done:
                chain = self.dplr_chain(sets[next_chain % NS_], st, o, next_chain)
            for item in list(active):
                try:
                    next(item[1])
                except StopIteration:
                    active.remove(item)
                    done.add(item[0])
            if chain is not None:
                try:
                    next(chain)
                except StopIteration:
                    chain = None
                    next_chain += 1
            if side is not None:
                try:
                    next(side)
                except StopIteration:
                    side = None
        if side is not None:
            for _ in side:
                pass

    def to_bd(self, eng_fn, dst, nch):
        for hh in range(2):
            lo, hi = hh * 64, hh * 64 + 64
            eng_fn(dst[lo:hi, :, lo:hi], lo, hi)

    def col(self, es, src_ap, name="col", n=128):
        t = self.c.sb([n, 1], F32, name, es)
        self.ld(t, t[:], src_ap.rearrange("(p o) -> p o", o=1))
        return t

    def rwkv(self, l, s):
        c, nc, Tn = self.c, self.nc, self.Tn
        SG = min(256, Tn)
        nch = SG // 64
        NSG = Tn // SG
        kc, kb = self.kc, self.kb
        BOb = kb[:, K_BO:K_BO + 128]
        BOf = kc[:, K_BO:K_BO + 128]
        w = self.w
        with ExitStack() as es:
            sets, st = self.dplr_env(es)
            Yacc = c.sb([128, Tn], F32, "Yacc", es)
            Bacc = c.sb([128, Tn], F32, "Bacc", es)
            ones = c.sb([128, SG], F32, "ones", es)
            self.memset("pool", ones[:], 1.0, [ones])
            W1 = c.sb([128, 8, 448], BF16, "W1", es)
            Pc = c.sb([128, SG + 1], F32, "Pc", es)
            carry = c.sb([128, 5], F32, "carry", es)
            mu5 = c.sb([128, 5], F32, "mu5", es)
            OS = []
            for i in range(2):
                S_ = {"PAR": c.sb([128, nch, 192], BF16, "PAR", es), "g": c.sb([128, nch], F32, "gcol", es)}
                for nm in ["B", "K", "V"]:
                    S_[nm] = c.sb([128, nch, 128], BF16, "bd" + nm, es)
                for nm in ["PAR", "B", "K", "V"]:
                    self.memset("pool", S_[nm][:], 0.0, [S_[nm]])
                OS.append(S_)
            w2b = c.sb([32, 128], BF16, "w2b", es)
            a2b = c.sb([32, 128], BF16, "a2b", es)
            F = {nm: c.sb([128, SG], F32, "f" + nm, es) for nm in ["r", "k", "v", "lw", "alr", "G", "lg", "E1", "E2", "E3", "kk", "kp", "t0", "t1"]}
            Gs = c.sb([128, nch], F32, "Gs", es)
            self.memset("pool", Gs[:, 0:1], 0.0, [Gs])
            Hb = {nm: c.sb([128, SG], BF16, "h" + nm, es) for nm in ["sq", "rkr"]}
            th = c.sb([32, SG], BF16, "th", es)
            xab = c.sb([32, SG], BF16, "xab", es)
            pp = [c.ps([128, 512], F32, "rpp", es)] * 2
            ppi = [0]

            def nextp():
                ppi[0] ^= 1
                return pp[ppi[0]]

            for pr in range(2):
                cs = slice(pr * 128, pr * 128 + 128)
                kk_c = self.col(es, w["rw_k_k"].h[l, cs], "kkc")
                ka_c = self.col(es, w["rw_k_a"].h[l, cs], "kac")
                rk_c = self.col(es, w["rw_r_k"].h[l].rearrange("h n -> (h n)")[cs], "rkc")
                lnw_c = self.col(es, w["rw_ln_w"].h[l, cs], "lnw")
                lnb_c = self.col(es, w["rw_ln_b"].h[l, cs], "lnb")
                for d in range(2):
                    with ExitStack() as e1:
                        stg = c.sb([128, 8, 128], F32, "stg", e1)
                        srcs = [(C_RKV + pr * 128, 128, pr * 128), (C_RKV + 256 + pr * 128, 128, 256 + pr * 128),
                                (C_RKV + 512 + pr * 128, 128, 512 + pr * 128), ((C_WAF if d == 0 else C_WAB), 32, 768),
                                ((C_WAF if d == 0 else C_WAB) + 32, 32, 800)]
                        o_ = 0
                        for j5, (c0, n, m0) in enumerate(srcs):
                            self.load_w(W1, o_, self.win(l, c0, n), n, stg)
                            self.ld(mu5, mu5[0:n, j5:j5 + 1], w["rw_mu"].h[l, d, m0:m0 + n].rearrange("(p o) -> p o", o=1))
                            o_ += n
                        s32 = c.sb([32, 256], F32, "s32", e1)
                        self.ld(s32, s32[:, 0:128], w["rw_w2"].h[l, d, :, cs])
                        self.ld(s32, s32[:, 128:256], w["rw_a2"].h[l, d, :, cs])
                        self.cp("dve", w2b[:], s32[:, 0:128], [s32], [w2b])
                        self.cp("dve", a2b[:], s32[:, 128:256], [s32], [a2b])
                        c.barrier()
                    self.memset("pool", carry[:], 0.0, [carry])
                    w0_c = self.col(es, w["rw_w0"].h[l, d, cs], "w0c")
                    a0_c = self.col(es, w["rw_a0"].h[l, d, cs], "a0c")
                    self.memset("pool", st["Zf"][:], 0.0, [st["Zf"]])
                    self.memset("pool", st["Zb"][:], 0.0, [st["Zb"]])
                    def prep(sg, S, d=d, w0_c=w0_c, a0_c=a0_c):
                        tau0 = sg * SG
                        PAR, BDB, BDK, BDV = S["PAR"], S["B"], S["K"], S["V"]

                        def shifted(j5, wc, M, out_ap, out_t):
                            p_ = nextp()
                            self.proj(p_, W1, wc, M, tau0, SG, d=d)
                            self.cp("pool", Pc[0:M, 0:1], carry[0:M, j5:j5 + 1], [carry], [Pc])
                            self.cp("act", Pc[0:M, 1:SG + 1], p_[0:M, 0:SG], [p_], [Pc])
                            self.cp("pool", carry[0:M, j5:j5 + 1], Pc[0:M, SG:SG + 1], [Pc], [carry])
                            self.tt("pool", F["t0"][0:M, :], Pc[0:M, 0:SG], Pc[0:M, 1:SG + 1], ALU.subtract, [Pc], [F["t0"]])
                            self.stt(out_ap, F["t0"][0:M, :], mu5[0:M, j5:j5 + 1], Pc[0:M, 1:SG + 1], ALU.mult, ALU.add, [F["t0"], mu5, Pc], [out_t])
                        shifted(0, 0, 128, F["r"][:], F["r"])
                        yield
                        shifted(1, 128, 128, F["k"][:], F["k"])
                        yield
                        shifted(2, 256, 128, F["v"][:], F["v"])
                        self.to_bd(lambda o_ap, lo, hi: self.cp("pool", o_ap, V3(F["v"][lo:hi, :], nch), [F["v"]], [BDV]), BDV, nch)
                        yield
                        shifted(3, 384, 32, F["t1"][0:32, :], F["t1"])
                        self.act(th[:], F["t1"][0:32, :], AF.Tanh, [F["t1"]], [th])
                        yield
                        shifted(4, 416, 32, xab[:], xab)
                        yield
                        p_ = nextp()
                        self.mm(p_[:, 0:SG], w2b[:], th[:], [w2b, th], [p_])
                        self.act(F["lw"][:], p_[:, 0:SG], AF.Sigmoid, [p_, w0_c], [F["lw"]], bias=w0_c[:, 0:1])
                        self.ts("dve", F["lw"][:], F["lw"][:], -math.exp(-0.5), None, ALU.mult, None, [F["lw"]], [F["lw"]])
                        p_ = nextp()
                        self.mm(p_[:, 0:SG], a2b[:], xab[:], [a2b, xab], [p_])
                        self.act(F["alr"][:], p_[:, 0:SG], AF.Sigmoid, [p_, a0_c], [F["alr"]], bias=a0_c[:, 0:1])
                        yield
                        self.c.op("dve", lambda: nc.vector.tensor_tensor_scan(out=F["G"][:], data0=ones[:], data1=F["lw"][:], initial=0.0,
                                                                                op0=ALU.mult, op1=ALU.add), [ones, F["lw"]], [F["G"]])
                        if nch > 1:
                            self.cp("pool", Gs[:, 1:nch], F["G"][:, 63:SG - 1:64], [F["G"]], [Gs])
                        self.tt("pool", V3(F["lg"][:], nch), V3(F["G"][:], nch), bc_last(Gs[:], 64), ALU.subtract, [F["G"], Gs], [F["lg"]])
                        yield
                        self.act(F["E1"][:], F["lg"][:], AF.Exp, [F["lg"]], [F["E1"]])
                        self.act(F["E3"][:], F["lg"][:], AF.Exp, [F["lg"]], [F["E3"]], scale=-1.0)
                        self.tt("pool", F["t0"][:], F["lg"][:], F["lw"][:], ALU.subtract, [F["lg"], F["lw"]], [F["t0"]])
                        self.act(F["E2"][:], F["t0"][:], AF.Exp, [F["t0"]], [F["E2"]])
                        self.cp("pool", S["g"][:], F["E1"][:, 63:SG:64], [F["E1"]], [S["g"]])
                        yield
                        self.ts("dve", F["t0"][:], F["k"][:], kk_c[:, 0:1], None, ALU.mult, None, [F["k"], kk_c], [F["t0"]])
                        self.tt("pool", Hb["sq"][:], F["t0"][:], F["t0"][:], ALU.mult, [F["t0"]], [Hb["sq"]])
                        p_ = nextp()
                        self.mm(p_[:, 0:SG], BOb, Hb["sq"][:], [kb, Hb["sq"]], [p_])
                        self.rsqrt_(F["t1"][:], p_[:, 0:SG], 1.0, self.epsc[:, 0:1], [p_, self.epsc], [F["t1"]])
                        self.tt("pool", F["kk"][:], F["t0"][:], F["t1"][:], ALU.mult, [F["t0"], F["t1"]], [F["kk"]])
                        yield
                        self.ts("dve", F["t0"][:], F["alr"][:], -1.0, ka_c[:, 0:1], ALU.add, ALU.mult, [F["alr"], ka_c], [F["t0"]])
                        self.stt(F["kp"][:], F["t0"][:], 1.0, F["k"][:], ALU.add, ALU.mult, [F["t0"], F["k"]], [F["kp"]])
                        self.tt("pool", F["t0"][:], F["r"][:], F["kp"][:], ALU.mult, [F["r"], F["kp"]], [F["t0"]])
                        self.ts("dve", Hb["rkr"][:], F["t0"][:], rk_c[:, 0:1], None, ALU.mult, None, [F["t0"], rk_c], [Hb["rkr"]])
                        yield
                        p_ = nextp()
                        self.mm(p_[:, 0:SG], BOb, Hb["rkr"][:], [kb, Hb["rkr"]], [p_])
                        if d == 0:
                            self.tt("dve", Bacc[:, tau0:tau0 + SG], p_[:, 0:SG], F["v"][:], ALU.mult, [p_, F["v"]], [Bacc])
                        else:
                            self.tt("dve", F["t1"][:], p_[:, 0:SG], F["v"][:], ALU.mult, [p_, F["v"]], [F["t1"]])
                            rb = rev(Bacc[:, Tn - tau0 - SG:Tn - tau0])
                            self.tt("pool", rb, rb, F["t1"][:], ALU.add, [Bacc, F["t1"]], [Bacc])
                        yield
                        self.tt("pool", PAR[:, :, 128:192], V3(F["r"][:], nch), V3(F["E1"][:], nch), ALU.mult, [F["r"], F["E1"]], [PAR])
                        self.to_bd(lambda o_ap, lo, hi: self.stt(o_ap, V3(F["kk"][lo:hi, :], nch), -1.0, V3(F["E2"][lo:hi, :], nch), ALU.mult, ALU.mult,
                                                                 [F["kk"], F["E2"]], [PAR]), PAR, nch)
                        yield
                        self.to_bd(lambda o_ap, lo, hi: self.tt("pool", o_ap, V3(F["kp"][lo:hi, :], nch), V3(F["E3"][lo:hi, :], nch), ALU.mult,
                                                                [F["kp"], F["E3"]], [BDK]), BDK, nch)
                        yield
                        self.tt("pool", F["t0"][:], F["kk"][:], F["alr"][:], ALU.mult, [F["kk"], F["alr"]], [F["t0"]])
                        self.to_bd(lambda o_ap, lo, hi: self.tt("pool", o_ap, V3(F["t0"][lo:hi, :], nch), V3(F["E3"][lo:hi, :], nch), ALU.mult,
                                                                [F["t0"], F["E3"]], [BDB]), BDB, nch)
                        yield

                    def mk_o(sg, S, d=d):
                        tau0 = sg * SG
                        PAR, BDB, BDK, BDV, G_ = S["PAR"], S["B"], S["K"], S["V"], S["g"]

                        def yout(ci, pY):
                            a0_ = tau0 + ci * 64
                            if d == 0:
                                self.cp("dve", Yacc[:, a0_:a0_ + 64], pY[:], [pY], [Yacc])
                            else:
                                ry = rev(Yacc[:, Tn - a0_ - 64:Tn - a0_])
                                self.tt("dve", ry, ry, pY[:], ALU.add, [Yacc, pY], [Yacc])
                        return dict(PAR=lambda ci: (PAR[:, ci, :], PAR), QB=lambda ci: (BDB[:, ci, :], BDB), QK=lambda ci: (BDK[:, ci, :], BDK),
                                    ZA=lambda ci: (PAR[:, ci, 0:128], PAR), ZR=lambda ci: (PAR[:, ci, 128:192], PAR),
                                    SB=lambda ci: (BDB[:, ci, :], BDB), SK=lambda ci: (BDK[:, ci, :], BDK), V=lambda ci: (BDV[:, ci, :], BDV),
                                    ms=lambda ci: (kc[:, K_MUS:K_MUS + 64], kc), mi=lambda ci: (kc[:, K_MUI:K_MUI + 64], kc),
                                    ml=lambda ci: (kc[:, K_MLS:K_MLS + 64], kc),
                                    gZ=lambda ci: (G_[:, ci:ci + 1], G_), gS=lambda ci: (G_[:, ci:ci + 1], G_), yout=yout)

                    for _ in prep(0, OS[0]):
                        pass
                    for sg in range(NSG):
                        side = prep(sg + 1, OS[(sg + 1) % 2]) if sg + 1 < NSG else None
                        self.dplr_run(sets, st, mk_o(sg, OS[sg % 2]), nch, side=side)
                with ExitStack() as e2:
                    Wz = c.sb([128, 8, 128], BF16, "Wz", e2)
                    stz = c.sb([128, 8, 128], F32, "stz", e2)
                    self.load_w(Wz, 0, self.win(l, C_Z + 256 + pr * 128, 128), 128, stz)
                    ob = [c.sb([128, SG], BF16, "rob", e2) for _ in range(2)]
                    for sg in range(NSG):
                        t0 = sg * SG
                        Y = Yacc[:, t0:t0 + SG]
                        p_ = nextp()
                        self.mm(p_[:, 0:SG], BOf, Y, [kc, Yacc], [p_])
                        self.stt(F["t0"][:], p_[:, 0:SG], -1.0 / 64, Y, ALU.mult, ALU.add, [p_, Yacc], [F["t0"]])
                        self.tt("pool", F["t1"][:], F["t0"][:], F["t0"][:], ALU.mult, [F["t0"]], [F["t1"]])
                        p_ = nextp()
                        self.mm(p_[:, 0:SG], BOf, F["t1"][:], [kc, F["t1"]], [p_])
                        self.rsqrt_(F["t1"][:], p_[:, 0:SG], 1.0 / 64, self.epsc[:, 1:2], [p_, self.epsc], [F["t1"]])
                        self.tt("pool", F["t0"][:], F["t0"][:], F["t1"][:], ALU.mult, [F["t0"], F["t1"]], [F["t0"]])
                        self.ts("dve", F["t0"][:], F["t0"][:], lnw_c[:, 0:1], lnb_c[:, 0:1], ALU.mult, ALU.add, [F["t0"], lnw_c, lnb_c], [F["t0"]])
                        self.tt("pool", F["t0"][:], F["t0"][:], Bacc[:, t0:t0 + SG], ALU.add, [F["t0"], Bacc], [F["t0"]])
                        p_ = nextp()
                        self.proj(p_, Wz, 0, 128, t0, SG)
                        self.act(F["t1"][:], p_[:, 0:SG], AF.Silu, [p_], [F["t1"]])
                        o_b = ob[sg % 2]
                        self.tt("pool", o_b[:], F["t0"][:], F["t1"][:], ALU.mult, [F["t0"], F["t1"]], [o_b])
                        r0 = 256 + pr * 128
                        self.c.dma("sp", self.mixT.h[r0:r0 + 128, t0:t0 + SG], o_b[:], o_b, reads=[o_b], writes=[self.mixT])
                    c.barrier()
            c.barrier()

    def gdn(self, l, s):
        c, nc, Tn = self.c, self.nc, self.Tn
        SG = min(256, Tn)
        nch = SG // 64
        NSG = Tn // SG
        TW = min(512, Tn)
        kc, kb = self.kc, self.kb
        BOb = kb[:, K_BO:K_BO + 128]
        BOf = kc[:, K_BO:K_BO + 128]
        w = self.w
        with ExitStack() as es:
            sets, st = self.dplr_env(es)
            Yacc = c.sb([128, Tn], F32, "Yacc", es)
            ones = c.sb([128, SG], F32, "ones", es)
            self.memset("pool", ones[:], 1.0, [ones])
            QKV = {nm: c.sb([128, Tn], BF16, "g" + nm, es) for nm in "qkv"}
            Wg = c.sb([128, 8, 16], BF16, "Wg", es)
            with ExitStack() as e1:
                stg = c.sb([128, 8, 16], F32, "stgg", e1)
                self.load_w(Wg, 0, self.win(l, C_GDB, 16), 16, stg)
                c.barrier()
            sg_t = c.sb([8, 4, 128], F32, "selg", es)
            self.ld(sg_t, sg_t[:], self.selg.h[:])
            dtb = self.col(es, w["gd_dt_bias"].h[l].rearrange("d h -> (d h)"), "dtb", 8)
            nA = self.col(es, w["gd_a_log"].h[l].rearrange("d h -> (d h)"), "nA", 8)
            self.act(nA[:], nA[:], AF.Exp, [nA], [nA])
            self.ts("pool", nA[:], nA[:], -1.0, None, ALU.mult, None, [nA], [nA])
            gnw = c.sb([128, 1], F32, "gnw", es)
            for hh in range(2):
                self.ld(gnw, gnw[hh * 64:(hh + 1) * 64, :], w["gd_norm_w"].h[l, :].rearrange("(p o) -> p o", o=1))
            F = {nm: c.sb([128, SG], F32, "f" + nm, es) for nm in ["g", "b", "G", "gc", "E1", "Es", "nbk", "t0", "Ex"]}
            g8 = c.sb([8, SG], F32, "g8", es)
            b8 = c.sb([8, SG], F32, "b8", es)
            Gs = c.sb([128, nch], F32, "Gs", es)
            gcT = c.sb([128, nch], F32, "gcT", es)
            self.memset("pool", Gs[:, 0:1], 0.0, [Gs])
            OS = []
            for i in range(2):
                S_ = {"PAR": c.sb([128, nch, 192], BF16, "gPAR", es), "g": c.sb([128, nch], F32, "ggcol", es),
                      "ZR": c.sb([128, SG], BF16, "gZR", es)}
                for nm in ["ZA", "K", "SK", "V"]:
                    S_[nm] = c.sb([128, nch, 128], BF16, "gbd" + nm, es)
                for nm in ["Ms", "Mi", "Ml"]:
                    S_[nm] = c.sb([128, nch, 64], F32, "g" + nm, es)
                for nm in ["PAR", "ZA", "K", "SK", "V"]:
                    self.memset("pool", S_[nm][:], 0.0, [S_[nm]])
                OS.append(S_)
            pp = [c.ps([128, 512], F32, "gpp", es)] * 2
            ppi = [0]

            def nextp():
                ppi[0] ^= 1
                return pp[ppi[0]]

            for pr in range(2):
                with ExitStack() as e1:
                    Wq = c.sb([128, 8, 384], BF16, "Wqkv", e1)
                    stg = c.sb([128, 8, 128], F32, "stgq", e1)
                    cw = c.sb([128, 3, 4], F32, "cw", e1)
                    for j in range(3):
                        ch0 = j * 256 + pr * 128
                        self.load_w(Wq, j * 128, self.win(l, C_QKV + ch0, 128), 128, stg)
                        self.ld(cw, cw[:, j, :], w["gd_conv_w"].h[l, :, ch0:ch0 + 128].rearrange("i p -> p i"), allow_slow_non_contiguous=True)
                    raw = c.sb([128, Tn + 3], BF16, "raw", e1)
                    self.memset("pool", raw[:, 0:1], 0.0, [raw])
                    self.memset("pool", raw[:, Tn + 1:Tn + 3], 0.0, [raw])
                    acc = c.sb([128, TW], F32, "acc", e1)
                    sl_ = c.sb([128, TW], F32, "sl", e1)
                    sqb = c.sb([128, TW], BF16, "sqb", e1)
                    rn = c.sb([128, TW], F32, "rn", e1)
                    for j, nm in enumerate("qkv"):
                        for tt_ in range(Tn // TW):
                            p_ = nextp()
                            self.proj(p_, Wq, j * 128, 128, tt_ * TW, TW)
                            self.cp(self.eng2(), raw[:, 1 + tt_ * TW:1 + (tt_ + 1) * TW], p_[:, 0:TW], [p_], [raw])
                        for tt_ in range(Tn // TW):
                            t0 = tt_ * TW
                            self.ts("dve", acc[:], raw[:, t0:t0 + TW], cw[:, j, 0:1], None, ALU.mult, None, [raw, cw], [acc])
                            for i in range(1, 4):
                                self.stt(acc[:], raw[:, t0 + i:t0 + i + TW], cw[:, j, i:i + 1], acc[:], ALU.mult, ALU.add, [raw, cw, acc], [acc])
                            if nm == "v":
                                self.act(QKV[nm][:, t0:t0 + TW], acc[:], AF.Silu, [acc], [QKV[nm]])
                            else:
                                self.act(sl_[:], acc[:], AF.Silu, [acc], [sl_])
                                self.tt("pool", sqb[:], sl_[:], sl_[:], ALU.mult, [sl_], [sqb])
                                p_ = nextp()
                                self.mm(p_[:, 0:TW], BOb, sqb[:], [kb, sqb], [p_])
                                self.rsqrt_(rn[:], p_[:, 0:TW], 1.0, self.epsc[:, 0:1], [p_, self.epsc], [rn])
                                if nm == "q":
                                    self.stt(QKV[nm][:, t0:t0 + TW], sl_[:], 0.125, rn[:], ALU.mult, ALU.mult, [sl_, rn], [QKV[nm]])
                                else:
                                    self.tt("pool", QKV[nm][:, t0:t0 + TW], sl_[:], rn[:], ALU.mult, [sl_, rn], [QKV[nm]])
                    c.barrier()
                for d in range(2):
                    self.memset("pool", st["Zf"][:], 0.0, [st["Zf"]])
                    self.memset("pool", st["Zb"][:], 0.0, [st["Zb"]])
                    def prep(sg, S, d=d, pr=pr):
                        tau0 = sg * SG
                        PAR, BDZA, BDK, BDSK, BDV, ZRt, Ms, Mi, Ml = S["PAR"], S["ZA"], S["K"], S["SK"], S["V"], S["ZR"], S["Ms"], S["Mi"], S["Ml"]

                        def sa(t):
                            return t[:, tau0:tau0 + SG] if d == 0 else rev(t[:, Tn - tau0 - SG:Tn - tau0])
                        p_ = nextp()
                        self.proj(p_, Wg, 0, 8, tau0, SG, d=d)
                        self.act(b8[:], p_[0:8, 0:SG], AF.Sigmoid, [p_], [b8])
                        p_ = nextp()
                        self.proj(p_, Wg, 8, 8, tau0, SG, d=d)
                        self.act(g8[:], p_[0:8, 0:SG], AF.Exp, [p_, dtb], [g8], bias=dtb[:, 0:1])
                        self.act(g8[:], g8[:], AF.Ln, [g8], [g8], bias=1.0)
                        self.ts("dve", g8[:], g8[:], nA[:, 0:1], None, ALU.mult, None, [g8, nA], [g8])
                        yield
                        p_ = nextp()
                        self.mm(p_[:, 0:SG], sg_t[:, d * 2 + pr, :], g8[:], [sg_t, g8], [p_])
                        self.cp("act", F["g"][:], p_[:, 0:SG], [p_], [F["g"]])
                        p_ = nextp()
                        self.mm(p_[:, 0:SG], sg_t[:, d * 2 + pr, :], b8[:], [sg_t, b8], [p_])
                        self.cp("dve", F["b"][:], p_[:, 0:SG], [p_], [F["b"]])
                        yield
                        self.c.op("dve", lambda: nc.vector.tensor_tensor_scan(out=F["G"][:], data0=ones[:], data1=F["g"][:], initial=0.0,
                                                                                op0=ALU.mult, op1=ALU.add), [ones, F["g"]], [F["G"]])
                        if nch > 1:
                            self.cp("pool", Gs[:, 1:nch], F["G"][:, 63:SG - 1:64], [F["G"]], [Gs])
                        self.tt("pool", V3(F["gc"][:], nch), V3(F["G"][:], nch), bc_last(Gs[:], 64), ALU.subtract, [F["G"], Gs], [F["gc"]])
                        yield
                        self.act(F["E1"][:], F["gc"][:], AF.Exp, [F["gc"]], [F["E1"]])
                        self.cp("pool", S["g"][:], F["E1"][:, 63:SG:64], [F["E1"]], [S["g"]])
                        self.tt("pool", V3(F["t0"][:], nch), bc_last(F["gc"][:, 63:SG:64], 64), V3(F["gc"][:], nch), ALU.subtract, [F["gc"]], [F["t0"]])
                        self.act(F["Es"][:], F["t0"][:], AF.Exp, [F["t0"]], [F["Es"]])
                        yield
                        self.tt("pool", V3(F["t0"][:], nch), V3(F["gc"][:], nch), bc_mid(kc[:, K_SEL:K_SEL + 64], nch), ALU.mult, [F["gc"], kc], [F["t0"]])
                        self.c.op("dve", lambda: nc.vector.reduce_sum(out=gcT[:], in_=V3(F["t0"][:], nch), axis=AX.X), [F["t0"]], [gcT])
                        self.tt("dve", V3(F["t0"][:], nch), V3(F["gc"][:], nch), bc_last(gcT[:], 64), ALU.subtract, [F["gc"], gcT], [F["t0"]])
                        yield
                        self.ts("dve", F["nbk"][:], F["t0"][:], 0.0, None, ALU.max, None, [F["t0"]], [F["nbk"]])
                        self.ts("dve", F["t0"][:], F["t0"][:], 0.0, None, ALU.min, None, [F["t0"]], [F["t0"]])
                        self.act(F["Ex"][:], F["t0"][:], AF.Exp, [F["t0"]], [F["Ex"]])
                        yield
                        self.tt("pool", Ms[:], V3(F["Ex"][:], nch), bc_mid(kc[:, K_MUS:K_MUS + 64], nch), ALU.mult, [F["Ex"], kc], [Ms])
                        self.tt("dve", Mi[:], V3(F["Ex"][:], nch), bc_mid(kc[:, K_MUI:K_MUI + 64], nch), ALU.mult, [F["Ex"], kc], [Mi])
                        yield
                        self.act(F["Ex"][:], F["nbk"][:], AF.Exp, [F["nbk"]], [F["Ex"]], scale=-1.0)
                        self.tt("pool", Ml[:], V3(F["Ex"][:], nch), bc_mid(kc[:, K_MLS:K_MLS + 64], nch), ALU.mult, [F["Ex"], kc], [Ml])
                        yield
                        kS, qS, vS = sa(QKV["k"]), sa(QKV["q"]), sa(QKV["v"])
                        self.stt(F["nbk"][:], kS, -1.0, F["b"][:], ALU.mult, ALU.mult, [QKV["k"], F["b"]], [F["nbk"]])
                        self.to_bd(lambda o_ap, lo, hi: self.cp("pool", o_ap, V3(F["nbk"][lo:hi, :], nch), [F["nbk"]], [PAR]), PAR, nch)
                        yield
                        self.to_bd(lambda o_ap, lo, hi: self.tt("dve", o_ap, V3(F["nbk"][lo:hi, :], nch), V3(F["E1"][lo:hi, :], nch), ALU.mult,
                                                                [F["nbk"], F["E1"]], [BDZA]), BDZA, nch)
                        self.cp("pool", PAR[:, :, 128:192], V3(qS, nch) if d == 0 else qS.rearrange("p (a b) -> p a b", a=nch), [QKV["q"]], [PAR])
                        yield
                        self.tt("pool", V3(ZRt[:], nch), PAR[:, :, 128:192], V3(F["E1"][:], nch), ALU.mult, [PAR, F["E1"]], [ZRt])
                        self.cp("act", F["t0"][:], kS, [QKV["k"]], [F["t0"]])
                        yield
                        self.to_bd(lambda o_ap, lo, hi: self.cp("pool", o_ap, V3(F["t0"][lo:hi, :], nch), [F["t0"]], [BDK]), BDK, nch)
                        self.to_bd(lambda o_ap, lo, hi: self.tt("dve", o_ap, V3(F["t0"][lo:hi, :], nch), V3(F["Es"][lo:hi, :], nch), ALU.mult,
                                                                [F["t0"], F["Es"]], [BDSK]), BDSK, nch)
                        yield
                        self.tt("dve", F["nbk"][:], vS, F["b"][:], ALU.mult, [QKV["v"], F["b"]], [F["nbk"]])
                        self.to_bd(lambda o_ap, lo, hi: self.cp("pool", o_ap, V3(F["nbk"][lo:hi, :], nch), [F["nbk"]], [BDV]), BDV, nch)
                        yield

                    def mk_o(sg, S, d=d):
                        tau0 = sg * SG
                        PAR, BDZA, BDK, BDSK, BDV, ZRt, Ms, Mi, Ml, G_ = (S["PAR"], S["ZA"], S["K"], S["SK"], S["V"], S["ZR"], S["Ms"], S["Mi"],
                                                                          S["Ml"], S["g"])

                        def yout(ci, pY):
                            a0_ = tau0 + ci * 64
                            if d == 0:
                                self.cp("dve", Yacc[:, a0_:a0_ + 64], pY[:], [pY], [Yacc])
                            else:
                                ry = rev(Yacc[:, Tn - a0_ - 64:Tn - a0_])
                                self.tt("dve", ry, ry, pY[:], ALU.add, [Yacc, pY], [Yacc])
                        return dict(PAR=lambda ci: (PAR[:, ci, :], PAR), QB=lambda ci: (BDK[:, ci, :], BDK), QK=lambda ci: (BDK[:, ci, :], BDK),
                                    ZA=lambda ci: (BDZA[:, ci, :], BDZA), ZR=lambda ci: (ZRt[:, ci * 64:ci * 64 + 64], ZRt),
                                    SB=lambda ci: (BDSK[:, ci, :], BDSK), SK=lambda ci: (BDSK[:, ci, :], BDSK), V=lambda ci: (BDV[:, ci, :], BDV),
                                    ms=lambda ci: (Ms[:, ci, :], Ms), mi=lambda ci: (Mi[:, ci, :], Mi), ml=lambda ci: (Ml[:, ci, :], Ml),
                                    gZ=lambda ci: (G_[:, ci:ci + 1], G_), gS=lambda ci: None, yout=yout, same=True)

                    for _ in prep(0, OS[0]):
                        pass
                    for sg in range(NSG):
                        side = prep(sg + 1, OS[(sg + 1) % 2]) if sg + 1 < NSG else None
                        self.dplr_run(sets, st, mk_o(sg, OS[sg % 2]), nch, side=side)
                with ExitStack() as e2:
                    Wz = c.sb([128, 8, 128], BF16, "Wz", e2)
                    stz = c.sb([128, 8, 128], F32, "stz", e2)
                    self.load_w(Wz, 0, self.win(l, C_Z + 768 + pr * 128, 128), 128, stz)
                    ob = [c.sb([128, SG], BF16, "gob", e2) for _ in range(2)]
                    for sg in range(NSG):
                        t0 = sg * SG
                        Y = Yacc[:, t0:t0 + SG]
                        self.tt("pool", F["t0"][:], Y, Y, ALU.mult, [Yacc], [F["t0"]])
                        p_ = nextp()
                        self.mm(p_[:, 0:SG], BOf, F["t0"][:], [kc, F["t0"]], [p_])
                        self.rsqrt_(F["Ex"][:], p_[:, 0:SG], 1.0 / 64, self.epsc[:, 0:1], [p_, self.epsc], [F["Ex"]])
                        self.stt(F["t0"][:], Y, gnw[:, 0:1], F["Ex"][:], ALU.mult, ALU.mult, [Yacc, gnw, F["Ex"]], [F["t0"]])
                        p_ = nextp()
                        self.proj(p_, Wz, 0, 128, t0, SG)
                        self.act(F["Ex"][:], p_[:, 0:SG], AF.Silu, [p_], [F["Ex"]])
                        o_b = ob[sg % 2]
                        self.tt("pool", o_b[:], F["t0"][:], F["Ex"][:], ALU.mult, [F["t0"], F["Ex"]], [o_b])
                        r0 = 768 + pr * 128
                        self.c.dma("sp", self.mixT.h[r0:r0 + 128, t0:t0 + SG], o_b[:], o_b, reads=[o_b], writes=[self.mixT])
                    c.barrier()
            c.barrier()

    def ssd(self, l, s):
        c, nc, Tn = self.c, self.nc, self.Tn
        L = 128
        NC_ = Tn // L
        TW = min(512, Tn)
        kc, kb = self.kc, self.kb
        IDb = kb[:, K_ID:K_ID + 128]
        w = self.w
        with ExitStack() as es:
            XC = [c.sb([128, Tn], BF16, "xc%d" % j, es) for j in range(6)]
            Yacc = [c.sb([128, Tn], F32, "Yc%d" % p, es) for p in range(2)]
            banks = [c.ps([128, 512], F32, "sbk", es) for _ in range(6)]

            def sl(b, lo, n):
                t = c.sub(banks[b][:, lo:lo + n])
                t.dep = banks[b].dep
                return t
            pY = [sl(0, 0, 128), sl(1, 0, 128)]
            pCB = [sl(2, 0, 128), sl(2, 128, 128)]
            pRep4 = c.sub(V3(banks[3][:, 0:512], 4))
            pRep4.dep = banks[3].dep
            pTb = c.sub(V3(banks[4][:, 0:256].bitcast(BF16), 4))
            pTb.dep = banks[4].dep
            pTf = sl(4, 256, 16)
            pZU4 = c.sub(V3(banks[5][:, 0:512], 4))
            pZU4.dep = banks[5].dep
            pPj = sl(4, 384, 128)
            pbig = [banks[3], banks[5]]
            with ExitStack() as e1:
                Wx = c.sb([128, 8, 768], BF16, "Wx", e1)
                stg = c.sb([128, 8, 128], F32, "stgx", e1)
                cw = c.sb([128, 6, 4], F32, "cwx", e1)
                cb = c.sb([128, 6], F32, "cbx", e1)
                self.ld(cb, cb[:], w["m_conv_b"].h[l, :].rearrange("(j p) -> p j", p=128), allow_slow_non_contiguous=True)
                for j in range(6):
                    self.load_w(Wx, j * 128, self.win(l, C_XBC + j * 128, 128), 128, stg)
                    self.ld(cw, cw[:, j, :], w["m_conv_w"].h[l, :, j * 128:(j + 1) * 128].rearrange("i p -> p i"), allow_slow_non_contiguous=True)
                raw = c.sb([128, Tn + 3], BF16, "rawx", e1)
                self.memset("pool", raw[:, 0:1], 0.0, [raw])
                self.memset("pool", raw[:, Tn + 1:Tn + 3], 0.0, [raw])
                acc = c.sb([128, TW], F32, "accx", e1)
                for j in range(6):
                    for tt_ in range(Tn // TW):
                        p_ = pbig[tt_ % 2]
                        self.proj(p_, Wx, j * 128, 128, tt_ * TW, TW)
                        self.cp(self.eng2(), raw[:, 1 + tt_ * TW:1 + (tt_ + 1) * TW], p_[:, 0:TW], [p_], [raw])
                    for tt_ in range(Tn // TW):
                        t0 = tt_ * TW
                        self.ts("dve", acc[:], raw[:, t0:t0 + TW], cw[:, j, 0:1], cb[:, j:j + 1], ALU.mult, ALU.add, [raw, cw, cb], [acc])
                        for i in range(1, 4):
                            self.stt(acc[:], raw[:, t0 + i:t0 + i + TW], cw[:, j, i:i + 1], acc[:], ALU.mult, ALU.add, [raw, cw, acc], [acc])
                        self.act(XC[j][:, t0:t0 + TW], acc[:], AF.Silu, [acc], [XC[j]])
                c.barrier()
            Wd = c.sb([128, 8, 8], BF16, "Wd", es)
            with ExitStack() as e1:
                stg = c.sb([128, 8, 8], F32, "stgd", e1)
                self.load_w(Wd, 0, self.win(l, C_MDT, 8), 8, stg)
                c.barrier()
            sm_t = c.sb([8, 8, 128], F32, "selm", es)
            self.ld(sm_t, sm_t[:], self.selm.h[:])
            dtb = self.col(es, w["m_dt_bias"].h[l].rearrange("d h -> (d h)"), "mdtb", 8)
            nA = self.col(es, w["m_a_log"].h[l].rearrange("d h -> (d h)"), "mnA", 8)
            self.act(nA[:], nA[:], AF.Exp, [nA], [nA])
            self.ts("pool", nA[:], nA[:], -1.0, None, ALU.mult, None, [nA], [nA])
            ones8 = c.sb([8, L], F32, "ones8", es)
            self.memset("pool", ones8[:], 1.0, [ones8])
            dt8 = c.sb([8, L], F32, "dt8", es)
            a8 = c.sb([8, L], F32, "a8", es)
            acs8 = c.sb([8, L], F32, "acs8", es)
            TT_ = c.sb([128, 16], F32, "TT", es)
            xsT = c.sb([128, 256], BF16, "xsT", es)
            BT = c.sb([128, 2, 128], BF16, "BT", es)
            CBs = [c.sb([128, 128], F32, "CBs", es) for _ in range(2)]
            dd = c.sb([128, 128], F32, "dd", es)
            Em = c.sb([128, 128], F32, "Em", es)
            SD = [c.sb([128, 128], BF16, "SD", es) for _ in range(4)]
            Er = c.sb([128, 128], F32, "Er", es)
            Ce = [c.sb([128, 128], BF16, "Ce", es) for _ in range(4)]
            cols = c.sb([128, 4, 4], F32, "cols", es)
            Xp = [c.sb([128, 128], BF16, "Xp%d" % i, es) for i in range(4)]
            Xe = [c.sb([128, 128], BF16, "Xe%d" % i, es) for i in range(4)]
            Zf = [c.sb([128, 128], F32, "sZf%d" % h, es) for h in range(4)]
            Zb = [c.sb([128, 128], BF16, "sZb%d" % h, es) for h in range(4)]
            for i in range(4):
                self.memset("pool", Xp[i][:], 0.0, [Xp[i]])
                self.memset("pool", Xe[i][:], 0.0, [Xe[i]])
            for d in range(2):
                for h in range(4):
                    self.memset("pool", Zf[h][:], 0.0, [Zf[h]])
                    self.memset("pool", Zb[h][:], 0.0, [Zb[h]])
                mask = kc[:, K_MU128:K_MU128 + 128] if d == 0 else kc[:, K_ML128:K_ML128 + 128]
                last = L - 1 if d == 0 else 0
                order = range(NC_) if d == 0 else range(NC_ - 1, -1, -1)
                for ci in order:
                    t0 = ci * L
                    tk = slice(t0, t0 + L)
                    self.proj(pPj, Wd, 0, 8, t0, L)
                    self.act(dt8[:], pPj[0:8, 0:L], AF.Exp, [pPj, dtb], [dt8], bias=dtb[:, 0:1])
                    self.act(dt8[:], dt8[:], AF.Ln, [dt8], [dt8], bias=1.0)
                    self.ts("dve", a8[:], dt8[:], nA[:, 0:1], None, ALU.mult, None, [dt8, nA], [a8])
                    if d == 0:
                        self.c.op("dve", lambda: nc.vector.tensor_tensor_scan(out=acs8[:], data0=ones8[:], data1=a8[:], initial=0.0,
                                                                                op0=ALU.mult, op1=ALU.add), [ones8, a8], [acs8])
                    else:
                        self.c.op("dve", lambda: nc.vector.tensor_tensor_scan(out=rev(acs8[:]), data0=ones8[:], data1=rev(a8[:]), initial=0.0,
                                                                                op0=ALU.mult, op1=ALU.add), [ones8, a8], [acs8])
                    self.mm(pTf[:, 0:8], dt8[:], kc[0:8, K_ID:K_ID + 8], [dt8, kc], [pTf])
                    self.mm(pTf[:, 8:16], acs8[:], kc[0:8, K_ID:K_ID + 8], [acs8, kc], [pTf])
                    self.cp("act", TT_[:], pTf[:], [pTf], [TT_])
                    for g in range(2):
                        self.mm(pCB[g][:], XC[2 + g][:, tk], XC[4 + g][:, tk], [XC[2 + g], XC[4 + g]], [pCB[g]])
                        self.cp("act", CBs[g][:], pCB[g][:], [pCB[g]], [CBs[g]])
                    for j in range(2):
                        self.tr(pTb[:, j, :], XC[j][:, tk], IDb, [XC[j], kb], [pTb])
                        self.tr(pTb[:, 2 + j, :], XC[2 + j][:, tk], IDb, [XC[2 + j], kb], [pTb])
                    self.cp("dve", V3(xsT[:], 2), pTb[:, 0:2, :], [pTb], [xsT])
                    self.cp("act", BT[:], pTb[:, 2:4, :], [pTb], [BT])
                    for h in range(4):
                        self.mm(pRep4[:, h, :], sm_t[:, d * 4 + h, :], acs8[:], [sm_t, acs8], [pRep4])
                    for h in range(4):
                        g, pr, hh = h // 2, h // 2, h % 2
                        dh = d * 4 + h
                        self.ts("dve", dd[:], pRep4[:, h, :], TT_[:, 8 + dh:9 + dh], 0.0, ALU.subtract, ALU.min, [pRep4, TT_], [dd])
                        self.act(Em[:], dd[:], AF.Exp, [dd], [Em])
                        self.tt("pool", Em[:], Em[:], mask, ALU.mult, [Em, kc], [Em])
                        self.tt("pool", SD[h][:], CBs[g][:], Em[:], ALU.mult, [CBs[g], Em], [SD[h]])
                        self.act(Er[:], pRep4[:, h, :], AF.Exp, [pRep4], [Er])
                        self.tt("pool", Ce[h][:], XC[4 + g][:, tk], Er[:], ALU.mult, [XC[4 + g], Er], [Ce[h]])
                        self.cp("dve", cols[:, h, 0:1], pRep4[:, h, last:last + 1], [pRep4], [cols])
                        self.act(cols[:, h, 1:2], TT_[:, 8 + dh:9 + dh], AF.Exp, [TT_, cols], [cols], scale=-1.0, bias=cols[:, h, 0:1])
                        self.act(cols[:, h, 2:3], cols[:, h, 0:1], AF.Exp, [cols], [cols])
                        xs_h = xsT[:, h * 64:(h + 1) * 64]
                        self.ts("dve", Xp[h][:, hh * 64:(hh + 1) * 64], xs_h, TT_[:, dh:dh + 1], None, ALU.mult, None, [xsT, TT_], [Xp[h]])
                        self.ts("dve", Xe[h][:, hh * 64:(hh + 1) * 64], Xp[h][:, hh * 64:(hh + 1) * 64], cols[:, h, 1:2], None, ALU.mult, None,
                                [Xp[h], cols], [Xe[h]])
                    for h in range(4):
                        g, pr, hh = h // 2, h // 2, h % 2
                        self.mm(pY[pr][:], Xp[h][:], SD[h][:], [Xp[h], SD[h]], [pY[pr]], start=(hh == 0), stop=False)
                        self.mm(pY[pr][:], Zb[h][:], Ce[h][:], [Zb[h], Ce[h]], [pY[pr]], start=False, stop=(hh == 1))
                    for h in range(4):
                        self.mm(pZU4[:, h, :], BT[:, h // 2, :], Xe[h][:], [BT, Xe[h]], [pZU4])
                    for pr in range(2):
                        if d == 0:
                            self.cp("act", Yacc[pr][:, tk], pY[pr][:], [pY[pr]], [Yacc[pr]])
                        else:
                            self.tt("dve", Yacc[pr][:, tk], Yacc[pr][:, tk], pY[pr][:], ALU.add, [Yacc[pr], pY[pr]], [Yacc[pr]])
                    for h in range(4):
                        self.stt(Zf[h][:], Zf[h][:], cols[:, h, 2:3], pZU4[:, h, :], ALU.mult, ALU.add, [Zf[h], cols, pZU4], [Zf[h]])
                        self.cp("act", Zb[h][:], Zf[h][:], [Zf[h]], [Zb[h]])
            with ExitStack() as e2:
                Wz = c.sb([128, 8, 256], BF16, "Wzc", e2)
                stz = c.sb([128, 8, 256], F32, "stzc", e2)
                self.load_w(Wz, 0, self.win(l, C_Z + 512, 256), 256, stz)
                Dc = c.sb([128, 2], F32, "Dc", e2)
                nw = c.sb([128, 2], F32, "nw", e2)
                for pr in range(2):
                    for hh in range(2):
                        self.ld(Dc, Dc[hh * 64:(hh + 1) * 64, pr:pr + 1], w["m_d"].h[l, pr * 2 + hh:pr * 2 + hh + 1].partition_broadcast(64))
                self.ld(nw, nw[:], w["m_norm_w"].h[l, :].rearrange("(j p) -> p j", p=128), allow_slow_non_contiguous=True)
                gt = [c.sb([128, TW], F32, "gt%d" % p, e2) for p in range(2)]
                sq = c.sb([128, TW], F32, "sqc", e2)
                sz = c.sb([128, TW], F32, "szc", e2)
                rs = c.sb([128, TW], F32, "rsc", e2)
                ob = [c.sb([128, 2, TW], BF16, "cob", e2) for _ in range(2)]
                for tt_ in range(Tn // TW):
                    t0 = tt_ * TW
                    for pr in range(2):
                        self.stt(gt[pr][:], XC[pr][:, t0:t0 + TW], Dc[:, pr:pr + 1], Yacc[pr][:, t0:t0 + TW], ALU.mult, ALU.add,
                                 [XC[pr], Dc, Yacc[pr]], [gt[pr]])
                        p_ = banks[2 + pr]
                        self.proj(p_, Wz, pr * 128, 128, t0, TW)
                        self.act(sz[:], p_[:, 0:TW], AF.Silu, [p_], [sz])
                        self.tt("pool", gt[pr][:], gt[pr][:], sz[:], ALU.mult, [gt[pr], sz], [gt[pr]])
                        self.tt("pool", sq[:], gt[pr][:], gt[pr][:], ALU.mult, [gt[pr]], [sq])
                        self.mm(banks[0][:, 0:TW], kc[:, K_ONE:K_ONE + 128], sq[:], [kc, sq], [banks[0]], start=(pr == 0), stop=(pr == 1))
                    self.rsqrt_(rs[:], banks[0][:, 0:TW], 1.0 / 256, self.epsc[:, 0:1], [banks[0], self.epsc], [rs])
                    o_b = ob[tt_ % 2]
                    for pr in range(2):
                        self.stt(o_b[:, pr, :], gt[pr][:], nw[:, pr:pr + 1], rs[:], ALU.mult, ALU.mult, [gt[pr], nw, rs], [o_b])
                    self.c.dma("sp", self.mixT.h[512:768, t0:t0 + TW].rearrange("(a p) t -> p a t", p=128), o_b[:], o_b, reads=[o_b], writes=[self.mixT])
                c.barrier()
            c.barrier()

    def zero_mix(self, i):
        c, Tn = self.c, self.Tn
        TW = min(512, Tn)
        with ExitStack() as es:
            z = c.sb([128, 2, TW], BF16, "zmix", es)
            self.memset("pool", z[:], 0.0, [z])
            for j in range(Tn // TW):
                self.c.dma("sp", self.mixT.h[i * 256:(i + 1) * 256, j * TW:(j + 1) * TW].rearrange("(a p) t -> p a t", p=128), z[:], z,
                           reads=[z], writes=[self.mixT])
            c.barrier()

    def build(self):
        self.declare()
        self.consts()
        for l in range(self.DEPTH):
            self.p0(l)
            for s in range(self.NS):
                self.p1(l, s)
                if "A" in self.mixers:
                    self.mla(l, s)
                if "B" in self.mixers:
                    self.rwkv(l, s)
                if "C" in self.mixers:
                    self.ssd(l, s)
                if "D" in self.mixers:
                    self.gdn(l, s)
                for i, m in enumerate("ABCD"):
                    if m not in self.mixers:
                        self.zero_mix(i)
                self.p6(l, s)
        self.c.finish([self.y] + ([self.mixT] if self.dbg else []))
        return self.nc


def make_consts(Tn):
    p = np.arange(128)
    kc = np.zeros((128, K_END), np.float32)
    kc[:, K_ID:K_ID + 128] = np.eye(128)
    col = np.arange(64)
    kc[:, K_MUS:K_MUS + 64] = ((p % 64)[:, None] < col[None, :])
    kc[:, K_MUI:K_MUI + 64] = ((p % 64)[:, None] <= col[None, :])
    kc[:, K_BO:K_BO + 128] = ((p // 64)[:, None] == (p // 64)[None, :])
    kc[:, K_ONE:K_ONE + 128] = 1.0
    kc[:, K_SEL:K_SEL + 64] = ((p % 64)[:, None] == col[None, :])
    kc[:, K_MU128:K_MU128 + 128] = (p[:, None] <= p[None, :])
    kc[:, K_ML128:K_ML128 + 128] = (p[:, None] >= p[None, :])
    kc[:, K_MLS:K_MLS + 64] = ((p % 64)[:, None] > col[None, :])
    inv = 1.0 / (10000.0 ** (np.arange(0, 32, 2, dtype=np.float32) / np.float32(32)))
    ang = np.arange(Tn, dtype=np.float32)[None, :] * inv.astype(np.float32)[:, None]
    rope = np.zeros((96, 2, Tn), np.float32)
    rope[0:64, 0, :] = 1.0
    rope[64:80, 0, :] = np.cos(ang)
    rope[80:96, 0, :] = np.cos(ang)
    rope[64:80, 1, :] = np.sin(ang)
    rope[80:96, 1, :] = np.sin(ang)
    selg = np.zeros((8, 4, 128), np.float32)
    for d in range(2):
        for pr in range(2):
            for hh in range(2):
                selg[d * 4 + pr * 2 + hh, d * 2 + pr, hh * 64:(hh + 1) * 64] = 1.0
    selm = np.zeros((8, 8, 128), np.float32)
    for r in range(8):
        selm[r, r, :] = 1.0
    return dict(kconst=kc, rope=rope, selg=selg, selm=selm)


WNAMES = ["w_ada", "b_ada", "g_pre", "g_post", "w_in", "w_out", "a_g_q", "a_w_uq", "a_g_kv", "a_w_ukv", "rw_mu", "rw_w0",
          "rw_w2", "rw_a0", "rw_a2", "rw_k_k", "rw_k_a", "rw_r_k", "rw_ln_w", "rw_ln_b", "m_conv_w", "m_conv_b",
          "m_dt_bias", "m_a_log", "m_d", "m_norm_w", "gd_conv_w", "gd_dt_bias", "gd_a_log", "gd_norm_w"]


def kernel(x_prompt, x_sample, c_prompt, c_sample, **weights):
    x_prompt = np.asarray(x_prompt, np.float32)
    x_sample = np.asarray(x_sample, np.float32)
    Tn = x_prompt.shape[1]
    DEPTH = np.asarray(weights["w_ada"]).shape[0]
    xs = np.concatenate([x_prompt, x_sample], axis=0)
    cs = np.concatenate([np.asarray(c_prompt, np.float32), np.asarray(c_sample, np.float32)], axis=0)
    nseq = xs.shape[0]
    NS = -(-nseq // N_CORES)
    b = Builder(Tn, NS, DEPTH)
    nc = b.build()
    consts = make_consts(Tn)
    wmap = {k: np.ascontiguousarray(np.asarray(weights[k], np.float32)) for k in WNAMES}
    in_maps = []
    for core in range(N_CORES):
        idx = [(core * NS + j) if (core * NS + j) < nseq else 0 for j in range(NS)]
        m = dict(wmap)
        m.update(consts)
        m["x"] = np.ascontiguousarray(xs[idx])
        m["c"] = np.ascontiguousarray(cs[idx])
        in_maps.append(m)
    res = run_bass_kernel_spmd(nc, in_maps, core_ids=list(range(N_CORES)))
    out = np.zeros_like(xs)
    for core in range(N_CORES):
        yc = res.results[core]["y"]
        for j in range(NS):
            sidx = core * NS + j
            if sidx < nseq:
                out[sidx] = yc[j]
    nb = x_prompt.shape[0]
    return (out[:nb], out[nb:])
```

```python
import math
import numpy as np
import concourse.bass as bass
import concourse.mybir as mybir
from concourse.bass_utils import run_bass_kernel_spmd
from contextlib import ExitStack

F32 = mybir.dt.float32
BF16 = mybir.dt.bfloat16
F32R = mybir.dt.float32r
AF = mybir.ActivationFunctionType
ALU = mybir.AluOpType
AX = mybir.AxisListType

D_MODEL = 1024
P_IN = 4024
N_CORES = 8
EPS = 1e-6
EMBED_WAITS = 1
C_CQ, C_CKV, C_KR, C_RKV, C_WAF, C_WAB, C_XBC, C_MDT, C_QKV, C_GDB, C_GDA, C_Z = (
    0, 384, 512, 544, 1312, 1376, 1440, 2208, 2216, 2984, 2992, 3000)
K_ID, K_MUS, K_MUI, K_BO, K_ONE, K_SEL, K_MU128, K_ML128, K_MLS, K_END = 0, 128, 192, 256, 384, 512, 576, 704, 832, 896


class Dep:
    __slots__ = ("w", "r", "name", "excl")

    def __init__(self, name=""):
        self.w = {}
        self.r = {}
        self.name = name
        self.excl = False


class T:
    def __init__(self, h, name, dep=None):
        self.h = h
        self.name = name
        self.dep = dep or Dep(name)
        self.dsem = None
        self.dcount = 0

    def __getitem__(self, k):
        return self.h[k]


class Ctx:
    def __init__(self, nc):
        self.nc = nc
        self.E = {"pe": nc.tensor, "act": nc.scalar, "dve": nc.vector, "pool": nc.gpsimd, "sp": nc.sync}
        self.sem, self.cnt, self.known = {}, {}, {}
        self.stack = ExitStack()
        for e in self.E:
            self.sem[e] = self.stack.enter_context(nc.semaphore("s_" + e))
            self.cnt[e] = 0
            self.known[e] = {}
        self.uid = 0
        self.ninst = 0
        self.dtiles = []
        self.free_dsems = []
        self.all_dsems = []
        self.pending = None

    def sb(self, shape, dt=F32, name=None, stack=None):
        self.uid += 1
        name = (name or "t") + "_%d" % self.uid
        h = (stack or self.stack).enter_context(self.nc.sbuf_tensor(name, list(shape), dt))
        return T(h, name)

    def ps(self, shape, dt=F32, name=None, stack=None):
        self.uid += 1
        name = (name or "p") + "_%d" % self.uid
        esz = 4 if dt == F32 else 2
        n = 1
        for v in shape[1:]:
            n *= v
        per_bank = 2048 // esz
        nb = -(-n // per_bank)
        h = (stack or self.stack).enter_context(self.nc.psum_tensor(name, [128, nb * per_bank], dt))
        v = h[0:shape[0], 0:n]
        if len(shape) == 3:
            v = v.rearrange("p (a b) -> p a b", a=shape[1])
        t = T(v, name)
        t.dep.excl = True
        return t

    def sub(self, ap, name="s"):
        self.uid += 1
        return T(ap, name + "_%d" % self.uid)

    def dram(self, name, shape, dt=F32, kind="Internal"):
        return T(self.nc.dram_tensor(name, list(shape), dt, kind=kind), name)

    def _wait(self, eng, sem, val):
        k = self.known[eng]
        sid = id(sem)
        if k.get(sid, 0) >= val:
            return
        k[sid] = val
        if self.pending is not None:
            self.pending.append((sem, val))
            return
        self.E[eng].wait_ge(sem, val)
        self.ninst += 1

    def _acquire(self, eng, reads, writes):
        own = id(self.sem[eng])
        for d in reads:
            for sid, (sem, val) in d.dep.w.items():
                self._wait(eng, sem, val)
            if d.dep.excl:
                for sid, (sem, val) in d.dep.r.items():
                    if sid != own:
                        self._wait(eng, sem, val)
        for d in writes:
            for sid, (sem, val) in d.dep.w.items():
                if sid != own or eng != "pe":
                    self._wait(eng, sem, val)
            for sid, (sem, val) in d.dep.r.items():
                self._wait(eng, sem, val)

    def _flush(self, eng, ins=None):
        p, self.pending = self.pending, None
        last = []
        if ins is not None and EMBED_WAITS:
            while p and len(last) < EMBED_WAITS:
                last.append(p.pop())
        for (sem, val) in p:
            self.E[eng].wait_ge(sem, val)
            self.ninst += 1
        return last

    def _record(self, sem, val, reads, writes):
        sid = id(sem)
        for d in reads:
            d.dep.r[sid] = (sem, val)
        for d in writes:
            d.dep.w[sid] = (sem, val)

    def op(self, eng, fn, reads=(), writes=()):
        self.pending = []
        self._acquire(eng, reads, writes)
        last = self._flush(eng, True)
        ins = fn()
        for (sem_, val_) in last:
            ins._wait_ge(sem_, val_)
        self.cnt[eng] += 1
        ins.then_inc(self.sem[eng], 1)
        self.ninst += 1
        self._record(self.sem[eng], self.cnt[eng], reads, writes)
        return ins

    def dma(self, q, out_ap, in_ap, sbt, reads=(), writes=(), **kw):
        ds = sbt.dsem
        if ds is None:
            if self.free_dsems:
                ds = self.free_dsems.pop()
            else:
                ds = [self.stack.enter_context(self.nc.semaphore("d%d" % len(self.all_dsems))), 0]
                self.all_dsems.append(ds)
            sbt.dsem = ds
            self.dtiles.append(sbt)
        if ds[1]:
            self._wait(q, ds[0], 16 * ds[1])
        self._acquire(q, reads, writes)
        ins = self.E[q].dma_start(out=out_ap, in_=in_ap, **kw)
        ds[1] += 1
        ins.then_inc(ds[0], 16)
        self.ninst += 1
        self._record(ds[0], 16 * ds[1], reads, writes)
        return ins

    def finish(self, out_tiles):
        for t in out_tiles:
            for sid, (sem, val) in t.dep.w.items():
                self._wait("sp", sem, val)

    def barrier(self):
        for e in self.E:
            for f in self.E:
                if self.cnt[f]:
                    self._wait(e, self.sem[f], self.cnt[f])
            for ds in self.all_dsems:
                if ds[1]:
                    self._wait(e, ds[0], 16 * ds[1])
        for t in self.dtiles:
            self.free_dsems.append(t.dsem)
            t.dsem = None
        self.dtiles = []


def V3(ap, a):
    return ap.rearrange("p (a b) -> p a b", a=a)


def bc_mid(ap2, n):
    return ap2.unsqueeze(1).to_broadcast([ap2.shape[0], n, ap2.shape[1]])


def bc_last(ap2, n):
    return ap2.unsqueeze(2).to_broadcast([ap2.shape[0], ap2.shape[1], n])


def rev(ap):
    pat = [list(x) for x in ap.ap]
    st, n = pat[-1]
    pat[-1] = [-st, n]
    return bass.AP(ap.tensor, ap.offset + st * (n - 1), pat)


class Builder:
    def __init__(self, Tn, NS, DEPTH, mixers="ABCD", dbg=False):
        self.Tn, self.NS, self.DEPTH, self.mixers, self.dbg = Tn, NS, DEPTH, mixers, dbg
        self.nc = bass.Bass("TRN2", target_bir_lowering=False)
        self.c = Ctx(self.nc)
        self.rr = 0

    def eng2(self):
        self.rr ^= 1
        return "dve" if self.rr else "act"

    def cp(self, eng, out, in_, R, W):
        nc = self.nc
        if eng == "act":
            return self.c.op("act", lambda: nc.scalar.copy(out=out, in_=in_), R, W)
        e = nc.vector if eng == "dve" else nc.gpsimd
        return self.c.op(eng, lambda: e.tensor_copy(out=out, in_=in_), R, W)

    def tt(self, eng, out, a, b, op, R, W):
        e = self.nc.vector if eng == "dve" else self.nc.gpsimd
        return self.c.op(eng, lambda: e.tensor_tensor(out=out, in0=a, in1=b, op=op), R, W)

    def ts(self, eng, out, a, s1, s2, op0, op1, R, W):
        e = self.nc.vector if eng == "dve" else self.nc.gpsimd
        if s2 is None:
            return self.c.op(eng, lambda: e.tensor_scalar(out=out, in0=a, scalar1=s1, scalar2=None, op0=op0), R, W)
        return self.c.op(eng, lambda: e.tensor_scalar(out=out, in0=a, scalar1=s1, scalar2=s2, op0=op0, op1=op1), R, W)

    def stt(self, out, a, s, b, op0, op1, R, W):
        nc = self.nc
        return self.c.op("dve", lambda: nc.vector.scalar_tensor_tensor(out=out, in0=a, scalar=s, in1=b, op0=op0, op1=op1), R, W)

    def act(self, out, in_, func, R, W, **kw):
        nc = self.nc
        return self.c.op("act", lambda: nc.scalar.activation(out=out, in_=in_, func=func, **kw), R, W)

    def mm(self, out, lhsT, rhs, R, W, start=True, stop=True, skip=False):
        nc = self.nc
        if skip:
            return self.c.op("pe", lambda: nc.tensor.matmul(out, lhsT=lhsT, rhs=rhs, start=False, stop=False, skip_group_check=True), R, W)
        return self.c.op("pe", lambda: nc.tensor.matmul(out, lhsT=lhsT, rhs=rhs, start=start, stop=stop), R, W)

    def tr(self, out, in_, ident, R, W):
        nc = self.nc
        return self.c.op("pe", lambda: nc.tensor.transpose(out=out, in_=in_, identity=ident), R, W)

    def memset(self, eng, ap, val, W):
        e = self.nc.vector if eng == "dve" else self.nc.gpsimd
        return self.c.op(eng, lambda: e.memset(ap, val), (), W)

    def ld(self, dst_tile, dst_ap, src_ap, R=(), **kw):
        return self.c.dma("sp", dst_ap, src_ap, dst_tile, reads=R, writes=[dst_tile], **kw)

    def rsqrt_(self, out, in_, scale, eps_ap, R, W):
        self.act(out, in_, AF.Ln, R, W, scale=scale, bias=eps_ap)
        self.act(out, out, AF.Exp, W, W, scale=-0.5)

    def load_w(self, dst, dcol, src3, n, st, neg=False, eng=None):
        KC = src3.shape[1]
        self.ld(st, st[:, 0:KC, 0:n], src3)
        eng = eng or self.eng2()
        if neg:
            nc = self.nc
            self.c.op("act", lambda: nc.scalar.mul(out=dst[:, 0:KC, dcol:dcol + n], in_=st[:, 0:KC, 0:n], mul=-1.0), [st], [dst])
        else:
            self.cp(eng, dst[:, 0:KC, dcol:dcol + n], st[:, 0:KC, 0:n], [st], [dst])

    def win(self, l, c0, n):
        return self.w["w_in"].h[l, :, c0:c0 + n].rearrange("(kc p) n -> p kc n", p=128)

    def hT_ap(self, kc, tau0, n, d=0, shift=0):
        Tn = self.Tn
        if d == 0:
            col = 1 + tau0 - shift
            return self.hT[:, kc, col:col + n]
        lo = Tn - tau0 - n + 1 + shift
        return rev(self.hT[:, kc, lo:lo + n])

    def proj(self, out_ps, Wt, wc, M, tau0, n, R_extra=(), d=0, W2=None, w2c=0):
        steps = [(Wt, wc, 0)] + ([(W2, w2c, 1)] if W2 is not None else [])
        tot = 8 * len(steps)
        i = 0
        for (Wx, cc, sh) in steps:
            for kc in range(8):
                self.mm(out_ps[0:M, 0:n], Wx[:, kc, cc:cc + M], self.hT_ap(kc, tau0, n, d, sh),
                        [Wx, self.hT_t], [out_ps], start=(i == 0), stop=(i == tot - 1))
                i += 1

    def declare(self):
        c, Tn, NS, DEPTH = self.c, self.Tn, self.NS, self.DEPTH
        shapes = dict(
            w_ada=[DEPTH, 1024, 3072], b_ada=[DEPTH, 3072], g_pre=[DEPTH, 1024], g_post=[DEPTH, 1024],
            w_in=[DEPTH, 1024, P_IN], w_out=[DEPTH, 1024, 1024], a_g_q=[DEPTH, 384], a_w_uq=[DEPTH, 384, 384],
            a_g_kv=[DEPTH, 128], a_w_ukv=[DEPTH, 128, 512], rw_mu=[DEPTH, 2, 832], rw_w0=[DEPTH, 2, 256],
            rw_w2=[DEPTH, 2, 32, 256], rw_a0=[DEPTH, 2, 256], rw_a2=[DEPTH, 2, 32, 256], rw_k_k=[DEPTH, 256],
            rw_k_a=[DEPTH, 256], rw_r_k=[DEPTH, 4, 64], rw_ln_w=[DEPTH, 256], rw_ln_b=[DEPTH, 256],
            m_conv_w=[DEPTH, 4, 768], m_conv_b=[DEPTH, 768], m_dt_bias=[DEPTH, 2, 4], m_a_log=[DEPTH, 2, 4],
            m_d=[DEPTH, 4], m_norm_w=[DEPTH, 256], gd_conv_w=[DEPTH, 4, 768], gd_dt_bias=[DEPTH, 2, 4],
            gd_a_log=[DEPTH, 2, 4], gd_norm_w=[DEPTH, 64])
        self.w = {k: c.dram(k, v, F32, kind="ExternalInput") for k, v in shapes.items()}
        self.x = c.dram("x", [NS, Tn, 1024], F32, kind="ExternalInput")
        self.cin = c.dram("c", [NS, 1024], F32, kind="ExternalInput")
        self.kconst = c.dram("kconst", [128, K_END], F32, kind="ExternalInput")
        self.rope = c.dram("rope", [96, 2, Tn], F32, kind="ExternalInput")
        self.selg = c.dram("selg", [8, 4, 128], F32, kind="ExternalInput")
        self.selm = c.dram("selm", [8, 8, 128], F32, kind="ExternalInput")
        self.y = c.dram("y", [NS, Tn, 1024], F32, kind="ExternalOutput")
        self.mixT = c.dram("mixT", [1024, Tn], BF16, kind="ExternalOutput" if self.dbg else "Internal")

    def consts(self):
        c = self.c
        self.kc = c.sb([128, K_END], F32, "kc")
        self.ld(self.kc, self.kc[:], self.kconst.h[:])
        self.kb = c.sb([128, K_END], BF16, "kb")
        self.cp("dve", self.kb[:], self.kc[:], [self.kc], [self.kb])
        self.epsc = c.sb([128, 2], F32, "epsc")
        self.memset("pool", self.epsc[:, 0:1], EPS, [self.epsc])
        self.memset("pool", self.epsc[:, 1:2], 64e-5, [self.epsc])
        self.hT_t = c.sb([128, 8, self.Tn + 2], BF16, "hT")
        self.hT = self.hT_t
        self.memset("pool", self.hT[:, :, 0:1], 0.0, [self.hT_t])
        self.memset("pool", self.hT[:, :, self.Tn + 1:self.Tn + 2], 0.0, [self.hT_t])
        self.gg = [c.sb([128, 1024], F32, "gg%d" % s) for s in range(self.NS)]
        self.a0 = c.sb([128, 8, self.NS], F32, "a0")
        self.a1 = c.sb([128, 8, self.NS], F32, "a1")

    def p0(self, l):
        c, nc, NS = self.c, self.nc, self.NS
        with ExitStack() as es:
            cT = c.sb([128, 8, NS], F32, "cT", es)
            for s in range(NS):
                self.ld(cT, cT[:, :, s], self.cin.h[s, :].rearrange("(kc p) -> p kc", p=128), allow_slow_non_contiguous=True)
            scb = c.sb([128, 8, NS], BF16, "scb", es)
            self.act(scb[:], cT[:], AF.Silu, [cT], [scb])
            wada = c.sb([128, 8, 3072], BF16, "wada", es)
            st = c.sb([128, 8, 512], F32, "st0", es)
            for j in range(6):
                self.load_w(wada, j * 512, self.w["w_ada"].h[l, :, j * 512:(j + 1) * 512].rearrange("(kc p) n -> p kc n", p=128), 512, st)
            bT = c.sb([128, 24], F32, "bT", es)
            for j3 in range(3):
                self.ld(bT, bT[:, j3 * 8:(j3 + 1) * 8], self.w["b_ada"].h[l, j3 * 1024:(j3 + 1) * 1024].rearrange("(j p) -> p j", p=128),
                        allow_slow_non_contiguous=True)
            gp = c.sb([128, 8], F32, "gp", es)
            self.ld(gp, gp[:], self.w["g_pre"].h[l, :].rearrange("(j p) -> p j", p=128), allow_slow_non_contiguous=True)
            pm = c.ps([128, 16 * NS], F32, "pm", es)
            for idx in range(16):
                for kc in range(8):
                    self.mm(pm[:, idx * NS:(idx + 1) * NS], wada[:, kc, idx * 128:(idx + 1) * 128], scb[:, kc, :],
                            [wada, scb], [pm], start=(kc == 0), stop=(kc == 7))
            pm3 = V3(pm[:], 16)
            self.tt("dve", self.a0[:], pm3[:, 0:8, :], bc_last(bT[:, 0:8], NS), ALU.add, [pm, bT], [self.a0])
            tmp = c.sb([128, 8, NS], F32, "tmp0", es)
            self.tt("dve", tmp[:], pm3[:, 8:16, :], bc_last(bT[:, 8:16], NS), ALU.add, [pm, bT], [tmp])
            self.ts("pool", tmp[:], tmp[:], 1.0, None, ALU.add, None, [tmp], [tmp])
            self.tt("pool", self.a1[:], tmp[:], bc_last(gp[:], NS), ALU.mult, [tmp, gp], [self.a1])
            bg = c.sb([128, 1024], F32, "bg", es)
            self.ld(bg, bg[:], self.w["b_ada"].h[l, 2048:3072].partition_broadcast(128))
            gpo = c.sb([128, 1024], F32, "gpo", es)
            self.ld(gpo, gpo[:], self.w["g_post"].h[l, :].partition_broadcast(128))
            for s in range(NS):
                pg = c.ps([128, 1024], F32, "pg", es) if s == 0 else pg
                for n in range(2):
                    for kc in range(8):
                        self.mm(pg[:, n * 512:(n + 1) * 512], scb[:, kc, s:s + 1].to_broadcast([128, 128]),
                                wada[:, kc, 2048 + n * 512:2048 + (n + 1) * 512], [scb, wada], [pg], start=(kc == 0), stop=(kc == 7))
                self.tt("dve", self.gg[s][:], pg[:], bg[:], ALU.add, [pg, bg], [self.gg[s]])
                self.tt("pool", self.gg[s][:], self.gg[s][:], gpo[:], ALU.mult, [self.gg[s], gpo], [self.gg[s]])
            c.barrier()

    def xsrc(self, l, s):
        return (self.x if l == 0 else self.y), s

    def p1(self, l, s):
        c, nc, Tn = self.c, self.nc, self.Tn
        src, _ = self.xsrc(l, s)
        with ExitStack() as es:
            xt = [c.sb([128, 1024], F32, "xt", es) for _ in range(2)]
            xn = [c.sb([128, 1024], BF16, "xn", es) for _ in range(2)]
            junk = c.sb([128, 1024], BF16, "junk", es)
            ss = [c.sb([128, 2], F32, "ss", es) for _ in range(2)]
            pt = [c.ps([128, 1024], BF16, "pt", es) for _ in range(2)]
            tm = [c.sb([128, 8, 128], F32, "tm", es) for _ in range(2)]
            for tt_ in range(Tn // 128):
                b = tt_ % 2
                self.ld(xt[b], xt[b][:], src.h[s, tt_ * 128:(tt_ + 1) * 128, :], R=[src])
                self.act(junk[:], xt[b][:], AF.Square, [xt[b]], [junk, ss[b]], accum_out=ss[b][:, 0:1])
                self.rsqrt_(ss[b][:, 1:2], ss[b][:, 0:1], 1.0 / 1024, self.epsc[:, 0:1], [ss[b], self.epsc], [ss[b]])
                self.ts("dve", xn[b][:], xt[b][:], ss[b][:, 1:2], None, ALU.mult, None, [xt[b], ss[b]], [xn[b]])
                for fc in range(8):
                    self.tr(pt[b][:, fc * 128:(fc + 1) * 128], xn[b][:, fc * 128:(fc + 1) * 128], self.kb[:, K_ID:K_ID + 128],
                            [xn[b], self.kb], [pt[b]])
                self.tt("dve", tm[b][:], V3(pt[b][:], 8), bc_last(self.a1[:, :, s], 128), ALU.mult, [pt[b], self.a1], [tm[b]])
                self.tt("pool", self.hT[:, :, 1 + tt_ * 128:1 + (tt_ + 1) * 128], tm[b][:], bc_last(self.a0[:, :, s], 128), ALU.add,
                        [tm[b], self.a0], [self.hT_t])
            c.barrier()

    def p6(self, l, s):
        c, nc, Tn = self.c, self.nc, self.Tn
        src, _ = self.xsrc(l, s)
        TW = min(512, Tn)
        with ExitStack() as es:
            wo = c.sb([128, 8, 1024], BF16, "wo", es)
            st = c.sb([128, 8, 512], F32, "st6", es)
            for j in range(2):
                self.load_w(wo, j * 512, self.w["w_out"].h[l, :, j * 512:(j + 1) * 512].rearrange("(kc p) n -> p kc n", p=128), 512, st)
            mt = [c.sb([128, 8, TW], BF16, "mt", es) for _ in range(2)]
            xt = [c.sb([128, 1024], F32, "xt6", es) for _ in range(2)]
            yt = [c.sb([128, 1024], F32, "yt6", es) for _ in range(2)]
            junk = c.sb([128, 1024], BF16, "junk6", es)
            ss = [c.sb([128, 2], F32, "ss6", es) for _ in range(2)]
            po = [c.ps([128, 1024], F32, "po", es) for _ in range(2)]
            k = 0
            for j in range(Tn // TW):
                mb = mt[j % 2]
                self.ld(mb, mb[:], self.mixT.h[:, j * TW:(j + 1) * TW].rearrange("(a p) t -> p a t", p=128), R=[self.mixT])
                for u in range(TW // 128):
                    b = k % 2
                    k += 1
                    t0 = j * TW + u * 128
                    self.ld(xt[b], xt[b][:], src.h[s, t0:t0 + 128, :], R=[src])
                    for n in range(2):
                        for kc in range(8):
                            self.mm(po[b][:, n * 512:(n + 1) * 512], mb[:, kc, u * 128:(u + 1) * 128], wo[:, kc, n * 512:(n + 1) * 512],
                                    [mb, wo], [po[b]], start=(kc == 0), stop=(kc == 7))
                    self.act(junk[:], po[b][:], AF.Square, [po[b]], [junk, ss[b]], accum_out=ss[b][:, 0:1])
                    self.rsqrt_(ss[b][:, 1:2], ss[b][:, 0:1], 1.0 / 1024, self.epsc[:, 0:1], [ss[b], self.epsc], [ss[b]])
                    self.tt("dve", yt[b][:], po[b][:], self.gg[s][:], ALU.mult, [po[b], self.gg[s]], [yt[b]])
                    self.stt(yt[b][:], yt[b][:], ss[b][:, 1:2], xt[b][:], ALU.mult, ALU.add, [yt[b], ss[b], xt[b]], [yt[b]])
                    self.c.dma("sp", self.y.h[s, t0:t0 + 128, :], yt[b][:], yt[b], reads=[yt[b]], writes=[self.y])
            c.barrier()

    def mla(self, l, s):
        c, nc, Tn = self.c, self.nc, self.Tn
        TW = min(512, Tn)
        NQG = Tn // TW
        NKB = Tn // 128
        U = TW // 128
        scale = 96 ** -0.5
        ID_b = self.kb[:, K_ID:K_ID + 128]
        ONE_b = self.kb[:, K_ONE:K_ONE + 128]
        with ExitStack() as es:
            Wm = c.sb([128, 8, 960], BF16, "Wm", es)
            Wq = c.sb([128, 3, 384], BF16, "Wq", es)
            Wqb = c.sb([128, 3, 384], BF16, "Wqb", es)
            Wkv = c.sb([128, 512], BF16, "Wkv", es)
            with ExitStack() as e1:
                st = c.sb([128, 8, 512], F32, "stA", e1)
                self.memset("pool", Wm[:, :, 512:704], 0.0, [Wm])
                self.load_w(Wm, 0, self.win(l, C_CQ, 384), 384, st)
                self.load_w(Wm, 384, self.win(l, C_CKV, 128), 128, st)
                self.load_w(Wm, 512 + 64, self.win(l, C_KR, 32), 32, st)
                self.load_w(Wm, 608 + 64, self.win(l, C_KR + 16, 16), 16, st, neg=True)
                self.load_w(Wm, 608 + 80, self.win(l, C_KR, 16), 16, st)
                self.load_w(Wm, 704, self.win(l, C_Z, 256), 256, st)
                stq = c.sb([128, 3, 384], F32, "stq", e1)
                self.ld(stq, stq[:], self.w["a_w_uq"].h[l].rearrange("(kc p) n -> p kc n", p=128))
                gq = c.sb([128, 3], F32, "gq", e1)
                self.ld(gq, gq[:], self.w["a_g_q"].h[l, :].rearrange("(j p) -> p j", p=128), allow_slow_non_contiguous=True)
                self.memset("pool", Wqb[:], 0.0, [Wqb])
                for fc in range(3):
                    self.ts("dve", Wq[:, fc, :], stq[:, fc, :], gq[:, fc:fc + 1], None, ALU.mult, None, [stq, gq], [Wq])
                    src4 = V3(Wq[:, fc, :], 4)
                    dst4 = V3(Wqb[:, fc, :], 4)
                    self.c.op("act", lambda s4=src4, d4=dst4: nc.scalar.mul(out=d4[:, :, 64:80], in_=s4[:, :, 80:96], mul=-1.0), [Wq], [Wqb])
                    self.cp("pool", dst4[:, :, 80:96], src4[:, :, 64:80], [Wq], [Wqb])
                stk = c.sb([128, 512], F32, "stk", e1)
                self.ld(stk, stk[:], self.w["a_w_ukv"].h[l])
                gkv = c.sb([128, 1], F32, "gkv", e1)
                self.ld(gkv, gkv[:], self.w["a_g_kv"].h[l, :].rearrange("(p o) -> p o", o=1))
                self.ts("dve", Wkv[:], stk[:], gkv[:, 0:1], None, ALU.mult, None, [stk, gkv], [Wkv])
                c.barrier()
            KT = [c.sb([96, Tn], BF16, "KT%d" % h, es) for h in range(4)]
            Va = c.sb([128, NKB, 4, 65], BF16, "Va", es)
            self.memset("pool", Va[:, :, :, 64:65], 1.0, [Va])
            cosx = c.sb([96, TW], F32, "cosx", es)
            sinx = c.sb([96, TW], F32, "sinx", es)
            t1 = c.sb([96, TW], F32, "t1", es)
            t2 = c.sb([96, TW], F32, "t2", es)
            with ExitStack() as e2:
                ckT = c.sb([128, TW], BF16, "ckT", e2)
                sqk = c.sb([128, TW], BF16, "sqk", e2)
                rk = c.sb([128, TW], F32, "rk", e2)
                rcol = c.sb([128, 2], F32, "rcol", e2)
                pA = [c.ps([128, 512], F32, "pA", e2) for _ in range(4)]
                pB = c.ps([128, 512], F32, "pB", e2)
                pV = c.ps([128, 256], F32, "pV", e2)
                pc = c.ps([128, 2], F32, "pc", e2)
                for j in range(NQG):
                    t0 = j * TW
                    self.ld(cosx, cosx[:], self.rope.h[:, 0, t0:t0 + TW])
                    self.ld(sinx, sinx[:], self.rope.h[:, 1, t0:t0 + TW])
                    self.proj(pB, Wm, 384, 128, t0, TW)
                    self.cp("act", ckT[:], pB[:, 0:TW], [pB], [ckT])
                    self.tt("pool", sqk[:], ckT[:], ckT[:], ALU.mult, [ckT], [sqk])
                    self.mm(pA[0][:, 0:TW], ONE_b, sqk[:], [self.kb, sqk], [pA[0]])
                    self.rsqrt_(rk[:], pA[0][:, 0:TW], 1.0 / 128, self.epsc[:, 0:1], [pA[0], self.epsc], [rk])
                    for h in range(4):
                        pa = pA[1 + h % 2]
                        self.mm(pa[0:64, 0:TW], Wkv[:, h * 128:h * 128 + 64], ckT[:], [Wkv, ckT], [pa])
                        self.tt("dve", KT[h][0:64, t0:t0 + TW], pa[0:64, 0:TW], rk[0:64, :], ALU.mult, [pa, rk], [KT[h]])
                    self.proj(pA[3], Wm, 512, 96, t0, TW)
                    self.proj(pB, Wm, 608, 96, t0, TW)
                    self.tt("dve", t1[64:96, :], pA[3][64:96, 0:TW], cosx[64:96, :], ALU.mult, [pA[3], cosx], [t1])
                    self.tt("dve", t2[64:96, :], pB[64:96, 0:TW], sinx[64:96, :], ALU.mult, [pB, sinx], [t2])
                    for h in range(4):
                        self.tt("pool", KT[h][64:96, t0:t0 + TW], t1[64:96, :], t2[64:96, :], ALU.add, [t1, t2], [KT[h]])
                    for u in range(U):
                        kb_ = (t0 + u * 128) // 128
                        self.mm(pc[:, 0:1], sqk[:, u * 128:(u + 1) * 128], self.kb[:, K_ONE:K_ONE + 1], [sqk, self.kb], [pc])
                        self.rsqrt_(rcol[:, 0:1], pc[:, 0:1], 1.0 / 128, self.epsc[:, 0:1], [pc, self.epsc], [rcol])
                        for h in range(4):
                            self.mm(pV[:, h * 64:(h + 1) * 64], ckT[:, u * 128:(u + 1) * 128], Wkv[:, h * 128 + 64:h * 128 + 128], [ckT, Wkv], [pV])
                        self.ts("dve", Va[:, kb_, :, 0:64], V3(pV[:], 4), rcol[:, 0:1], None, ALU.mult, None, [pV, rcol], [Va])
                c.barrier()
            with ExitStack() as e3:
                NB = 3
                pS = [c.ps([128, 512], F32, "pS", e3) for _ in range(NB)]
                pz = c.ps([128, 512], F32, "pz", e3)
                pO = [c.ps([128, 512], F32, "pO", e3) for _ in range(2)]
                pBC = c.ps([128, 512], F32, "pBC", e3)
                Pt = [c.sb([128, TW], BF16, "Pt", e3) for _ in range(NB)]
                cqT = c.sb([128, 3, TW], BF16, "cqT", e3)
                sq = c.sb([128, 3, TW], BF16, "sq", e3)
                rq = c.sb([128, TW], F32, "rq", e3)
                CR = c.sb([96, TW], F32, "CR", e3)
                SR = c.sb([96, TW], F32, "SR", e3)
                qg_t = [c.sb([96, TW], BF16, "qg%d" % h, e3) for h in range(4)]
                rs = c.sb([128, TW], F32, "rs", e3)
                bcs = c.sb([64, TW], F32, "bcs", e3)
                sz = c.sb([64, TW], F32, "sz", e3)
                on = c.sb([64, TW], F32, "on", e3)
                og = [c.sb([64, TW], BF16, "og", e3) for _ in range(2)]
                it = 0
                for qg in range(NQG):
                    q0 = qg * TW
                    self.ld(cosx, cosx[:], self.rope.h[:, 0, q0:q0 + TW])
                    self.ld(sinx, sinx[:], self.rope.h[:, 1, q0:q0 + TW])
                    for fc in range(3):
                        self.proj(pS[fc], Wm, fc * 128, 128, q0, TW)
                        self.cp("act", cqT[:, fc, :], pS[fc][:, 0:TW], [pS[fc]], [cqT])
                        self.tt("pool", sq[:, fc, :], cqT[:, fc, :], cqT[:, fc, :], ALU.mult, [cqT], [sq])
                    for fc in range(3):
                        self.mm(pz[:, 0:TW], ONE_b, sq[:, fc, :], [self.kb, sq], [pz], start=(fc == 0), stop=(fc == 2))
                    self.rsqrt_(rq[:], pz[:, 0:TW], 1.0 / 384, self.epsc[:, 0:1], [pz, self.epsc], [rq])
                    self.tt("pool", CR[:], cosx[:], rq[0:96, :], ALU.mult, [cosx, rq], [CR])
                    self.tt("pool", SR[:], sinx[:], rq[0:96, :], ALU.mult, [sinx, rq], [SR])
                    for h in range(4):
                        pa, pb = pS[0], pS[1]
                        for fc in range(3):
                            self.mm(pa[0:96, 0:TW], Wq[:, fc, h * 96:(h + 1) * 96], cqT[:, fc, :], [Wq, cqT], [pa], start=(fc == 0), stop=(fc == 2))
                        for fc in range(3):
                            self.mm(pb[0:96, 0:TW], Wqb[:, fc, h * 96:(h + 1) * 96], cqT[:, fc, :], [Wqb, cqT], [pb], start=(fc == 0), stop=(fc == 2))
                        self.tt("dve", t1[:], pa[0:96, 0:TW], CR[:], ALU.mult, [pa, CR], [t1])
                        self.tt("dve", t2[:], pb[0:96, 0:TW], SR[:], ALU.mult, [pb, SR], [t2])
                        self.tt("pool", qg_t[h][:], t1[:], t2[:], ALU.add, [t1, t2], [qg_t[h]])
                    for h in range(4):
                        po_ = pO[h % 2]
                        def score(kb_, b):
                            self.mm(pS[b][:, 0:TW], KT[h][:, kb_ * 128:(kb_ + 1) * 128], qg_t[h][:], [KT[h], qg_t[h]], [pS[b]])
                        score(0, it % NB)
                        if NKB > 1:
                            score(1, (it + 1) % NB)
                        for kb_ in range(NKB):
                            b = it % NB
                            it += 1
                            if kb_ + 2 < NKB:
                                score(kb_ + 2, (it + 1) % NB)
                            self.act(Pt[b][:], pS[b][:, 0:TW], AF.Exp, [pS[b]], [Pt[b]], scale=scale)
                            self.mm(po_[0:65, 0:TW], Va[:, kb_, h, :], Pt[b][:], [Va, Pt[b]], [po_], start=(kb_ == 0), stop=(kb_ == NKB - 1))
                        self.proj(pz, Wm, 704 + h * 64, 64, q0, TW)
                        self.act(sz[:], pz[0:64, 0:TW], AF.Silu, [pz], [sz])
                        self.c.op("dve", lambda p=po_: nc.vector.reciprocal(out=rs[64:65, :], in_=p[64:65, 0:TW]), [po_], [rs])
                        self.mm(pBC[0:64, 0:TW], self.kc[64:65, K_ONE:K_ONE + 64], rs[64:65, :], [self.kc, rs], [pBC])
                        self.cp("act", bcs[:], pBC[0:64, 0:TW], [pBC], [bcs])
                        self.tt("dve", on[:], po_[0:64, 0:TW], bcs[:], ALU.mult, [po_, bcs], [on])
                        ob = og[h % 2]
                        self.tt("pool", ob[:], on[:], sz[:], ALU.mult, [on, sz], [ob])
                        self.c.dma("sp", self.mixT.h[h * 64:(h + 1) * 64, q0:q0 + TW], ob[:], ob, reads=[ob], writes=[self.mixT])
            c.barrier()

    def dplr_env(self, es, nsets=3):
        c = self.c
        sets = []

        def sl(bank, lo, n=128):
            t = c.sub(bank[:, lo:lo + n])
            t.dep = bank.dep
            return t
        cb = c.ps([128, 512], F32, "dpc", es)
        chain = {"pW": sl(cb, 0), "pU": sl(cb, 128), "pZ": sl(cb, 256), "pY": sl(cb, 384, 64)}
        for i in range(nsets):
            e = dict(chain)
            bA = c.ps([128, 512], F32, "dpa", es)
            bB = c.ps([128, 512], F32, "dpb", es)
            e["pQB"], e["pQK"], e["pNL"] = sl(bA, 0, 192), sl(bA, 192, 192), sl(bA, 384)
            e["pXP"] = sl(bA, 0, 256)
            e["pPt"] = sl(bB, 0)
            e["pT"] = c.sub(V3(bB[:, 256:448].bitcast(BF16), 3))
            e["pT"].dep = bB.dep
            for nm in ["Nt", "Pt0", "Pt1"]:
                e[nm] = c.sb([128, 128], F32R, "d" + nm, es)
            for nm in ["XP0", "XP1"]:
                e[nm] = c.sb([128, 256], F32R, "d" + nm, es)
            for nm in ["AK", "AR", "TT", "W", "U"]:
                e[nm] = c.sb([128, 128], BF16, "d" + nm, es)
            e["Tr"] = c.sb([128, 3, 128], BF16, "dTr", es)
            sets.append(e)
        st = {"Zf": c.sb([128, 128], F32, "Zf", es), "Zb": c.sb([128, 128], BF16, "Zb", es), "Zg": c.sb([128, 128], F32, "Zg", es)}
        return sets, st

    def dplr_pre(self, e, o, ci):
        kc, kb = self.kc, self.kb
        IDf = kc[:, K_ID:K_ID + 128]
        IDb = kb[:, K_ID:K_ID + 128]
        PAR, QB, QK = o["PAR"](ci), o["QB"](ci), o["QK"](ci)
        ms, mi, ml = o["ms"](ci), o["mi"](ci), o["ml"](ci)
        PA = PAR[0][:, 0:128]
        same = o.get("same", False)
        self.mm(e["pQB"][:], QB[0], PAR[0], [QB[1], PAR[1]], [e["pQB"]])
        if not same:
            self.mm(e["pQK"][:], QK[0], PAR[0], [QK[1], PAR[1]], [e["pQK"]])
        self.mm(e["pNL"][:], PA, QB[0], [QB[1], PAR[1]], [e["pNL"]])
        for j, nm in enumerate(["SB", "SK", "V"]):
            if same and nm == "SB":
                continue
            src = o[nm](ci)
            self.tr(e["pT"][:, j, :], src[0], IDb, [src[1], kb], [e["pT"]])
        XP = e["XP0"]
        self.tt("dve", V3(XP[:, 128:256], 2), V3(e["pQB"][:, 0:128], 2), bc_mid(ms[0], 2), ALU.mult, [e["pQB"], ms[1]], [XP])
        self.tt("dve", V3(e["Nt"][:], 2), V3(e["pNL"][:], 2), bc_mid(ml[0], 2), ALU.mult, [e["pNL"], ml[1]], [e["Nt"]])
        self.tt("dve", e["AR"][:, 0:64], e["pQB"][:, 128:192], mi[0], ALU.mult, [e["pQB"], mi[1]], [e["AR"]])
        if same:
            self.cp("pool", e["AK"][:], XP[:, 128:256], [XP], [e["AK"]])
            self.cp("act", e["Tr"][:, 1:3, :], e["pT"][:, 1:3, :], [e["pT"]], [e["Tr"]])
        else:
            self.tt("dve", V3(e["AK"][:], 2), V3(e["pQK"][:, 0:128], 2), bc_mid(ms[0], 2), ALU.mult, [e["pQK"], ms[1]], [e["AK"]])
            self.tt("dve", e["AR"][:, 64:128], e["pQK"][:, 128:192], mi[0], ALU.mult, [e["pQK"], mi[1]], [e["AR"]])
            self.cp("act", e["Tr"][:], e["pT"][:], [e["pT"]], [e["Tr"]])
        self.tt("pool", XP[:, 0:128], XP[:, 128:256], IDf, ALU.add, [XP, kc], [XP])
        yield
        Pt = e["Nt"]
        Ptn = e["Pt1"]
        self.mm(e["pPt"][:], XP[:, 128:256], Pt[:], [XP, Pt], [e["pPt"]])
        self.mm(e["pXP"][:, 128:256], Pt[:], XP[:, 128:256], [XP, Pt], [e["pXP"]])
        XPn = e["XP1"]
        self.cp("act", Ptn[:], e["pPt"][:], [e["pPt"]], [Ptn])
        self.cp("act", XPn[:, 128:256], e["pXP"][:, 128:256], [e["pXP"]], [XPn])
        self.cp("pool", XPn[:, 0:128], XP[:, 0:128], [XP], [XPn])
        XP, Pt = XPn, Ptn
        yield
        for k in range(1, 6):
            XPn = e["XP%d" % (k % 2)]
            Ptn = e["Pt%d" % (k % 2)]
            if k < 4:
                self.mm(e["pXP"][:], Pt[:], XP[:], [Pt, XP], [e["pXP"]])
            else:
                self.mm(e["pXP"][:, 0:128], Pt[:], XP[:, 0:128], [Pt, XP], [e["pXP"]])
            if k < 5:
                self.mm(e["pPt"][:], XP[:, 128:256], Pt[:], [XP, Pt], [e["pPt"]])
            self.tt("dve", XPn[:, 0:128], XP[:, 0:128], e["pXP"][:, 0:128], ALU.add, [XP, e["pXP"]], [XPn])
            if k < 4:
                self.cp("act", XPn[:, 128:256], e["pXP"][:, 128:256], [e["pXP"]], [XPn])
            if k < 5:
                self.cp("act", Ptn[:], e["pPt"][:], [e["pPt"]], [Ptn])
            XP, Pt = XPn, Ptn
            yield
        self.cp("pool", e["TT"][:], XP[:, 0:128], [XP], [e["TT"]])

    def dplr_chain(self, e, st, o, ci):
        nc = self.nc
        Zf, Zb, Zg = st["Zf"], st["Zb"], st["Zg"]
        ZA, ZR = o["ZA"](ci), o["ZR"](ci)
        SBt, SKt, Vt = e["Tr"][:, 0, :], e["Tr"][:, 1, :], e["Tr"][:, 2, :]
        gz, gs = o["gZ"](ci), o["gS"](ci)
        self.c.op("act", lambda: nc.scalar.mul(out=Zg[:], in_=Zf[:], mul=gz[0]), [Zf, gz[1]], [Zg])
        self.mm(e["pW"][:], ZA[0], Zb[:], [ZA[1], Zb], [e["pW"]], start=True, stop=False)
        self.mm(e["pW"][:], e["AK"][:], Vt, [e["AK"], e["Tr"]], [e["pW"]], start=False, stop=True)
        self.cp("act", e["W"][:], e["pW"][:], [e["pW"]], [e["W"]])
        yield
        self.mm(e["pU"][:], e["TT"][:], e["W"][:], [e["TT"], e["W"]], [e["pU"]])
        if o.get("same", False):
            self.tt("dve", e["U"][:], e["pU"][:], Vt, ALU.add, [e["pU"], e["Tr"]], [e["U"]])
            yield
            self.mm(e["pZ"][:], SKt, e["U"][:], [e["Tr"], e["U"]], [e["pZ"]])
            self.mm(e["pY"][:], Zb[:], ZR[0], [Zb, ZR[1]], [e["pY"]], start=True, stop=False)
            self.mm(e["pY"][:], e["U"][:], e["AR"][:, 0:64], [e["U"], e["AR"]], [e["pY"]], start=False, stop=True)
        else:
            self.cp("dve", e["U"][:], e["pU"][:], [e["pU"]], [e["U"]])
            yield
            self.mm(e["pZ"][:], SBt, e["U"][:], [e["Tr"], e["U"]], [e["pZ"]], start=True, stop=False)
            self.mm(e["pZ"][:], SKt, Vt, [e["Tr"]], [e["pZ"]], start=False, stop=True)
            self.mm(e["pY"][:], Zb[:], ZR[0], [Zb, ZR[1]], [e["pY"]], start=True, stop=False)
            self.mm(e["pY"][:], e["U"][:], e["AR"][:, 0:64], [e["U"], e["AR"]], [e["pY"]], start=False, stop=False)
            self.mm(e["pY"][:], Vt, e["AR"][:, 64:128], [e["Tr"], e["AR"]], [e["pY"]], start=False, stop=True)
        if gs is None:
            self.tt("dve", Zb[:], e["pZ"][:], Zg[:], ALU.add, [e["pZ"], Zg], [Zb])
            self.tt("dve", Zf[:], e["pZ"][:], Zg[:], ALU.add, [e["pZ"], Zg], [Zf])
        else:
            self.stt(Zb[:], e["pZ"][:], gs[0], Zg[:], ALU.mult, ALU.add, [e["pZ"], gs[1], Zg], [Zb])
            self.stt(Zf[:], e["pZ"][:], gs[0], Zg[:], ALU.mult, ALU.add, [e["pZ"], gs[1], Zg], [Zf])
        o["yout"](ci, e["pY"])
        yield

    def dplr_run(self, sets, st, o, nch, side=None):
        NS_ = len(sets)
        active, done = [], set()
        chain, next_chain, next_pre = None, 0, 0
        while next_chain < nch:
            while next_pre < nch and next_pre < next_chain + NS_ and len(active) < NS_:
                active.append([next_pre, self.dplr_pre(sets[next_pre % NS_], o, next_pre)])
                next_pre += 1
            if chain is None and next_chain in done:
                chain = self.dplr_chain(sets[next_chain % NS_], st, o, next_chain)
            for item in list(active):
                try:
                    next(item[1])
                except StopIteration:
                    active.remove(item)
                    done.add(item[0])
            if chain is not None:
                try:
                    next(chain)
                except StopIteration:
                    chain = None
                    next_chain += 1
            if side is not None:
                try:
                    next(side)
                except StopIteration:
                    side = None
        if side is not None:
            for _ in side:
                pass

    def to_bd(self, eng_fn, dst, nch):
        for hh in range(2):
            lo, hi = hh * 64, hh * 64 + 64
            eng_fn(dst[lo:hi, :, lo:hi], lo, hi)

    def col(self, es, src_ap, name="col", n=128):
        t = self.c.sb([n, 1], F32, name, es)
        self.ld(t, t[:], src_ap.rearrange("(p o) -> p o", o=1))
        return t

    def rwkv(self, l, s):
        c, nc, Tn = self.c, self.nc, self.Tn
        SG = min(256, Tn)
        nch = SG // 64
        NSG = Tn // SG
        kc, kb = self.kc, self.kb
        BOb = kb[:, K_BO:K_BO + 128]
        BOf = kc[:, K_BO:K_BO + 128]
        w = self.w
        with ExitStack() as es:
            sets, st = self.dplr_env(es)
            Yacc = c.sb([128, Tn], F32, "Yacc", es)
            Bacc = c.sb([128, Tn], F32, "Bacc", es)
            ones = c.sb([128, SG], F32, "ones", es)
            self.memset("pool", ones[:], 1.0, [ones])
            W1 = c.sb([128, 8, 448], BF16, "W1", es)
            Pc = c.sb([128, SG + 1], F32, "Pc", es)
            carry = c.sb([128, 5], F32, "carry", es)
            mu5 = c.sb([128, 5], F32, "mu5", es)
            OS = []
            for i in range(2):
                S_ = {"PAR": c.sb([128, nch, 192], BF16, "PAR", es), "g": c.sb([128, nch], F32, "gcol", es)}
                for nm in ["B", "K", "V"]:
                    S_[nm] = c.sb([128, nch, 128], BF16, "bd" + nm, es)
                for nm in ["PAR", "B", "K", "V"]:
                    self.memset("pool", S_[nm][:], 0.0, [S_[nm]])
                OS.append(S_)
            w2b = c.sb([32, 128], BF16, "w2b", es)
            a2b = c.sb([32, 128], BF16, "a2b", es)
            F = {nm: c.sb([128, SG], F32, "f" + nm, es) for nm in ["r", "k", "v", "lw", "alr", "G", "lg", "E1", "E2", "E3", "kk", "kp", "t0", "t1"]}
            Gs = c.sb([128, nch], F32, "Gs", es)
            self.memset("pool", Gs[:, 0:1], 0.0, [Gs])
            Hb = {nm: c.sb([128, SG], BF16, "h" + nm, es) for nm in ["sq", "rkr"]}
            th = c.sb([32, SG], BF16, "th", es)
            xab = c.sb([32, SG], BF16, "xab", es)
            pp = [c.ps([128, 512], F32, "rpp", es)] * 2
            ppi = [0]

            def nextp():
                ppi[0] ^= 1
                return pp[ppi[0]]

            for pr in range(2):
                cs = slice(pr * 128, pr * 128 + 128)
                kk_c = self.col(es, w["rw_k_k"].h[l, cs], "kkc")
                ka_c = self.col(es, w["rw_k_a"].h[l, cs], "kac")
                rk_c = self.col(es, w["rw_r_k"].h[l].rearrange("h n -> (h n)")[cs], "rkc")
                lnw_c = self.col(es, w["rw_ln_w"].h[l, cs], "lnw")
                lnb_c = self.col(es, w["rw_ln_b"].h[l, cs], "lnb")
                for d in range(2):
                    with ExitStack() as e1:
                        stg = c.sb([128, 8, 128], F32, "stg", e1)
                        srcs = [(C_RKV + pr * 128, 128, pr * 128), (C_RKV + 256 + pr * 128, 128, 256 + pr * 128),
                                (C_RKV + 512 + pr * 128, 128, 512 + pr * 128), ((C_WAF if d == 0 else C_WAB), 32, 768),
                                ((C_WAF if d == 0 else C_WAB) + 32, 32, 800)]
                        o_ = 0
                        for j5, (c0, n, m0) in enumerate(srcs):
                            self.load_w(W1, o_, self.win(l, c0, n), n, stg)
                            self.ld(mu5, mu5[0:n, j5:j5 + 1], w["rw_mu"].h[l, d, m0:m0 + n].rearrange("(p o) -> p o", o=1))
                            o_ += n
                        s32 = c.sb([32, 256], F32, "s32", e1)
                        self.ld(s32, s32[:, 0:128], w["rw_w2"].h[l, d, :, cs])
                        self.ld(s32, s32[:, 128:256], w["rw_a2"].h[l, d, :, cs])
                        self.cp("dve", w2b[:], s32[:, 0:128], [s32], [w2b])
                        self.cp("dve", a2b[:], s32[:, 128:256], [s32], [a2b])
                        c.barrier()
                    self.memset("pool", carry[:], 0.0, [carry])
                    w0_c = self.col(es, w["rw_w0"].h[l, d, cs], "w0c")
                    a0_c = self.col(es, w["rw_a0"].h[l, d, cs], "a0c")
                    self.memset("pool", st["Zf"][:], 0.0, [st["Zf"]])
                    self.memset("pool", st["Zb"][:], 0.0, [st["Zb"]])
                    def prep(sg, S, d=d, w0_c=w0_c, a0_c=a0_c):
                        tau0 = sg * SG
                        PAR, BDB, BDK, BDV = S["PAR"], S["B"], S["K"], S["V"]

                        def shifted(j5, wc, M, out_ap, out_t):
                            p_ = nextp()
                            self.proj(p_, W1, wc, M, tau0, SG, d=d)
                            self.cp("pool", Pc[0:M, 0:1], carry[0:M, j5:j5 + 1], [carry], [Pc])
                            self.cp("act", Pc[0:M, 1:SG + 1], p_[0:M, 0:SG], [p_], [Pc])
                            self.cp("pool", carry[0:M, j5:j5 + 1], Pc[0:M, SG:SG + 1], [Pc], [carry])
                            self.tt("pool", F["t0"][0:M, :], Pc[0:M, 0:SG], Pc[0:M, 1:SG + 1], ALU.subtract, [Pc], [F["t0"]])
                            self.stt(out_ap, F["t0"][0:M, :], mu5[0:M, j5:j5 + 1], Pc[0:M, 1:SG + 1], ALU.mult, ALU.add, [F["t0"], mu5, Pc], [out_t])
                        shifted(0, 0, 128, F["r"][:], F["r"])
                        yield
                        shifted(1, 128, 128, F["k"][:], F["k"])
                        yield
                        shifted(2, 256, 128, F["v"][:], F["v"])
                        self.to_bd(lambda o_ap, lo, hi: self.cp("pool", o_ap, V3(F["v"][lo:hi, :], nch), [F["v"]], [BDV]), BDV, nch)
                        yield
                        shifted(3, 384, 32, F["t1"][0:32, :], F["t1"])
                        self.act(th[:], F["t1"][0:32, :], AF.Tanh, [F["t1"]], [th])
                        yield
                        shifted(4, 416, 32, xab[:], xab)
                        yield
                        p_ = nextp()
                        self.mm(p_[:, 0:SG], w2b[:], th[:], [w2b, th], [p_])
                        self.act(F["lw"][:], p_[:, 0:SG], AF.Sigmoid, [p_, w0_c], [F["lw"]], bias=w0_c[:, 0:1])
                        self.ts("dve", F["lw"][:], F["lw"][:], -math.exp(-0.5), None, ALU.mult, None, [F["lw"]], [F["lw"]])
                        p_ = nextp()
                        self.mm(p_[:, 0:SG], a2b[:], xab[:], [a2b, xab], [p_])
                        self.act(F["alr"][:], p_[:, 0:SG], AF.Sigmoid, [p_, a0_c], [F["alr"]], bias=a0_c[:, 0:1])
                        yield
                        self.c.op("dve", lambda: nc.vector.tensor_tensor_scan(out=F["G"][:], data0=ones[:], data1=F["lw"][:], initial=0.0,
                                                                                op0=ALU.mult, op1=ALU.add), [ones, F["lw"]], [F["G"]])
                        if nch > 1:
                            self.cp("pool", Gs[:, 1:nch], F["G"][:, 63:SG - 1:64], [F["G"]], [Gs])
                        self.tt("pool", V3(F["lg"][:], nch), V3(F["G"][:], nch), bc_last(Gs[:], 64), ALU.subtract, [F["G"], Gs], [F["lg"]])
                        yield
                        self.act(F["E1"][:], F["lg"][:], AF.Exp, [F["lg"]], [F["E1"]])
                        self.act(F["E3"][:], F["lg"][:], AF.Exp, [F["lg"]], [F["E3"]], scale=-1.0)
                        self.tt("pool", F["t0"][:], F["lg"][:], F["lw"][:], ALU.subtract, [F["lg"], F["lw"]], [F["t0"]])
                        self.act(F["E2"][:], F["t0"][:], AF.Exp, [F["t0"]], [F["E2"]])
                        self.cp("pool", S["g"][:], F["E1"][:, 63:SG:64], [F["E1"]], [S["g"]])
                        yield
                        self.ts("dve", F["t0"][:], F["k"][:], kk_c[:, 0:1], None, ALU.mult, None, [F["k"], kk_c], [F["t0"]])
                        self.tt("pool", Hb["sq"][:], F["t0"][:], F["t0"][:], ALU.mult, [F["t0"]], [Hb["sq"]])
                        p_ = nextp()
                        self.mm(p_[:, 0:SG], BOb, Hb["sq"][:], [kb, Hb["sq"]], [p_])
                        self.rsqrt_(F["t1"][:], p_[:, 0:SG], 1.0, self.epsc[:, 0:1], [p_, self.epsc], [F["t1"]])
                        self.tt("pool", F["kk"][:], F["t0"][:], F["t1"][:], ALU.mult, [F["t0"], F["t1"]], [F["kk"]])
                        yield
                        self.ts("dve", F["t0"][:], F["alr"][:], -1.0, ka_c[:, 0:1], ALU.add, ALU.mult, [F["alr"], ka_c], [F["t0"]])
                        self.stt(F["kp"][:], F["t0"][:], 1.0, F["k"][:], ALU.add, ALU.mult, [F["t0"], F["k"]], [F["kp"]])
                        self.tt("pool", F["t0"][:], F["r"][:], F["kp"][:], ALU.mult, [F["r"], F["kp"]], [F["t0"]])
                        self.ts("dve", Hb["rkr"][:], F["t0"][:], rk_c[:, 0:1], None, ALU.mult, None, [F["t0"], rk_c], [Hb["rkr"]])
                        yield
                        p_ = nextp()
                        self.mm(p_[:, 0:SG], BOb, Hb["rkr"][:], [kb, Hb["rkr"]], [p_])
                        if d == 0:
                            self.tt("dve", Bacc[:, tau0:tau0 + SG], p_[:, 0:SG], F["v"][:], ALU.mult, [p_, F["v"]], [Bacc])
                        else:
                            self.tt("dve", F["t1"][:], p_[:, 0:SG], F["v"][:], ALU.mult, [p_, F["v"]], [F["t1"]])
                            rb = rev(Bacc[:, Tn - tau0 - SG:Tn - tau0])
                            self.tt("pool", rb, rb, F["t1"][:], ALU.add, [Bacc, F["t1"]], [Bacc])
                        yield
                        self.tt("pool", PAR[:, :, 128:192], V3(F["r"][:], nch), V3(F["E1"][:], nch), ALU.mult, [F["r"], F["E1"]], [PAR])
                        self.to_bd(lambda o_ap, lo, hi: self.stt(o_ap, V3(F["kk"][lo:hi, :], nch), -1.0, V3(F["E2"][lo:hi, :], nch), ALU.mult, ALU.mult,
                                                                 [F["kk"], F["E2"]], [PAR]), PAR, nch)
                        yield
                        self.to_bd(lambda o_ap, lo, hi: self.tt("pool", o_ap, V3(F["kp"][lo:hi, :], nch), V3(F["E3"][lo:hi, :], nch), ALU.mult,
                                                                [F["kp"], F["E3"]], [BDK]), BDK, nch)
                        yield
                        self.tt("pool", F["t0"][:], F["kk"][:], F["alr"][:], ALU.mult, [F["kk"], F["alr"]], [F["t0"]])
                        self.to_bd(lambda o_ap, lo, hi: self.tt("pool", o_ap, V3(F["t0"][lo:hi, :], nch), V3(F["E3"][lo:hi, :], nch), ALU.mult,
                                                                [F["t0"], F["E3"]], [BDB]), BDB, nch)
                        yield

                    def mk_o(sg, S, d=d):
                        tau0 = sg * SG
                        PAR, BDB, BDK, BDV, G_ = S["PAR"], S["B"], S["K"], S["V"], S["g"]

                        def yout(ci, pY):
                            a0_ = tau0 + ci * 64
                            if d == 0:
                                self.cp("dve", Yacc[:, a0_:a0_ + 64], pY[:], [pY], [Yacc])
                            else:
                                ry = rev(Yacc[:, Tn - a0_ - 64:Tn - a0_])
                                self.tt("dve", ry, ry, pY[:], ALU.add, [Yacc, pY], [Yacc])
                        return dict(PAR=lambda ci: (PAR[:, ci, :], PAR), QB=lambda ci: (BDB[:, ci, :], BDB), QK=lambda ci: (BDK[:, ci, :], BDK),
                                    ZA=lambda ci: (PAR[:, ci, 0:128], PAR), ZR=lambda ci: (PAR[:, ci, 128:192], PAR),
                                    SB=lambda ci: (BDB[:, ci, :], BDB), SK=lambda ci: (BDK[:, ci, :], BDK), V=lambda ci: (BDV[:, ci, :], BDV),
                                    ms=lambda ci: (kc[:, K_MUS:K_MUS + 64], kc), mi=lambda ci: (kc[:, K_MUI:K_MUI + 64], kc),
                                    ml=lambda ci: (kc[:, K_MLS:K_MLS + 64], kc),
                                    gZ=lambda ci: (G_[:, ci:ci + 1], G_), gS=lambda ci: (G_[:, ci:ci + 1], G_), yout=yout)

                    for _ in prep(0, OS[0]):
                        pass
                    for sg in range(NSG):
                        side = prep(sg + 1, OS[(sg + 1) % 2]) if sg + 1 < NSG else None
                        self.dplr_run(sets, st, mk_o(sg, OS[sg % 2]), nch, side=side)
                with ExitStack() as e2:
                    Wz = c.sb([128, 8, 128], BF16, "Wz", e2)
                    stz = c.sb([128, 8, 128], F32, "stz", e2)
                    self.load_w(Wz, 0, self.win(l, C_Z + 256 + pr * 128, 128), 128, stz)
                    ob = [c.sb([128, SG], BF16, "rob", e2) for _ in range(2)]
                    for sg in range(NSG):
                        t0 = sg * SG
                        Y = Yacc[:, t0:t0 + SG]
                        p_ = nextp()
                        self.mm(p_[:, 0:SG], BOf, Y, [kc, Yacc], [p_])
                        self.stt(F["t0"][:], p_[:, 0:SG], -1.0 / 64, Y, ALU.mult, ALU.add, [p_, Yacc], [F["t0"]])
                        self.tt("pool", F["t1"][:], F["t0"][:], F["t0"][:], ALU.mult, [F["t0"]], [F["t1"]])
                        p_ = nextp()
                        self.mm(p_[:, 0:SG], BOf, F["t1"][:], [kc, F["t1"]], [p_])
                        self.rsqrt_(F["t1"][:], p_[:, 0:SG], 1.0 / 64, self.epsc[:, 1:2], [p_, self.epsc], [F["t1"]])
                        self.tt("pool", F["t0"][:], F["t0"][:], F["t1"][:], ALU.mult, [F["t0"], F["t1"]], [F["t0"]])
                        self.ts("dve", F["t0"][:], F["t0"][:], lnw_c[:, 0:1], lnb_c[:, 0:1], ALU.mult, ALU.add, [F["t0"], lnw_c, lnb_c], [F["t0"]])
                        self.tt("pool", F["t0"][:], F["t0"][:], Bacc[:, t0:t0 + SG], ALU.add, [F["t0"], Bacc], [F["t0"]])
                        p_ = nextp()
                        self.proj(p_, Wz, 0, 128, t0, SG)
                        self.act(F["t1"][:], p_[:, 0:SG], AF.Silu, [p_], [F["t1"]])
                        o_b = ob[sg % 2]
                        self.tt("pool", o_b[:], F["t0"][:], F["t1"][:], ALU.mult, [F["t0"], F["t1"]], [o_b])
                        r0 = 256 + pr * 128
                        self.c.dma("sp", self.mixT.h[r0:r0 + 128, t0:t0 + SG], o_b[:], o_b, reads=[o_b], writes=[self.mixT])
                    c.barrier()
            c.barrier()

    def gdn(self, l, s):
        c, nc, Tn = self.c, self.nc, self.Tn
        SG = min(256, Tn)
        nch = SG // 64
        NSG = Tn // SG
        TW = min(512, Tn)
        kc, kb = self.kc, self.kb
        BOb = kb[:, K_BO:K_BO + 128]
        BOf = kc[:, K_BO:K_BO + 128]
        w = self.w
        with ExitStack() as es:
            sets, st = self.dplr_env(es)
            Yacc = c.sb([128, Tn], F32, "Yacc", es)
            ones = c.sb([128, SG], F32, "ones", es)
            self.memset("pool", ones[:], 1.0, [ones])
            QKV = {nm: c.sb([128, Tn], BF16, "g" + nm, es) for nm in "qkv"}
            Wg = c.sb([128, 8, 16], BF16, "Wg", es)
            with ExitStack() as e1:
                stg = c.sb([128, 8, 16], F32, "stgg", e1)
                self.load_w(Wg, 0, self.win(l, C_GDB, 16), 16, stg)
                c.barrier()
            sg_t = c.sb([8, 4, 128], F32, "selg", es)
            self.ld(sg_t, sg_t[:], self.selg.h[:])
            dtb = self.col(es, w["gd_dt_bias"].h[l].rearrange("d h -> (d h)"), "dtb", 8)
            nA = self.col(es, w["gd_a_log"].h[l].rearrange("d h -> (d h)"), "nA", 8)
            self.act(nA[:], nA[:], AF.Exp, [nA], [nA])
            self.ts("pool", nA[:], nA[:], -1.0, None, ALU.mult, None, [nA], [nA])
            gnw = c.sb([128, 1], F32, "gnw", es)
            for hh in range(2):
                self.ld(gnw, gnw[hh * 64:(hh + 1) * 64, :], w["gd_norm_w"].h[l, :].rearrange("(p o) -> p o", o=1))
            F = {nm: c.sb([128, SG], F32, "f" + nm, es) for nm in ["g", "b", "G", "gc", "E1", "Es", "nbk", "t0", "Ex"]}
            g8 = c.sb([8, SG], F32, "g8", es)
            b8 = c.sb([8, SG], F32, "b8", es)
            Gs = c.sb([128, nch], F32, "Gs", es)
            gcT = c.sb([128, nch], F32, "gcT", es)
            self.memset("pool", Gs[:, 0:1], 0.0, [Gs])
            OS = []
            for i in range(2):
                S_ = {"PAR": c.sb([128, nch, 192], BF16, "gPAR", es), "g": c.sb([128, nch], F32, "ggcol", es),
                      "ZR": c.sb([128, SG], BF16, "gZR", es)}
                for nm in ["ZA", "K", "SK", "V"]:
                    S_[nm] = c.sb([128, nch, 128], BF16, "gbd" + nm, es)
                for nm in ["Ms", "Mi", "Ml"]:
                    S_[nm] = c.sb([128, nch, 64], F32, "g" + nm, es)
                for nm in ["PAR", "ZA", "K", "SK", "V"]:
                    self.memset("pool", S_[nm][:], 0.0, [S_[nm]])
                OS.append(S_)
            pp = [c.ps([128, 512], F32, "gpp", es)] * 2
            ppi = [0]

            def nextp():
                ppi[0] ^= 1
                return pp[ppi[0]]

            for pr in range(2):
                with ExitStack() as e1:
                    Wq = c.sb([128, 8, 384], BF16, "Wqkv", e1)
                    stg = c.sb([128, 8, 128], F32, "stgq", e1)
                    cw = c.sb([128, 3, 4], F32, "cw", e1)
                    for j in range(3):
                        ch0 = j * 256 + pr * 128
                        self.load_w(Wq, j * 128, self.win(l, C_QKV + ch0, 128), 128, stg)
                        self.ld(cw, cw[:, j, :], w["gd_conv_w"].h[l, :, ch0:ch0 + 128].rearrange("i p -> p i"), allow_slow_non_contiguous=True)
                    raw = c.sb([128, Tn + 3], BF16, "raw", e1)
                    self.memset("pool", raw[:, 0:1], 0.0, [raw])
                    self.memset("pool", raw[:, Tn + 1:Tn + 3], 0.0, [raw])
                    acc = c.sb([128, TW], F32, "acc", e1)
                    sl_ = c.sb([128, TW], F32, "sl", e1)
                    sqb = c.sb([128, TW], BF16, "sqb", e1)
                    rn = c.sb([128, TW], F32, "rn", e1)
                    for j, nm in enumerate("qkv"):
                        for tt_ in range(Tn // TW):
                            p_ = nextp()
                            self.proj(p_, Wq, j * 128, 128, tt_ * TW, TW)
                            self.cp(self.eng2(), raw[:, 1 + tt_ * TW:1 + (tt_ + 1) * TW], p_[:, 0:TW], [p_], [raw])
                        for tt_ in range(Tn // TW):
                            t0 = tt_ * TW
                            self.ts("dve", acc[:], raw[:, t0:t0 + TW], cw[:, j, 0:1], None, ALU.mult, None, [raw, cw], [acc])
                            for i in range(1, 4):
                                self.stt(acc[:], raw[:, t0 + i:t0 + i + TW], cw[:, j, i:i + 1], acc[:], ALU.mult, ALU.add, [raw, cw, acc], [acc])
                            if nm == "v":
                                self.act(QKV[nm][:, t0:t0 + TW], acc[:], AF.Silu, [acc], [QKV[nm]])
                            else:
                                self.act(sl_[:], acc[:], AF.Silu, [acc], [sl_])
                                self.tt("pool", sqb[:], sl_[:], sl_[:], ALU.mult, [sl_], [sqb])
                                p_ = nextp()
                                self.mm(p_[:, 0:TW], BOb, sqb[:], [kb, sqb], [p_])
                                self.rsqrt_(rn[:], p_[:, 0:TW], 1.0, self.epsc[:, 0:1], [p_, self.epsc], [rn])
                                if nm == "q":
                                    self.stt(QKV[nm][:, t0:t0 + TW], sl_[:], 0.125, rn[:], ALU.mult, ALU.mult, [sl_, rn], [QKV[nm]])
                                else:
                                    self.tt("pool", QKV[nm][:, t0:t0 + TW], sl_[:], rn[:], ALU.mult, [sl_, rn], [QKV[nm]])
                    c.barrier()
                for d in range(2):
                    self.memset("pool", st["Zf"][:], 0.0, [st["Zf"]])
                    self.memset("pool", st["Zb"][:], 0.0, [st["Zb"]])
                    def prep(sg, S, d=d, pr=pr):
                        tau0 = sg * SG
                        PAR, BDZA, BDK, BDSK, BDV, ZRt, Ms, Mi, Ml = S["PAR"], S["ZA"], S["K"], S["SK"], S["V"], S["ZR"], S["Ms"], S["Mi"], S["Ml"]

                        def sa(t):
                            return t[:, tau0:tau0 + SG] if d == 0 else rev(t[:, Tn - tau0 - SG:Tn - tau0])
                        p_ = nextp()
                        self.proj(p_, Wg, 0, 8, tau0, SG, d=d)
                        self.act(b8[:], p_[0:8, 0:SG], AF.Sigmoid, [p_], [b8])
                        p_ = nextp()
                        self.proj(p_, Wg, 8, 8, tau0, SG, d=d)
                        self.act(g8[:], p_[0:8, 0:SG], AF.Exp, [p_, dtb], [g8], bias=dtb[:, 0:1])
                        self.act(g8[:], g8[:], AF.Ln, [g8], [g8], bias=1.0)
                        self.ts("dve", g8[:], g8[:], nA[:, 0:1], None, ALU.mult, None, [g8, nA], [g8])
                        yield
                        p_ = nextp()
                        self.mm(p_[:, 0:SG], sg_t[:, d * 2 + pr, :], g8[:], [sg_t, g8], [p_])
                        self.cp("act", F["g"][:], p_[:, 0:SG], [p_], [F["g"]])
                        p_ = nextp()
                        self.mm(p_[:, 0:SG], sg_t[:, d * 2 + pr, :], b8[:], [sg_t, b8], [p_])
                        self.cp("dve", F["b"][:], p_[:, 0:SG], [p_], [F["b"]])
                        yield
                        self.c.op("dve", lambda: nc.vector.tensor_tensor_scan(out=F["G"][:], data0=ones[:], data1=F["g"][:], initial=0.0,
                                                                                op0=ALU.mult, op1=ALU.add), [ones, F["g"]], [F["G"]])
                        if nch > 1:
                            self.cp("pool", Gs[:, 1:nch], F["G"][:, 63:SG - 1:64], [F["G"]], [Gs])
                        self.tt("pool", V3(F["gc"][:], nch), V3(F["G"][:], nch), bc_last(Gs[:], 64), ALU.subtract, [F["G"], Gs], [F["gc"]])
                        yield
                        self.act(F["E1"][:], F["gc"][:], AF.Exp, [F["gc"]], [F["E1"]])
                        self.cp("pool", S["g"][:], F["E1"][:, 63:SG:64], [F["E1"]], [S["g"]])
                        self.tt("pool", V3(F["t0"][:], nch), bc_last(F["gc"][:, 63:SG:64], 64), V3(F["gc"][:], nch), ALU.subtract, [F["gc"]], [F["t0"]])
                        self.act(F["Es"][:], F["t0"][:], AF.Exp, [F["t0"]], [F["Es"]])
                        yield
                        self.tt("pool", V3(F["t0"][:], nch), V3(F["gc"][:], nch), bc_mid(kc[:, K_SEL:K_SEL + 64], nch), ALU.mult, [F["gc"], kc], [F["t0"]])
                        self.c.op("dve", lambda: nc.vector.reduce_sum(out=gcT[:], in_=V3(F["t0"][:], nch), axis=AX.X), [F["t0"]], [gcT])
                        self.tt("dve", V3(F["t0"][:], nch), V3(F["gc"][:], nch), bc_last(gcT[:], 64), ALU.subtract, [F["gc"], gcT], [F["t0"]])
                        yield
                        self.ts("dve", F["nbk"][:], F["t0"][:], 0.0, None, ALU.max, None, [F["t0"]], [F["nbk"]])
                        self.ts("dve", F["t0"][:], F["t0"][:], 0.0, None, ALU.min, None, [F["t0"]], [F["t0"]])
                        self.act(F["Ex"][:], F["t0"][:], AF.Exp, [F["t0"]], [F["Ex"]])
                        yield
                        self.tt("pool", Ms[:], V3(F["Ex"][:], nch), bc_mid(kc[:, K_MUS:K_MUS + 64], nch), ALU.mult, [F["Ex"], kc], [Ms])
                        self.tt("dve", Mi[:], V3(F["Ex"][:], nch), bc_mid(kc[:, K_MUI:K_MUI + 64], nch), ALU.mult, [F["Ex"], kc], [Mi])
                        yield
                        self.act(F["Ex"][:], F["nbk"][:], AF.Exp, [F["nbk"]], [F["Ex"]], scale=-1.0)
                        self.tt("pool", Ml[:], V3(F["Ex"][:], nch), bc_mid(kc[:, K_MLS:K_MLS + 64], nch), ALU.mult, [F["Ex"], kc], [Ml])
                        yield
                        kS, qS, vS = sa(QKV["k"]), sa(QKV["q"]), sa(QKV["v"])
                        self.stt(F["nbk"][:], kS, -1.0, F["b"][:], ALU.mult, ALU.mult, [QKV["k"], F["b"]], [F["nbk"]])
                        self.to_bd(lambda o_ap, lo, hi: self.cp("pool", o_ap, V3(F["nbk"][lo:hi, :], nch), [F["nbk"]], [PAR]), PAR, nch)
                        yield
                        self.to_bd(lambda o_ap, lo, hi: self.tt("dve", o_ap, V3(F["nbk"][lo:hi, :], nch), V3(F["E1"][lo:hi, :], nch), ALU.mult,
                                                                [F["nbk"], F["E1"]], [BDZA]), BDZA, nch)
                        self.cp("pool", PAR[:, :, 128:192], V3(qS, nch) if d == 0 else qS.rearrange("p (a b) -> p a b", a=nch), [QKV["q"]], [PAR])
                        yield
                        self.tt("pool", V3(ZRt[:], nch), PAR[:, :, 128:192], V3(F["E1"][:], nch), ALU.mult, [PAR, F["E1"]], [ZRt])
                        self.cp("act", F["t0"][:], kS, [QKV["k"]], [F["t0"]])
                        yield
                        self.to_bd(lambda o_ap, lo, hi: self.cp("pool", o_ap, V3(F["t0"][lo:hi, :], nch), [F["t0"]], [BDK]), BDK, nch)
                        self.to_bd(lambda o_ap, lo, hi: self.tt("dve", o_ap, V3(F["t0"][lo:hi, :], nch), V3(F["Es"][lo:hi, :], nch), ALU.mult,
                                                                [F["t0"], F["Es"]], [BDSK]), BDSK, nch)
                        yield
                        self.tt("dve", F["nbk"][:], vS, F["b"][:], ALU.mult, [QKV["v"], F["b"]], [F["nbk"]])
                        self.to_bd(lambda o_ap, lo, hi: self.cp("pool", o_ap, V3(F["nbk"][lo:hi, :], nch), [F["nbk"]], [BDV]), BDV, nch)
                        yield

                    def mk_o(sg, S, d=d):
                        tau0 = sg * SG
                        PAR, BDZA, BDK, BDSK, BDV, ZRt, Ms, Mi, Ml, G_ = (S["PAR"], S["ZA"], S["K"], S["SK"], S["V"], S["ZR"], S["Ms"], S["Mi"],
                                                                          S["Ml"], S["g"])

                        def yout(ci, pY):
                            a0_ = tau0 + ci * 64
                            if d == 0:
                                self.cp("dve", Yacc[:, a0_:a0_ + 64], pY[:], [pY], [Yacc])
                            else:
                                ry = rev(Yacc[:, Tn - a0_ - 64:Tn - a0_])
                                self.tt("dve", ry, ry, pY[:], ALU.add, [Yacc, pY], [Yacc])
                        return dict(PAR=lambda ci: (PAR[:, ci, :], PAR), QB=lambda ci: (BDK[:, ci, :], BDK), QK=lambda ci: (BDK[:, ci, :], BDK),
                                    ZA=lambda ci: (BDZA[:, ci, :], BDZA), ZR=lambda ci: (ZRt[:, ci * 64:ci * 64 + 64], ZRt),
                                    SB=lambda ci: (BDSK[:, ci, :], BDSK), SK=lambda ci: (BDSK[:, ci, :], BDSK), V=lambda ci: (BDV[:, ci, :], BDV),
                                    ms=lambda ci: (Ms[:, ci, :], Ms), mi=lambda ci: (Mi[:, ci, :], Mi), ml=lambda ci: (Ml[:, ci, :], Ml),
                                    gZ=lambda ci: (G_[:, ci:ci + 1], G_), gS=lambda ci: None, yout=yout, same=True)

                    for _ in prep(0, OS[0]):
                        pass
                    for sg in range(NSG):
                        side = prep(sg + 1, OS[(sg + 1) % 2]) if sg + 1 < NSG else None
                        self.dplr_run(sets, st, mk_o(sg, OS[sg % 2]), nch, side=side)
                with ExitStack() as e2:
                    Wz = c.sb([128, 8, 128], BF16, "Wz", e2)
                    stz = c.sb([128, 8, 128], F32, "stz", e2)
                    self.load_w(Wz, 0, self.win(l, C_Z + 768 + pr * 128, 128), 128, stz)
                    ob = [c.sb([128, SG], BF16, "gob", e2) for _ in range(2)]
                    for sg in range(NSG):
                        t0 = sg * SG
                        Y = Yacc[:, t0:t0 + SG]
                        self.tt("pool", F["t0"][:], Y, Y, ALU.mult, [Yacc], [F["t0"]])
                        p_ = nextp()
                        self.mm(p_[:, 0:SG], BOf, F["t0"][:], [kc, F["t0"]], [p_])
                        self.rsqrt_(F["Ex"][:], p_[:, 0:SG], 1.0 / 64, self.epsc[:, 0:1], [p_, self.epsc], [F["Ex"]])
                        self.stt(F["t0"][:], Y, gnw[:, 0:1], F["Ex"][:], ALU.mult, ALU.mult, [Yacc, gnw, F["Ex"]], [F["t0"]])
                        p_ = nextp()
                        self.proj(p_, Wz, 0, 128, t0, SG)
                        self.act(F["Ex"][:], p_[:, 0:SG], AF.Silu, [p_], [F["Ex"]])
                        o_b = ob[sg % 2]
                        self.tt("pool", o_b[:], F["t0"][:], F["Ex"][:], ALU.mult, [F["t0"], F["Ex"]], [o_b])
                        r0 = 768 + pr * 128
                        self.c.dma("sp", self.mixT.h[r0:r0 + 128, t0:t0 + SG], o_b[:], o_b, reads=[o_b], writes=[self.mixT])
                    c.barrier()
            c.barrier()

    def ssd(self, l, s):
        c, nc, Tn = self.c, self.nc, self.Tn
        L = 128
        NC_ = Tn // L
        TW = min(512, Tn)
        kc, kb = self.kc, self.kb
        IDb = kb[:, K_ID:K_ID + 128]
        w = self.w
        with ExitStack() as es:
            XC = [c.sb([128, Tn], BF16, "xc%d" % j, es) for j in range(6)]
            Yacc = [c.sb([128, Tn], F32, "Yc%d" % p, es) for p in range(2)]
            banks = [c.ps([128, 512], F32, "sbk", es) for _ in range(6)]

            def sl(b, lo, n):
                t = c.sub(banks[b][:, lo:lo + n])
                t.dep = banks[b].dep
                return t
            pY = [sl(0, 0, 128), sl(1, 0, 128)]
            pCB = [sl(2, 0, 128), sl(2, 128, 128)]
            pRep4 = c.sub(V3(banks[3][:, 0:512], 4))
            pRep4.dep = banks[3].dep
            pTb = c.sub(V3(banks[4][:, 0:256].bitcast(BF16), 4))
            pTb.dep = banks[4].dep
            pTf = sl(4, 256, 16)
            pZU4 = c.sub(V3(banks[5][:, 0:512], 4))
            pZU4.dep = banks[5].dep
            pPj = sl(4, 384, 128)
            pbig = [banks[3], banks[5]]
            with ExitStack() as e1:
                Wx = c.sb([128, 8, 768], BF16, "Wx", e1)
                stg = c.sb([128, 8, 128], F32, "stgx", e1)
                cw = c.sb([128, 6, 4], F32, "cwx", e1)
                cb = c.sb([128, 6], F32, "cbx", e1)
                self.ld(cb, cb[:], w["m_conv_b"].h[l, :].rearrange("(j p) -> p j", p=128), allow_slow_non_contiguous=True)
                for j in range(6):
                    self.load_w(Wx, j * 128, self.win(l, C_XBC + j * 128, 128), 128, stg)
                    self.ld(cw, cw[:, j, :], w["m_conv_w"].h[l, :, j * 128:(j + 1) * 128].rearrange("i p -> p i"), allow_slow_non_contiguous=True)
                raw = c.sb([128, Tn + 3], BF16, "rawx", e1)
                self.memset("pool", raw[:, 0:1], 0.0, [raw])
                self.memset("pool", raw[:, Tn + 1:Tn + 3], 0.0, [raw])
                acc = c.sb([128, TW], F32, "accx", e1)
                for j in range(6):
                    for tt_ in range(Tn // TW):
                        p_ = pbig[tt_ % 2]
                        self.proj(p_, Wx, j * 128, 128, tt_ * TW, TW)
                        self.cp(self.eng2(), raw[:, 1 + tt_ * TW:1 + (tt_ + 1) * TW], p_[:, 0:TW], [p_], [raw])
                    for tt_ in range(Tn // TW):
                        t0 = tt_ * TW
                        self.ts("dve", acc[:], raw[:, t0:t0 + TW], cw[:, j, 0:1], cb[:, j:j + 1], ALU.mult, ALU.add, [raw, cw, cb], [acc])
                        for i in range(1, 4):
                            self.stt(acc[:], raw[:, t0 + i:t0 + i + TW], cw[:, j, i:i + 1], acc[:], ALU.mult, ALU.add, [raw, cw, acc], [acc])
                        self.act(XC[j][:, t0:t0 + TW], acc[:], AF.Silu, [acc], [XC[j]])
                c.barrier()
            Wd = c.sb([128, 8, 8], BF16, "Wd", es)
            with ExitStack() as e1:
                stg = c.sb([128, 8, 8], F32, "stgd", e1)
                self.load_w(Wd, 0, self.win(l, C_MDT, 8), 8, stg)
                c.barrier()
            sm_t = c.sb([8, 8, 128], F32, "selm", es)
            self.ld(sm_t, sm_t[:], self.selm.h[:])
            dtb = self.col(es, w["m_dt_bias"].h[l].rearrange("d h -> (d h)"), "mdtb", 8)
            nA = self.col(es, w["m_a_log"].h[l].rearrange("d h -> (d h)"), "mnA", 8)
            self.act(nA[:], nA[:], AF.Exp, [nA], [nA])
            self.ts("pool", nA[:], nA[:], -1.0, None, ALU.mult, None, [nA], [nA])
            ones8 = c.sb([8, L], F32, "ones8", es)
            self.memset("pool", ones8[:], 1.0, [ones8])
            dt8 = c.sb([8, L], F32, "dt8", es)
            a8 = c.sb([8, L], F32, "a8", es)
            acs8 = c.sb([8, L], F32, "acs8", es)
            TT_ = c.sb([128, 16], F32, "TT", es)
            xsT = c.sb([128, 256], BF16, "xsT", es)
            BT = c.sb([128, 2, 128], BF16, "BT", es)
            CBs = [c.sb([128, 128], F32, "CBs", es) for _ in range(2)]
            dd = c.sb([128, 128], F32, "dd", es)
            Em = c.sb([128, 128], F32, "Em", es)
            SD = [c.sb([128, 128], BF16, "SD", es) for _ in range(4)]
            Er = c.sb([128, 128], F32, "Er", es)
            Ce = [c.sb([128, 128], BF16, "Ce", es) for _ in range(4)]
            cols = c.sb([128, 4, 4], F32, "cols", es)
            Xp = [c.sb([128, 128], BF16, "Xp%d" % i, es) for i in range(4)]
            Xe = [c.sb([128, 128], BF16, "Xe%d" % i, es) for i in range(4)]
            Zf = [c.sb([128, 128], F32, "sZf%d" % h, es) for h in range(4)]
            Zb = [c.sb([128, 128], BF16, "sZb%d" % h, es) for h in range(4)]
            for i in range(4):
                self.memset("pool", Xp[i][:], 0.0, [Xp[i]])
                self.memset("pool", Xe[i][:], 0.0, [Xe[i]])
            for d in range(2):
                for h in range(4):
                    self.memset("pool", Zf[h][:], 0.0, [Zf[h]])
                    self.memset("pool", Zb[h][:], 0.0, [Zb[h]])
                mask = kc[:, K_MU128:K_MU128 + 128] if d == 0 else kc[:, K_ML128:K_ML128 + 128]
                last = L - 1 if d == 0 else 0
                order = range(NC_) if d == 0 else range(NC_ - 1, -1, -1)
                for ci in order:
                    t0 = ci * L
                    tk = slice(t0, t0 + L)
                    self.proj(pPj, Wd, 0, 8, t0, L)
                    self.act(dt8[:], pPj[0:8, 0:L], AF.Exp, [pPj, dtb], [dt8], bias=dtb[:, 0:1])
                    self.act(dt8[:], dt8[:], AF.Ln, [dt8], [dt8], bias=1.0)
                    self.ts("dve", a8[:], dt8[:], nA[:, 0:1], None, ALU.mult, None, [dt8, nA], [a8])
                    if d == 0:
                        self.c.op("dve", lambda: nc.vector.tensor_tensor_scan(out=acs8[:], data0=ones8[:], data1=a8[:], initial=0.0,
                                                                                op0=ALU.mult, op1=ALU.add), [ones8, a8], [acs8])
                    else:
                        self.c.op("dve", lambda: nc.vector.tensor_tensor_scan(out=rev(acs8[:]), data0=ones8[:], data1=rev(a8[:]), initial=0.0,
                                                                                op0=ALU.mult, op1=ALU.add), [ones8, a8], [acs8])
                    self.mm(pTf[:, 0:8], dt8[:], kc[0:8, K_ID:K_ID + 8], [dt8, kc], [pTf])
                    self.mm(pTf[:, 8:16], acs8[:], kc[0:8, K_ID:K_ID + 8], [acs8, kc], [pTf])
                    self.cp("act", TT_[:], pTf[:], [pTf], [TT_])
                    for g in range(2):
                        self.mm(pCB[g][:], XC[2 + g][:, tk], XC[4 + g][:, tk], [XC[2 + g], XC[4 + g]], [pCB[g]])
                        self.cp("act", CBs[g][:], pCB[g][:], [pCB[g]], [CBs[g]])
                    for j in range(2):
                        self.tr(pTb[:, j, :], XC[j][:, tk], IDb, [XC[j], kb], [pTb])
                        self.tr(pTb[:, 2 + j, :], XC[2 + j][:, tk], IDb, [XC[2 + j], kb], [pTb])
                    self.cp("dve", V3(xsT[:], 2), pTb[:, 0:2, :], [pTb], [xsT])
                    self.cp("act", BT[:], pTb[:, 2:4, :], [pTb], [BT])
                    for h in range(4):
                        self.mm(pRep4[:, h, :], sm_t[:, d * 4 + h, :], acs8[:], [sm_t, acs8], [pRep4])
                    for h in range(4):
                        g, pr, hh = h // 2, h // 2, h % 2
                        dh = d * 4 + h
                        self.ts("dve", dd[:], pRep4[:, h, :], TT_[:, 8 + dh:9 + dh], 0.0, ALU.subtract, ALU.min, [pRep4, TT_], [dd])
                        self.act(Em[:], dd[:], AF.Exp, [dd], [Em])
                        self.tt("pool", Em[:], Em[:], mask, ALU.mult, [Em, kc], [Em])
                        self.tt("pool", SD[h][:], CBs[g][:], Em[:], ALU.mult, [CBs[g], Em], [SD[h]])
                        self.act(Er[:], pRep4[:, h, :], AF.Exp, [pRep4], [Er])
                        self.tt("pool", Ce[h][:], XC[4 + g][:, tk], Er[:], ALU.mult, [XC[4 + g], Er], [Ce[h]])
                        self.cp("dve", cols[:, h, 0:1], pRep4[:, h, last:last + 1], [pRep4], [cols])
                        self.act(cols[:, h, 1:2], TT_[:, 8 + dh:9 + dh], AF.Exp, [TT_, cols], [cols], scale=-1.0, bias=cols[:, h, 0:1])
                        self.act(cols[:, h, 2:3], cols[:, h, 0:1], AF.Exp, [cols], [cols])
                        xs_h = xsT[:, h * 64:(h + 1) * 64]
                        self.ts("dve", Xp[h][:, hh * 64:(hh + 1) * 64], xs_h, TT_[:, dh:dh + 1], None, ALU.mult, None, [xsT, TT_], [Xp[h]])
                        self.ts("dve", Xe[h][:, hh * 64:(hh + 1) * 64], Xp[h][:, hh * 64:(hh + 1) * 64], cols[:, h, 1:2], None, ALU.mult, None,
                                [Xp[h], cols], [Xe[h]])
                    for h in range(4):
                        g, pr, hh = h // 2, h // 2, h % 2
                        self.mm(pY[pr][:], Xp[h][:], SD[h][:], [Xp[h], SD[h]], [pY[pr]], start=(hh == 0), stop=False)
                        self.mm(pY[pr][:], Zb[h][:], Ce[h][:], [Zb[h], Ce[h]], [pY[pr]], start=False, stop=(hh == 1))
                    for h in range(4):
                        self.mm(pZU4[:, h, :], BT[:, h // 2, :], Xe[h][:], [BT, Xe[h]], [pZU4])
                    for pr in range(2):
                        if d == 0:
                            self.cp("act", Yacc[pr][:, tk], pY[pr][:], [pY[pr]], [Yacc[pr]])
                        else:
                            self.tt("dve", Yacc[pr][:, tk], Yacc[pr][:, tk], pY[pr][:], ALU.add, [Yacc[pr], pY[pr]], [Yacc[pr]])
                    for h in range(4):
                        self.stt(Zf[h][:], Zf[h][:], cols[:, h, 2:3], pZU4[:, h, :], ALU.mult, ALU.add, [Zf[h], cols, pZU4], [Zf[h]])
                        self.cp("act", Zb[h][:], Zf[h][:], [Zf[h]], [Zb[h]])
            with ExitStack() as e2:
                Wz = c.sb([128, 8, 256], BF16, "Wzc", e2)
                stz = c.sb([128, 8, 256], F32, "stzc", e2)
                self.load_w(Wz, 0, self.win(l, C_Z + 512, 256), 256, stz)
                Dc = c.sb([128, 2], F32, "Dc", e2)
                nw = c.sb([128, 2], F32, "nw", e2)
                for pr in range(2):
                    for hh in range(2):
                        self.ld(Dc, Dc[hh * 64:(hh + 1) * 64, pr:pr + 1], w["m_d"].h[l, pr * 2 + hh:pr * 2 + hh + 1].partition_broadcast(64))
                self.ld(nw, nw[:], w["m_norm_w"].h[l, :].rearrange("(j p) -> p j", p=128), allow_slow_non_contiguous=True)
                gt = [c.sb([128, TW], F32, "gt%d" % p, e2) for p in range(2)]
                sq = c.sb([128, TW], F32, "sqc", e2)
                sz = c.sb([128, TW], F32, "szc", e2)
                rs = c.sb([128, TW], F32, "rsc", e2)
                ob = [c.sb([128, 2, TW], BF16, "cob", e2) for _ in range(2)]
                for tt_ in range(Tn // TW):
                    t0 = tt_ * TW
                    for pr in range(2):
                        self.stt(gt[pr][:], XC[pr][:, t0:t0 + TW], Dc[:, pr:pr + 1], Yacc[pr][:, t0:t0 + TW], ALU.mult, ALU.add,
                                 [XC[pr], Dc, Yacc[pr]], [gt[pr]])
                        p_ = banks[2 + pr]
                        self.proj(p_, Wz, pr * 128, 128, t0, TW)
                        self.act(sz[:], p_[:, 0:TW], AF.Silu, [p_], [sz])
                        self.tt("pool", gt[pr][:], gt[pr][:], sz[:], ALU.mult, [gt[pr], sz], [gt[pr]])
                        self.tt("pool", sq[:], gt[pr][:], gt[pr][:], ALU.mult, [gt[pr]], [sq])
                        self.mm(banks[0][:, 0:TW], kc[:, K_ONE:K_ONE + 128], sq[:], [kc, sq], [banks[0]], start=(pr == 0), stop=(pr == 1))
                    self.rsqrt_(rs[:], banks[0][:, 0:TW], 1.0 / 256, self.epsc[:, 0:1], [banks[0], self.epsc], [rs])
                    o_b = ob[tt_ % 2]
                    for pr in range(2):
                        self.stt(o_b[:, pr, :], gt[pr][:], nw[:, pr:pr + 1], rs[:], ALU.mult, ALU.mult, [gt[pr], nw, rs], [o_b])
                    self.c.dma("sp", self.mixT.h[512:768, t0:t0 + TW].rearrange("(a p) t -> p a t", p=128), o_b[:], o_b, reads=[o_b], writes=[self.mixT])
                c.barrier()
            c.barrier()

    def zero_mix(self, i):
        c, Tn = self.c, self.Tn
        TW = min(512, Tn)
        with ExitStack() as es:
            z = c.sb([128, 2, TW], BF16, "zmix", es)
            self.memset("pool", z[:], 0.0, [z])
            for j in range(Tn // TW):
                self.c.dma("sp", self.mixT.h[i * 256:(i + 1) * 256, j * TW:(j + 1) * TW].rearrange("(a p) t -> p a t", p=128), z[:], z,
                           reads=[z], writes=[self.mixT])
            c.barrier()

    def build(self):
        self.declare()
        self.consts()
        for l in range(self.DEPTH):
            self.p0(l)
            for s in range(self.NS):
                self.p1(l, s)
                if "A" in self.mixers:
                    self.mla(l, s)
                if "B" in self.mixers:
                    self.rwkv(l, s)
                if "C" in self.mixers:
                    self.ssd(l, s)
                if "D" in self.mixers:
                    self.gdn(l, s)
                for i, m in enumerate("ABCD"):
                    if m not in self.mixers:
                        self.zero_mix(i)
                self.p6(l, s)
        self.c.finish([self.y] + ([self.mixT] if self.dbg else []))
        return self.nc


def make_consts(Tn):
    p = np.arange(128)
    kc = np.zeros((128, K_END), np.float32)
    kc[:, K_ID:K_ID + 128] = np.eye(128)
    col = np.arange(64)
    kc[:, K_MUS:K_MUS + 64] = ((p % 64)[:, None] < col[None, :])
    kc[:, K_MUI:K_MUI + 64] = ((p % 64)[:, None] <= col[None, :])
    kc[:, K_BO:K_BO + 128] = ((p // 64)[:, None] == (p // 64)[None, :])
    kc[:, K_ONE:K_ONE + 128] = 1.0
    kc[:, K_SEL:K_SEL + 64] = ((p % 64)[:, None] == col[None, :])
    kc[:, K_MU128:K_MU128 + 128] = (p[:, None] <= p[None, :])
    kc[:, K_ML128:K_ML128 + 128] = (p[:, None] >= p[None, :])
    kc[:, K_MLS:K_MLS + 64] = ((p % 64)[:, None] > col[None, :])
    inv = 1.0 / (10000.0 ** (np.arange(0, 32, 2, dtype=np.float32) / np.float32(32)))
    ang = np.arange(Tn, dtype=np.float32)[None, :] * inv.astype(np.float32)[:, None]
    rope = np.zeros((96, 2, Tn), np.float32)
    rope[0:64, 0, :] = 1.0
    rope[64:80, 0, :] = np.cos(ang)
    rope[80:96, 0, :] = np.cos(ang)
    rope[64:80, 1, :] = np.sin(ang)
    rope[80:96, 1, :] = np.sin(ang)
    selg = np.zeros((8, 4, 128), np.float32)
    for d in range(2):
        for pr in range(2):
            for hh in range(2):
                selg[d * 4 + pr * 2 + hh, d * 2 + pr, hh * 64:(hh + 1) * 64] = 1.0
    selm = np.zeros((8, 8, 128), np.float32)
    for r in range(8):
        selm[r, r, :] = 1.0
    return dict(kconst=kc, rope=rope, selg=selg, selm=selm)


WNAMES = ["w_ada", "b_ada", "g_pre", "g_post", "w_in", "w_out", "a_g_q", "a_w_uq", "a_g_kv", "a_w_ukv", "rw_mu", "rw_w0",
          "rw_w2", "rw_a0", "rw_a2", "rw_k_k", "rw_k_a", "rw_r_k", "rw_ln_w", "rw_ln_b", "m_conv_w", "m_conv_b",
          "m_dt_bias", "m_a_log", "m_d", "m_norm_w", "gd_conv_w", "gd_dt_bias", "gd_a_log", "gd_norm_w"]


def kernel(x_prompt, x_sample, c_prompt, c_sample, **weights):
    x_prompt = np.asarray(x_prompt, np.float32)
    x_sample = np.asarray(x_sample, np.float32)
    Tn = x_prompt.shape[1]
    DEPTH = np.asarray(weights["w_ada"]).shape[0]
    xs = np.concatenate([x_prompt, x_sample], axis=0)
    cs = np.concatenate([np.asarray(c_prompt, np.float32), np.asarray(c_sample, np.float32)], axis=0)
    nseq = xs.shape[0]
    NS = -(-nseq // N_CORES)
    b = Builder(Tn, NS, DEPTH)
    nc = b.build()
    consts = make_consts(Tn)
    wmap = {k: np.ascontiguousarray(np.asarray(weights[k], np.float32)) for k in WNAMES}
    in_maps = []
    for core in range(N_CORES):
        idx = [(core * NS + j) if (core * NS + j) < nseq else 0 for j in range(NS)]
        m = dict(wmap)
        m.update(consts)
        m["x"] = np.ascontiguousarray(xs[idx])
        m["c"] = np.ascontiguousarray(cs[idx])
        in_maps.append(m)
    res = run_bass_kernel_spmd(nc, in_maps, core_ids=list(range(N_CORES)))
    out = np.zeros_like(xs)
    for core in range(N_CORES):
        yc = res.results[core]["y"]
        for j in range(NS):
            sidx = core * NS + j
            if sidx < nseq:
                out[sidx] = yc[j]
    nb = x_prompt.shape[0]
    return (out[:nb], out[nb:])
```

```python
import math
import numpy as np
import concourse.bass as bass
import concourse.mybir as mybir
from concourse.bass_utils import run_bass_kernel_spmd
from contextlib import ExitStack

F32 = mybir.dt.float32
BF16 = mybir.dt.bfloat16
F32R = mybir.dt.float32r
AF = mybir.ActivationFunctionType
ALU = mybir.AluOpType
AX = mybir.AxisListType

D_MODEL = 1024
P_IN = 4024
N_CORES = 8
EPS = 1e-6
EMBED_WAITS = 1
C_CQ, C_CKV, C_KR, C_RKV, C_WAF, C_WAB, C_XBC, C_MDT, C_QKV, C_GDB, C_GDA, C_Z = (
    0, 384, 512, 544, 1312, 1376, 1440, 2208, 2216, 2984, 2992, 3000)
K_ID, K_MUS, K_MUI, K_BO, K_ONE, K_SEL, K_MU128, K_ML128, K_MLS, K_END = 0, 128, 192, 256, 384, 512, 576, 704, 832, 896


class Dep:
    __slots__ = ("w", "r", "name", "excl")

    def __init__(self, name=""):
        self.w = {}
        self.r = {}
        self.name = name
        self.excl = False


class T:
    def __init__(self, h, name, dep=None):
        self.h = h
        self.name = name
        self.dep = dep or Dep(name)
        self.dsem = None
        self.dcount = 0

    def __getitem__(self, k):
        return self.h[k]


class Ctx:
    def __init__(self, nc):
        self.nc = nc
        self.E = {"pe": nc.tensor, "act": nc.scalar, "dve": nc.vector, "pool": nc.gpsimd, "sp": nc.sync}
        self.sem, self.cnt, self.known = {}, {}, {}
        self.stack = ExitStack()
        for e in self.E:
            self.sem[e] = self.stack.enter_context(nc.semaphore("s_" + e))
            self.cnt[e] = 0
            self.known[e] = {}
        self.uid = 0
        self.ninst = 0
        self.dtiles = []
        self.free_dsems = []
        self.all_dsems = []
        self.pending = None

    def sb(self, shape, dt=F32, name=None, stack=None):
        self.uid += 1
        name = (name or "t") + "_%d" % self.uid
        h = (stack or self.stack).enter_context(self.nc.sbuf_tensor(name, list(shape), dt))
        return T(h, name)

    def ps(self, shape, dt=F32, name=None, stack=None):
        self.uid += 1
        name = (name or "p") + "_%d" % self.uid
        esz = 4 if dt == F32 else 2
        n = 1
        for v in shape[1:]:
            n *= v
        per_bank = 2048 // esz
        nb = -(-n // per_bank)
        h = (stack or self.stack).enter_context(self.nc.psum_tensor(name, [128, nb * per_bank], dt))
        v = h[0:shape[0], 0:n]
        if len(shape) == 3:
            v = v.rearrange("p (a b) -> p a b", a=shape[1])
        t = T(v, name)
        t.dep.excl = True
        return t

    def sub(self, ap, name="s"):
        self.uid += 1
        return T(ap, name + "_%d" % self.uid)

    def dram(self, name, shape, dt=F32, kind="Internal"):
        return T(self.nc.dram_tensor(name, list(shape), dt, kind=kind), name)

    def _wait(self, eng, sem, val):
        k = self.known[eng]
        sid = id(sem)
        if k.get(sid, 0) >= val:
            return
        k[sid] = val
        if self.pending is not None:
            self.pending.append((sem, val))
            return
        self.E[eng].wait_ge(sem, val)
        self.ninst += 1

    def _acquire(self, eng, reads, writes):
        own = id(self.sem[eng])
        for d in reads:
            for sid, (sem, val) in d.dep.w.items():
                self._wait(eng, sem, val)
            if d.dep.excl:
                for sid, (sem, val) in d.dep.r.items():
                    if sid != own:
                        self._wait(eng, sem, val)
        for d in writes:
            for sid, (sem, val) in d.dep.w.items():
                if sid != own or eng != "pe":
                    self._wait(eng, sem, val)
            for sid, (sem, val) in d.dep.r.items():
                self._wait(eng, sem, val)

    def _flush(self, eng, ins=None):
        p, self.pending = self.pending, None
        last = []
        if ins is not None and EMBED_WAITS:
            while p and len(last) < EMBED_WAITS:
                last.append(p.pop())
        for (sem, val) in p:
            self.E[eng].wait_ge(sem, val)
            self.ninst += 1
        return last

    def _record(self, sem, val, reads, writes):
        sid = id(sem)
        for d in reads:
            d.dep.r[sid] = (sem, val)
        for d in writes:
            d.dep.w[sid] = (sem, val)

    def op(self, eng, fn, reads=(), writes=()):
        self.pending = []
        self._acquire(eng, reads, writes)
        last = self._flush(eng, True)
        ins = fn()
        for (sem_, val_) in last:
            ins._wait_ge(sem_, val_)
        self.cnt[eng] += 1
        ins.then_inc(self.sem[eng], 1)
        self.ninst += 1
        self._record(self.sem[eng], self.cnt[eng], reads, writes)
        return ins

    def dma(self, q, out_ap, in_ap, sbt, reads=(), writes=(), **kw):
        ds = sbt.dsem
        if ds is None:
            if self.free_dsems:
                ds = self.free_dsems.pop()
            else:
                ds = [self.stack.enter_context(self.nc.semaphore("d%d" % len(self.all_dsems))), 0]
                self.all_dsems.append(ds)
            sbt.dsem = ds
            self.dtiles.append(sbt)
        if ds[1]:
            self._wait(q, ds[0], 16 * ds[1])
        self._acquire(q, reads, writes)
        ins = self.E[q].dma_start(out=out_ap, in_=in_ap, **kw)
        ds[1] += 1
        ins.then_inc(ds[0], 16)
        self.ninst += 1
        self._record(ds[0], 16 * ds[1], reads, writes)
        return ins

    def finish(self, out_tiles):
        for t in out_tiles:
            for sid, (sem, val) in t.dep.w.items():
                self._wait("sp", sem, val)

    def barrier(self):
        for e in self.E:
            for f in self.E:
                if self.cnt[f]:
                    self._wait(e, self.sem[f], self.cnt[f])
            for ds in self.all_dsems:
                if ds[1]:
                    self._wait(e, ds[0], 16 * ds[1])
        for t in self.dtiles:
            self.free_dsems.append(t.dsem)
            t.dsem = None
        self.dtiles = []


def V3(ap, a):
    return ap.rearrange("p (a b) -> p a b", a=a)


def bc_mid(ap2, n):
    return ap2.unsqueeze(1).to_broadcast([ap2.shape[0], n, ap2.shape[1]])


def bc_last(ap2, n):
    return ap2.unsqueeze(2).to_broadcast([ap2.shape[0], ap2.shape[1], n])


def rev(ap):
    pat = [list(x) for x in ap.ap]
    st, n = pat[-1]
    pat[-1] = [-st, n]
    return bass.AP(ap.tensor, ap.offset + st * (n - 1), pat)


class Builder:
    def __init__(self, Tn, NS, DEPTH, mixers="ABCD", dbg=False):
        self.Tn, self.NS, self.DEPTH, self.mixers, self.dbg = Tn, NS, DEPTH, mixers, dbg
        self.nc = bass.Bass("TRN2", target_bir_lowering=False)
        self.c = Ctx(self.nc)
        self.rr = 0

    def eng2(self):
        self.rr ^= 1
        return "dve" if self.rr else "act"

    def cp(self, eng, out, in_, R, W):
        nc = self.nc
        if eng == "act":
            return self.c.op("act", lambda: nc.scalar.copy(out=out, in_=in_), R, W)
        e = nc.vector if eng == "dve" else nc.gpsimd
        return self.c.op(eng, lambda: e.tensor_copy(out=out, in_=in_), R, W)

    def tt(self, eng, out, a, b, op, R, W):
        e = self.nc.vector if eng == "dve" else self.nc.gpsimd
        return self.c.op(eng, lambda: e.tensor_tensor(out=out, in0=a, in1=b, op=op), R, W)

    def ts(self, eng, out, a, s1, s2, op0, op1, R, W):
        e = self.nc.vector if eng == "dve" else self.nc.gpsimd
        if s2 is None:
            return self.c.op(eng, lambda: e.tensor_scalar(out=out, in0=a, scalar1=s1, scalar2=None, op0=op0), R, W)
        return self.c.op(eng, lambda: e.tensor_scalar(out=out, in0=a, scalar1=s1, scalar2=s2, op0=op0, op1=op1), R, W)

    def stt(self, out, a, s, b, op0, op1, R, W):
        nc = self.nc
        return self.c.op("dve", lambda: nc.vector.scalar_tensor_tensor(out=out, in0=a, scalar=s, in1=b, op0=op0, op1=op1), R, W)

    def act(self, out, in_, func, R, W, **kw):
        nc = self.nc
        return self.c.op("act", lambda: nc.scalar.activation(out=out, in_=in_, func=func, **kw), R, W)

    def mm(self, out, lhsT, rhs, R, W, start=True, stop=True, skip=False):
        nc = self.nc
        if skip:
            return self.c.op("pe", lambda: nc.tensor.matmul(out, lhsT=lhsT, rhs=rhs, start=False, stop=False, skip_group_check=True), R, W)
        return self.c.op("pe", lambda: nc.tensor.matmul(out, lhsT=lhsT, rhs=rhs, start=start, stop=stop), R, W)

    def tr(self, out, in_, ident, R, W):
        nc = self.nc
        return self.c.op("pe", lambda: nc.tensor.transpose(out=out, in_=in_, identity=ident), R, W)

    def memset(self, eng, ap, val, W):
        e = self.nc.vector if eng == "dve" else self.nc.gpsimd
        return self.c.op(eng, lambda: e.memset(ap, val), (), W)

    def ld(self, dst_tile, dst_ap, src_ap, R=(), **kw):
        return self.c.dma("sp", dst_ap, src_ap, dst_tile, reads=R, writes=[dst_tile], **kw)

    def rsqrt_(self, out, in_, scale, eps_ap, R, W):
        self.act(out, in_, AF.Ln, R, W, scale=scale, bias=eps_ap)
        self.act(out, out, AF.Exp, W, W, scale=-0.5)

    def load_w(self, dst, dcol, src3, n, st, neg=False, eng=None):
        KC = src3.shape[1]
        self.ld(st, st[:, 0:KC, 0:n], src3)
        eng = eng or self.eng2()
        if neg:
            nc = self.nc
            self.c.op("act", lambda: nc.scalar.mul(out=dst[:, 0:KC, dcol:dcol + n], in_=st[:, 0:KC, 0:n], mul=-1.0), [st], [dst])
        else:
            self.cp(eng, dst[:, 0:KC, dcol:dcol + n], st[:, 0:KC, 0:n], [st], [dst])

    def win(self, l, c0, n):
        return self.w["w_in"].h[l, :, c0:c0 + n].rearrange("(kc p) n -> p kc n", p=128)

    def hT_ap(self, kc, tau0, n, d=0, shift=0):
        Tn = self.Tn
        if d == 0:
            col = 1 + tau0 - shift
            return self.hT[:, kc, col:col + n]
        lo = Tn - tau0 - n + 1 + shift
        return rev(self.hT[:, kc, lo:lo + n])

    def proj(self, out_ps, Wt, wc, M, tau0, n, R_extra=(), d=0, W2=None, w2c=0):
        steps = [(Wt, wc, 0)] + ([(W2, w2c, 1)] if W2 is not None else [])
        tot = 8 * len(steps)
        i = 0
        for (Wx, cc, sh) in steps:
            for kc in range(8):
                self.mm(out_ps[0:M, 0:n], Wx[:, kc, cc:cc + M], self.hT_ap(kc, tau0, n, d, sh),
                        [Wx, self.hT_t], [out_ps], start=(i == 0), stop=(i == tot - 1))
                i += 1

    def declare(self):
        c, Tn, NS, DEPTH = self.c, self.Tn, self.NS, self.DEPTH
        shapes = dict(
            w_ada=[DEPTH, 1024, 3072], b_ada=[DEPTH, 3072], g_pre=[DEPTH, 1024], g_post=[DEPTH, 1024],
            w_in=[DEPTH, 1024, P_IN], w_out=[DEPTH, 1024, 1024], a_g_q=[DEPTH, 384], a_w_uq=[DEPTH, 384, 384],
            a_g_kv=[DEPTH, 128], a_w_ukv=[DEPTH, 128, 512], rw_mu=[DEPTH, 2, 832], rw_w0=[DEPTH, 2, 256],
            rw_w2=[DEPTH, 2, 32, 256], rw_a0=[DEPTH, 2, 256], rw_a2=[DEPTH, 2, 32, 256], rw_k_k=[DEPTH, 256],
            rw_k_a=[DEPTH, 256], rw_r_k=[DEPTH, 4, 64], rw_ln_w=[DEPTH, 256], rw_ln_b=[DEPTH, 256],
            m_conv_w=[DEPTH, 4, 768], m_conv_b=[DEPTH, 768], m_dt_bias=[DEPTH, 2, 4], m_a_log=[DEPTH, 2, 4],
            m_d=[DEPTH, 4], m_norm_w=[DEPTH, 256], gd_conv_w=[DEPTH, 4, 768], gd_dt_bias=[DEPTH, 2, 4],
            gd_a_log=[DEPTH, 2, 4], gd_norm_w=[DEPTH, 64])
        self.w = {k: c.dram(k, v, F32, kind="ExternalInput") for k, v in shapes.items()}
        self.x = c.dram("x", [NS, Tn, 1024], F32, kind="ExternalInput")
        self.cin = c.dram("c", [NS, 1024], F32, kind="ExternalInput")
        self.kconst = c.dram("kconst", [128, K_END], F32, kind="ExternalInput")
        self.rope = c.dram("rope", [96, 2, Tn], F32, kind="ExternalInput")
        self.selg = c.dram("selg", [8, 4, 128], F32, kind="ExternalInput")
        self.selm = c.dram("selm", [8, 8, 128], F32, kind="ExternalInput")
        self.y = c.dram("y", [NS, Tn, 1024], F32, kind="ExternalOutput")
        self.mixT = c.dram("mixT", [1024, Tn], BF16, kind="ExternalOutput" if self.dbg else "Internal")

    def consts(self):
        c = self.c
        self.kc = c.sb([128, K_END], F32, "kc")
        self.ld(self.kc, self.kc[:], self.kconst.h[:])
        self.kb = c.sb([128, K_END], BF16, "kb")
        self.cp("dve", self.kb[:], self.kc[:], [self.kc], [self.kb])
        self.epsc = c.sb([128, 2], F32, "epsc")
        self.memset("pool", self.epsc[:, 0:1], EPS, [self.epsc])
        self.memset("pool", self.epsc[:, 1:2], 64e-5, [self.epsc])
        self.hT_t = c.sb([128, 8, self.Tn + 2], BF16, "hT")
        self.hT = self.hT_t
        self.memset("pool", self.hT[:, :, 0:1], 0.0, [self.hT_t])
        self.memset("pool", self.hT[:, :, self.Tn + 1:self.Tn + 2], 0.0, [self.hT_t])
        self.gg = [c.sb([128, 1024], F32, "gg%d" % s) for s in range(self.NS)]
        self.a0 = c.sb([128, 8, self.NS], F32, "a0")
        self.a1 = c.sb([128, 8, self.NS], F32, "a1")

    def p0(self, l):
        c, nc, NS = self.c, self.nc, self.NS
        with ExitStack() as es:
            cT = c.sb([128, 8, NS], F32, "cT", es)
            for s in range(NS):
                self.ld(cT, cT[:, :, s], self.cin.h[s, :].rearrange("(kc p) -> p kc", p=128), allow_slow_non_contiguous=True)
            scb = c.sb([128, 8, NS], BF16, "scb", es)
            self.act(scb[:], cT[:], AF.Silu, [cT], [scb])
            wada = c.sb([128, 8, 3072], BF16, "wada", es)
            st = c.sb([128, 8, 512], F32, "st0", es)
            for j in range(6):
                self.load_w(wada, j * 512, self.w["w_ada"].h[l, :, j * 512:(j + 1) * 512].rearrange("(kc p) n -> p kc n", p=128), 512, st)
            bT = c.sb([128, 24], F32, "bT", es)
            for j3 in range(3):
                self.ld(bT, bT[:, j3 * 8:(j3 + 1) * 8], self.w["b_ada"].h[l, j3 * 1024:(j3 + 1) * 1024].rearrange("(j p) -> p j", p=128),
                        allow_slow_non_contiguous=True)
            gp = c.sb([128, 8], F32, "gp", es)
            self.ld(gp, gp[:], self.w["g_pre"].h[l, :].rearrange("(j p) -> p j", p=128), allow_slow_non_contiguous=True)
            pm = c.ps([128, 16 * NS], F32, "pm", es)
            for idx in range(16):
                for kc in range(8):
                    self.mm(pm[:, idx * NS:(idx + 1) * NS], wada[:, kc, idx * 128:(idx + 1) * 128], scb[:, kc, :],
                            [wada, scb], [pm], start=(kc == 0), stop=(kc == 7))
            pm3 = V3(pm[:], 16)
            self.tt("dve", self.a0[:], pm3[:, 0:8, :], bc_last(bT[:, 0:8], NS), ALU.add, [pm, bT], [self.a0])
            tmp = c.sb([128, 8, NS], F32, "tmp0", es)
            self.tt("dve", tmp[:], pm3[:, 8:16, :], bc_last(bT[:, 8:16], NS), ALU.add, [pm, bT], [tmp])
            self.ts("pool", tmp[:], tmp[:], 1.0, None, ALU.add, None, [tmp], [tmp])
            self.tt("pool", self.a1[:], tmp[:], bc_last(gp[:], NS), ALU.mult, [tmp, gp], [self.a1])
            bg = c.sb([128, 1024], F32, "bg", es)
            self.ld(bg, bg[:], self.w["b_ada"].h[l, 2048:3072].partition_broadcast(128))
            gpo = c.sb([128, 1024], F32, "gpo", es)
            self.ld(gpo, gpo[:], self.w["g_post"].h[l, :].partition_broadcast(128))
            for s in range(NS):
                pg = c.ps([128, 1024], F32, "pg", es) if s == 0 else pg
                for n in range(2):
                    for kc in range(8):
                        self.mm(pg[:, n * 512:(n + 1) * 512], scb[:, kc, s:s + 1].to_broadcast([128, 128]),
                                wada[:, kc, 2048 + n * 512:2048 + (n + 1) * 512], [scb, wada], [pg], start=(kc == 0), stop=(kc == 7))
                self.tt("dve", self.gg[s][:], pg[:], bg[:], ALU.add, [pg, bg], [self.gg[s]])
                self.tt("pool", self.gg[s][:], self.gg[s][:], gpo[:], ALU.mult, [self.gg[s], gpo], [self.gg[s]])
            c.barrier()

    def xsrc(self, l, s):
        return (self.x if l == 0 else self.y), s

    def p1(self, l, s):
        c, nc, Tn = self.c, self.nc, self.Tn
        src, _ = self.xsrc(l, s)
        with ExitStack() as es:
            xt = [c.sb([128, 1024], F32, "xt", es) for _ in range(2)]
            xn = [c.sb([128, 1024], BF16, "xn", es) for _ in range(2)]
            junk = c.sb([128, 1024], BF16, "junk", es)
            ss = [c.sb([128, 2], F32, "ss", es) for _ in range(2)]
            pt = [c.ps([128, 1024], BF16, "pt", es) for _ in range(2)]
            tm = [c.sb([128, 8, 128], F32, "tm", es) for _ in range(2)]
            for tt_ in range(Tn // 128):
                b = tt_ % 2
                self.ld(xt[b], xt[b][:], src.h[s, tt_ * 128:(tt_ + 1) * 128, :], R=[src])
                self.act(junk[:], xt[b][:], AF.Square, [xt[b]], [junk, ss[b]], accum_out=ss[b][:, 0:1])
                self.rsqrt_(ss[b][:, 1:2], ss[b][:, 0:1], 1.0 / 1024, self.epsc[:, 0:1], [ss[b], self.epsc], [ss[b]])
                self.ts("dve", xn[b][:], xt[b][:], ss[b][:, 1:2], None, ALU.mult, None, [xt[b], ss[b]], [xn[b]])
                for fc in range(8):
                    self.tr(pt[b][:, fc * 128:(fc + 1) * 128], xn[b][:, fc * 128:(fc + 1) * 128], self.kb[:, K_ID:K_ID + 128],
                            [xn[b], self.kb], [pt[b]])
                self.tt("dve", tm[b][:], V3(pt[b][:], 8), bc_last(self.a1[:, :, s], 128), ALU.mult, [pt[b], self.a1], [tm[b]])
                self.tt("pool", self.hT[:, :, 1 + tt_ * 128:1 + (tt_ + 1) * 128], tm[b][:], bc_last(self.a0[:, :, s], 128), ALU.add,
                        [tm[b], self.a0], [self.hT_t])
            c.barrier()

    def p6(self, l, s):
        c, nc, Tn = self.c, self.nc, self.Tn
        src, _ = self.xsrc(l, s)
        TW = min(512, Tn)
        with ExitStack() as es:
            wo = c.sb([128, 8, 1024], BF16, "wo", es)
            st = c.sb([128, 8, 512], F32, "st6", es)
            for j in range(2):
                self.load_w(wo, j * 512, self.w["w_out"].h[l, :, j * 512:(j + 1) * 512].rearrange("(kc p) n -> p kc n", p=128), 512, st)
            mt = [c.sb([128, 8, TW], BF16, "mt", es) for _ in range(2)]
            xt = [c.sb([128, 1024], F32, "xt6", es) for _ in range(2)]
            yt = [c.sb([128, 1024], F32, "yt6", es) for _ in range(2)]
            junk = c.sb([128, 1024], BF16, "junk6", es)
            ss = [c.sb([128, 2], F32, "ss6", es) for _ in range(2)]
            po = [c.ps([128, 1024], F32, "po", es) for _ in range(2)]
            k = 0
            for j in range(Tn // TW):
                mb = mt[j % 2]
                self.ld(mb, mb[:], self.mixT.h[:, j * TW:(j + 1) * TW].rearrange("(a p) t -> p a t", p=128), R=[self.mixT])
                for u in range(TW // 128):
                    b = k % 2
                    k += 1
                    t0 = j * TW + u * 128
                    self.ld(xt[b], xt[b][:], src.h[s, t0:t0 + 128, :], R=[src])
                    for n in range(2):
                        for kc in range(8):
                            self.mm(po[b][:, n * 512:(n + 1) * 512], mb[:, kc, u * 128:(u + 1) * 128], wo[:, kc, n * 512:(n + 1) * 512],
                                    [mb, wo], [po[b]], start=(kc == 0), stop=(kc == 7))
                    self.act(junk[:], po[b][:], AF.Square, [po[b]], [junk, ss[b]], accum_out=ss[b][:, 0:1])
                    self.rsqrt_(ss[b][:, 1:2], ss[b][:, 0:1], 1.0 / 1024, self.epsc[:, 0:1], [ss[b], self.epsc], [ss[b]])
                    self.tt("dve", yt[b][:], po[b][:], self.gg[s][:], ALU.mult, [po[b], self.gg[s]], [yt[b]])
                    self.stt(yt[b][:], yt[b][:], ss[b][:, 1:2], xt[b][:], ALU.mult, ALU.add, [yt[b], ss[b], xt[b]], [yt[b]])
                    self.c.dma("sp", self.y.h[s, t0:t0 + 128, :], yt[b][:], yt[b], reads=[yt[b]], writes=[self.y])
            c.barrier()

    def mla(self, l, s):
        c, nc, Tn = self.c, self.nc, self.Tn
        TW = min(512, Tn)
        NQG = Tn // TW
        NKB = Tn // 128
        U = TW // 128
        scale = 96 ** -0.5
        ID_b = self.kb[:, K_ID:K_ID + 128]
        ONE_b = self.kb[:, K_ONE:K_ONE + 128]
        with ExitStack() as es:
            Wm = c.sb([128, 8, 960], BF16, "Wm", es)
            Wq = c.sb([128, 3, 384], BF16, "Wq", es)
            Wqb = c.sb([128, 3, 384], BF16, "Wqb", es)
            Wkv = c.sb([128, 512], BF16, "Wkv", es)
            with ExitStack() as e1:
                st = c.sb([128, 8, 512], F32, "stA", e1)
                self.memset("pool", Wm[:, :, 512:704], 0.0, [Wm])
                self.load_w(Wm, 0, self.win(l, C_CQ, 384), 384, st)
                self.load_w(Wm, 384, self.win(l, C_CKV, 128), 128, st)
                self.load_w(Wm, 512 + 64, self.win(l, C_KR, 32), 32, st)
                self.load_w(Wm, 608 + 64, self.win(l, C_KR + 16, 16), 16, st, neg=True)
                self.load_w(Wm, 608 + 80, self.win(l, C_KR, 16), 16, st)
                self.load_w(Wm, 704, self.win(l, C_Z, 256), 256, st)
                stq = c.sb([128, 3, 384], F32, "stq", e1)
                self.ld(stq, stq[:], self.w["a_w_uq"].h[l].rearrange("(kc p) n -> p kc n", p=128))
                gq = c.sb([128, 3], F32, "gq", e1)
                self.ld(gq, gq[:], self.w["a_g_q"].h[l, :].rearrange("(j p) -> p j", p=128), allow_slow_non_contiguous=True)
                self.memset("pool", Wqb[:], 0.0, [Wqb])
                for fc in range(3):
                    self.ts("dve", Wq[:, fc, :], stq[:, fc, :], gq[:, fc:fc + 1], None, ALU.mult, None, [stq, gq], [Wq])
                    src4 = V3(Wq[:, fc, :], 4)
                    dst4 = V3(Wqb[:, fc, :], 4)
                    self.c.op("act", lambda s4=src4, d4=dst4: nc.scalar.mul(out=d4[:, :, 64:80], in_=s4[:, :, 80:96], mul=-1.0), [Wq], [Wqb])
                    self.cp("pool", dst4[:, :, 80:96], src4[:, :, 64:80], [Wq], [Wqb])
                stk = c.sb([128, 512], F32, "stk", e1)
                self.ld(stk, stk[:], self.w["a_w_ukv"].h[l])
                gkv = c.sb([128, 1], F32, "gkv", e1)
                self.ld(gkv, gkv[:], self.w["a_g_kv"].h[l, :].rearrange("(p o) -> p o", o=1))
                self.ts("dve", Wkv[:], stk[:], gkv[:, 0:1], None, ALU.mult, None, [stk, gkv], [Wkv])
                c.barrier()
            KT = [c.sb([96, Tn], BF16, "KT%d" % h, es) for h in range(4)]
            Va = c.sb([128, NKB, 4, 65], BF16, "Va", es)
            self.memset("pool", Va[:, :, :, 64:65], 1.0, [Va])
            cosx = c.sb([96, TW], F32, "cosx", es)
            sinx = c.sb([96, TW], F32, "sinx", es)
            t1 = c.sb([96, TW], F32, "t1", es)
            t2 = c.sb([96, TW], F32, "t2", es)
            with ExitStack() as e2:
                ckT = c.sb([128, TW], BF16, "ckT", e2)
                sqk = c.sb([128, TW], BF16, "sqk", e2)
                rk = c.sb([128, TW], F32, "rk", e2)
                rcol = c.sb([128, 2], F32, "rcol", e2)
                pA = [c.ps([128, 512], F32, "pA", e2) for _ in range(4)]
                pB = c.ps([128, 512], F32, "pB", e2)
                pV = c.ps([128, 256], F32, "pV", e2)
                pc = c.ps([128, 2], F32, "pc", e2)
                for j in range(NQG):
                    t0 = j * TW
                    self.ld(cosx, cosx[:], self.rope.h[:, 0, t0:t0 + TW])
                    self.ld(sinx, sinx[:], self.rope.h[:, 1, t0:t0 + TW])
                    self.proj(pB, Wm, 384, 128, t0, TW)
                    self.cp("act", ckT[:], pB[:, 0:TW], [pB], [ckT])
                    self.tt("pool", sqk[:], ckT[:], ckT[:], ALU.mult, [ckT], [sqk])
                    self.mm(pA[0][:, 0:TW], ONE_b, sqk[:], [self.kb, sqk], [pA[0]])
                    self.rsqrt_(rk[:], pA[0][:, 0:TW], 1.0 / 128, self.epsc[:, 0:1], [pA[0], self.epsc], [rk])
                    for h in range(4):
                        pa = pA[1 + h % 2]
                        self.mm(pa[0:64, 0:TW], Wkv[:, h * 128:h * 128 + 64], ckT[:], [Wkv, ckT], [pa])
                        self.tt("dve", KT[h][0:64, t0:t0 + TW], pa[0:64, 0:TW], rk[0:64, :], ALU.mult, [pa, rk], [KT[h]])
                    self.proj(pA[3], Wm, 512, 96, t0, TW)
                    self.proj(pB, Wm, 608, 96, t0, TW)
                    self.tt("dve", t1[64:96, :], pA[3][64:96, 0:TW], cosx[64:96, :], ALU.mult, [pA[3], cosx], [t1])
                    self.tt("dve", t2[64:96, :], pB[64:96, 0:TW], sinx[64:96, :], ALU.mult, [pB, sinx], [t2])
                    for h in range(4):
                        self.tt("pool", KT[h][64:96, t0:t0 + TW], t1[64:96, :], t2[64:96, :], ALU.add, [t1, t2], [KT[h]])
                    for u in range(U):
                        kb_ = (t0 + u * 128) // 128
                        self.mm(pc[:, 0:1], sqk[:, u * 128:(u + 1) * 128], self.kb[:, K_ONE:K_ONE + 1], [sqk, self.kb], [pc])
                        self.rsqrt_(rcol[:, 0:1], pc[:, 0:1], 1.0 / 128, self.epsc[:, 0:1], [pc, self.epsc], [rcol])
                        for h in range(4):
                            self.mm(pV[:, h * 64:(h + 1) * 64], ckT[:, u * 128:(u + 1) * 128], Wkv[:, h * 128 + 64:h * 128 + 128], [ckT, Wkv], [pV])
                        self.ts("dve", Va[:, kb_, :, 0:64], V3(pV[:], 4), rcol[:, 0:1], None, ALU.mult, None, [pV, rcol], [Va])
                c.barrier()
            with ExitStack() as e3:
                NB = 3
                pS = [c.ps([128, 512], F32, "pS", e3) for _ in range(NB)]
                pz = c.ps([128, 512], F32, "pz", e3)
                pO = [c.ps([128, 512], F32, "pO", e3) for _ in range(2)]
                pBC = c.ps([128, 512], F32, "pBC", e3)
                Pt = [c.sb([128, TW], BF16, "Pt", e3) for _ in range(NB)]
                cqT = c.sb([128, 3, TW], BF16, "cqT", e3)
                sq = c.sb([128, 3, TW], BF16, "sq", e3)
                rq = c.sb([128, TW], F32, "rq", e3)
                CR = c.sb([96, TW], F32, "CR", e3)
                SR = c.sb([96, TW], F32, "SR", e3)
                qg_t = [c.sb([96, TW], BF16, "qg%d" % h, e3) for h in range(4)]
                rs = c.sb([128, TW], F32, "rs", e3)
                bcs = c.sb([64, TW], F32, "bcs", e3)
                sz = c.sb([64, TW], F32, "sz", e3)
                on = c.sb([64, TW], F32, "on", e3)
                og = [c.sb([64, TW], BF16, "og", e3) for _ in range(2)]
                it = 0
                for qg in range(NQG):
                    q0 = qg * TW
                    self.ld(cosx, cosx[:], self.rope.h[:, 0, q0:q0 + TW])
                    self.ld(sinx, sinx[:], self.rope.h[:, 1, q0:q0 + TW])
                    for fc in range(3):
                        self.proj(pS[fc], Wm, fc * 128, 128, q0, TW)
                        self.cp("act", cqT[:, fc, :], pS[fc][:, 0:TW], [pS[fc]], [cqT])
                        self.tt("pool", sq[:, fc, :], cqT[:, fc, :], cqT[:, fc, :], ALU.mult, [cqT], [sq])
                    for fc in range(3):
                        self.mm(pz[:, 0:TW], ONE_b, sq[:, fc, :], [self.kb, sq], [pz], start=(fc == 0), stop=(fc == 2))
                    self.rsqrt_(rq[:], pz[:, 0:TW], 1.0 / 384, self.epsc[:, 0:1], [pz, self.epsc], [rq])
                    self.tt("pool", CR[:], cosx[:], rq[0:96, :], ALU.mult, [cosx, rq], [CR])
                    self.tt("pool", SR[:], sinx[:], rq[0:96, :], ALU.mult, [sinx, rq], [SR])
                    for h in range(4):
                        pa, pb = pS[0], pS[1]
                        for fc in range(3):
                            self.mm(pa[0:96, 0:TW], Wq[:, fc, h * 96:(h + 1) * 96], cqT[:, fc, :], [Wq, cqT], [pa], start=(fc == 0), stop=(fc == 2))
                        for fc in range(3):
                            self.mm(pb[0:96, 0:TW], Wqb[:, fc, h * 96:(h + 1) * 96], cqT[:, fc, :], [Wqb, cqT], [pb], start=(fc == 0), stop=(fc == 2))
                        self.tt("dve", t1[:], pa[0:96, 0:TW], CR[:], ALU.mult, [pa, CR], [t1])
                        self.tt("dve", t2[:], pb[0:96, 0:TW], SR[:], ALU.mult, [pb, SR], [t2])
                        self.tt("pool", qg_t[h][:], t1[:], t2[:], ALU.add, [t1, t2], [qg_t[h]])
                    for h in range(4):
                        po_ = pO[h % 2]
                        def score(kb_, b):
                            self.mm(pS[b][:, 0:TW], KT[h][:, kb_ * 128:(kb_ + 1) * 128], qg_t[h][:], [KT[h], qg_t[h]], [pS[b]])
                        score(0, it % NB)
                        if NKB > 1:
                            score(1, (it + 1) % NB)
                        for kb_ in range(NKB):
                            b = it % NB
                            it += 1
                            if kb_ + 2 < NKB:
                                score(kb_ + 2, (it + 1) % NB)
                            self.act(Pt[b][:], pS[b][:, 0:TW], AF.Exp, [pS[b]], [Pt[b]], scale=scale)
                            self.mm(po_[0:65, 0:TW], Va[:, kb_, h, :], Pt[b][:], [Va, Pt[b]], [po_], start=(kb_ == 0), stop=(kb_ == NKB - 1))
                        self.proj(pz, Wm, 704 + h * 64, 64, q0, TW)
                        self.act(sz[:], pz[0:64, 0:TW], AF.Silu, [pz], [sz])
                        self.c.op("dve", lambda p=po_: nc.vector.reciprocal(out=rs[64:65, :], in_=p[64:65, 0:TW]), [po_], [rs])
                        self.mm(pBC[0:64, 0:TW], self.kc[64:65, K_ONE:K_ONE + 64], rs[64:65, :], [self.kc, rs], [pBC])
                        self.cp("act", bcs[:], pBC[0:64, 0:TW], [pBC], [bcs])
                        self.tt("dve", on[:], po_[0:64, 0:TW], bcs[:], ALU.mult, [po_, bcs], [on])
                        ob = og[h % 2]
                        self.tt("pool", ob[:], on[:], sz[:], ALU.mult, [on, sz], [ob])
                        self.c.dma("sp", self.mixT.h[h * 64:(h + 1) * 64, q0:q0 + TW], ob[:], ob, reads=[ob], writes=[self.mixT])
            c.barrier()

    def dplr_env(self, es, nsets=3):
        c = self.c
        sets = []

        def sl(bank, lo, n=128):
            t = c.sub(bank[:, lo:lo + n])
            t.dep = bank.dep
            return t
        cb = c.ps([128, 512], F32, "dpc", es)
        chain = {"pW": sl(cb, 0), "pU": sl(cb, 128), "pZ": sl(cb, 256), "pY": sl(cb, 384, 64)}
        for i in range(nsets):
            e = dict(chain)
            bA = c.ps([128, 512], F32, "dpa", es)
            bB = c.ps([128, 512], F32, "dpb", es)
            e["pQB"], e["pQK"], e["pNL"] = sl(bA, 0, 192), sl(bA, 192, 192), sl(bA, 384)
            e["pXP"] = sl(bA, 0, 256)
            e["pPt"] = sl(bB, 0)
            e["pT"] = c.sub(V3(bB[:, 256:448].bitcast(BF16), 3))
            e["pT"].dep = bB.dep
            for nm in ["Nt", "Pt0", "Pt1"]:
                e[nm] = c.sb([128, 128], F32R, "d" + nm, es)
            for nm in ["XP0", "XP1"]:
                e[nm] = c.sb([128, 256], F32R, "d" + nm, es)
            for nm in ["AK", "AR", "TT", "W", "U"]:
                e[nm] = c.sb([128, 128], BF16, "d" + nm, es)
            e["Tr"] = c.sb([128, 3, 128], BF16, "dTr", es)
            sets.append(e)
        st = {"Zf": c.sb([128, 128], F32, "Zf", es), "Zb": c.sb([128, 128], BF16, "Zb", es), "Zg": c.sb([128, 128], F32, "Zg", es)}
        return sets, st

    def dplr_pre(self, e, o, ci):
        kc, kb = self.kc, self.kb
        IDf = kc[:, K_ID:K_ID + 128]
        IDb = kb[:, K_ID:K_ID + 128]
        PAR, QB, QK = o["PAR"](ci), o["QB"](ci), o["QK"](ci)
        ms, mi, ml = o["ms"](ci), o["mi"](ci), o["ml"](ci)
        PA = PAR[0][:, 0:128]
        same = o.get("same", False)
        self.mm(e["pQB"][:], QB[0], PAR[0], [QB[1], PAR[1]], [e["pQB"]])
        if not same:
            self.mm(e["pQK"][:], QK[0], PAR[0], [QK[1], PAR[1]], [e["pQK"]])
        self.mm(e["pNL"][:], PA, QB[0], [QB[1], PAR[1]], [e["pNL"]])
        for j, nm in enumerate(["SB", "SK", "V"]):
            if same and nm == "SB":
                continue
            src = o[nm](ci)
            self.tr(e["pT"][:, j, :], src[0], IDb, [src[1], kb], [e["pT"]])
        XP = e["XP0"]
        self.tt("dve", V3(XP[:, 128:256], 2), V3(e["pQB"][:, 0:128], 2), bc_mid(ms[0], 2), ALU.mult, [e["pQB"], ms[1]], [XP])
        self.tt("dve", V3(e["Nt"][:], 2), V3(e["pNL"][:], 2), bc_mid(ml[0], 2), ALU.mult, [e["pNL"], ml[1]], [e["Nt"]])
        self.tt("dve", e["AR"][:, 0:64], e["pQB"][:, 128:192], mi[0], ALU.mult, [e["pQB"], mi[1]], [e["AR"]])
        if same:
            self.cp("act", e["AK"][:], XP[:, 128:256], [XP], [e["AK"]])
            self.cp("act", e["Tr"][:, 1:3, :], e["pT"][:, 1:3, :], [e["pT"]], [e["Tr"]])
        else:
            self.tt("dve", V3(e["AK"][:], 2), V3(e["pQK"][:, 0:128], 2), bc_mid(ms[0], 2), ALU.mult, [e["pQK"], ms[1]], [e["AK"]])
            self.tt("dve", e["AR"][:, 64:128], e["pQK"][:, 128:192], mi[0], ALU.mult, [e["pQK"], mi[1]], [e["AR"]])
            self.cp("act", e["Tr"][:], e["pT"][:], [e["pT"]], [e["Tr"]])
        self.tt("dve", XP[:, 0:128], XP[:, 128:256], IDf, ALU.add, [XP, kc], [XP])
        yield
        Pt = e["Nt"]
        Ptn = e["Pt1"]
        self.mm(e["pPt"][:], XP[:, 128:256], Pt[:], [XP, Pt], [e["pPt"]])
        self.mm(e["pXP"][:, 128:256], Pt[:], XP[:, 128:256], [XP, Pt], [e["pXP"]])
        XPn = e["XP1"]
        self.cp("act", Ptn[:], e["pPt"][:], [e["pPt"]], [Ptn])
        self.cp("act", XPn[:, 128:256], e["pXP"][:, 128:256], [e["pXP"]], [XPn])
        self.cp("dve", XPn[:, 0:128], XP[:, 0:128], [XP], [XPn])
        XP, Pt = XPn, Ptn
        yield
        for k in range(1, 6):
            XPn = e["XP%d" % (k % 2)]
            Ptn = e["Pt%d" % (k % 2)]
            if k < 4:
                self.mm(e["pXP"][:], Pt[:], XP[:], [Pt, XP], [e["pXP"]])
            else:
                self.mm(e["pXP"][:, 0:128], Pt[:], XP[:, 0:128], [Pt, XP], [e["pXP"]])
            if k < 5:
                self.mm(e["pPt"][:], XP[:, 128:256], Pt[:], [XP, Pt], [e["pPt"]])
            self.tt("dve", XPn[:, 0:128], XP[:, 0:128], e["pXP"][:, 0:128], ALU.add, [XP, e["pXP"]], [XPn])
            if k < 4:
                self.cp("act", XPn[:, 128:256], e["pXP"][:, 128:256], [e["pXP"]], [XPn])
            if k < 5:
                self.cp("act", Ptn[:], e["pPt"][:], [e["pPt"]], [Ptn])
            XP, Pt = XPn, Ptn
            yield
        self.cp("act", e["TT"][:], XP[:, 0:128], [XP], [e["TT"]])

    def dplr_chain(self, e, st, o, ci):
        nc = self.nc
        Zf, Zb, Zg = st["Zf"], st["Zb"], st["Zg"]
        ZA, ZR = o["ZA"](ci), o["ZR"](ci)
        SBt, SKt, Vt = e["Tr"][:, 0, :], e["Tr"][:, 1, :], e["Tr"][:, 2, :]
        gz, gs = o["gZ"](ci), o["gS"](ci)
        self.c.op("act", lambda: nc.scalar.mul(out=Zg[:], in_=Zf[:], mul=gz[0]), [Zf, gz[1]], [Zg])
        self.mm(e["pW"][:], ZA[0], Zb[:], [ZA[1], Zb], [e["pW"]], start=True, stop=False)
        self.mm(e["pW"][:], e["AK"][:], Vt, [e["AK"], e["Tr"]], [e["pW"]], start=False, stop=True)
        self.cp("act", e["W"][:], e["pW"][:], [e["pW"]], [e["W"]])
        yield
        self.mm(e["pU"][:], e["TT"][:], e["W"][:], [e["TT"], e["W"]], [e["pU"]])
        if o.get("same", False):
            self.tt("dve", e["U"][:], e["pU"][:], Vt, ALU.add, [e["pU"], e["Tr"]], [e["U"]])
            yield
            self.mm(e["pZ"][:], SKt, e["U"][:], [e["Tr"], e["U"]], [e["pZ"]])
            self.mm(e["pY"][:], Zb[:], ZR[0], [Zb, ZR[1]], [e["pY"]], start=True, stop=False)
            self.mm(e["pY"][:], e["U"][:], e["AR"][:, 0:64], [e["U"], e["AR"]], [e["pY"]], start=False, stop=True)
        else:
            self.cp("dve", e["U"][:], e["pU"][:], [e["pU"]], [e["U"]])
            yield
            self.mm(e["pZ"][:], SBt, e["U"][:], [e["Tr"], e["U"]], [e["pZ"]], start=True, stop=False)
            self.mm(e["pZ"][:], SKt, Vt, [e["Tr"]], [e["pZ"]], start=False, stop=True)
            self.mm(e["pY"][:], Zb[:], ZR[0], [Zb, ZR[1]], [e["pY"]], start=True, stop=False)
            self.mm(e["pY"][:], e["U"][:], e["AR"][:, 0:64], [e["U"], e["AR"]], [e["pY"]], start=False, stop=False)
            self.mm(e["pY"][:], Vt, e["AR"][:, 64:128], [e["Tr"], e["AR"]], [e["pY"]], start=False, stop=True)
        if gs is None:
            self.tt("dve", Zb[:], e["pZ"][:], Zg[:], ALU.add, [e["pZ"], Zg], [Zb])
            self.tt("dve", Zf[:], e["pZ"][:], Zg[:], ALU.add, [e["pZ"], Zg], [Zf])
        else:
            self.stt(Zb[:], e["pZ"][:], gs[0], Zg[:], ALU.mult, ALU.add, [e["pZ"], gs[1], Zg], [Zb])
            self.stt(Zf[:], e["pZ"][:], gs[0], Zg[:], ALU.mult, ALU.add, [e["pZ"], gs[1], Zg], [Zf])
        o["yout"](ci, e["pY"])
        yield

    def dplr_run(self, sets, st, o, nch, side=None):
        NS_ = len(sets)
        active, done = [], set()
        chain, next_chain, next_pre = None, 0, 0
        while next_chain < nch:
            while next_pre < nch and next_pre < next_chain + NS_ and len(active) < NS_:
                active.append([next_pre, self.dplr_pre(sets[next_pre % NS_], o, next_pre)])
                next_pre += 1
            if chain is None and next_chain in done:
                chain = self.dplr_chain(sets[next_chain % NS_], st, o, next_chain)
            for item in list(active):
                try:
                    next(item[1])
                except StopIteration:
                    active.remove(item)
                    done.add(item[0])
            if chain is not None:
                try:
                    next(chain)
                except StopIteration:
                    chain = None
                    next_chain += 1
            if side is not None:
                try:
                    next(side)
                except StopIteration:
                    side = None
        if side is not None:
            for _ in side:
                pass

    def to_bd(self, eng_fn, dst, nch):
        for hh in range(2):
            lo, hi = hh * 64, hh * 64 + 64
            eng_fn(dst[lo:hi, :, lo:hi], lo, hi)

    def col(self, es, src_ap, name="col", n=128):
        t = self.c.sb([n, 1], F32, name, es)
        self.ld(t, t[:], src_ap.rearrange("(p o) -> p o", o=1))
        return t

    def rwkv(self, l, s):
        c, nc, Tn = self.c, self.nc, self.Tn
        SG = min(256, Tn)
        nch = SG // 64
        NSG = Tn // SG
        kc, kb = self.kc, self.kb
        BOb = kb[:, K_BO:K_BO + 128]
        BOf = kc[:, K_BO:K_BO + 128]
        w = self.w
        with ExitStack() as es:
            sets, st = self.dplr_env(es)
            Yacc = c.sb([128, Tn], F32, "Yacc", es)
            Bacc = c.sb([128, Tn], F32, "Bacc", es)
            ones = c.sb([128, SG], F32, "ones", es)
            self.memset("pool", ones[:], 1.0, [ones])
            W1 = c.sb([128, 8, 448], BF16, "W1", es)
            Pc = c.sb([128, SG + 1], F32, "Pc", es)
            carry = c.sb([128, 5], F32, "carry", es)
            mu5 = c.sb([128, 5], F32, "mu5", es)
            OS = []
            for i in range(2):
                S_ = {"PAR": c.sb([128, nch, 192], BF16, "PAR", es), "g": c.sb([128, nch], F32, "gcol", es)}
                for nm in ["B", "K", "V"]:
                    S_[nm] = c.sb([128, nch, 128], BF16, "bd" + nm, es)
                for nm in ["PAR", "B", "K", "V"]:
                    self.memset("pool", S_[nm][:], 0.0, [S_[nm]])
                OS.append(S_)
            w2b = c.sb([32, 128], BF16, "w2b", es)
            a2b = c.sb([32, 128], BF16, "a2b", es)
            F = {nm: c.sb([128, SG], F32, "f" + nm, es) for nm in ["r", "k", "v", "lw", "alr", "G", "lg", "E1", "E2", "E3", "kk", "kp", "t0", "t1"]}
            Gs = c.sb([128, nch], F32, "Gs", es)
            self.memset("pool", Gs[:, 0:1], 0.0, [Gs])
            Hb = {nm: c.sb([128, SG], BF16, "h" + nm, es) for nm in ["sq", "rkr"]}
            th = c.sb([32, SG], BF16, "th", es)
            xab = c.sb([32, SG], BF16, "xab", es)
            pp = [c.ps([128, 512], F32, "rpp", es)] * 2
            ppi = [0]

            def nextp():
                ppi[0] ^= 1
                return pp[ppi[0]]

            for pr in range(2):
                cs = slice(pr * 128, pr * 128 + 128)
                kk_c = self.col(es, w["rw_k_k"].h[l, cs], "kkc")
                ka_c = self.col(es, w["rw_k_a"].h[l, cs], "kac")
                rk_c = self.col(es, w["rw_r_k"].h[l].rearrange("h n -> (h n)")[cs], "rkc")
                lnw_c = self.col(es, w["rw_ln_w"].h[l, cs], "lnw")
                lnb_c = self.col(es, w["rw_ln_b"].h[l, cs], "lnb")
                for d in range(2):
                    with ExitStack() as e1:
                        stg = c.sb([128, 8, 128], F32, "stg", e1)
                        srcs = [(C_RKV + pr * 128, 128, pr * 128), (C_RKV + 256 + pr * 128, 128, 256 + pr * 128),
                                (C_RKV + 512 + pr * 128, 128, 512 + pr * 128), ((C_WAF if d == 0 else C_WAB), 32, 768),
                                ((C_WAF if d == 0 else C_WAB) + 32, 32, 800)]
                        o_ = 0
                        for j5, (c0, n, m0) in enumerate(srcs):
                            self.load_w(W1, o_, self.win(l, c0, n), n, stg)
                            self.ld(mu5, mu5[0:n, j5:j5 + 1], w["rw_mu"].h[l, d, m0:m0 + n].rearrange("(p o) -> p o", o=1))
                            o_ += n
                        s32 = c.sb([32, 256], F32, "s32", e1)
                        self.ld(s32, s32[:, 0:128], w["rw_w2"].h[l, d, :, cs])
                        self.ld(s32, s32[:, 128:256], w["rw_a2"].h[l, d, :, cs])
                        self.cp("dve", w2b[:], s32[:, 0:128], [s32], [w2b])
                        self.cp("dve", a2b[:], s32[:, 128:256], [s32], [a2b])
                        c.barrier()
                    self.memset("pool", carry[:], 0.0, [carry])
                    w0_c = self.col(es, w["rw_w0"].h[l, d, cs], "w0c")
                    a0_c = self.col(es, w["rw_a0"].h[l, d, cs], "a0c")
                    self.memset("pool", st["Zf"][:], 0.0, [st["Zf"]])
                    self.memset("pool", st["Zb"][:], 0.0, [st["Zb"]])
                    def prep(sg, S, d=d, w0_c=w0_c, a0_c=a0_c):
                        tau0 = sg * SG
                        PAR, BDB, BDK, BDV = S["PAR"], S["B"], S["K"], S["V"]

                        def shifted(j5, wc, M, out_ap, out_t):
                            p_ = nextp()
                            self.proj(p_, W1, wc, M, tau0, SG, d=d)
                            self.cp("pool", Pc[0:M, 0:1], carry[0:M, j5:j5 + 1], [carry], [Pc])
                            self.cp("act", Pc[0:M, 1:SG + 1], p_[0:M, 0:SG], [p_], [Pc])
                            self.cp("pool", carry[0:M, j5:j5 + 1], Pc[0:M, SG:SG + 1], [Pc], [carry])
                            self.tt("pool", F["t0"][0:M, :], Pc[0:M, 0:SG], Pc[0:M, 1:SG + 1], ALU.subtract, [Pc], [F["t0"]])
                            self.stt(out_ap, F["t0"][0:M, :], mu5[0:M, j5:j5 + 1], Pc[0:M, 1:SG + 1], ALU.mult, ALU.add, [F["t0"], mu5, Pc], [out_t])
                        shifted(0, 0, 128, F["r"][:], F["r"])
                        yield
                        shifted(1, 128, 128, F["k"][:], F["k"])
                        yield
                        shifted(2, 256, 128, F["v"][:], F["v"])
                        self.to_bd(lambda o_ap, lo, hi: self.cp("act", o_ap, V3(F["v"][lo:hi, :], nch), [F["v"]], [BDV]), BDV, nch)
                        yield
                        shifted(3, 384, 32, F["t1"][0:32, :], F["t1"])
                        self.act(th[:], F["t1"][0:32, :], AF.Tanh, [F["t1"]], [th])
                        yield
                        shifted(4, 416, 32, xab[:], xab)
                        yield
                        p_ = nextp()
                        self.mm(p_[:, 0:SG], w2b[:], th[:], [w2b, th], [p_])
                        self.act(F["lw"][:], p_[:, 0:SG], AF.Sigmoid, [p_, w0_c], [F["lw"]], bias=w0_c[:, 0:1])
                        self.ts("dve", F["lw"][:], F["lw"][:], -math.exp(-0.5), None, ALU.mult, None, [F["lw"]], [F["lw"]])
                        p_ = nextp()
                        self.mm(p_[:, 0:SG], a2b[:], xab[:], [a2b, xab], [p_])
                        self.act(F["alr"][:], p_[:, 0:SG], AF.Sigmoid, [p_, a0_c], [F["alr"]], bias=a0_c[:, 0:1])
                        yield
                        self.c.op("dve", lambda: nc.vector.tensor_tensor_scan(out=F["G"][:], data0=ones[:], data1=F["lw"][:], initial=0.0,
                                                                                op0=ALU.mult, op1=ALU.add), [ones, F["lw"]], [F["G"]])
                        if nch > 1:
                            self.cp("pool", Gs[:, 1:nch], F["G"][:, 63:SG - 1:64], [F["G"]], [Gs])
                        self.tt("pool", V3(F["lg"][:], nch), V3(F["G"][:], nch), bc_last(Gs[:], 64), ALU.subtract, [F["G"], Gs], [F["lg"]])
                        yield
                        self.act(F["E1"][:], F["lg"][:], AF.Exp, [F["lg"]], [F["E1"]])
                        self.act(F["E3"][:], F["lg"][:], AF.Exp, [F["lg"]], [F["E3"]], scale=-1.0)
                        self.tt("pool", F["t0"][:], F["lg"][:], F["lw"][:], ALU.subtract, [F["lg"], F["lw"]], [F["t0"]])
                        self.act(F["E2"][:], F["t0"][:], AF.Exp, [F["t0"]], [F["E2"]])
                        self.cp("pool", S["g"][:], F["E1"][:, 63:SG:64], [F["E1"]], [S["g"]])
                        yield
                        self.ts("dve", F["t0"][:], F["k"][:], kk_c[:, 0:1], None, ALU.mult, None, [F["k"], kk_c], [F["t0"]])
                        self.tt("pool", Hb["sq"][:], F["t0"][:], F["t0"][:], ALU.mult, [F["t0"]], [Hb["sq"]])
                        p_ = nextp()
                        self.mm(p_[:, 0:SG], BOb, Hb["sq"][:], [kb, Hb["sq"]], [p_])
                        self.rsqrt_(F["t1"][:], p_[:, 0:SG], 1.0, self.epsc[:, 0:1], [p_, self.epsc], [F["t1"]])
                        self.tt("pool", F["kk"][:], F["t0"][:], F["t1"][:], ALU.mult, [F["t0"], F["t1"]], [F["kk"]])
                        yield
                        self.ts("dve", F["t0"][:], F["alr"][:], -1.0, ka_c[:, 0:1], ALU.add, ALU.mult, [F["alr"], ka_c], [F["t0"]])
                        self.stt(F["kp"][:], F["t0"][:], 1.0, F["k"][:], ALU.add, ALU.mult, [F["t0"], F["k"]], [F["kp"]])
                        self.tt("pool", F["t0"][:], F["r"][:], F["kp"][:], ALU.mult, [F["r"], F["kp"]], [F["t0"]])
                        self.ts("dve", Hb["rkr"][:], F["t0"][:], rk_c[:, 0:1], None, ALU.mult, None, [F["t0"], rk_c], [Hb["rkr"]])
                        yield
                        p_ = nextp()
                        self.mm(p_[:, 0:SG], BOb, Hb["rkr"][:], [kb, Hb["rkr"]], [p_])
                        if d == 0:
                            self.tt("dve", Bacc[:, tau0:tau0 + SG], p_[:, 0:SG], F["v"][:], ALU.mult, [p_, F["v"]], [Bacc])
                        else:
                            self.tt("dve", F["t1"][:], p_[:, 0:SG], F["v"][:], ALU.mult, [p_, F["v"]], [F["t1"]])
                            rb = rev(Bacc[:, Tn - tau0 - SG:Tn - tau0])
                            self.tt("pool", rb, rb, F["t1"][:], ALU.add, [Bacc, F["t1"]], [Bacc])
                        yield
                        self.tt("pool", PAR[:, :, 128:192], V3(F["r"][:], nch), V3(F["E1"][:], nch), ALU.mult, [F["r"], F["E1"]], [PAR])
                        self.to_bd(lambda o_ap, lo, hi: self.stt(o_ap, V3(F["kk"][lo:hi, :], nch), -1.0, V3(F["E2"][lo:hi, :], nch), ALU.mult, ALU.mult,
                                                                 [F["kk"], F["E2"]], [PAR]), PAR, nch)
                        yield
                        self.to_bd(lambda o_ap, lo, hi: self.tt("dve", o_ap, V3(F["kp"][lo:hi, :], nch), V3(F["E3"][lo:hi, :], nch), ALU.mult,
                                                                [F["kp"], F["E3"]], [BDK]), BDK, nch)
                        yield
                        self.tt("pool", F["t0"][:], F["kk"][:], F["alr"][:], ALU.mult, [F["kk"], F["alr"]], [F["t0"]])
                        self.to_bd(lambda o_ap, lo, hi: self.tt("dve", o_ap, V3(F["t0"][lo:hi, :], nch), V3(F["E3"][lo:hi, :], nch), ALU.mult,
                                                                [F["t0"], F["E3"]], [BDB]), BDB, nch)
                        yield

                    def mk_o(sg, S, d=d):
                        tau0 = sg * SG
                        PAR, BDB, BDK, BDV, G_ = S["PAR"], S["B"], S["K"], S["V"], S["g"]

                        def yout(ci, pY):
                            a0_ = tau0 + ci * 64
                            if d == 0:
                                self.cp("dve", Yacc[:, a0_:a0_ + 64], pY[:], [pY], [Yacc])
                            else:
                                ry = rev(Yacc[:, Tn - a0_ - 64:Tn - a0_])
                                self.tt("dve", ry, ry, pY[:], ALU.add, [Yacc, pY], [Yacc])
                        return dict(PAR=lambda ci: (PAR[:, ci, :], PAR), QB=lambda ci: (BDB[:, ci, :], BDB), QK=lambda ci: (BDK[:, ci, :], BDK),
                                    ZA=lambda ci: (PAR[:, ci, 0:128], PAR), ZR=lambda ci: (PAR[:, ci, 128:192], PAR),
                                    SB=lambda ci: (BDB[:, ci, :], BDB), SK=lambda ci: (BDK[:, ci, :], BDK), V=lambda ci: (BDV[:, ci, :], BDV),
                                    ms=lambda ci: (kc[:, K_MUS:K_MUS + 64], kc), mi=lambda ci: (kc[:, K_MUI:K_MUI + 64], kc),
                                    ml=lambda ci: (kc[:, K_MLS:K_MLS + 64], kc),
                                    gZ=lambda ci: (G_[:, ci:ci + 1], G_), gS=lambda ci: (G_[:, ci:ci + 1], G_), yout=yout)

                    for _ in prep(0, OS[0]):
                        pass
                    for sg in range(NSG):
                        side = prep(sg + 1, OS[(sg + 1) % 2]) if sg + 1 < NSG else None
                        self.dplr_run(sets, st, mk_o(sg, OS[sg % 2]), nch, side=side)
                with ExitStack() as e2:
                    Wz = c.sb([128, 8, 128], BF16, "Wz", e2)
                    stz = c.sb([128, 8, 128], F32, "stz", e2)
                    self.load_w(Wz, 0, self.win(l, C_Z + 256 + pr * 128, 128), 128, stz)
                    ob = [c.sb([128, SG], BF16, "rob", e2) for _ in range(2)]
                    for sg in range(NSG):
                        t0 = sg * SG
                        Y = Yacc[:, t0:t0 + SG]
                        p_ = nextp()
                        self.mm(p_[:, 0:SG], BOf, Y, [kc, Yacc], [p_])
                        self.stt(F["t0"][:], p_[:, 0:SG], -1.0 / 64, Y, ALU.mult, ALU.add, [p_, Yacc], [F["t0"]])
                        self.tt("pool", F["t1"][:], F["t0"][:], F["t0"][:], ALU.mult, [F["t0"]], [F["t1"]])
                        p_ = nextp()
                        self.mm(p_[:, 0:SG], BOf, F["t1"][:], [kc, F["t1"]], [p_])
                        self.rsqrt_(F["t1"][:], p_[:, 0:SG], 1.0 / 64, self.epsc[:, 1:2], [p_, self.epsc], [F["t1"]])
                        self.tt("pool", F["t0"][:], F["t0"][:], F["t1"][:], ALU.mult, [F["t0"], F["t1"]], [F["t0"]])
                        self.ts("dve", F["t0"][:], F["t0"][:], lnw_c[:, 0:1], lnb_c[:, 0:1], ALU.mult, ALU.add, [F["t0"], lnw_c, lnb_c], [F["t0"]])
                        self.tt("pool", F["t0"][:], F["t0"][:], Bacc[:, t0:t0 + SG], ALU.add, [F["t0"], Bacc], [F["t0"]])
                        p_ = nextp()
                        self.proj(p_, Wz, 0, 128, t0, SG)
                        self.act(F["t1"][:], p_[:, 0:SG], AF.Silu, [p_], [F["t1"]])
                        o_b = ob[sg % 2]
                        self.tt("pool", o_b[:], F["t0"][:], F["t1"][:], ALU.mult, [F["t0"], F["t1"]], [o_b])
                        r0 = 256 + pr * 128
                        self.c.dma("sp", self.mixT.h[r0:r0 + 128, t0:t0 + SG], o_b[:], o_b, reads=[o_b], writes=[self.mixT])
                    c.barrier()
            c.barrier()

    def gdn(self, l, s):
        c, nc, Tn = self.c, self.nc, self.Tn
        SG = min(256, Tn)
        nch = SG // 64
        NSG = Tn // SG
        TW = min(512, Tn)
        kc, kb = self.kc, self.kb
        BOb = kb[:, K_BO:K_BO + 128]
        BOf = kc[:, K_BO:K_BO + 128]
        w = self.w
        with ExitStack() as es:
            sets, st = self.dplr_env(es)
            Yacc = c.sb([128, Tn], F32, "Yacc", es)
            ones = c.sb([128, SG], F32, "ones", es)
            self.memset("pool", ones[:], 1.0, [ones])
            QKV = {nm: c.sb([128, Tn], BF16, "g" + nm, es) for nm in "qkv"}
            Wg = c.sb([128, 8, 16], BF16, "Wg", es)
            with ExitStack() as e1:
                stg = c.sb([128, 8, 16], F32, "stgg", e1)
                self.load_w(Wg, 0, self.win(l, C_GDB, 16), 16, stg)
                c.barrier()
            sg_t = c.sb([8, 4, 128], F32, "selg", es)
            self.ld(sg_t, sg_t[:], self.selg.h[:])
            dtb = self.col(es, w["gd_dt_bias"].h[l].rearrange("d h -> (d h)"), "dtb", 8)
            nA = self.col(es, w["gd_a_log"].h[l].rearrange("d h -> (d h)"), "nA", 8)
            self.act(nA[:], nA[:], AF.Exp, [nA], [nA])
            self.ts("pool", nA[:], nA[:], -1.0, None, ALU.mult, None, [nA], [nA])
            gnw = c.sb([128, 1], F32, "gnw", es)
            for hh in range(2):
                self.ld(gnw, gnw[hh * 64:(hh + 1) * 64, :], w["gd_norm_w"].h[l, :].rearrange("(p o) -> p o", o=1))
            F = {nm: c.sb([128, SG], F32, "f" + nm, es) for nm in ["g", "b", "G", "gc", "E1", "Es", "nbk", "t0", "Ex"]}
            g8 = c.sb([8, SG], F32, "g8", es)
            b8 = c.sb([8, SG], F32, "b8", es)
            Gs = c.sb([128, nch], F32, "Gs", es)
            gcT = c.sb([128, nch], F32, "gcT", es)
            self.memset("pool", Gs[:, 0:1], 0.0, [Gs])
            OS = []
            for i in range(2):
                S_ = {"PAR": c.sb([128, nch, 192], BF16, "gPAR", es), "g": c.sb([128, nch], F32, "ggcol", es),
                      "ZR": c.sb([128, SG], BF16, "gZR", es)}
                for nm in ["ZA", "K", "SK", "V"]:
                    S_[nm] = c.sb([128, nch, 128], BF16, "gbd" + nm, es)
                for nm in ["Ms", "Mi", "Ml"]:
                    S_[nm] = c.sb([128, nch, 64], F32, "g" + nm, es)
                for nm in ["PAR", "ZA", "K", "SK", "V"]:
                    self.memset("pool", S_[nm][:], 0.0, [S_[nm]])
                OS.append(S_)
            pp = [c.ps([128, 512], F32, "gpp", es)] * 2
            ppi = [0]

            def nextp():
                ppi[0] ^= 1
                return pp[ppi[0]]

            for pr in range(2):
                with ExitStack() as e1:
                    Wq = c.sb([128, 8, 384], BF16, "Wqkv", e1)
                    stg = c.sb([128, 8, 128], F32, "stgq", e1)
                    cw = c.sb([128, 3, 4], F32, "cw", e1)
                    for j in range(3):
                        ch0 = j * 256 + pr * 128
                        self.load_w(Wq, j * 128, self.win(l, C_QKV + ch0, 128), 128, stg)
                        self.ld(cw, cw[:, j, :], w["gd_conv_w"].h[l, :, ch0:ch0 + 128].rearrange("i p -> p i"), allow_slow_non_contiguous=True)
                    raw = c.sb([128, Tn + 3], BF16, "raw", e1)
                    self.memset("pool", raw[:, 0:1], 0.0, [raw])
                    self.memset("pool", raw[:, Tn + 1:Tn + 3], 0.0, [raw])
                    acc = c.sb([128, TW], F32, "acc", e1)
                    sl_ = c.sb([128, TW], F32, "sl", e1)
                    sqb = c.sb([128, TW], BF16, "sqb", e1)
                    rn = c.sb([128, TW], F32, "rn", e1)
                    for j, nm in enumerate("qkv"):
                        for tt_ in range(Tn // TW):
                            p_ = nextp()
                            self.proj(p_, Wq, j * 128, 128, tt_ * TW, TW)
                            self.cp(self.eng2(), raw[:, 1 + tt_ * TW:1 + (tt_ + 1) * TW], p_[:, 0:TW], [p_], [raw])
                        for tt_ in range(Tn // TW):
                            t0 = tt_ * TW
                            self.ts("dve", acc[:], raw[:, t0:t0 + TW], cw[:, j, 0:1], None, ALU.mult, None, [raw, cw], [acc])
                            for i in range(1, 4):
                                self.stt(acc[:], raw[:, t0 + i:t0 + i + TW], cw[:, j, i:i + 1], acc[:], ALU.mult, ALU.add, [raw, cw, acc], [acc])
                            if nm == "v":
                                self.act(QKV[nm][:, t0:t0 + TW], acc[:], AF.Silu, [acc], [QKV[nm]])
                            else:
                                self.act(sl_[:], acc[:], AF.Silu, [acc], [sl_])
                                self.tt("pool", sqb[:], sl_[:], sl_[:], ALU.mult, [sl_], [sqb])
                                p_ = nextp()
                                self.mm(p_[:, 0:TW], BOb, sqb[:], [kb, sqb], [p_])
                                self.rsqrt_(rn[:], p_[:, 0:TW], 1.0, self.epsc[:, 0:1], [p_, self.epsc], [rn])
                                if nm == "q":
                                    self.stt(QKV[nm][:, t0:t0 + TW], sl_[:], 0.125, rn[:], ALU.mult, ALU.mult, [sl_, rn], [QKV[nm]])
                                else:
                                    self.tt("pool", QKV[nm][:, t0:t0 + TW], sl_[:], rn[:], ALU.mult, [sl_, rn], [QKV[nm]])
                    c.barrier()
                for d in range(2):
                    self.memset("pool", st["Zf"][:], 0.0, [st["Zf"]])
                    self.memset("pool", st["Zb"][:], 0.0, [st["Zb"]])
                    def prep(sg, S, d=d, pr=pr):
                        tau0 = sg * SG
                        PAR, BDZA, BDK, BDSK, BDV, ZRt, Ms, Mi, Ml = S["PAR"], S["ZA"], S["K"], S["SK"], S["V"], S["ZR"], S["Ms"], S["Mi"], S["Ml"]

                        def sa(t):
                            return t[:, tau0:tau0 + SG] if d == 0 else rev(t[:, Tn - tau0 - SG:Tn - tau0])
                        p_ = nextp()
                        self.proj(p_, Wg, 0, 8, tau0, SG, d=d)
                        self.act(b8[:], p_[0:8, 0:SG], AF.Sigmoid, [p_], [b8])
                        p_ = nextp()
                        self.proj(p_, Wg, 8, 8, tau0, SG, d=d)
                        self.act(g8[:], p_[0:8, 0:SG], AF.Exp, [p_, dtb], [g8], bias=dtb[:, 0:1])
                        self.act(g8[:], g8[:], AF.Ln, [g8], [g8], bias=1.0)
                        self.ts("dve", g8[:], g8[:], nA[:, 0:1], None, ALU.mult, None, [g8, nA], [g8])
                        yield
                        p_ = nextp()
                        self.mm(p_[:, 0:SG], sg_t[:, d * 2 + pr, :], g8[:], [sg_t, g8], [p_])
                        self.cp("act", F["g"][:], p_[:, 0:SG], [p_], [F["g"]])
                        p_ = nextp()
                        self.mm(p_[:, 0:SG], sg_t[:, d * 2 + pr, :], b8[:], [sg_t, b8], [p_])
                        self.cp("dve", F["b"][:], p_[:, 0:SG], [p_], [F["b"]])
                        yield
                        self.c.op("dve", lambda: nc.vector.tensor_tensor_scan(out=F["G"][:], data0=ones[:], data1=F["g"][:], initial=0.0,
                                                                                op0=ALU.mult, op1=ALU.add), [ones, F["g"]], [F["G"]])
                        if nch > 1:
                            self.cp("pool", Gs[:, 1:nch], F["G"][:, 63:SG - 1:64], [F["G"]], [Gs])
                        self.tt("pool", V3(F["gc"][:], nch), V3(F["G"][:], nch), bc_last(Gs[:], 64), ALU.subtract, [F["G"], Gs], [F["gc"]])
                        yield
                        self.act(F["E1"][:], F["gc"][:], AF.Exp, [F["gc"]], [F["E1"]])
                        self.cp("pool", S["g"][:], F["E1"][:, 63:SG:64], [F["E1"]], [S["g"]])
                        self.tt("pool", V3(F["t0"][:], nch), bc_last(F["gc"][:, 63:SG:64], 64), V3(F["gc"][:], nch), ALU.subtract, [F["gc"]], [F["t0"]])
                        self.act(F["Es"][:], F["t0"][:], AF.Exp, [F["t0"]], [F["Es"]])
                        yield
                        self.tt("pool", V3(F["t0"][:], nch), V3(F["gc"][:], nch), bc_mid(kc[:, K_SEL:K_SEL + 64], nch), ALU.mult, [F["gc"], kc], [F["t0"]])
                        self.c.op("dve", lambda: nc.vector.reduce_sum(out=gcT[:], in_=V3(F["t0"][:], nch), axis=AX.X), [F["t0"]], [gcT])
                        self.tt("dve", V3(F["t0"][:], nch), V3(F["gc"][:], nch), bc_last(gcT[:], 64), ALU.subtract, [F["gc"], gcT], [F["t0"]])
                        yield
                        self.ts("dve", F["nbk"][:], F["t0"][:], 0.0, None, ALU.max, None, [F["t0"]], [F["nbk"]])
                        self.ts("dve", F["t0"][:], F["t0"][:], 0.0, None, ALU.min, None, [F["t0"]], [F["t0"]])
                        self.act(F["Ex"][:], F["t0"][:], AF.Exp, [F["t0"]], [F["Ex"]])
                        yield
                        self.tt("dve", Ms[:], V3(F["Ex"][:], nch), bc_mid(kc[:, K_MUS:K_MUS + 64], nch), ALU.mult, [F["Ex"], kc], [Ms])
                        self.tt("dve", Mi[:], V3(F["Ex"][:], nch), bc_mid(kc[:, K_MUI:K_MUI + 64], nch), ALU.mult, [F["Ex"], kc], [Mi])
                        yield
                        self.act(F["Ex"][:], F["nbk"][:], AF.Exp, [F["nbk"]], [F["Ex"]], scale=-1.0)
                        self.tt("dve", Ml[:], V3(F["Ex"][:], nch), bc_mid(kc[:, K_MLS:K_MLS + 64], nch), ALU.mult, [F["Ex"], kc], [Ml])
                        yield
                        kS, qS, vS = sa(QKV["k"]), sa(QKV["q"]), sa(QKV["v"])
                        self.stt(F["nbk"][:], kS, -1.0, F["b"][:], ALU.mult, ALU.mult, [QKV["k"], F["b"]], [F["nbk"]])
                        self.to_bd(lambda o_ap, lo, hi: self.cp("act", o_ap, V3(F["nbk"][lo:hi, :], nch), [F["nbk"]], [PAR]), PAR, nch)
                        yield
                        self.to_bd(lambda o_ap, lo, hi: self.tt("dve", o_ap, V3(F["nbk"][lo:hi, :], nch), V3(F["E1"][lo:hi, :], nch), ALU.mult,
                                                                [F["nbk"], F["E1"]], [BDZA]), BDZA, nch)
                        self.cp("pool", PAR[:, :, 128:192], V3(qS, nch) if d == 0 else qS.rearrange("p (a b) -> p a b", a=nch), [QKV["q"]], [PAR])
                        yield
                        self.tt("pool", V3(ZRt[:], nch), PAR[:, :, 128:192], V3(F["E1"][:], nch), ALU.mult, [PAR, F["E1"]], [ZRt])
                        self.cp("act", F["t0"][:], kS, [QKV["k"]], [F["t0"]])
                        yield
                        self.to_bd(lambda o_ap, lo, hi: self.cp("act", o_ap, V3(F["t0"][lo:hi, :], nch), [F["t0"]], [BDK]), BDK, nch)
                        self.to_bd(lambda o_ap, lo, hi: self.tt("dve", o_ap, V3(F["t0"][lo:hi, :], nch), V3(F["Es"][lo:hi, :], nch), ALU.mult,
                                                                [F["t0"], F["Es"]], [BDSK]), BDSK, nch)
                        yield
                        self.tt("dve", F["nbk"][:], vS, F["b"][:], ALU.mult, [QKV["v"], F["b"]], [F["nbk"]])
                        self.to_bd(lambda o_ap, lo, hi: self.cp("act", o_ap, V3(F["nbk"][lo:hi, :], nch), [F["nbk"]], [BDV]), BDV, nch)
                        yield

                    def mk_o(sg, S, d=d):
                        tau0 = sg * SG
                        PAR, BDZA, BDK, BDSK, BDV, ZRt, Ms, Mi, Ml, G_ = (S["PAR"], S["ZA"], S["K"], S["SK"], S["V"], S["ZR"], S["Ms"], S["Mi"],
                                                                          S["Ml"], S["g"])

                        def yout(ci, pY):
                            a0_ = tau0 + ci * 64
                            if d == 0:
                                self.cp("dve", Yacc[:, a0_:a0_ + 64], pY[:], [pY], [Yacc])
                            else:
                                ry = rev(Yacc[:, Tn - a0_ - 64:Tn - a0_])
                                self.tt("dve", ry, ry, pY[:], ALU.add, [Yacc, pY], [Yacc])
                        return dict(PAR=lambda ci: (PAR[:, ci, :], PAR), QB=lambda ci: (BDK[:, ci, :], BDK), QK=lambda ci: (BDK[:, ci, :], BDK),
                                    ZA=lambda ci: (BDZA[:, ci, :], BDZA), ZR=lambda ci: (ZRt[:, ci * 64:ci * 64 + 64], ZRt),
                                    SB=lambda ci: (BDSK[:, ci, :], BDSK), SK=lambda ci: (BDSK[:, ci, :], BDSK), V=lambda ci: (BDV[:, ci, :], BDV),
                                    ms=lambda ci: (Ms[:, ci, :], Ms), mi=lambda ci: (Mi[:, ci, :], Mi), ml=lambda ci: (Ml[:, ci, :], Ml),
                                    gZ=lambda ci: (G_[:, ci:ci + 1], G_), gS=lambda ci: None, yout=yout, same=True)

                    for _ in prep(0, OS[0]):
                        pass
                    for sg in range(NSG):
                        side = prep(sg + 1, OS[(sg + 1) % 2]) if sg + 1 < NSG else None
                        self.dplr_run(sets, st, mk_o(sg, OS[sg % 2]), nch, side=side)
                with ExitStack() as e2:
                    Wz = c.sb([128, 8, 128], BF16, "Wz", e2)
                    stz = c.sb([128, 8, 128], F32, "stz", e2)
                    self.load_w(Wz, 0, self.win(l, C_Z + 768 + pr * 128, 128), 128, stz)
                    ob = [c.sb([128, SG], BF16, "gob", e2) for _ in range(2)]
                    for sg in range(NSG):
                        t0 = sg * SG
                        Y = Yacc[:, t0:t0 + SG]
                        self.tt("pool", F["t0"][:], Y, Y, ALU.mult, [Yacc], [F["t0"]])
                        p_ = nextp()
                        self.mm(p_[:, 0:SG], BOf, F["t0"][:], [kc, F["t0"]], [p_])
                        self.rsqrt_(F["Ex"][:], p_[:, 0:SG], 1.0 / 64, self.epsc[:, 0:1], [p_, self.epsc], [F["Ex"]])
                        self.stt(F["t0"][:], Y, gnw[:, 0:1], F["Ex"][:], ALU.mult, ALU.mult, [Yacc, gnw, F["Ex"]], [F["t0"]])
                        p_ = nextp()
                        self.proj(p_, Wz, 0, 128, t0, SG)
                        self.act(F["Ex"][:], p_[:, 0:SG], AF.Silu, [p_], [F["Ex"]])
                        o_b = ob[sg % 2]
                        self.tt("pool", o_b[:], F["t0"][:], F["Ex"][:], ALU.mult, [F["t0"], F["Ex"]], [o_b])
                        r0 = 768 + pr * 128
                        self.c.dma("sp", self.mixT.h[r0:r0 + 128, t0:t0 + SG], o_b[:], o_b, reads=[o_b], writes=[self.mixT])
                    c.barrier()
            c.barrier()

    def ssd(self, l, s):
        c, nc, Tn = self.c, self.nc, self.Tn
        L = 128
        NC_ = Tn // L
        TW = min(512, Tn)
        kc, kb = self.kc, self.kb
        IDb = kb[:, K_ID:K_ID + 128]
        w = self.w
        with ExitStack() as es:
            XC = [c.sb([128, Tn], BF16, "xc%d" % j, es) for j in range(6)]
            Yacc = [c.sb([128, Tn], F32, "Yc%d" % p, es) for p in range(2)]
            banks = [c.ps([128, 512], F32, "sbk", es) for _ in range(6)]

            def sl(b, lo, n):
                t = c.sub(banks[b][:, lo:lo + n])
                t.dep = banks[b].dep
                return t
            pY = [sl(0, 0, 128), sl(1, 0, 128)]
            pCB = [sl(2, 0, 128), sl(2, 128, 128)]
            pRep4 = c.sub(V3(banks[3][:, 0:512], 4))
            pRep4.dep = banks[3].dep
            pTb = c.sub(V3(banks[4][:, 0:256].bitcast(BF16), 4))
            pTb.dep = banks[4].dep
            pTf = sl(4, 256, 16)
            pZU4 = c.sub(V3(banks[5][:, 0:512], 4))
            pZU4.dep = banks[5].dep
            pPj = sl(4, 384, 128)
            pbig = [banks[3], banks[5]]
            with ExitStack() as e1:
                Wx = c.sb([128, 8, 768], BF16, "Wx", e1)
                stg = c.sb([128, 8, 128], F32, "stgx", e1)
                cw = c.sb([128, 6, 4], F32, "cwx", e1)
                cb = c.sb([128, 6], F32, "cbx", e1)
                self.ld(cb, cb[:], w["m_conv_b"].h[l, :].rearrange("(j p) -> p j", p=128), allow_slow_non_contiguous=True)
                for j in range(6):
                    self.load_w(Wx, j * 128, self.win(l, C_XBC + j * 128, 128), 128, stg)
                    self.ld(cw, cw[:, j, :], w["m_conv_w"].h[l, :, j * 128:(j + 1) * 128].rearrange("i p -> p i"), allow_slow_non_contiguous=True)
                raw = c.sb([128, Tn + 3], BF16, "rawx", e1)
                self.memset("pool", raw[:, 0:1], 0.0, [raw])
                self.memset("pool", raw[:, Tn + 1:Tn + 3], 0.0, [raw])
                acc = c.sb([128, TW], F32, "accx", e1)
                for j in range(6):
                    for tt_ in range(Tn // TW):
                        p_ = pbig[tt_ % 2]
                        self.proj(p_, Wx, j * 128, 128, tt_ * TW, TW)
                        self.cp(self.eng2(), raw[:, 1 + tt_ * TW:1 + (tt_ + 1) * TW], p_[:, 0:TW], [p_], [raw])
                    for tt_ in range(Tn // TW):
                        t0 = tt_ * TW
                        self.ts("dve", acc[:], raw[:, t0:t0 + TW], cw[:, j, 0:1], cb[:, j:j + 1], ALU.mult, ALU.add, [raw, cw, cb], [acc])
                        for i in range(1, 4):
                            self.stt(acc[:], raw[:, t0 + i:t0 + i + TW], cw[:, j, i:i + 1], acc[:], ALU.mult, ALU.add, [raw, cw, acc], [acc])
                        self.act(XC[j][:, t0:t0 + TW], acc[:], AF.Silu, [acc], [XC[j]])
                c.barrier()
            Wd = c.sb([128, 8, 8], BF16, "Wd", es)
            with ExitStack() as e1:
                stg = c.sb([128, 8, 8], F32, "stgd", e1)
                self.load_w(Wd, 0, self.win(l, C_MDT, 8), 8, stg)
                c.barrier()
            sm_t = c.sb([8, 8, 128], F32, "selm", es)
            self.ld(sm_t, sm_t[:], self.selm.h[:])
            dtb = self.col(es, w["m_dt_bias"].h[l].rearrange("d h -> (d h)"), "mdtb", 8)
            nA = self.col(es, w["m_a_log"].h[l].rearrange("d h -> (d h)"), "mnA", 8)
            self.act(nA[:], nA[:], AF.Exp, [nA], [nA])
            self.ts("pool", nA[:], nA[:], -1.0, None, ALU.mult, None, [nA], [nA])
            ones8 = c.sb([8, L], F32, "ones8", es)
            self.memset("pool", ones8[:], 1.0, [ones8])
            dt8 = c.sb([8, L], F32, "dt8", es)
            a8 = c.sb([8, L], F32, "a8", es)
            acs8 = c.sb([8, L], F32, "acs8", es)
            TT_ = c.sb([128, 16], F32, "TT", es)
            xsT = c.sb([128, 256], BF16, "xsT", es)
            BT = c.sb([128, 2, 128], BF16, "BT", es)
            CBs = [c.sb([128, 128], F32, "CBs", es) for _ in range(2)]
            dd = c.sb([128, 128], F32, "dd", es)
            Em = c.sb([128, 128], F32, "Em", es)
            SD = [c.sb([128, 128], BF16, "SD", es) for _ in range(4)]
            Er = c.sb([128, 128], F32, "Er", es)
            Ce = [c.sb([128, 128], BF16, "Ce", es) for _ in range(4)]
            cols = c.sb([128, 4, 4], F32, "cols", es)
            Xp = [c.sb([128, 128], BF16, "Xp%d" % i, es) for i in range(4)]
            Xe = [c.sb([128, 128], BF16, "Xe%d" % i, es) for i in range(4)]
            Zf = [c.sb([128, 128], F32, "sZf%d" % h, es) for h in range(4)]
            Zb = [c.sb([128, 128], BF16, "sZb%d" % h, es) for h in range(4)]
            for i in range(4):
                self.memset("pool", Xp[i][:], 0.0, [Xp[i]])
                self.memset("pool", Xe[i][:], 0.0, [Xe[i]])
            for d in range(2):
                for h in range(4):
                    self.memset("pool", Zf[h][:], 0.0, [Zf[h]])
                    self.memset("pool", Zb[h][:], 0.0, [Zb[h]])
                mask = kc[:, K_MU128:K_MU128 + 128] if d == 0 else kc[:, K_ML128:K_ML128 + 128]
                last = L - 1 if d == 0 else 0
                order = range(NC_) if d == 0 else range(NC_ - 1, -1, -1)
                for ci in order:
                    t0 = ci * L
                    tk = slice(t0, t0 + L)
                    self.proj(pPj, Wd, 0, 8, t0, L)
                    self.act(dt8[:], pPj[0:8, 0:L], AF.Exp, [pPj, dtb], [dt8], bias=dtb[:, 0:1])
                    self.act(dt8[:], dt8[:], AF.Ln, [dt8], [dt8], bias=1.0)
                    self.ts("dve", a8[:], dt8[:], nA[:, 0:1], None, ALU.mult, None, [dt8, nA], [a8])
                    if d == 0:
                        self.c.op("dve", lambda: nc.vector.tensor_tensor_scan(out=acs8[:], data0=ones8[:], data1=a8[:], initial=0.0,
                                                                                op0=ALU.mult, op1=ALU.add), [ones8, a8], [acs8])
                    else:
                        self.c.op("dve", lambda: nc.vector.tensor_tensor_scan(out=rev(acs8[:]), data0=ones8[:], data1=rev(a8[:]), initial=0.0,
                                                                                op0=ALU.mult, op1=ALU.add), [ones8, a8], [acs8])
                    self.mm(pTf[:, 0:8], dt8[:], kc[0:8, K_ID:K_ID + 8], [dt8, kc], [pTf])
                    self.mm(pTf[:, 8:16], acs8[:], kc[0:8, K_ID:K_ID + 8], [acs8, kc], [pTf])
                    self.cp("act", TT_[:], pTf[:], [pTf], [TT_])
                    for g in range(2):
                        self.mm(pCB[g][:], XC[2 + g][:, tk], XC[4 + g][:, tk], [XC[2 + g], XC[4 + g]], [pCB[g]])
                        self.cp("act", CBs[g][:], pCB[g][:], [pCB[g]], [CBs[g]])
                    for j in range(2):
                        self.tr(pTb[:, j, :], XC[j][:, tk], IDb, [XC[j], kb], [pTb])
                        self.tr(pTb[:, 2 + j, :], XC[2 + j][:, tk], IDb, [XC[2 + j], kb], [pTb])
                    self.cp("dve", V3(xsT[:], 2), pTb[:, 0:2, :], [pTb], [xsT])
                    self.cp("act", BT[:], pTb[:, 2:4, :], [pTb], [BT])
                    for h in range(4):
                        self.mm(pRep4[:, h, :], sm_t[:, d * 4 + h, :], acs8[:], [sm_t, acs8], [pRep4])
                    for h in range(4):
                        g, pr, hh = h // 2, h // 2, h % 2
                        dh = d * 4 + h
                        self.ts("dve", dd[:], pRep4[:, h, :], TT_[:, 8 + dh:9 + dh], 0.0, ALU.subtract, ALU.min, [pRep4, TT_], [dd])
                        self.act(Em[:], dd[:], AF.Exp, [dd], [Em])
                        self.tt("pool", Em[:], Em[:], mask, ALU.mult, [Em, kc], [Em])
                        self.tt("pool", SD[h][:], CBs[g][:], Em[:], ALU.mult, [CBs[g], Em], [SD[h]])
                        self.act(Er[:], pRep4[:, h, :], AF.Exp, [pRep4], [Er])
                        self.tt("pool", Ce[h][:], XC[4 + g][:, tk], Er[:], ALU.mult, [XC[4 + g], Er], [Ce[h]])
                        self.cp("dve", cols[:, h, 0:1], pRep4[:, h, last:last + 1], [pRep4], [cols])
                        self.act(cols[:, h, 1:2], TT_[:, 8 + dh:9 + dh], AF.Exp, [TT_, cols], [cols], scale=-1.0, bias=cols[:, h, 0:1])
                        self.act(cols[:, h, 2:3], cols[:, h, 0:1], AF.Exp, [cols], [cols])
                        xs_h = xsT[:, h * 64:(h + 1) * 64]
                        self.ts("dve", Xp[h][:, hh * 64:(hh + 1) * 64], xs_h, TT_[:, dh:dh + 1], None, ALU.mult, None, [xsT, TT_], [Xp[h]])
                        self.ts("dve", Xe[h][:, hh * 64:(hh + 1) * 64], Xp[h][:, hh * 64:(hh + 1) * 64], cols[:, h, 1:2], None, ALU.mult, None,
                                [Xp[h], cols], [Xe[h]])
                    for h in range(4):
                        g, pr, hh = h // 2, h // 2, h % 2
                        self.mm(pY[pr][:], Xp[h][:], SD[h][:], [Xp[h], SD[h]], [pY[pr]], start=(hh == 0), stop=False)
                        self.mm(pY[pr][:], Zb[h][:], Ce[h][:], [Zb[h], Ce[h]], [pY[pr]], start=False, stop=(hh == 1))
                    for h in range(4):
                        self.mm(pZU4[:, h, :], BT[:, h // 2, :], Xe[h][:], [BT, Xe[h]], [pZU4])
                    for pr in range(2):
                        if d == 0:
                            self.cp("act", Yacc[pr][:, tk], pY[pr][:], [pY[pr]], [Yacc[pr]])
                        else:
                            self.tt("dve", Yacc[pr][:, tk], Yacc[pr][:, tk], pY[pr][:], ALU.add, [Yacc[pr], pY[pr]], [Yacc[pr]])
                    for h in range(4):
                        self.stt(Zf[h][:], Zf[h][:], cols[:, h, 2:3], pZU4[:, h, :], ALU.mult, ALU.add, [Zf[h], cols, pZU4], [Zf[h]])
                        self.cp("act", Zb[h][:], Zf[h][:], [Zf[h]], [Zb[h]])
            with ExitStack() as e2:
                Wz = c.sb([128, 8, 256], BF16, "Wzc", e2)
                stz = c.sb([128, 8, 256], F32, "stzc", e2)
                self.load_w(Wz, 0, self.win(l, C_Z + 512, 256), 256, stz)
                Dc = c.sb([128, 2], F32, "Dc", e2)
                nw = c.sb([128, 2], F32, "nw", e2)
                for pr in range(2):
                    for hh in range(2):
                        self.ld(Dc, Dc[hh * 64:(hh + 1) * 64, pr:pr + 1], w["m_d"].h[l, pr * 2 + hh:pr * 2 + hh + 1].partition_broadcast(64))
                self.ld(nw, nw[:], w["m_norm_w"].h[l, :].rearrange("(j p) -> p j", p=128), allow_slow_non_contiguous=True)
                gt = [c.sb([128, TW], F32, "gt%d" % p, e2) for p in range(2)]
                sq = c.sb([128, TW], F32, "sqc", e2)
                sz = c.sb([128, TW], F32, "szc", e2)
                rs = c.sb([128, TW], F32, "rsc", e2)
                ob = [c.sb([128, 2, TW], BF16, "cob", e2) for _ in range(2)]
                for tt_ in range(Tn // TW):
                    t0 = tt_ * TW
                    for pr in range(2):
                        self.stt(gt[pr][:], XC[pr][:, t0:t0 + TW], Dc[:, pr:pr + 1], Yacc[pr][:, t0:t0 + TW], ALU.mult, ALU.add,
                                 [XC[pr], Dc, Yacc[pr]], [gt[pr]])
                        p_ = banks[2 + pr]
                        self.proj(p_, Wz, pr * 128, 128, t0, TW)
                        self.act(sz[:], p_[:, 0:TW], AF.Silu, [p_], [sz])
                        self.tt("pool", gt[pr][:], gt[pr][:], sz[:], ALU.mult, [gt[pr], sz], [gt[pr]])
                        self.tt("pool", sq[:], gt[pr][:], gt[pr][:], ALU.mult, [gt[pr]], [sq])
                        self.mm(banks[0][:, 0:TW], kc[:, K_ONE:K_ONE + 128], sq[:], [kc, sq], [banks[0]], start=(pr == 0), stop=(pr == 1))
                    self.rsqrt_(rs[:], banks[0][:, 0:TW], 1.0 / 256, self.epsc[:, 0:1], [banks[0], self.epsc], [rs])
                    o_b = ob[tt_ % 2]
                    for pr in range(2):
                        self.stt(o_b[:, pr, :], gt[pr][:], nw[:, pr:pr + 1], rs[:], ALU.mult, ALU.mult, [gt[pr], nw, rs], [o_b])
                    self.c.dma("sp", self.mixT.h[512:768, t0:t0 + TW].rearrange("(a p) t -> p a t", p=128), o_b[:], o_b, reads=[o_b], writes=[self.mixT])
                c.barrier()
            c.barrier()

    def zero_mix(self, i):
        c, Tn = self.c, self.Tn
        TW = min(512, Tn)
        with ExitStack() as es:
            z = c.sb([128, 2, TW], BF16, "zmix", es)
            self.memset("pool", z[:], 0.0, [z])
            for j in range(Tn // TW):
                self.c.dma("sp", self.mixT.h[i * 256:(i + 1) * 256, j * TW:(j + 1) * TW].rearrange("(a p) t -> p a t", p=128), z[:], z,
                           reads=[z], writes=[self.mixT])
            c.barrier()

    def build(self):
        self.declare()
        self.consts()
        for l in range(self.DEPTH):
            self.p0(l)
            for s in range(self.NS):
                self.p1(l, s)
                if "A" in self.mixers:
                    self.mla(l, s)
                if "B" in self.mixers:
                    self.rwkv(l, s)
                if "C" in self.mixers:
                    self.ssd(l, s)
                if "D" in self.mixers:
                    self.gdn(l, s)
                for i, m in enumerate("ABCD"):
                    if m not in self.mixers:
                        self.zero_mix(i)
                self.p6(l, s)
        self.c.finish([self.y] + ([self.mixT] if self.dbg else []))
        return self.nc


def make_consts(Tn):
    p = np.arange(128)
    kc = np.zeros((128, K_END), np.float32)
    kc[:, K_ID:K_ID + 128] = np.eye(128)
    col = np.arange(64)
    kc[:, K_MUS:K_MUS + 64] = ((p % 64)[:, None] < col[None, :])
    kc[:, K_MUI:K_MUI + 64] = ((p % 64)[:, None] <= col[None, :])
    kc[:, K_BO:K_BO + 128] = ((p // 64)[:, None] == (p // 64)[None, :])
    kc[:, K_ONE:K_ONE + 128] = 1.0
    kc[:, K_SEL:K_SEL + 64] = ((p % 64)[:, None] == col[None, :])
    kc[:, K_MU128:K_MU128 + 128] = (p[:, None] <= p[None, :])
    kc[:, K_ML128:K_ML128 + 128] = (p[:, None] >= p[None, :])
    kc[:, K_MLS:K_MLS + 64] = ((p % 64)[:, None] > col[None, :])
    inv = 1.0 / (10000.0 ** (np.arange(0, 32, 2, dtype=np.float32) / np.float32(32)))
    ang = np.arange(Tn, dtype=np.float32)[None, :] * inv.astype(np.float32)[:, None]
    rope = np.zeros((96, 2, Tn), np.float32)
    rope[0:64, 0, :] = 1.0
    rope[64:80, 0, :] = np.cos(ang)
    rope[80:96, 0, :] = np.cos(ang)
    rope[64:80, 1, :] = np.sin(ang)
    rope[80:96, 1, :] = np.sin(ang)
    selg = np.zeros((8, 4, 128), np.float32)
    for d in range(2):
        for pr in range(2):
            for hh in range(2):
                selg[d * 4 + pr * 2 + hh, d * 2 + pr, hh * 64:(hh + 1) * 64] = 1.0
    selm = np.zeros((8, 8, 128), np.float32)
    for r in range(8):
        selm[r, r, :] = 1.0
    return dict(kconst=kc, rope=rope, selg=selg, selm=selm)


WNAMES = ["w_ada", "b_ada", "g_pre", "g_post", "w_in", "w_out", "a_g_q", "a_w_uq", "a_g_kv", "a_w_ukv", "rw_mu", "rw_w0",
          "rw_w2", "rw_a0", "rw_a2", "rw_k_k", "rw_k_a", "rw_r_k", "rw_ln_w", "rw_ln_b", "m_conv_w", "m_conv_b",
          "m_dt_bias", "m_a_log", "m_d", "m_norm_w", "gd_conv_w", "gd_dt_bias", "gd_a_log", "gd_norm_w"]


def kernel(x_prompt, x_sample, c_prompt, c_sample, **weights):
    x_prompt = np.asarray(x_prompt, np.float32)
    x_sample = np.asarray(x_sample, np.float32)
    Tn = x_prompt.shape[1]
    DEPTH = np.asarray(weights["w_ada"]).shape[0]
    xs = np.concatenate([x_prompt, x_sample], axis=0)
    cs = np.concatenate([np.asarray(c_prompt, np.float32), np.asarray(c_sample, np.float32)], axis=0)
    nseq = xs.shape[0]
    NS = -(-nseq // N_CORES)
    b = Builder(Tn, NS, DEPTH)
    nc = b.build()
    consts = make_consts(Tn)
    wmap = {k: np.ascontiguousarray(np.asarray(weights[k], np.float32)) for k in WNAMES}
    in_maps = []
    for core in range(N_CORES):
        idx = [(core * NS + j) if (core * NS + j) < nseq else 0 for j in range(NS)]
        m = dict(wmap)
        m.update(consts)
        m["x"] = np.ascontiguousarray(xs[idx])
        m["c"] = np.ascontiguousarray(cs[idx])
        in_maps.append(m)
    res = run_bass_kernel_spmd(nc, in_maps, core_ids=list(range(N_CORES)))
    out = np.zeros_like(xs)
    for core in range(N_CORES):
        yc = res.results[core]["y"]
        for j in range(NS):
            sidx = core * NS + j
            if sidx < nseq:
                out[sidx] = yc[j]
    nb = x_prompt.shape[0]
    return (out[:nb], out[nb:])
```

```python
import math
import numpy as np
import concourse.bass as bass
import concourse.mybir as mybir
from concourse.bass_utils import run_bass_kernel_spmd
from contextlib import ExitStack

F32 = mybir.dt.float32
BF16 = mybir.dt.bfloat16
F32R = mybir.dt.float32r
AF = mybir.ActivationFunctionType
ALU = mybir.AluOpType
AX = mybir.AxisListType

D_MODEL = 1024
P_IN = 4024
N_CORES = 8
EPS = 1e-6
EMBED_WAITS = 1
C_CQ, C_CKV, C_KR, C_RKV, C_WAF, C_WAB, C_XBC, C_MDT, C_QKV, C_GDB, C_GDA, C_Z = (
    0, 384, 512, 544, 1312, 1376, 1440, 2208, 2216, 2984, 2992, 3000)
K_ID, K_MUS, K_MUI, K_BO, K_ONE, K_SEL, K_MU128, K_ML128, K_MLS, K_END = 0, 128, 192, 256, 384, 512, 576, 704, 832, 896


class Dep:
    __slots__ = ("w", "r", "name", "excl")

    def __init__(self, name=""):
        self.w = {}
        self.r = {}
        self.name = name
        self.excl = False


class T:
    def __init__(self, h, name, dep=None):
        self.h = h
        self.name = name
        self.dep = dep or Dep(name)
        self.dsem = None
        self.dcount = 0

    def __getitem__(self, k):
        return self.h[k]


class Ctx:
    def __init__(self, nc):
        self.nc = nc
        self.E = {"pe": nc.tensor, "act": nc.scalar, "dve": nc.vector, "pool": nc.gpsimd, "sp": nc.sync}
        self.sem, self.cnt, self.known = {}, {}, {}
        self.stack = ExitStack()
        for e in self.E:
            self.sem[e] = self.stack.enter_context(nc.semaphore("s_" + e))
            self.cnt[e] = 0
            self.known[e] = {}
        self.uid = 0
        self.ninst = 0
        self.dtiles = []
        self.free_dsems = []
        self.all_dsems = []
        self.pending = None

    def sb(self, shape, dt=F32, name=None, stack=None):
        self.uid += 1
        name = (name or "t") + "_%d" % self.uid
        h = (stack or self.stack).enter_context(self.nc.sbuf_tensor(name, list(shape), dt))
        return T(h, name)

    def ps(self, shape, dt=F32, name=None, stack=None):
        self.uid += 1
        name = (name or "p") + "_%d" % self.uid
        esz = 4 if dt == F32 else 2
        n = 1
        for v in shape[1:]:
            n *= v
        per_bank = 2048 // esz
        nb = -(-n // per_bank)
        h = (stack or self.stack).enter_context(self.nc.psum_tensor(name, [128, nb * per_bank], dt))
        v = h[0:shape[0], 0:n]
        if len(shape) == 3:
            v = v.rearrange("p (a b) -> p a b", a=shape[1])
        t = T(v, name)
        t.dep.excl = True
        return t

    def sub(self, ap, name="s"):
        self.uid += 1
        return T(ap, name + "_%d" % self.uid)

    def dram(self, name, shape, dt=F32, kind="Internal"):
        return T(self.nc.dram_tensor(name, list(shape), dt, kind=kind), name)

    def _wait(self, eng, sem, val):
        k = self.known[eng]
        sid = id(sem)
        if k.get(sid, 0) >= val:
            return
        k[sid] = val
        if self.pending is not None:
            self.pending.append((sem, val))
            return
        self.E[eng].wait_ge(sem, val)
        self.ninst += 1

    def _acquire(self, eng, reads, writes):
        own = id(self.sem[eng])
        for d in reads:
            for sid, (sem, val) in d.dep.w.items():
                self._wait(eng, sem, val)
            if d.dep.excl:
                for sid, (sem, val) in d.dep.r.items():
                    if sid != own:
                        self._wait(eng, sem, val)
        for d in writes:
            for sid, (sem, val) in d.dep.w.items():
                if sid != own or eng != "pe":
                    self._wait(eng, sem, val)
            for sid, (sem, val) in d.dep.r.items():
                self._wait(eng, sem, val)

    def _flush(self, eng, ins=None):
        p, self.pending = self.pending, None
        last = []
        if ins is not None and EMBED_WAITS:
            while p and len(last) < EMBED_WAITS:
                last.append(p.pop())
        for (sem, val) in p:
            self.E[eng].wait_ge(sem, val)
            self.ninst += 1
        return last

    def _record(self, sem, val, reads, writes):
        sid = id(sem)
        for d in reads:
            d.dep.r[sid] = (sem, val)
        for d in writes:
            d.dep.w[sid] = (sem, val)

    def op(self, eng, fn, reads=(), writes=()):
        self.pending = []
        self._acquire(eng, reads, writes)
        last = self._flush(eng, True)
        ins = fn()
        for (sem_, val_) in last:
            ins._wait_ge(sem_, val_)
        self.cnt[eng] += 1
        ins.then_inc(self.sem[eng], 1)
        self.ninst += 1
        self._record(self.sem[eng], self.cnt[eng], reads, writes)
        return ins

    def dma(self, q, out_ap, in_ap, sbt, reads=(), writes=(), **kw):
        ds = sbt.dsem
        if ds is None:
            if self.free_dsems:
                ds = self.free_dsems.pop()
            else:
                ds = [self.stack.enter_context(self.nc.semaphore("d%d" % len(self.all_dsems))), 0]
                self.all_dsems.append(ds)
            sbt.dsem = ds
            self.dtiles.append(sbt)
        if ds[1]:
            self._wait(q, ds[0], 16 * ds[1])
        self._acquire(q, reads, writes)
        ins = self.E[q].dma_start(out=out_ap, in_=in_ap, **kw)
        ds[1] += 1
        ins.then_inc(ds[0], 16)
        self.ninst += 1
        self._record(ds[0], 16 * ds[1], reads, writes)
        return ins

    def finish(self, out_tiles):
        for t in out_tiles:
            for sid, (sem, val) in t.dep.w.items():
                self._wait("sp", sem, val)

    def barrier(self):
        for e in self.E:
            for f in self.E:
                if self.cnt[f]:
                    self._wait(e, self.sem[f], self.cnt[f])
            for ds in self.all_dsems:
                if ds[1]:
                    self._wait(e, ds[0], 16 * ds[1])
        for t in self.dtiles:
            self.free_dsems.append(t.dsem)
            t.dsem = None
        self.dtiles = []


def V3(ap, a):
    return ap.rearrange("p (a b) -> p a b", a=a)


def bc_mid(ap2, n):
    return ap2.unsqueeze(1).to_broadcast([ap2.shape[0], n, ap2.shape[1]])


def bc_last(ap2, n):
    return ap2.unsqueeze(2).to_broadcast([ap2.shape[0], ap2.shape[1], n])


def rev(ap):
    pat = [list(x) for x in ap.ap]
    st, n = pat[-1]
    pat[-1] = [-st, n]
    return bass.AP(ap.tensor, ap.offset + st * (n - 1), pat)


class Builder:
    def __init__(self, Tn, NS, DEPTH, mixers="ABCD", dbg=False):
        self.Tn, self.NS, self.DEPTH, self.mixers, self.dbg = Tn, NS, DEPTH, mixers, dbg
        self.nc = bass.Bass("TRN2", target_bir_lowering=False)
        self.c = Ctx(self.nc)
        self.rr = 0

    def eng2(self):
        self.rr ^= 1
        return "dve" if self.rr else "act"

    def cp(self, eng, out, in_, R, W):
        nc = self.nc
        if eng == "act":
            return self.c.op("act", lambda: nc.scalar.copy(out=out, in_=in_), R, W)
        e = nc.vector if eng == "dve" else nc.gpsimd
        return self.c.op(eng, lambda: e.tensor_copy(out=out, in_=in_), R, W)

    def tt(self, eng, out, a, b, op, R, W):
        e = self.nc.vector if eng == "dve" else self.nc.gpsimd
        return self.c.op(eng, lambda: e.tensor_tensor(out=out, in0=a, in1=b, op=op), R, W)

    def ts(self, eng, out, a, s1, s2, op0, op1, R, W):
        e = self.nc.vector if eng == "dve" else self.nc.gpsimd
        if s2 is None:
            return self.c.op(eng, lambda: e.tensor_scalar(out=out, in0=a, scalar1=s1, scalar2=None, op0=op0), R, W)
        return self.c.op(eng, lambda: e.tensor_scalar(out=out, in0=a, scalar1=s1, scalar2=s2, op0=op0, op1=op1), R, W)

    def stt(self, out, a, s, b, op0, op1, R, W):
        nc = self.nc
        return self.c.op("dve", lambda: nc.vector.scalar_tensor_tensor(out=out, in0=a, scalar=s, in1=b, op0=op0, op1=op1), R, W)

    def act(self, out, in_, func, R, W, **kw):
        nc = self.nc
        return self.c.op("act", lambda: nc.scalar.activation(out=out, in_=in_, func=func, **kw), R, W)

    def mm(self, out, lhsT, rhs, R, W, start=True, stop=True, skip=False):
        nc = self.nc
        if skip:
            return self.c.op("pe", lambda: nc.tensor.matmul(out, lhsT=lhsT, rhs=rhs, start=False, stop=False, skip_group_check=True), R, W)
        return self.c.op("pe", lambda: nc.tensor.matmul(out, lhsT=lhsT, rhs=rhs, start=start, stop=stop), R, W)

    def tr(self, out, in_, ident, R, W):
        nc = self.nc
        return self.c.op("pe", lambda: nc.tensor.transpose(out=out, in_=in_, identity=ident), R, W)

    def memset(self, eng, ap, val, W):
        e = self.nc.vector if eng == "dve" else self.nc.gpsimd
        return self.c.op(eng, lambda: e.memset(ap, val), (), W)

    def ld(self, dst_tile, dst_ap, src_ap, R=(), **kw):
        return self.c.dma("sp", dst_ap, src_ap, dst_tile, reads=R, writes=[dst_tile], **kw)

    def rsqrt_(self, out, in_, scale, eps_ap, R, W):
        self.act(out, in_, AF.Ln, R, W, scale=scale, bias=eps_ap)
        self.act(out, out, AF.Exp, W, W, scale=-0.5)

    def load_w(self, dst, dcol, src3, n, st, neg=False, eng=None):
        KC = src3.shape[1]
        self.ld(st, st[:, 0:KC, 0:n], src3)
        eng = eng or self.eng2()
        if neg:
            nc = self.nc
            self.c.op("act", lambda: nc.scalar.mul(out=dst[:, 0:KC, dcol:dcol + n], in_=st[:, 0:KC, 0:n], mul=-1.0), [st], [dst])
        else:
            self.cp(eng, dst[:, 0:KC, dcol:dcol + n], st[:, 0:KC, 0:n], [st], [dst])

    def win(self, l, c0, n):
        return self.w["w_in"].h[l, :, c0:c0 + n].rearrange("(kc p) n -> p kc n", p=128)

    def hT_ap(self, kc, tau0, n, d=0, shift=0):
        Tn = self.Tn
        if d == 0:
            col = 1 + tau0 - shift
            return self.hT[:, kc, col:col + n]
        lo = Tn - tau0 - n + 1 + shift
        return rev(self.hT[:, kc, lo:lo + n])

    def proj(self, out_ps, Wt, wc, M, tau0, n, R_extra=(), d=0, W2=None, w2c=0):
        steps = [(Wt, wc, 0)] + ([(W2, w2c, 1)] if W2 is not None else [])
        tot = 8 * len(steps)
        i = 0
        for (Wx, cc, sh) in steps:
            for kc in range(8):
                self.mm(out_ps[0:M, 0:n], Wx[:, kc, cc:cc + M], self.hT_ap(kc, tau0, n, d, sh),
                        [Wx, self.hT_t], [out_ps], start=(i == 0), stop=(i == tot - 1))
                i += 1

    def declare(self):
        c, Tn, NS, DEPTH = self.c, self.Tn, self.NS, self.DEPTH
        shapes = dict(
            w_ada=[DEPTH, 1024, 3072], b_ada=[DEPTH, 3072], g_pre=[DEPTH, 1024], g_post=[DEPTH, 1024],
            w_in=[DEPTH, 1024, P_IN], w_out=[DEPTH, 1024, 1024], a_g_q=[DEPTH, 384], a_w_uq=[DEPTH, 384, 384],
            a_g_kv=[DEPTH, 128], a_w_ukv=[DEPTH, 128, 512], rw_mu=[DEPTH, 2, 832], rw_w0=[DEPTH, 2, 256],
            rw_w2=[DEPTH, 2, 32, 256], rw_a0=[DEPTH, 2, 256], rw_a2=[DEPTH, 2, 32, 256], rw_k_k=[DEPTH, 256],
            rw_k_a=[DEPTH, 256], rw_r_k=[DEPTH, 4, 64], rw_ln_w=[DEPTH, 256], rw_ln_b=[DEPTH, 256],
            m_conv_w=[DEPTH, 4, 768], m_conv_b=[DEPTH, 768], m_dt_bias=[DEPTH, 2, 4], m_a_log=[DEPTH, 2, 4],
            m_d=[DEPTH, 4], m_norm_w=[DEPTH, 256], gd_conv_w=[DEPTH, 4, 768], gd_dt_bias=[DEPTH, 2, 4],
            gd_a_log=[DEPTH, 2, 4], gd_norm_w=[DEPTH, 64])
        self.w = {k: c.dram(k, v, F32, kind="ExternalInput") for k, v in shapes.items()}
        self.x = c.dram("x", [NS, Tn, 1024], F32, kind="ExternalInput")
        self.cin = c.dram("c", [NS, 1024], F32, kind="ExternalInput")
        self.kconst = c.dram("kconst", [128, K_END], F32, kind="ExternalInput")
        self.rope = c.dram("rope", [96, 2, Tn], F32, kind="ExternalInput")
        self.selg = c.dram("selg", [8, 4, 128], F32, kind="ExternalInput")
        self.selm = c.dram("selm", [8, 8, 128], F32, kind="ExternalInput")
        self.y = c.dram("y", [NS, Tn, 1024], F32, kind="ExternalOutput")
        self.mixT = c.dram("mixT", [1024, Tn], BF16, kind="ExternalOutput" if self.dbg else "Internal")

    def consts(self):
        c = self.c
        self.kc = c.sb([128, K_END], F32, "kc")
        self.ld(self.kc, self.kc[:], self.kconst.h[:])
        self.kb = c.sb([128, K_END], BF16, "kb")
        self.cp("dve", self.kb[:], self.kc[:], [self.kc], [self.kb])
        self.epsc = c.sb([128, 2], F32, "epsc")
        self.memset("pool", self.epsc[:, 0:1], EPS, [self.epsc])
        self.memset("pool", self.epsc[:, 1:2], 64e-5, [self.epsc])
        self.hT_t = c.sb([128, 8, self.Tn + 2], BF16, "hT")
        self.hT = self.hT_t
        self.memset("pool", self.hT[:, :, 0:1], 0.0, [self.hT_t])
        self.memset("pool", self.hT[:, :, self.Tn + 1:self.Tn + 2], 0.0, [self.hT_t])
        self.gg = [c.sb([128, 1024], F32, "gg%d" % s) for s in range(self.NS)]
        self.a0 = c.sb([128, 8, self.NS], F32, "a0")
        self.a1 = c.sb([128, 8, self.NS], F32, "a1")

    def p0(self, l):
        c, nc, NS = self.c, self.nc, self.NS
        with ExitStack() as es:
            cT = c.sb([128, 8, NS], F32, "cT", es)
            for s in range(NS):
                self.ld(cT, cT[:, :, s], self.cin.h[s, :].rearrange("(kc p) -> p kc", p=128), allow_slow_non_contiguous=True)
            scb = c.sb([128, 8, NS], BF16, "scb", es)
            self.act(scb[:], cT[:], AF.Silu, [cT], [scb])
            wada = c.sb([128, 8, 3072], BF16, "wada", es)
            st = c.sb([128, 8, 512], F32, "st0", es)
            for j in range(6):
                self.load_w(wada, j * 512, self.w["w_ada"].h[l, :, j * 512:(j + 1) * 512].rearrange("(kc p) n -> p kc n", p=128), 512, st)
            bT = c.sb([128, 24], F32, "bT", es)
            for j3 in range(3):
                self.ld(bT, bT[:, j3 * 8:(j3 + 1) * 8], self.w["b_ada"].h[l, j3 * 1024:(j3 + 1) * 1024].rearrange("(j p) -> p j", p=128),
                        allow_slow_non_contiguous=True)
            gp = c.sb([128, 8], F32, "gp", es)
            self.ld(gp, gp[:], self.w["g_pre"].h[l, :].rearrange("(j p) -> p j", p=128), allow_slow_non_contiguous=True)
            pm = c.ps([128, 16 * NS], F32, "pm", es)
            for idx in range(16):
                for kc in range(8):
                    self.mm(pm[:, idx * NS:(idx + 1) * NS], wada[:, kc, idx * 128:(idx + 1) * 128], scb[:, kc, :],
                            [wada, scb], [pm], start=(kc == 0), stop=(kc == 7))
            pm3 = V3(pm[:], 16)
            self.tt("dve", self.a0[:], pm3[:, 0:8, :], bc_last(bT[:, 0:8], NS), ALU.add, [pm, bT], [self.a0])
            tmp = c.sb([128, 8, NS], F32, "tmp0", es)
            self.tt("dve", tmp[:], pm3[:, 8:16, :], bc_last(bT[:, 8:16], NS), ALU.add, [pm, bT], [tmp])
            self.ts("pool", tmp[:], tmp[:], 1.0, None, ALU.add, None, [tmp], [tmp])
            self.tt("pool", self.a1[:], tmp[:], bc_last(gp[:], NS), ALU.mult, [tmp, gp], [self.a1])
            bg = c.sb([128, 1024], F32, "bg", es)
            self.ld(bg, bg[:], self.w["b_ada"].h[l, 2048:3072].partition_broadcast(128))
            gpo = c.sb([128, 1024], F32, "gpo", es)
            self.ld(gpo, gpo[:], self.w["g_post"].h[l, :].partition_broadcast(128))
            for s in range(NS):
                pg = c.ps([128, 1024], F32, "pg", es) if s == 0 else pg
                for n in range(2):
                    for kc in range(8):
                        self.mm(pg[:, n * 512:(n + 1) * 512], scb[:, kc, s:s + 1].to_broadcast([128, 128]),
                                wada[:, kc, 2048 + n * 512:2048 + (n + 1) * 512], [scb, wada], [pg], start=(kc == 0), stop=(kc == 7))
                self.tt("dve", self.gg[s][:], pg[:], bg[:], ALU.add, [pg, bg], [self.gg[s]])
                self.tt("pool", self.gg[s][:], self.gg[s][:], gpo[:], ALU.mult, [self.gg[s], gpo], [self.gg[s]])
            c.barrier()

    def xsrc(self, l, s):
        return (self.x if l == 0 else self.y), s

    def p1(self, l, s):
        c, nc, Tn = self.c, self.nc, self.Tn
        src, _ = self.xsrc(l, s)
        with ExitStack() as es:
            xt = [c.sb([128, 1024], F32, "xt", es) for _ in range(2)]
            xn = [c.sb([128, 1024], BF16, "xn", es) for _ in range(2)]
            junk = c.sb([128, 1024], BF16, "junk", es)
            ss = [c.sb([128, 2], F32, "ss", es) for _ in range(2)]
            pt = [c.ps([128, 1024], BF16, "pt", es) for _ in range(2)]
            tm = [c.sb([128, 8, 128], F32, "tm", es) for _ in range(2)]
            for tt_ in range(Tn // 128):
                b = tt_ % 2
                self.ld(xt[b], xt[b][:], src.h[s, tt_ * 128:(tt_ + 1) * 128, :], R=[src])
                self.act(junk[:], xt[b][:], AF.Square, [xt[b]], [junk, ss[b]], accum_out=ss[b][:, 0:1])
                self.rsqrt_(ss[b][:, 1:2], ss[b][:, 0:1], 1.0 / 1024, self.epsc[:, 0:1], [ss[b], self.epsc], [ss[b]])
                self.ts("dve", xn[b][:], xt[b][:], ss[b][:, 1:2], None, ALU.mult, None, [xt[b], ss[b]], [xn[b]])
                for fc in range(8):
                    self.tr(pt[b][:, fc * 128:(fc + 1) * 128], xn[b][:, fc * 128:(fc + 1) * 128], self.kb[:, K_ID:K_ID + 128],
                            [xn[b], self.kb], [pt[b]])
                self.tt("dve", tm[b][:], V3(pt[b][:], 8), bc_last(self.a1[:, :, s], 128), ALU.mult, [pt[b], self.a1], [tm[b]])
                self.tt("pool", self.hT[:, :, 1 + tt_ * 128:1 + (tt_ + 1) * 128], tm[b][:], bc_last(self.a0[:, :, s], 128), ALU.add,
                        [tm[b], self.a0], [self.hT_t])
            c.barrier()

    def p6(self, l, s):
        c, nc, Tn = self.c, self.nc, self.Tn
        src, _ = self.xsrc(l, s)
        TW = min(512, Tn)
        with ExitStack() as es:
            wo = c.sb([128, 8, 1024], BF16, "wo", es)
            st = c.sb([128, 8, 512], F32, "st6", es)
            for j in range(2):
                self.load_w(wo, j * 512, self.w["w_out"].h[l, :, j * 512:(j + 1) * 512].rearrange("(kc p) n -> p kc n", p=128), 512, st)
            mt = [c.sb([128, 8, TW], BF16, "mt", es) for _ in range(2)]
            xt = [c.sb([128, 1024], F32, "xt6", es) for _ in range(2)]
            yt = [c.sb([128, 1024], F32, "yt6", es) for _ in range(2)]
            junk = c.sb([128, 1024], BF16, "junk6", es)
            ss = [c.sb([128, 2], F32, "ss6", es) for _ in range(2)]
            po = [c.ps([128, 1024], F32, "po", es) for _ in range(2)]
            k = 0
            for j in range(Tn // TW):
                mb = mt[j % 2]
                self.ld(mb, mb[:], self.mixT.h[:, j * TW:(j + 1) * TW].rearrange("(a p) t -> p a t", p=128), R=[self.mixT])
                for u in range(TW // 128):
                    b = k % 2
                    k += 1
                    t0 = j * TW + u * 128
                    self.ld(xt[b], xt[b][:], src.h[s, t0:t0 + 128, :], R=[src])
                    for n in range(2):
                        for kc in range(8):
                            self.mm(po[b][:, n * 512:(n + 1) * 512], mb[:, kc, u * 128:(u + 1) * 128], wo[:, kc, n * 512:(n + 1) * 512],
                                    [mb, wo], [po[b]], start=(kc == 0), stop=(kc == 7))
                    self.act(junk[:], po[b][:], AF.Square, [po[b]], [junk, ss[b]], accum_out=ss[b][:, 0:1])
                    self.rsqrt_(ss[b][:, 1:2], ss[b][:, 0:1], 1.0 / 1024, self.epsc[:, 0:1], [ss[b], self.epsc], [ss[b]])
                    self.tt("dve", yt[b][:], po[b][:], self.gg[s][:], ALU.mult, [po[b], self.gg[s]], [yt[b]])
                    self.stt(yt[b][:], yt[b][:], ss[b][:, 1:2], xt[b][:], ALU.mult, ALU.add, [yt[b], ss[b], xt[b]], [yt[b]])
                    self.c.dma("sp", self.y.h[s, t0:t0 + 128, :], yt[b][:], yt[b], reads=[yt[b]], writes=[self.y])
            c.barrier()

    def mla(self, l, s):
        c, nc, Tn = self.c, self.nc, self.Tn
        TW = min(512, Tn)
        NQG = Tn // TW
        NKB = Tn // 128
        U = TW // 128
        scale = 96 ** -0.5
        ID_b = self.kb[:, K_ID:K_ID + 128]
        ONE_b = self.kb[:, K_ONE:K_ONE + 128]
        with ExitStack() as es:
            Wm = c.sb([128, 8, 960], BF16, "Wm", es)
            Wq = c.sb([128, 3, 384], BF16, "Wq", es)
            Wqb = c.sb([128, 3, 384], BF16, "Wqb", es)
            Wkv = c.sb([128, 512], BF16, "Wkv", es)
            with ExitStack() as e1:
                st = c.sb([128, 8, 512], F32, "stA", e1)
                self.memset("pool", Wm[:, :, 512:704], 0.0, [Wm])
                self.load_w(Wm, 0, self.win(l, C_CQ, 384), 384, st)
                self.load_w(Wm, 384, self.win(l, C_CKV, 128), 128, st)
                self.load_w(Wm, 512 + 64, self.win(l, C_KR, 32), 32, st)
                self.load_w(Wm, 608 + 64, self.win(l, C_KR + 16, 16), 16, st, neg=True)
                self.load_w(Wm, 608 + 80, self.win(l, C_KR, 16), 16, st)
                self.load_w(Wm, 704, self.win(l, C_Z, 256), 256, st)
                stq = c.sb([128, 3, 384], F32, "stq", e1)
                self.ld(stq, stq[:], self.w["a_w_uq"].h[l].rearrange("(kc p) n -> p kc n", p=128))
                gq = c.sb([128, 3], F32, "gq", e1)
                self.ld(gq, gq[:], self.w["a_g_q"].h[l, :].rearrange("(j p) -> p j", p=128), allow_slow_non_contiguous=True)
                self.memset("pool", Wqb[:], 0.0, [Wqb])
                for fc in range(3):
                    self.ts("dve", Wq[:, fc, :], stq[:, fc, :], gq[:, fc:fc + 1], None, ALU.mult, None, [stq, gq], [Wq])
                    src4 = V3(Wq[:, fc, :], 4)
                    dst4 = V3(Wqb[:, fc, :], 4)
                    self.c.op("act", lambda s4=src4, d4=dst4: nc.scalar.mul(out=d4[:, :, 64:80], in_=s4[:, :, 80:96], mul=-1.0), [Wq], [Wqb])
                    self.cp("pool", dst4[:, :, 80:96], src4[:, :, 64:80], [Wq], [Wqb])
                stk = c.sb([128, 512], F32, "stk", e1)
                self.ld(stk, stk[:], self.w["a_w_ukv"].h[l])
                gkv = c.sb([128, 1], F32, "gkv", e1)
                self.ld(gkv, gkv[:], self.w["a_g_kv"].h[l, :].rearrange("(p o) -> p o", o=1))
                self.ts("dve", Wkv[:], stk[:], gkv[:, 0:1], None, ALU.mult, None, [stk, gkv], [Wkv])
                c.barrier()
            KT = [c.sb([96, Tn], BF16, "KT%d" % h, es) for h in range(4)]
            Va = c.sb([128, NKB, 4, 65], BF16, "Va", es)
            self.memset("pool", Va[:, :, :, 64:65], 1.0, [Va])
            cosx = c.sb([96, TW], F32, "cosx", es)
            sinx = c.sb([96, TW], F32, "sinx", es)
            t1 = c.sb([96, TW], F32, "t1", es)
            t2 = c.sb([96, TW], F32, "t2", es)
            with ExitStack() as e2:
                ckT = c.sb([128, TW], BF16, "ckT", e2)
                sqk = c.sb([128, TW], BF16, "sqk", e2)
                rk = c.sb([128, TW], F32, "rk", e2)
                rcol = c.sb([128, 2], F32, "rcol", e2)
                pA = [c.ps([128, 512], F32, "pA", e2) for _ in range(4)]
                pB = c.ps([128, 512], F32, "pB", e2)
                pV = c.ps([128, 256], F32, "pV", e2)
                pc = c.ps([128, 2], F32, "pc", e2)
                for j in range(NQG):
                    t0 = j * TW
                    self.ld(cosx, cosx[:], self.rope.h[:, 0, t0:t0 + TW])
                    self.ld(sinx, sinx[:], self.rope.h[:, 1, t0:t0 + TW])
                    self.proj(pB, Wm, 384, 128, t0, TW)
                    self.cp("act", ckT[:], pB[:, 0:TW], [pB], [ckT])
                    self.tt("pool", sqk[:], ckT[:], ckT[:], ALU.mult, [ckT], [sqk])
                    self.mm(pA[0][:, 0:TW], ONE_b, sqk[:], [self.kb, sqk], [pA[0]])
                    self.rsqrt_(rk[:], pA[0][:, 0:TW], 1.0 / 128, self.epsc[:, 0:1], [pA[0], self.epsc], [rk])
                    for h in range(4):
                        pa = pA[1 + h % 2]
                        self.mm(pa[0:64, 0:TW], Wkv[:, h * 128:h * 128 + 64], ckT[:], [Wkv, ckT], [pa])
                        self.tt("dve", KT[h][0:64, t0:t0 + TW], pa[0:64, 0:TW], rk[0:64, :], ALU.mult, [pa, rk], [KT[h]])
                    self.proj(pA[3], Wm, 512, 96, t0, TW)
                    self.proj(pB, Wm, 608, 96, t0, TW)
                    self.tt("dve", t1[64:96, :], pA[3][64:96, 0:TW], cosx[64:96, :], ALU.mult, [pA[3], cosx], [t1])
                    self.tt("dve", t2[64:96, :], pB[64:96, 0:TW], sinx[64:96, :], ALU.mult, [pB, sinx], [t2])
                    for h in range(4):
                        self.tt("pool", KT[h][64:96, t0:t0 + TW], t1[64:96, :], t2[64:96, :], ALU.add, [t1, t2], [KT[h]])
                    for u in range(U):
                        kb_ = (t0 + u * 128) // 128
                        self.mm(pc[:, 0:1], sqk[:, u * 128:(u + 1) * 128], self.kb[:, K_ONE:K_ONE + 1], [sqk, self.kb], [pc])
                        self.rsqrt_(rcol[:, 0:1], pc[:, 0:1], 1.0 / 128, self.epsc[:, 0:1], [pc, self.epsc], [rcol])
                        for h in range(4):
                            self.mm(pV[:, h * 64:(h + 1) * 64], ckT[:, u * 128:(u + 1) * 128], Wkv[:, h * 128 + 64:h * 128 + 128], [ckT, Wkv], [pV])
                        self.ts("dve", Va[:, kb_, :, 0:64], V3(pV[:], 4), rcol[:, 0:1], None, ALU.mult, None, [pV, rcol], [Va])
                c.barrier()
            with ExitStack() as e3:
                NB = 3
                pS = [c.ps([128, 512], F32, "pS", e3) for _ in range(NB)]
                pz = c.ps([128, 512], F32, "pz", e3)
                pO = [c.ps([128, 512], F32, "pO", e3) for _ in range(2)]
                pBC = c.ps([128, 512], F32, "pBC", e3)
                Pt = [c.sb([128, TW], BF16, "Pt", e3) for _ in range(NB)]
                cqT = c.sb([128, 3, TW], BF16, "cqT", e3)
                sq = c.sb([128, 3, TW], BF16, "sq", e3)
                rq = c.sb([128, TW], F32, "rq", e3)
                CR = c.sb([96, TW], F32, "CR", e3)
                SR = c.sb([96, TW], F32, "SR", e3)
                qg_t = [c.sb([96, TW], BF16, "qg%d" % h, e3) for h in range(4)]
                rs = c.sb([128, TW], F32, "rs", e3)
                bcs = c.sb([64, TW], F32, "bcs", e3)
                sz = c.sb([64, TW], F32, "sz", e3)
                on = c.sb([64, TW], F32, "on", e3)
                og = [c.sb([64, TW], BF16, "og", e3) for _ in range(2)]
                it = 0
                for qg in range(NQG):
                    q0 = qg * TW
                    self.ld(cosx, cosx[:], self.rope.h[:, 0, q0:q0 + TW])
                    self.ld(sinx, sinx[:], self.rope.h[:, 1, q0:q0 + TW])
                    for fc in range(3):
                        self.proj(pS[fc], Wm, fc * 128, 128, q0, TW)
                        self.cp("act", cqT[:, fc, :], pS[fc][:, 0:TW], [pS[fc]], [cqT])
                        self.tt("pool", sq[:, fc, :], cqT[:, fc, :], cqT[:, fc, :], ALU.mult, [cqT], [sq])
                    for fc in range(3):
                        self.mm(pz[:, 0:TW], ONE_b, sq[:, fc, :], [self.kb, sq], [pz], start=(fc == 0), stop=(fc == 2))
                    self.rsqrt_(rq[:], pz[:, 0:TW], 1.0 / 384, self.epsc[:, 0:1], [pz, self.epsc], [rq])
                    self.tt("pool", CR[:], cosx[:], rq[0:96, :], ALU.mult, [cosx, rq], [CR])
                    self.tt("pool", SR[:], sinx[:], rq[0:96, :], ALU.mult, [sinx, rq], [SR])
                    for h in range(4):
                        pa, pb = pS[0], pS[1]
                        for fc in range(3):
                            self.mm(pa[0:96, 0:TW], Wq[:, fc, h * 96:(h + 1) * 96], cqT[:, fc, :], [Wq, cqT], [pa], start=(fc == 0), stop=(fc == 2))
                        for fc in range(3):
                            self.mm(pb[0:96, 0:TW], Wqb[:, fc, h * 96:(h + 1) * 96], cqT[:, fc, :], [Wqb, cqT], [pb], start=(fc == 0), stop=(fc == 2))
                        self.tt("dve", t1[:], pa[0:96, 0:TW], CR[:], ALU.mult, [pa, CR], [t1])
                        self.tt("dve", t2[:], pb[0:96, 0:TW], SR[:], ALU.mult, [pb, SR], [t2])
                        self.tt("pool", qg_t[h][:], t1[:], t2[:], ALU.add, [t1, t2], [qg_t[h]])
                    for h in range(4):
                        po_ = pO[h % 2]
                        def score(kb_, b):
                            self.mm(pS[b][:, 0:TW], KT[h][:, kb_ * 128:(kb_ + 1) * 128], qg_t[h][:], [KT[h], qg_t[h]], [pS[b]])
                        score(0, it % NB)
                        if NKB > 1:
                            score(1, (it + 1) % NB)
                        for kb_ in range(NKB):
                            b = it % NB
                            it += 1
                            if kb_ + 2 < NKB:
                                score(kb_ + 2, (it + 1) % NB)
                            self.act(Pt[b][:], pS[b][:, 0:TW], AF.Exp, [pS[b]], [Pt[b]], scale=scale)
                            self.mm(po_[0:65, 0:TW], Va[:, kb_, h, :], Pt[b][:], [Va, Pt[b]], [po_], start=(kb_ == 0), stop=(kb_ == NKB - 1))
                        self.proj(pz, Wm, 704 + h * 64, 64, q0, TW)
                        self.act(sz[:], pz[0:64, 0:TW], AF.Silu, [pz], [sz])
                        self.c.op("dve", lambda p=po_: nc.vector.reciprocal(out=rs[64:65, :], in_=p[64:65, 0:TW]), [po_], [rs])
                        self.mm(pBC[0:64, 0:TW], self.kc[64:65, K_ONE:K_ONE + 64], rs[64:65, :], [self.kc, rs], [pBC])
                        self.cp("act", bcs[:], pBC[0:64, 0:TW], [pBC], [bcs])
                        self.tt("dve", on[:], po_[0:64, 0:TW], bcs[:], ALU.mult, [po_, bcs], [on])
                        ob = og[h % 2]
                        self.tt("pool", ob[:], on[:], sz[:], ALU.mult, [on, sz], [ob])
                        self.c.dma("sp", self.mixT.h[h * 64:(h + 1) * 64, q0:q0 + TW], ob[:], ob, reads=[ob], writes=[self.mixT])
            c.barrier()

    def dplr_env(self, es, nsets=3):
        c = self.c
        sets = []

        def sl(bank, lo, n=128):
            t = c.sub(bank[:, lo:lo + n])
            t.dep = bank.dep
            return t
        cb = c.ps([128, 512], F32, "dpc", es)
        chain = {"pW": sl(cb, 0), "pU": sl(cb, 128), "pZ": sl(cb, 256), "pY": sl(cb, 384, 64)}
        for i in range(nsets):
            e = dict(chain)
            bA = c.ps([128, 512], F32, "dpa", es)
            bB = c.ps([128, 512], F32, "dpb", es)
            e["pQB"], e["pQK"], e["pNL"] = sl(bA, 0, 192), sl(bA, 192, 192), sl(bA, 384)
            e["pXP"] = sl(bA, 0, 256)
            e["pPt"] = sl(bB, 0)
            e["pT"] = c.sub(V3(bB[:, 256:448].bitcast(BF16), 3))
            e["pT"].dep = bB.dep
            for nm in ["Nt", "Pt0", "Pt1"]:
                e[nm] = c.sb([128, 128], F32R, "d" + nm, es)
            for nm in ["XP0", "XP1"]:
                e[nm] = c.sb([128, 256], F32R, "d" + nm, es)
            for nm in ["AK", "AR", "TT", "W", "U"]:
                e[nm] = c.sb([128, 128], BF16, "d" + nm, es)
            e["Tr"] = c.sb([128, 3, 128], BF16, "dTr", es)
            sets.append(e)
        st = {"Zf": c.sb([128, 128], F32, "Zf", es), "Zb": c.sb([128, 128], BF16, "Zb", es), "Zg": c.sb([128, 128], F32, "Zg", es)}
        return sets, st

    def dplr_pre(self, e, o, ci):
        kc, kb = self.kc, self.kb
        IDf = kc[:, K_ID:K_ID + 128]
        IDb = kb[:, K_ID:K_ID + 128]
        PAR, QB, QK = o["PAR"](ci), o["QB"](ci), o["QK"](ci)
        ms, mi, ml = o["ms"](ci), o["mi"](ci), o["ml"](ci)
        PA = PAR[0][:, 0:128]
        same = o.get("same", False)
        self.mm(e["pQB"][:], QB[0], PAR[0], [QB[1], PAR[1]], [e["pQB"]])
        if not same:
            self.mm(e["pQK"][:], QK[0], PAR[0], [QK[1], PAR[1]], [e["pQK"]])
        self.mm(e["pNL"][:], PA, QB[0], [QB[1], PAR[1]], [e["pNL"]])
        for j, nm in enumerate(["SB", "SK", "V"]):
            if same and nm == "SB":
                continue
            src = o[nm](ci)
            self.tr(e["pT"][:, j, :], src[0], IDb, [src[1], kb], [e["pT"]])
        XP = e["XP0"]
        self.tt("dve", V3(XP[:, 128:256], 2), V3(e["pQB"][:, 0:128], 2), bc_mid(ms[0], 2), ALU.mult, [e["pQB"], ms[1]], [XP])
        self.tt("dve", V3(e["Nt"][:], 2), V3(e["pNL"][:], 2), bc_mid(ml[0], 2), ALU.mult, [e["pNL"], ml[1]], [e["Nt"]])
        self.tt("dve", e["AR"][:, 0:64], e["pQB"][:, 128:192], mi[0], ALU.mult, [e["pQB"], mi[1]], [e["AR"]])
        if same:
            self.cp("act", e["AK"][:], XP[:, 128:256], [XP], [e["AK"]])
            self.cp("act", e["Tr"][:, 1:3, :], e["pT"][:, 1:3, :], [e["pT"]], [e["Tr"]])
        else:
            self.tt("dve", V3(e["AK"][:], 2), V3(e["pQK"][:, 0:128], 2), bc_mid(ms[0], 2), ALU.mult, [e["pQK"], ms[1]], [e["AK"]])
            self.tt("dve", e["AR"][:, 64:128], e["pQK"][:, 128:192], mi[0], ALU.mult, [e["pQK"], mi[1]], [e["AR"]])
            self.cp("act", e["Tr"][:], e["pT"][:], [e["pT"]], [e["Tr"]])
        self.tt("dve", XP[:, 0:128], XP[:, 128:256], IDf, ALU.add, [XP, kc], [XP])
        yield
        Pt = e["Nt"]
        Ptn = e["Pt1"]
        self.mm(e["pPt"][:], XP[:, 128:256], Pt[:], [XP, Pt], [e["pPt"]])
        self.mm(e["pXP"][:, 128:256], Pt[:], XP[:, 128:256], [XP, Pt], [e["pXP"]])
        XPn = e["XP1"]
        self.cp("act", Ptn[:], e["pPt"][:], [e["pPt"]], [Ptn])
        self.cp("dve", XPn[:, 128:256], e["pXP"][:, 128:256], [e["pXP"]], [XPn])
        self.cp("dve", XPn[:, 0:128], XP[:, 0:128], [XP], [XPn])
        XP, Pt = XPn, Ptn
        yield
        for k in range(1, 6):
            XPn = e["XP%d" % (k % 2)]
            Ptn = e["Pt%d" % (k % 2)]
            if k < 4:
                self.mm(e["pXP"][:], Pt[:], XP[:], [Pt, XP], [e["pXP"]])
            else:
                self.mm(e["pXP"][:, 0:128], Pt[:], XP[:, 0:128], [Pt, XP], [e["pXP"]])
            if k < 5:
                self.mm(e["pPt"][:], XP[:, 128:256], Pt[:], [XP, Pt], [e["pPt"]])
            self.tt("dve", XPn[:, 0:128], XP[:, 0:128], e["pXP"][:, 0:128], ALU.add, [XP, e["pXP"]], [XPn])
            if k < 4:
                self.cp("dve", XPn[:, 128:256], e["pXP"][:, 128:256], [e["pXP"]], [XPn])
            if k < 5:
                self.cp("act", Ptn[:], e["pPt"][:], [e["pPt"]], [Ptn])
            XP, Pt = XPn, Ptn
            yield
        self.cp("act", e["TT"][:], XP[:, 0:128], [XP], [e["TT"]])

    def dplr_chain(self, e, st, o, ci):
        nc = self.nc
        Zf, Zb, Zg = st["Zf"], st["Zb"], st["Zg"]
        ZA, ZR = o["ZA"](ci), o["ZR"](ci)
        SBt, SKt, Vt = e["Tr"][:, 0, :], e["Tr"][:, 1, :], e["Tr"][:, 2, :]
        gz, gs = o["gZ"](ci), o["gS"](ci)
        self.c.op("act", lambda: nc.scalar.mul(out=Zg[:], in_=Zf[:], mul=gz[0]), [Zf, gz[1]], [Zg])
        self.mm(e["pW"][:], ZA[0], Zb[:], [ZA[1], Zb], [e["pW"]], start=True, stop=False)
        self.mm(e["pW"][:], e["AK"][:], Vt, [e["AK"], e["Tr"]], [e["pW"]], start=False, stop=True)
        self.cp("act", e["W"][:], e["pW"][:], [e["pW"]], [e["W"]])
        yield
        self.mm(e["pU"][:], e["TT"][:], e["W"][:], [e["TT"], e["W"]], [e["pU"]])
        if o.get("same", False):
            self.tt("dve", e["U"][:], e["pU"][:], Vt, ALU.add, [e["pU"], e["Tr"]], [e["U"]])
            yield
            self.mm(e["pZ"][:], SKt, e["U"][:], [e["Tr"], e["U"]], [e["pZ"]])
            self.mm(e["pY"][:], Zb[:], ZR[0], [Zb, ZR[1]], [e["pY"]], start=True, stop=False)
            self.mm(e["pY"][:], e["U"][:], e["AR"][:, 0:64], [e["U"], e["AR"]], [e["pY"]], start=False, stop=True)
        else:
            self.cp("dve", e["U"][:], e["pU"][:], [e["pU"]], [e["U"]])
            yield
            self.mm(e["pZ"][:], SBt, e["U"][:], [e["Tr"], e["U"]], [e["pZ"]], start=True, stop=False)
            self.mm(e["pZ"][:], SKt, Vt, [e["Tr"]], [e["pZ"]], start=False, stop=True)
            self.mm(e["pY"][:], Zb[:], ZR[0], [Zb, ZR[1]], [e["pY"]], start=True, stop=False)
            self.mm(e["pY"][:], e["U"][:], e["AR"][:, 0:64], [e["U"], e["AR"]], [e["pY"]], start=False, stop=False)
            self.mm(e["pY"][:], Vt, e["AR"][:, 64:128], [e["Tr"], e["AR"]], [e["pY"]], start=False, stop=True)
        if gs is None:
            self.tt("dve", Zb[:], e["pZ"][:], Zg[:], ALU.add, [e["pZ"], Zg], [Zb])
            self.tt("dve", Zf[:], e["pZ"][:], Zg[:], ALU.add, [e["pZ"], Zg], [Zf])
        else:
            self.stt(Zb[:], e["pZ"][:], gs[0], Zg[:], ALU.mult, ALU.add, [e["pZ"], gs[1], Zg], [Zb])
            self.stt(Zf[:], e["pZ"][:], gs[0], Zg[:], ALU.mult, ALU.add, [e["pZ"], gs[1], Zg], [Zf])
        o["yout"](ci, e["pY"])
        yield

    def dplr_run(self, sets, st, o, nch, side=None):
        NS_ = len(sets)
        active, done = [], set()
        chain, next_chain, next_pre = None, 0, 0
        while next_chain < nch:
            while next_pre < nch and next_pre < next_chain + NS_ and len(active) < NS_:
                active.append([next_pre, self.dplr_pre(sets[next_pre % NS_], o, next_pre)])
                next_pre += 1
            if chain is None and next_chain in done:
                chain = self.dplr_chain(sets[next_chain % NS_], st, o, next_chain)
            for item in list(active):
                try:
                    next(item[1])
                except StopIteration:
                    active.remove(item)
                    done.add(item[0])
            if chain is not None:
                try:
                    next(chain)
                except StopIteration:
                    chain = None
                    next_chain += 1
            if side is not None:
                try:
                    next(side)
                except StopIteration:
                    side = None
        if side is not None:
            for _ in side:
                pass

    def to_bd(self, eng_fn, dst, nch):
        for hh in range(2):
            lo, hi = hh * 64, hh * 64 + 64
            eng_fn(dst[lo:hi, :, lo:hi], lo, hi)

    def col(self, es, src_ap, name="col", n=128):
        t = self.c.sb([n, 1], F32, name, es)
        self.ld(t, t[:], src_ap.rearrange("(p o) -> p o", o=1))
        return t

    def rwkv(self, l, s):
        c, nc, Tn = self.c, self.nc, self.Tn
        SG = min(256, Tn)
        nch = SG // 64
        NSG = Tn // SG
        kc, kb = self.kc, self.kb
        BOb = kb[:, K_BO:K_BO + 128]
        BOf = kc[:, K_BO:K_BO + 128]
        w = self.w
        with ExitStack() as es:
            sets, st = self.dplr_env(es)
            Yacc = c.sb([128, Tn], F32, "Yacc", es)
            Bacc = c.sb([128, Tn], F32, "Bacc", es)
            ones = c.sb([128, SG], F32, "ones", es)
            self.memset("pool", ones[:], 1.0, [ones])
            W1 = c.sb([128, 8, 448], BF16, "W1", es)
            Pc = c.sb([128, SG + 1], F32, "Pc", es)
            carry = c.sb([128, 5], F32, "carry", es)
            mu5 = c.sb([128, 5], F32, "mu5", es)
            OS = []
            for i in range(2):
                S_ = {"PAR": c.sb([128, nch, 192], BF16, "PAR", es), "g": c.sb([128, nch], F32, "gcol", es)}
                for nm in ["B", "K", "V"]:
                    S_[nm] = c.sb([128, nch, 128], BF16, "bd" + nm, es)
                for nm in ["PAR", "B", "K", "V"]:
                    self.memset("pool", S_[nm][:], 0.0, [S_[nm]])
                OS.append(S_)
            w2b = c.sb([32, 128], BF16, "w2b", es)
            a2b = c.sb([32, 128], BF16, "a2b", es)
            F = {nm: c.sb([128, SG], F32, "f" + nm, es) for nm in ["r", "k", "v", "lw", "alr", "G", "lg", "E1", "E2", "E3", "kk", "kp", "t0", "t1"]}
            Gs = c.sb([128, nch], F32, "Gs", es)
            self.memset("pool", Gs[:, 0:1], 0.0, [Gs])
            Hb = {nm: c.sb([128, SG], BF16, "h" + nm, es) for nm in ["sq", "rkr"]}
            th = c.sb([32, SG], BF16, "th", es)
            xab = c.sb([32, SG], BF16, "xab", es)
            pp = [c.ps([128, 512], F32, "rpp", es)] * 2
            ppi = [0]

            def nextp():
                ppi[0] ^= 1
                return pp[ppi[0]]

            for pr in range(2):
                cs = slice(pr * 128, pr * 128 + 128)
                kk_c = self.col(es, w["rw_k_k"].h[l, cs], "kkc")
                ka_c = self.col(es, w["rw_k_a"].h[l, cs], "kac")
                rk_c = self.col(es, w["rw_r_k"].h[l].rearrange("h n -> (h n)")[cs], "rkc")
                lnw_c = self.col(es, w["rw_ln_w"].h[l, cs], "lnw")
                lnb_c = self.col(es, w["rw_ln_b"].h[l, cs], "lnb")
                for d in range(2):
                    with ExitStack() as e1:
                        stg = c.sb([128, 8, 128], F32, "stg", e1)
                        srcs = [(C_RKV + pr * 128, 128, pr * 128), (C_RKV + 256 + pr * 128, 128, 256 + pr * 128),
                                (C_RKV + 512 + pr * 128, 128, 512 + pr * 128), ((C_WAF if d == 0 else C_WAB), 32, 768),
                                ((C_WAF if d == 0 else C_WAB) + 32, 32, 800)]
                        o_ = 0
                        for j5, (c0, n, m0) in enumerate(srcs):
                            self.load_w(W1, o_, self.win(l, c0, n), n, stg)
                            self.ld(mu5, mu5[0:n, j5:j5 + 1], w["rw_mu"].h[l, d, m0:m0 + n].rearrange("(p o) -> p o", o=1))
                            o_ += n
                        s32 = c.sb([32, 256], F32, "s32", e1)
                        self.ld(s32, s32[:, 0:128], w["rw_w2"].h[l, d, :, cs])
                        self.ld(s32, s32[:, 128:256], w["rw_a2"].h[l, d, :, cs])
                        self.cp("dve", w2b[:], s32[:, 0:128], [s32], [w2b])
                        self.cp("dve", a2b[:], s32[:, 128:256], [s32], [a2b])
                        c.barrier()
                    self.memset("pool", carry[:], 0.0, [carry])
                    w0_c = self.col(es, w["rw_w0"].h[l, d, cs], "w0c")
                    a0_c = self.col(es, w["rw_a0"].h[l, d, cs], "a0c")
                    self.memset("pool", st["Zf"][:], 0.0, [st["Zf"]])
                    self.memset("pool", st["Zb"][:], 0.0, [st["Zb"]])
                    def prep(sg, S, d=d, w0_c=w0_c, a0_c=a0_c):
                        tau0 = sg * SG
                        PAR, BDB, BDK, BDV = S["PAR"], S["B"], S["K"], S["V"]

                        def shifted(j5, wc, M, out_ap, out_t):
                            p_ = nextp()
                            self.proj(p_, W1, wc, M, tau0, SG, d=d)
                            self.cp("pool", Pc[0:M, 0:1], carry[0:M, j5:j5 + 1], [carry], [Pc])
                            self.cp("act", Pc[0:M, 1:SG + 1], p_[0:M, 0:SG], [p_], [Pc])
                            self.cp("pool", carry[0:M, j5:j5 + 1], Pc[0:M, SG:SG + 1], [Pc], [carry])
                            self.tt("dve", F["t0"][0:M, :], Pc[0:M, 0:SG], Pc[0:M, 1:SG + 1], ALU.subtract, [Pc], [F["t0"]])
                            self.stt(out_ap, F["t0"][0:M, :], mu5[0:M, j5:j5 + 1], Pc[0:M, 1:SG + 1], ALU.mult, ALU.add, [F["t0"], mu5, Pc], [out_t])
                        shifted(0, 0, 128, F["r"][:], F["r"])
                        yield
                        shifted(1, 128, 128, F["k"][:], F["k"])
                        yield
                        shifted(2, 256, 128, F["v"][:], F["v"])
                        self.to_bd(lambda o_ap, lo, hi: self.cp("act", o_ap, V3(F["v"][lo:hi, :], nch), [F["v"]], [BDV]), BDV, nch)
                        yield
                        shifted(3, 384, 32, F["t1"][0:32, :], F["t1"])
                        self.act(th[:], F["t1"][0:32, :], AF.Tanh, [F["t1"]], [th])
                        yield
                        shifted(4, 416, 32, xab[:], xab)
                        yield
                        p_ = nextp()
                        self.mm(p_[:, 0:SG], w2b[:], th[:], [w2b, th], [p_])
                        self.act(F["lw"][:], p_[:, 0:SG], AF.Sigmoid, [p_, w0_c], [F["lw"]], bias=w0_c[:, 0:1])
                        self.ts("dve", F["lw"][:], F["lw"][:], -math.exp(-0.5), None, ALU.mult, None, [F["lw"]], [F["lw"]])
                        p_ = nextp()
                        self.mm(p_[:, 0:SG], a2b[:], xab[:], [a2b, xab], [p_])
                        self.act(F["alr"][:], p_[:, 0:SG], AF.Sigmoid, [p_, a0_c], [F["alr"]], bias=a0_c[:, 0:1])
                        yield
                        self.c.op("dve", lambda: nc.vector.tensor_tensor_scan(out=F["G"][:], data0=ones[:], data1=F["lw"][:], initial=0.0,
                                                                                op0=ALU.mult, op1=ALU.add), [ones, F["lw"]], [F["G"]])
                        if nch > 1:
                            self.cp("pool", Gs[:, 1:nch], F["G"][:, 63:SG - 1:64], [F["G"]], [Gs])
                        self.tt("dve", V3(F["lg"][:], nch), V3(F["G"][:], nch), bc_last(Gs[:], 64), ALU.subtract, [F["G"], Gs], [F["lg"]])
                        yield
                        self.act(F["E1"][:], F["lg"][:], AF.Exp, [F["lg"]], [F["E1"]])
                        self.act(F["E3"][:], F["lg"][:], AF.Exp, [F["lg"]], [F["E3"]], scale=-1.0)
                        self.tt("pool", F["t0"][:], F["lg"][:], F["lw"][:], ALU.subtract, [F["lg"], F["lw"]], [F["t0"]])
                        self.act(F["E2"][:], F["t0"][:], AF.Exp, [F["t0"]], [F["E2"]])
                        self.cp("pool", S["g"][:], F["E1"][:, 63:SG:64], [F["E1"]], [S["g"]])
                        yield
                        self.ts("dve", F["t0"][:], F["k"][:], kk_c[:, 0:1], None, ALU.mult, None, [F["k"], kk_c], [F["t0"]])
                        self.tt("pool", Hb["sq"][:], F["t0"][:], F["t0"][:], ALU.mult, [F["t0"]], [Hb["sq"]])
                        p_ = nextp()
                        self.mm(p_[:, 0:SG], BOb, Hb["sq"][:], [kb, Hb["sq"]], [p_])
                        self.rsqrt_(F["t1"][:], p_[:, 0:SG], 1.0, self.epsc[:, 0:1], [p_, self.epsc], [F["t1"]])
                        self.tt("dve", F["kk"][:], F["t0"][:], F["t1"][:], ALU.mult, [F["t0"], F["t1"]], [F["kk"]])
                        yield
                        self.ts("dve", F["t0"][:], F["alr"][:], -1.0, ka_c[:, 0:1], ALU.add, ALU.mult, [F["alr"], ka_c], [F["t0"]])
                        self.stt(F["kp"][:], F["t0"][:], 1.0, F["k"][:], ALU.add, ALU.mult, [F["t0"], F["k"]], [F["kp"]])
                        self.tt("pool", F["t0"][:], F["r"][:], F["kp"][:], ALU.mult, [F["r"], F["kp"]], [F["t0"]])
                        self.ts("dve", Hb["rkr"][:], F["t0"][:], rk_c[:, 0:1], None, ALU.mult, None, [F["t0"], rk_c], [Hb["rkr"]])
                        yield
                        p_ = nextp()
                        self.mm(p_[:, 0:SG], BOb, Hb["rkr"][:], [kb, Hb["rkr"]], [p_])
                        if d == 0:
                            self.tt("dve", Bacc[:, tau0:tau0 + SG], p_[:, 0:SG], F["v"][:], ALU.mult, [p_, F["v"]], [Bacc])
                        else:
                            self.tt("dve", F["t1"][:], p_[:, 0:SG], F["v"][:], ALU.mult, [p_, F["v"]], [F["t1"]])
                            rb = rev(Bacc[:, Tn - tau0 - SG:Tn - tau0])
                            self.tt("pool", rb, rb, F["t1"][:], ALU.add, [Bacc, F["t1"]], [Bacc])
                        yield
                        self.tt("dve", PAR[:, :, 128:192], V3(F["r"][:], nch), V3(F["E1"][:], nch), ALU.mult, [F["r"], F["E1"]], [PAR])
                        self.to_bd(lambda o_ap, lo, hi: self.stt(o_ap, V3(F["kk"][lo:hi, :], nch), -1.0, V3(F["E2"][lo:hi, :], nch), ALU.mult, ALU.mult,
                                                                 [F["kk"], F["E2"]], [PAR]), PAR, nch)
                        yield
                        self.to_bd(lambda o_ap, lo, hi: self.tt("dve", o_ap, V3(F["kp"][lo:hi, :], nch), V3(F["E3"][lo:hi, :], nch), ALU.mult,
                                                                [F["kp"], F["E3"]], [BDK]), BDK, nch)
                        yield
                        self.tt("pool", F["t0"][:], F["kk"][:], F["alr"][:], ALU.mult, [F["kk"], F["alr"]], [F["t0"]])
                        self.to_bd(lambda o_ap, lo, hi: self.tt("dve", o_ap, V3(F["t0"][lo:hi, :], nch), V3(F["E3"][lo:hi, :], nch), ALU.mult,
                                                                [F["t0"], F["E3"]], [BDB]), BDB, nch)
                        yield

                    def mk_o(sg, S, d=d):
                        tau0 = sg * SG
                        PAR, BDB, BDK, BDV, G_ = S["PAR"], S["B"], S["K"], S["V"], S["g"]

                        def yout(ci, pY):
                            a0_ = tau0 + ci * 64
                            if d == 0:
                                self.cp("dve", Yacc[:, a0_:a0_ + 64], pY[:], [pY], [Yacc])
                            else:
                                ry = rev(Yacc[:, Tn - a0_ - 64:Tn - a0_])
                                self.tt("dve", ry, ry, pY[:], ALU.add, [Yacc, pY], [Yacc])
                        return dict(PAR=lambda ci: (PAR[:, ci, :], PAR), QB=lambda ci: (BDB[:, ci, :], BDB), QK=lambda ci: (BDK[:, ci, :], BDK),
                                    ZA=lambda ci: (PAR[:, ci, 0:128], PAR), ZR=lambda ci: (PAR[:, ci, 128:192], PAR),
                                    SB=lambda ci: (BDB[:, ci, :], BDB), SK=lambda ci: (BDK[:, ci, :], BDK), V=lambda ci: (BDV[:, ci, :], BDV),
                                    ms=lambda ci: (kc[:, K_MUS:K_MUS + 64], kc), mi=lambda ci: (kc[:, K_MUI:K_MUI + 64], kc),
                                    ml=lambda ci: (kc[:, K_MLS:K_MLS + 64], kc),
                                    gZ=lambda ci: (G_[:, ci:ci + 1], G_), gS=lambda ci: (G_[:, ci:ci + 1], G_), yout=yout)

                    for _ in prep(0, OS[0]):
                        pass
                    for sg in range(NSG):
                        side = prep(sg + 1, OS[(sg + 1) % 2]) if sg + 1 < NSG else None
                        self.dplr_run(sets, st, mk_o(sg, OS[sg % 2]), nch, side=side)
                with ExitStack() as e2:
                    Wz = c.sb([128, 8, 128], BF16, "Wz", e2)
                    stz = c.sb([128, 8, 128], F32, "stz", e2)
                    self.load_w(Wz, 0, self.win(l, C_Z + 256 + pr * 128, 128), 128, stz)
                    ob = [c.sb([128, SG], BF16, "rob", e2) for _ in range(2)]
                    for sg in range(NSG):
                        t0 = sg * SG
                        Y = Yacc[:, t0:t0 + SG]
                        p_ = nextp()
                        self.mm(p_[:, 0:SG], BOf, Y, [kc, Yacc], [p_])
                        self.stt(F["t0"][:], p_[:, 0:SG], -1.0 / 64, Y, ALU.mult, ALU.add, [p_, Yacc], [F["t0"]])
                        self.tt("pool", F["t1"][:], F["t0"][:], F["t0"][:], ALU.mult, [F["t0"]], [F["t1"]])
                        p_ = nextp()
                        self.mm(p_[:, 0:SG], BOf, F["t1"][:], [kc, F["t1"]], [p_])
                        self.rsqrt_(F["t1"][:], p_[:, 0:SG], 1.0 / 64, self.epsc[:, 1:2], [p_, self.epsc], [F["t1"]])
                        self.tt("pool", F["t0"][:], F["t0"][:], F["t1"][:], ALU.mult, [F["t0"], F["t1"]], [F["t0"]])
                        self.ts("dve", F["t0"][:], F["t0"][:], lnw_c[:, 0:1], lnb_c[:, 0:1], ALU.mult, ALU.add, [F["t0"], lnw_c, lnb_c], [F["t0"]])
                        self.tt("pool", F["t0"][:], F["t0"][:], Bacc[:, t0:t0 + SG], ALU.add, [F["t0"], Bacc], [F["t0"]])
                        p_ = nextp()
                        self.proj(p_, Wz, 0, 128, t0, SG)
                        self.act(F["t1"][:], p_[:, 0:SG], AF.Silu, [p_], [F["t1"]])
                        o_b = ob[sg % 2]
                        self.tt("pool", o_b[:], F["t0"][:], F["t1"][:], ALU.mult, [F["t0"], F["t1"]], [o_b])
                        r0 = 256 + pr * 128
                        self.c.dma("sp", self.mixT.h[r0:r0 + 128, t0:t0 + SG], o_b[:], o_b, reads=[o_b], writes=[self.mixT])
                    c.barrier()
            c.barrier()

    def gdn(self, l, s):
        c, nc, Tn = self.c, self.nc, self.Tn
        SG = min(256, Tn)
        nch = SG // 64
        NSG = Tn // SG
        TW = min(512, Tn)
        kc, kb = self.kc, self.kb
        BOb = kb[:, K_BO:K_BO + 128]
        BOf = kc[:, K_BO:K_BO + 128]
        w = self.w
        with ExitStack() as es:
            sets, st = self.dplr_env(es)
            Yacc = c.sb([128, Tn], F32, "Yacc", es)
            ones = c.sb([128, SG], F32, "ones", es)
            self.memset("pool", ones[:], 1.0, [ones])
            QKV = {nm: c.sb([128, Tn], BF16, "g" + nm, es) for nm in "qkv"}
            Wg = c.sb([128, 8, 16], BF16, "Wg", es)
            with ExitStack() as e1:
                stg = c.sb([128, 8, 16], F32, "stgg", e1)
                self.load_w(Wg, 0, self.win(l, C_GDB, 16), 16, stg)
                c.barrier()
            sg_t = c.sb([8, 4, 128], F32, "selg", es)
            self.ld(sg_t, sg_t[:], self.selg.h[:])
            dtb = self.col(es, w["gd_dt_bias"].h[l].rearrange("d h -> (d h)"), "dtb", 8)
            nA = self.col(es, w["gd_a_log"].h[l].rearrange("d h -> (d h)"), "nA", 8)
            self.act(nA[:], nA[:], AF.Exp, [nA], [nA])
            self.ts("pool", nA[:], nA[:], -1.0, None, ALU.mult, None, [nA], [nA])
            gnw = c.sb([128, 1], F32, "gnw", es)
            for hh in range(2):
                self.ld(gnw, gnw[hh * 64:(hh + 1) * 64, :], w["gd_norm_w"].h[l, :].rearrange("(p o) -> p o", o=1))
            F = {nm: c.sb([128, SG], F32, "f" + nm, es) for nm in ["g", "b", "G", "gc", "E1", "Es", "nbk", "t0", "Ex"]}
            g8 = c.sb([8, SG], F32, "g8", es)
            b8 = c.sb([8, SG], F32, "b8", es)
            Gs = c.sb([128, nch], F32, "Gs", es)
            gcT = c.sb([128, nch], F32, "gcT", es)
            self.memset("pool", Gs[:, 0:1], 0.0, [Gs])
            OS = []
            for i in range(2):
                S_ = {"PAR": c.sb([128, nch, 192], BF16, "gPAR", es), "g": c.sb([128, nch], F32, "ggcol", es),
                      "ZR": c.sb([128, SG], BF16, "gZR", es)}
                for nm in ["ZA", "K", "SK", "V"]:
                    S_[nm] = c.sb([128, nch, 128], BF16, "gbd" + nm, es)
                for nm in ["Ms", "Mi", "Ml"]:
                    S_[nm] = c.sb([128, nch, 64], F32, "g" + nm, es)
                for nm in ["PAR", "ZA", "K", "SK", "V"]:
                    self.memset("pool", S_[nm][:], 0.0, [S_[nm]])
                OS.append(S_)
            pp = [c.ps([128, 512], F32, "gpp", es)] * 2
            ppi = [0]

            def nextp():
                ppi[0] ^= 1
                return pp[ppi[0]]

            for pr in range(2):
                with ExitStack() as e1:
                    Wq = c.sb([128, 8, 384], BF16, "Wqkv", e1)
                    stg = c.sb([128, 8, 128], F32, "stgq", e1)
                    cw = c.sb([128, 3, 4], F32, "cw", e1)
                    for j in range(3):
                        ch0 = j * 256 + pr * 128
                        self.load_w(Wq, j * 128, self.win(l, C_QKV + ch0, 128), 128, stg)
                        self.ld(cw, cw[:, j, :], w["gd_conv_w"].h[l, :, ch0:ch0 + 128].rearrange("i p -> p i"), allow_slow_non_contiguous=True)
                    raw = c.sb([128, Tn + 3], BF16, "raw", e1)
                    self.memset("pool", raw[:, 0:1], 0.0, [raw])
                    self.memset("pool", raw[:, Tn + 1:Tn + 3], 0.0, [raw])
                    acc = c.sb([128, TW], F32, "acc", e1)
                    sl_ = c.sb([128, TW], F32, "sl", e1)
                    sqb = c.sb([128, TW], BF16, "sqb", e1)
                    rn = c.sb([128, TW], F32, "rn", e1)
                    for j, nm in enumerate("qkv"):
                        for tt_ in range(Tn // TW):
                            p_ = nextp()
                            self.proj(p_, Wq, j * 128, 128, tt_ * TW, TW)
                            self.cp(self.eng2(), raw[:, 1 + tt_ * TW:1 + (tt_ + 1) * TW], p_[:, 0:TW], [p_], [raw])
                        for tt_ in range(Tn // TW):
                            t0 = tt_ * TW
                            self.ts("dve", acc[:], raw[:, t0:t0 + TW], cw[:, j, 0:1], None, ALU.mult, None, [raw, cw], [acc])
                            for i in range(1, 4):
                                self.stt(acc[:], raw[:, t0 + i:t0 + i + TW], cw[:, j, i:i + 1], acc[:], ALU.mult, ALU.add, [raw, cw, acc], [acc])
                            if nm == "v":
                                self.act(QKV[nm][:, t0:t0 + TW], acc[:], AF.Silu, [acc], [QKV[nm]])
                            else:
                                self.act(sl_[:], acc[:], AF.Silu, [acc], [sl_])
                                self.tt("pool", sqb[:], sl_[:], sl_[:], ALU.mult, [sl_], [sqb])
                                p_ = nextp()
                                self.mm(p_[:, 0:TW], BOb, sqb[:], [kb, sqb], [p_])
                                self.rsqrt_(rn[:], p_[:, 0:TW], 1.0, self.epsc[:, 0:1], [p_, self.epsc], [rn])
                                if nm == "q":
                                    self.stt(QKV[nm][:, t0:t0 + TW], sl_[:], 0.125, rn[:], ALU.mult, ALU.mult, [sl_, rn], [QKV[nm]])
                                else:
                                    self.tt("pool", QKV[nm][:, t0:t0 + TW], sl_[:], rn[:], ALU.mult, [sl_, rn], [QKV[nm]])
                    c.barrier()
                for d in range(2):
                    self.memset("pool", st["Zf"][:], 0.0, [st["Zf"]])
                    self.memset("pool", st["Zb"][:], 0.0, [st["Zb"]])
                    def prep(sg, S, d=d, pr=pr):
                        tau0 = sg * SG
                        PAR, BDZA, BDK, BDSK, BDV, ZRt, Ms, Mi, Ml = S["PAR"], S["ZA"], S["K"], S["SK"], S["V"], S["ZR"], S["Ms"], S["Mi"], S["Ml"]

                        def sa(t):
                            return t[:, tau0:tau0 + SG] if d == 0 else rev(t[:, Tn - tau0 - SG:Tn - tau0])
                        p_ = nextp()
                        self.proj(p_, Wg, 0, 8, tau0, SG, d=d)
                        self.act(b8[:], p_[0:8, 0:SG], AF.Sigmoid, [p_], [b8])
                        p_ = nextp()
                        self.proj(p_, Wg, 8, 8, tau0, SG, d=d)
                        self.act(g8[:], p_[0:8, 0:SG], AF.Exp, [p_, dtb], [g8], bias=dtb[:, 0:1])
                        self.act(g8[:], g8[:], AF.Ln, [g8], [g8], bias=1.0)
                        self.ts("dve", g8[:], g8[:], nA[:, 0:1], None, ALU.mult, None, [g8, nA], [g8])
                        yield
                        p_ = nextp()
                        self.mm(p_[:, 0:SG], sg_t[:, d * 2 + pr, :], g8[:], [sg_t, g8], [p_])
                        self.cp("act", F["g"][:], p_[:, 0:SG], [p_], [F["g"]])
                        p_ = nextp()
                        self.mm(p_[:, 0:SG], sg_t[:, d * 2 + pr, :], b8[:], [sg_t, b8], [p_])
                        self.cp("dve", F["b"][:], p_[:, 0:SG], [p_], [F["b"]])
                        yield
                        self.c.op("dve", lambda: nc.vector.tensor_tensor_scan(out=F["G"][:], data0=ones[:], data1=F["g"][:], initial=0.0,
                                                                                op0=ALU.mult, op1=ALU.add), [ones, F["g"]], [F["G"]])
                        if nch > 1:
                            self.cp("pool", Gs[:, 1:nch], F["G"][:, 63:SG - 1:64], [F["G"]], [Gs])
                        self.tt("pool", V3(F["gc"][:], nch), V3(F["G"][:], nch), bc_last(Gs[:], 64), ALU.subtract, [F["G"], Gs], [F["gc"]])
                        yield
                        self.act(F["E1"][:], F["gc"][:], AF.Exp, [F["gc"]], [F["E1"]])
                        self.cp("pool", S["g"][:], F["E1"][:, 63:SG:64], [F["E1"]], [S["g"]])
                        self.tt("pool", V3(F["t0"][:], nch), bc_last(F["gc"][:, 63:SG:64], 64), V3(F["gc"][:], nch), ALU.subtract, [F["gc"]], [F["t0"]])
                        self.act(F["Es"][:], F["t0"][:], AF.Exp, [F["t0"]], [F["Es"]])
                        yield
                        self.tt("pool", V3(F["t0"][:], nch), V3(F["gc"][:], nch), bc_mid(kc[:, K_SEL:K_SEL + 64], nch), ALU.mult, [F["gc"], kc], [F["t0"]])
                        self.c.op("dve", lambda: nc.vector.reduce_sum(out=gcT[:], in_=V3(F["t0"][:], nch), axis=AX.X), [F["t0"]], [gcT])
                        self.tt("dve", V3(F["t0"][:], nch), V3(F["gc"][:], nch), bc_last(gcT[:], 64), ALU.subtract, [F["gc"], gcT], [F["t0"]])
                        yield
                        self.ts("dve", F["nbk"][:], F["t0"][:], 0.0, None, ALU.max, None, [F["t0"]], [F["nbk"]])
                        self.ts("dve", F["t0"][:], F["t0"][:], 0.0, None, ALU.min, None, [F["t0"]], [F["t0"]])
                        self.act(F["Ex"][:], F["t0"][:], AF.Exp, [F["t0"]], [F["Ex"]])
                        yield
                        self.tt("dve", Ms[:], V3(F["Ex"][:], nch), bc_mid(kc[:, K_MUS:K_MUS + 64], nch), ALU.mult, [F["Ex"], kc], [Ms])
                        self.tt("dve", Mi[:], V3(F["Ex"][:], nch), bc_mid(kc[:, K_MUI:K_MUI + 64], nch), ALU.mult, [F["Ex"], kc], [Mi])
                        yield
                        self.act(F["Ex"][:], F["nbk"][:], AF.Exp, [F["nbk"]], [F["Ex"]], scale=-1.0)
                        self.tt("dve", Ml[:], V3(F["Ex"][:], nch), bc_mid(kc[:, K_MLS:K_MLS + 64], nch), ALU.mult, [F["Ex"], kc], [Ml])
                        yield
                        kS, qS, vS = sa(QKV["k"]), sa(QKV["q"]), sa(QKV["v"])
                        self.stt(F["nbk"][:], kS, -1.0, F["b"][:], ALU.mult, ALU.mult, [QKV["k"], F["b"]], [F["nbk"]])
                        self.to_bd(lambda o_ap, lo, hi: self.cp("act", o_ap, V3(F["nbk"][lo:hi, :], nch), [F["nbk"]], [PAR]), PAR, nch)
                        yield
                        self.to_bd(lambda o_ap, lo, hi: self.tt("dve", o_ap, V3(F["nbk"][lo:hi, :], nch), V3(F["E1"][lo:hi, :], nch), ALU.mult,
                                                                [F["nbk"], F["E1"]], [BDZA]), BDZA, nch)
                        self.cp("pool", PAR[:, :, 128:192], V3(qS, nch) if d == 0 else qS.rearrange("p (a b) -> p a b", a=nch), [QKV["q"]], [PAR])
                        yield
                        self.tt("pool", V3(ZRt[:], nch), PAR[:, :, 128:192], V3(F["E1"][:], nch), ALU.mult, [PAR, F["E1"]], [ZRt])
                        self.cp("act", F["t0"][:], kS, [QKV["k"]], [F["t0"]])
                        yield
                        self.to_bd(lambda o_ap, lo, hi: self.cp("act", o_ap, V3(F["t0"][lo:hi, :], nch), [F["t0"]], [BDK]), BDK, nch)
                        self.to_bd(lambda o_ap, lo, hi: self.tt("dve", o_ap, V3(F["t0"][lo:hi, :], nch), V3(F["Es"][lo:hi, :], nch), ALU.mult,
                                                                [F["t0"], F["Es"]], [BDSK]), BDSK, nch)
                        yield
                        self.tt("dve", F["nbk"][:], vS, F["b"][:], ALU.mult, [QKV["v"], F["b"]], [F["nbk"]])
                        self.to_bd(lambda o_ap, lo, hi: self.cp("act", o_ap, V3(F["nbk"][lo:hi, :], nch), [F["nbk"]], [BDV]), BDV, nch)
                        yield

                    def mk_o(sg, S, d=d):
                        tau0 = sg * SG
                        PAR, BDZA, BDK, BDSK, BDV, ZRt, Ms, Mi, Ml, G_ = (S["PAR"], S["ZA"], S["K"], S["SK"], S["V"], S["ZR"], S["Ms"], S["Mi"],
                                                                          S["Ml"], S["g"])

                        def yout(ci, pY):
                            a0_ = tau0 + ci * 64
                            if d == 0:
                                self.cp("dve", Yacc[:, a0_:a0_ + 64], pY[:], [pY], [Yacc])
                            else:
                                ry = rev(Yacc[:, Tn - a0_ - 64:Tn - a0_])
                                self.tt("dve", ry, ry, pY[:], ALU.add, [Yacc, pY], [Yacc])
                        return dict(PAR=lambda ci: (PAR[:, ci, :], PAR), QB=lambda ci: (BDK[:, ci, :], BDK), QK=lambda ci: (BDK[:, ci, :], BDK),
                                    ZA=lambda ci: (BDZA[:, ci, :], BDZA), ZR=lambda ci: (ZRt[:, ci * 64:ci * 64 + 64], ZRt),
                                    SB=lambda ci: (BDSK[:, ci, :], BDSK), SK=lambda ci: (BDSK[:, ci, :], BDSK), V=lambda ci: (BDV[:, ci, :], BDV),
                                    ms=lambda ci: (Ms[:, ci, :], Ms), mi=lambda ci: (Mi[:, ci, :], Mi), ml=lambda ci: (Ml[:, ci, :], Ml),
                                    gZ=lambda ci: (G_[:, ci:ci + 1], G_), gS=lambda ci: None, yout=yout, same=True)

                    for _ in prep(0, OS[0]):
                        pass
                    for sg in range(NSG):
                        side = prep(sg + 1, OS[(sg + 1) % 2]) if sg + 1 < NSG else None
                        self.dplr_run(sets, st, mk_o(sg, OS[sg % 2]), nch, side=side)
                with ExitStack() as e2:
                    Wz = c.sb([128, 8, 128], BF16, "Wz", e2)
                    stz = c.sb([128, 8, 128], F32, "stz", e2)
                    self.load_w(Wz, 0, self.win(l, C_Z + 768 + pr * 128, 128), 128, stz)
                    ob = [c.sb([128, SG], BF16, "gob", e2) for _ in range(2)]
                    for sg in range(NSG):
                        t0 = sg * SG
                        Y = Yacc[:, t0:t0 + SG]
                        self.tt("pool", F["t0"][:], Y, Y, ALU.mult, [Yacc], [F["t0"]])
                        p_ = nextp()
                        self.mm(p_[:, 0:SG], BOf, F["t0"][:], [kc, F["t0"]], [p_])
                        self.rsqrt_(F["Ex"][:], p_[:, 0:SG], 1.0 / 64, self.epsc[:, 0:1], [p_, self.epsc], [F["Ex"]])
                        self.stt(F["t0"][:], Y, gnw[:, 0:1], F["Ex"][:], ALU.mult, ALU.mult, [Yacc, gnw, F["Ex"]], [F["t0"]])
                        p_ = nextp()
                        self.proj(p_, Wz, 0, 128, t0, SG)
                        self.act(F["Ex"][:], p_[:, 0:SG], AF.Silu, [p_], [F["Ex"]])
                        o_b = ob[sg % 2]
                        self.tt("pool", o_b[:], F["t0"][:], F["Ex"][:], ALU.mult, [F["t0"], F["Ex"]], [o_b])
                        r0 = 768 + pr * 128
                        self.c.dma("sp", self.mixT.h[r0:r0 + 128, t0:t0 + SG], o_b[:], o_b, reads=[o_b], writes=[self.mixT])
                    c.barrier()
            c.barrier()

    def ssd(self, l, s):
        c, nc, Tn = self.c, self.nc, self.Tn
        L = 128
        NC_ = Tn // L
        TW = min(512, Tn)
        kc, kb = self.kc, self.kb
        IDb = kb[:, K_ID:K_ID + 128]
        w = self.w
        with ExitStack() as es:
            XC = [c.sb([128, Tn], BF16, "xc%d" % j, es) for j in range(6)]
            Yacc = [c.sb([128, Tn], F32, "Yc%d" % p, es) for p in range(2)]
            banks = [c.ps([128, 512], F32, "sbk", es) for _ in range(6)]

            def sl(b, lo, n):
                t = c.sub(banks[b][:, lo:lo + n])
                t.dep = banks[b].dep
                return t
            pY = [sl(0, 0, 128), sl(1, 0, 128)]
            pCB = [sl(2, 0, 128), sl(2, 128, 128)]
            pRep4 = c.sub(V3(banks[3][:, 0:512], 4))
            pRep4.dep = banks[3].dep
            pTb = c.sub(V3(banks[4][:, 0:256].bitcast(BF16), 4))
            pTb.dep = banks[4].dep
            pTf = sl(4, 256, 16)
            pZU4 = c.sub(V3(banks[5][:, 0:512], 4))
            pZU4.dep = banks[5].dep
            pPj = sl(4, 384, 128)
            pbig = [banks[3], banks[5]]
            with ExitStack() as e1:
                Wx = c.sb([128, 8, 768], BF16, "Wx", e1)
                stg = c.sb([128, 8, 128], F32, "stgx", e1)
                cw = c.sb([128, 6, 4], F32, "cwx", e1)
                cb = c.sb([128, 6], F32, "cbx", e1)
                self.ld(cb, cb[:], w["m_conv_b"].h[l, :].rearrange("(j p) -> p j", p=128), allow_slow_non_contiguous=True)
                for j in range(6):
                    self.load_w(Wx, j * 128, self.win(l, C_XBC + j * 128, 128), 128, stg)
                    self.ld(cw, cw[:, j, :], w["m_conv_w"].h[l, :, j * 128:(j + 1) * 128].rearrange("i p -> p i"), allow_slow_non_contiguous=True)
                raw = c.sb([128, Tn + 3], BF16, "rawx", e1)
                self.memset("pool", raw[:, 0:1], 0.0, [raw])
                self.memset("pool", raw[:, Tn + 1:Tn + 3], 0.0, [raw])
                acc = c.sb([128, TW], F32, "accx", e1)
                for j in range(6):
                    for tt_ in range(Tn // TW):
                        p_ = pbig[tt_ % 2]
                        self.proj(p_, Wx, j * 128, 128, tt_ * TW, TW)
                        self.cp(self.eng2(), raw[:, 1 + tt_ * TW:1 + (tt_ + 1) * TW], p_[:, 0:TW], [p_], [raw])
                    for tt_ in range(Tn // TW):
                        t0 = tt_ * TW
                        self.ts("dve", acc[:], raw[:, t0:t0 + TW], cw[:, j, 0:1], cb[:, j:j + 1], ALU.mult, ALU.add, [raw, cw, cb], [acc])
                        for i in range(1, 4):
                            self.stt(acc[:], raw[:, t0 + i:t0 + i + TW], cw[:, j, i:i + 1], acc[:], ALU.mult, ALU.add, [raw, cw, acc], [acc])
                        self.act(XC[j][:, t0:t0 + TW], acc[:], AF.Silu, [acc], [XC[j]])
                c.barrier()
            Wd = c.sb([128, 8, 8], BF16, "Wd", es)
            with ExitStack() as e1:
                stg = c.sb([128, 8, 8], F32, "stgd", e1)
                self.load_w(Wd, 0, self.win(l, C_MDT, 8), 8, stg)
                c.barrier()
            sm_t = c.sb([8, 8, 128], F32, "selm", es)
            self.ld(sm_t, sm_t[:], self.selm.h[:])
            dtb = self.col(es, w["m_dt_bias"].h[l].rearrange("d h -> (d h)"), "mdtb", 8)
            nA = self.col(es, w["m_a_log"].h[l].rearrange("d h -> (d h)"), "mnA", 8)
            self.act(nA[:], nA[:], AF.Exp, [nA], [nA])
            self.ts("pool", nA[:], nA[:], -1.0, None, ALU.mult, None, [nA], [nA])
            ones8 = c.sb([8, L], F32, "ones8", es)
            self.memset("pool", ones8[:], 1.0, [ones8])
            dt8 = c.sb([8, L], F32, "dt8", es)
            a8 = c.sb([8, L], F32, "a8", es)
            acs8 = c.sb([8, L], F32, "acs8", es)
            TT_ = c.sb([128, 16], F32, "TT", es)
            xsT = c.sb([128, 256], BF16, "xsT", es)
            BT = c.sb([128, 2, 128], BF16, "BT", es)
            CBs = [c.sb([128, 128], F32, "CBs", es) for _ in range(2)]
            dd = c.sb([128, 128], F32, "dd", es)
            Em = c.sb([128, 128], F32, "Em", es)
            SD = [c.sb([128, 128], BF16, "SD", es) for _ in range(4)]
            Er = c.sb([128, 128], F32, "Er", es)
            Ce = [c.sb([128, 128], BF16, "Ce", es) for _ in range(4)]
            cols = c.sb([128, 4, 4], F32, "cols", es)
            Xp = [c.sb([128, 128], BF16, "Xp%d" % i, es) for i in range(4)]
            Xe = [c.sb([128, 128], BF16, "Xe%d" % i, es) for i in range(4)]
            Zf = [c.sb([128, 128], F32, "sZf%d" % h, es) for h in range(4)]
            Zb = [c.sb([128, 128], BF16, "sZb%d" % h, es) for h in range(4)]
            for i in range(4):
                self.memset("pool", Xp[i][:], 0.0, [Xp[i]])
                self.memset("pool", Xe[i][:], 0.0, [Xe[i]])
            for d in range(2):
                for h in range(4):
                    self.memset("pool", Zf[h][:], 0.0, [Zf[h]])
                    self.memset("pool", Zb[h][:], 0.0, [Zb[h]])
                mask = kc[:, K_MU128:K_MU128 + 128] if d == 0 else kc[:, K_ML128:K_ML128 + 128]
                last = L - 1 if d == 0 else 0
                order = range(NC_) if d == 0 else range(NC_ - 1, -1, -1)
                for ci in order:
                    t0 = ci * L
                    tk = slice(t0, t0 + L)
                    self.proj(pPj, Wd, 0, 8, t0, L)
                    self.act(dt8[:], pPj[0:8, 0:L], AF.Exp, [pPj, dtb], [dt8], bias=dtb[:, 0:1])
                    self.act(dt8[:], dt8[:], AF.Ln, [dt8], [dt8], bias=1.0)
                    self.ts("dve", a8[:], dt8[:], nA[:, 0:1], None, ALU.mult, None, [dt8, nA], [a8])
                    if d == 0:
                        self.c.op("dve", lambda: nc.vector.tensor_tensor_scan(out=acs8[:], data0=ones8[:], data1=a8[:], initial=0.0,
                                                                                op0=ALU.mult, op1=ALU.add), [ones8, a8], [acs8])
                    else:
                        self.c.op("dve", lambda: nc.vector.tensor_tensor_scan(out=rev(acs8[:]), data0=ones8[:], data1=rev(a8[:]), initial=0.0,
                                                                                op0=ALU.mult, op1=ALU.add), [ones8, a8], [acs8])
                    self.mm(pTf[:, 0:8], dt8[:], kc[0:8, K_ID:K_ID + 8], [dt8, kc], [pTf])
                    self.mm(pTf[:, 8:16], acs8[:], kc[0:8, K_ID:K_ID + 8], [acs8, kc], [pTf])
                    self.cp("act", TT_[:], pTf[:], [pTf], [TT_])
                    for g in range(2):
                        self.mm(pCB[g][:], XC[2 + g][:, tk], XC[4 + g][:, tk], [XC[2 + g], XC[4 + g]], [pCB[g]])
                        self.cp("act", CBs[g][:], pCB[g][:], [pCB[g]], [CBs[g]])
                    for j in range(2):
                        self.tr(pTb[:, j, :], XC[j][:, tk], IDb, [XC[j], kb], [pTb])
                        self.tr(pTb[:, 2 + j, :], XC[2 + j][:, tk], IDb, [XC[2 + j], kb], [pTb])
                    self.cp("dve", V3(xsT[:], 2), pTb[:, 0:2, :], [pTb], [xsT])
                    self.cp("act", BT[:], pTb[:, 2:4, :], [pTb], [BT])
                    for h in range(4):
                        self.mm(pRep4[:, h, :], sm_t[:, d * 4 + h, :], acs8[:], [sm_t, acs8], [pRep4])
                    for h in range(4):
                        g, pr, hh = h // 2, h // 2, h % 2
                        dh = d * 4 + h
                        self.ts("dve", dd[:], pRep4[:, h, :], TT_[:, 8 + dh:9 + dh], 0.0, ALU.subtract, ALU.min, [pRep4, TT_], [dd])
                        self.act(Em[:], dd[:], AF.Exp, [dd], [Em])
                        self.tt("pool", Em[:], Em[:], mask, ALU.mult, [Em, kc], [Em])
                        self.tt("pool", SD[h][:], CBs[g][:], Em[:], ALU.mult, [CBs[g], Em], [SD[h]])
                        self.act(Er[:], pRep4[:, h, :], AF.Exp, [pRep4], [Er])
                        self.tt("pool", Ce[h][:], XC[4 + g][:, tk], Er[:], ALU.mult, [XC[4 + g], Er], [Ce[h]])
                        self.cp("dve", cols[:, h, 0:1], pRep4[:, h, last:last + 1], [pRep4], [cols])
                        self.act(cols[:, h, 1:2], TT_[:, 8 + dh:9 + dh], AF.Exp, [TT_, cols], [cols], scale=-1.0, bias=cols[:, h, 0:1])
                        self.act(cols[:, h, 2:3], cols[:, h, 0:1], AF.Exp, [cols], [cols])
                        xs_h = xsT[:, h * 64:(h + 1) * 64]
                        self.ts("dve", Xp[h][:, hh * 64:(hh + 1) * 64], xs_h, TT_[:, dh:dh + 1], None, ALU.mult, None, [xsT, TT_], [Xp[h]])
                        self.ts("dve", Xe[h][:, hh * 64:(hh + 1) * 64], Xp[h][:, hh * 64:(hh + 1) * 64], cols[:, h, 1:2], None, ALU.mult, None,
                                [Xp[h], cols], [Xe[h]])
                    for h in range(4):
                        g, pr, hh = h // 2, h // 2, h % 2
                        self.mm(pY[pr][:], Xp[h][:], SD[h][:], [Xp[h], SD[h]], [pY[pr]], start=(hh == 0), stop=False)
                        self.mm(pY[pr][:], Zb[h][:], Ce[h][:], [Zb[h], Ce[h]], [pY[pr]], start=False, stop=(hh == 1))
                    for h in range(4):
                        self.mm(pZU4[:, h, :], BT[:, h // 2, :], Xe[h][:], [BT, Xe[h]], [pZU4])
                    for pr in range(2):
                        if d == 0:
                            self.cp("act", Yacc[pr][:, tk], pY[pr][:], [pY[pr]], [Yacc[pr]])
                        else:
                            self.tt("dve", Yacc[pr][:, tk], Yacc[pr][:, tk], pY[pr][:], ALU.add, [Yacc[pr], pY[pr]], [Yacc[pr]])
                    for h in range(4):
                        self.stt(Zf[h][:], Zf[h][:], cols[:, h, 2:3], pZU4[:, h, :], ALU.mult, ALU.add, [Zf[h], cols, pZU4], [Zf[h]])
                        self.cp("act", Zb[h][:], Zf[h][:], [Zf[h]], [Zb[h]])
            with ExitStack() as e2:
                Wz = c.sb([128, 8, 256], BF16, "Wzc", e2)
                stz = c.sb([128, 8, 256], F32, "stzc", e2)
                self.load_w(Wz, 0, self.win(l, C_Z + 512, 256), 256, stz)
                Dc = c.sb([128, 2], F32, "Dc", e2)
                nw = c.sb([128, 2], F32, "nw", e2)
                for pr in range(2):
                    for hh in range(2):
                        self.ld(Dc, Dc[hh * 64:(hh + 1) * 64, pr:pr + 1], w["m_d"].h[l, pr * 2 + hh:pr * 2 + hh + 1].partition_broadcast(64))
                self.ld(nw, nw[:], w["m_norm_w"].h[l, :].rearrange("(j p) -> p j", p=128), allow_slow_non_contiguous=True)
                gt = [c.sb([128, TW], F32, "gt%d" % p, e2) for p in range(2)]
                sq = c.sb([128, TW], F32, "sqc", e2)
                sz = c.sb([128, TW], F32, "szc", e2)
                rs = c.sb([128, TW], F32, "rsc", e2)
                ob = [c.sb([128, 2, TW], BF16, "cob", e2) for _ in range(2)]
                for tt_ in range(Tn // TW):
                    t0 = tt_ * TW
                    for pr in range(2):
                        self.stt(gt[pr][:], XC[pr][:, t0:t0 + TW], Dc[:, pr:pr + 1], Yacc[pr][:, t0:t0 + TW], ALU.mult, ALU.add,
                                 [XC[pr], Dc, Yacc[pr]], [gt[pr]])
                        p_ = banks[2 + pr]
                        self.proj(p_, Wz, pr * 128, 128, t0, TW)
                        self.act(sz[:], p_[:, 0:TW], AF.Silu, [p_], [sz])
                        self.tt("pool", gt[pr][:], gt[pr][:], sz[:], ALU.mult, [gt[pr], sz], [gt[pr]])
                        self.tt("pool", sq[:], gt[pr][:], gt[pr][:], ALU.mult, [gt[pr]], [sq])
                        self.mm(banks[0][:, 0:TW], kc[:, K_ONE:K_ONE + 128], sq[:], [kc, sq], [banks[0]], start=(pr == 0), stop=(pr == 1))
                    self.rsqrt_(rs[:], banks[0][:, 0:TW], 1.0 / 256, self.epsc[:, 0:1], [banks[0], self.epsc], [rs])
                    o_b = ob[tt_ % 2]
                    for pr in range(2):
                        self.stt(o_b[:, pr, :], gt[pr][:], nw[:, pr:pr + 1], rs[:], ALU.mult, ALU.mult, [gt[pr], nw, rs], [o_b])
                    self.c.dma("sp", self.mixT.h[512:768, t0:t0 + TW].rearrange("(a p) t -> p a t", p=128), o_b[:], o_b, reads=[o_b], writes=[self.mixT])
                c.barrier()
            c.barrier()

    def zero_mix(self, i):
        c, Tn = self.c, self.Tn
        TW = min(512, Tn)
        with ExitStack() as es:
            z = c.sb([128, 2, TW], BF16, "zmix", es)
            self.memset("pool", z[:], 0.0, [z])
            for j in range(Tn // TW):
                self.c.dma("sp", self.mixT.h[i * 256:(i + 1) * 256, j * TW:(j + 1) * TW].rearrange("(a p) t -> p a t", p=128), z[:], z,
                           reads=[z], writes=[self.mixT])
            c.barrier()

    def build(self):
        self.declare()
        self.consts()
        for l in range(self.DEPTH):
            self.p0(l)
            for s in range(self.NS):
                self.p1(l, s)
                if "A" in self.mixers:
                    self.mla(l, s)
                if "B" in self.mixers:
                    self.rwkv(l, s)
                if "C" in self.mixers:
                    self.ssd(l, s)
                if "D" in self.mixers:
                    self.gdn(l, s)
                for i, m in enumerate("ABCD"):
                    if m not in self.mixers:
                        self.zero_mix(i)
                self.p6(l, s)
        self.c.finish([self.y] + ([self.mixT] if self.dbg else []))
        return self.nc


def make_consts(Tn):
    p = np.arange(128)
    kc = np.zeros((128, K_END), np.float32)
    kc[:, K_ID:K_ID + 128] = np.eye(128)
    col = np.arange(64)
    kc[:, K_MUS:K_MUS + 64] = ((p % 64)[:, None] < col[None, :])
    kc[:, K_MUI:K_MUI + 64] = ((p % 64)[:, None] <= col[None, :])
    kc[:, K_BO:K_BO + 128] = ((p // 64)[:, None] == (p // 64)[None, :])
    kc[:, K_ONE:K_ONE + 128] = 1.0
    kc[:, K_SEL:K_SEL + 64] = ((p % 64)[:, None] == col[None, :])
    kc[:, K_MU128:K_MU128 + 128] = (p[:, None] <= p[None, :])
    kc[:, K_ML128:K_ML128 + 128] = (p[:, None] >= p[None, :])
    kc[:, K_MLS:K_MLS + 64] = ((p % 64)[:, None] > col[None, :])
    inv = 1.0 / (10000.0 ** (np.arange(0, 32, 2, dtype=np.float32) / np.float32(32)))
    ang = np.arange(Tn, dtype=np.float32)[None, :] * inv.astype(np.float32)[:, None]
    rope = np.zeros((96, 2, Tn), np.float32)
    rope[0:64, 0, :] = 1.0
    rope[64:80, 0, :] = np.cos(ang)
    rope[80:96, 0, :] = np.cos(ang)
    rope[64:80, 1, :] = np.sin(ang)
    rope[80:96, 1, :] = np.sin(ang)
    selg = np.zeros((8, 4, 128), np.float32)
    for d in range(2):
        for pr in range(2):
            for hh in range(2):
                selg[d * 4 + pr * 2 + hh, d * 2 + pr, hh * 64:(hh + 1) * 64] = 1.0
    selm = np.zeros((8, 8, 128), np.float32)
    for r in range(8):
        selm[r, r, :] = 1.0
    return dict(kconst=kc, rope=rope, selg=selg, selm=selm)


WNAMES = ["w_ada", "b_ada", "g_pre", "g_post", "w_in", "w_out", "a_g_q", "a_w_uq", "a_g_kv", "a_w_ukv", "rw_mu", "rw_w0",
          "rw_w2", "rw_a0", "rw_a2", "rw_k_k", "rw_k_a", "rw_r_k", "rw_ln_w", "rw_ln_b", "m_conv_w", "m_conv_b",
          "m_dt_bias", "m_a_log", "m_d", "m_norm_w", "gd_conv_w", "gd_dt_bias", "gd_a_log", "gd_norm_w"]


def kernel(x_prompt, x_sample, c_prompt, c_sample, **weights):
    x_prompt = np.asarray(x_prompt, np.float32)
    x_sample = np.asarray(x_sample, np.float32)
    Tn = x_prompt.shape[1]
    DEPTH = np.asarray(weights["w_ada"]).shape[0]
    xs = np.concatenate([x_prompt, x_sample], axis=0)
    cs = np.concatenate([np.asarray(c_prompt, np.float32), np.asarray(c_sample, np.float32)], axis=0)
    nseq = xs.shape[0]
    NS = -(-nseq // N_CORES)
    b = Builder(Tn, NS, DEPTH)
    nc = b.build()
    consts = make_consts(Tn)
    wmap = {k: np.ascontiguousarray(np.asarray(weights[k], np.float32)) for k in WNAMES}
    in_maps = []
    for core in range(N_CORES):
        idx = [(core * NS + j) if (core * NS + j) < nseq else 0 for j in range(NS)]
        m = dict(wmap)
        m.update(consts)
        m["x"] = np.ascontiguousarray(xs[idx])
        m["c"] = np.ascontiguousarray(cs[idx])
        in_maps.append(m)
    res = run_bass_kernel_spmd(nc, in_maps, core_ids=list(range(N_CORES)))
    out = np.zeros_like(xs)
    for core in range(N_CORES):
        yc = res.results[core]["y"]
        for j in range(NS):
            sidx = core * NS + j
            if sidx < nseq:
                out[sidx] = yc[j]
    nb = x_prompt.shape[0]
    return (out[:nb], out[nb:])
```
